# Optimizing a Trainium2 kernel written in Bass

```python
import math
import jax, jax.numpy as jnp
from jax import lax
import numpy as np

D_MODEL = 1024
BATCH = 8
SEQ = 2048
DEPTH = 1
DEC_BATCH = 8
DEC_SEQ = 16
PAST_LEN = 4096

CHUNK = 64
A_HEADS = 4
A_DK = 128
A_DV = 128
A_WIDTH = A_HEADS * A_DV
CONV_W = 4
GDN_CHUNK = CHUNK
B_HEADS = 4
B_HD = 64
B_WIDTH = B_HEADS * B_HD
IDX_HEADS = 8
IDX_HD = 32
IDX_SCALE = (IDX_HEADS ** -0.5) * (IDX_HD ** -0.5)
TOPK_MAX = 256
Q_BLOCK = 128
N_MEM = 256
M_HEADS = 4
M_HD = 64
M_WIDTH = M_HEADS * M_HD
D_MIX = A_WIDTH + B_WIDTH + M_WIDTH
EPS = 1e-6
IN_SPLITS = (3 * A_WIDTH, A_WIDTH, A_HEADS, A_HEADS,
             B_WIDTH, B_WIDTH, B_WIDTH, B_WIDTH, IDX_HEADS * IDX_HD, IDX_HD, IDX_HEADS,
             M_WIDTH, M_WIDTH)
IN_WIDTH = sum(IN_SPLITS)

kernel_name = 'hybrid_gdn_dsa_mem_stream_step'


def rms_norm(x, g):
    xf = x.astype(jnp.float32)
    y = xf * lax.rsqrt(jnp.mean(xf * xf, axis=-1, keepdims=True) + EPS)
    return (y * g.astype(jnp.float32)).astype(x.dtype)


def l2_norm(x):
    return x * lax.rsqrt(jnp.sum(x * x, axis=-1, keepdims=True) + EPS)


def split_cols(p):
    offs = [int(o) for o in np.cumsum(IN_SPLITS)[:-1]]
    return jnp.split(p, offs, axis=-1)


def causal_conv(buf, x, w):
    xp = jnp.concatenate([buf.astype(x.dtype), x], axis=1)
    t_len = x.shape[1]
    y = xp[:, 0:t_len] * w[0]
    for j in range(1, CONV_W):
        y = y + xp[:, j:j + t_len] * w[j]
    return jax.nn.silu(y), xp[:, -(CONV_W - 1):]


def gated_delta(q, k, v, g, beta, s0, chunk):
    bsz, t_len, nh, dk = q.shape
    dv = v.shape[-1]
    n = t_len // chunk

    def blk(t):
        return jnp.moveaxis(t.reshape(bsz, n, chunk, nh, *t.shape[3:]), 3, 2)

    q, k, v, g, beta = blk(q), blk(k), blk(v), blk(g), blk(beta)
    gc = jnp.cumsum(g, axis=-1)
    tri = jnp.tril(jnp.ones((chunk, chunk), bool))
    strict = tri & ~jnp.eye(chunk, dtype=bool)
    diff = gc[..., :, None] - gc[..., None, :]
    decay = jnp.where(tri, jnp.exp(jnp.where(tri, diff, 0.0)), 0.0)
    kb = k * beta[..., None]
    lmat = jnp.where(strict, jnp.einsum('bnhid,bnhjd->bnhij', kb, k) * decay, 0.0)
    amat = lmat + jnp.eye(chunk, dtype=jnp.float32)
    rhs = jnp.concatenate([v * beta[..., None], kb * jnp.exp(gc)[..., None]], axis=-1)
    sol = lax.linalg.triangular_solve(amat, rhs, left_side=True, lower=True)
    u0, wk = sol[..., :dv], sol[..., dv:]
    qk = jnp.einsum('bnhid,bnhjd->bnhij', q, k) * decay
    q_dec = q * jnp.exp(gc)[..., None]
    k_dec = k * jnp.exp(gc[..., -1:] - gc)[..., None]
    last = jnp.exp(gc[..., -1])

    def step(s, xs):
        qk_c, qd_c, kd_c, u0_c, w_c, last_c = xs
        u = u0_c - jnp.einsum('bhcd,bhde->bhce', w_c, s)
        o = jnp.einsum('bhcd,bhde->bhce', qd_c, s) + jnp.einsum('bhij,bhje->bhie', qk_c, u)
        s = s * last_c[..., None, None] + jnp.einsum('bhcd,bhce->bhde', kd_c, u)
        return s, o

    xs = tuple(jnp.moveaxis(t, 1, 0) for t in (qk, q_dec, k_dec, u0, wk, last))
    s_fin, o = lax.scan(step, s0, xs)
    o = jnp.transpose(o, (1, 0, 3, 2, 4)).reshape(bsz, t_len, nh, dv)
    return o, s_fin


def sparse_block(q, qi, wi, q_pos, k, v, ki, k_pos, n_sel):
    f32 = jnp.float32
    adm = (k_pos[None, :] // CHUNK) <= (q_pos[:, None] // CHUNK)
    logits = jnp.einsum('bqhd,bsd->bqhs', qi.astype(f32), ki.astype(f32))
    score = jnp.einsum('bqh,bqhs->bqs', wi.astype(f32), jax.nn.relu(logits))
    score = jnp.where(adm[None], score, -jnp.inf)
    _, idx = lax.top_k(score, n_sel)
    valid = (k_pos[idx] // CHUNK) <= (q_pos[None, :, None] // CHUNK)
    ks = jax.vmap(lambda a, i: a[i])(k, idx)
    vs = jax.vmap(lambda a, i: a[i])(v, idx)
    s = jnp.einsum('bqhd,bqkhd->bqhk', q.astype(f32), ks.astype(f32)) * (B_HD ** -0.5)
    s = jnp.where(valid[:, :, None, :], s, -jnp.inf)
    p = jax.nn.softmax(s, axis=-1)
    return jnp.einsum('bqhk,bqkhd->bqhd', p, vs.astype(f32))


def sparse_attention(q, qi, wi, q_pos, k, v, ki, k_pos, n_sel):
    bsz, t_len = q.shape[0], q.shape[1]
    if t_len <= Q_BLOCK:
        return sparse_block(q, qi, wi, q_pos, k, v, ki, k_pos, n_sel)
    nb = t_len // Q_BLOCK

    def blocks(t):
        return jnp.moveaxis(t.reshape(bsz, nb, Q_BLOCK, *t.shape[2:]), 1, 0)

    xs = (blocks(q), blocks(qi), blocks(wi), q_pos.reshape(nb, Q_BLOCK))
    o = lax.map(lambda a: sparse_block(a[0], a[1], a[2], a[3], k, v, ki, k_pos, n_sel), xs)
    return jnp.moveaxis(o, 0, 1).reshape(bsz, t_len, *o.shape[3:])


def memory_kv(mem, g_mem, w_mem_kv, g_km):
    bsz = mem.shape[0]
    m = rms_norm(mem, g_mem) @ w_mem_kv
    mk, mv = jnp.split(m, 2, axis=-1)
    mk = rms_norm(mk.reshape(bsz, N_MEM, M_HEADS, M_HD), g_km)
    mv = mv.reshape(bsz, N_MEM, M_HEADS, M_HD)
    return mk, mv


def layer(x, mem_k, mem_v, conv_buf, s0, past_k, past_v, past_ki, gdn_chunk,
          g_in, w_in, conv_w, a_log, dt_bias, g_o, g_qb, g_kb, g_ki, g_qm, w_out):
    f32 = jnp.float32
    bsz, t_len, _ = x.shape
    p_len = past_k.shape[1]
    h = rms_norm(x, g_in)
    (qkv_a, z_a, b_a, a_a, q_b, k_b, v_b, z_b, q_i, k_i, w_i, q_m, z_m) = split_cols(h @ w_in)

    conv_out, conv_new = causal_conv(conv_buf, qkv_a, conv_w)
    q_a, k_a, v_a = jnp.split(conv_out.astype(f32), 3, axis=-1)
    q_a = l2_norm(q_a.reshape(bsz, t_len, A_HEADS, A_DK)) * (A_DK ** -0.5)
    k_a = l2_norm(k_a.reshape(bsz, t_len, A_HEADS, A_DK))
    v_a = v_a.reshape(bsz, t_len, A_HEADS, A_DV)
    beta = jax.nn.sigmoid(b_a.astype(f32))
    g_log = -jnp.exp(a_log.astype(f32)) * jax.nn.softplus(a_a.astype(f32) + dt_bias.astype(f32))
    o_a, s_new = gated_delta(q_a, k_a, v_a, g_log, beta, s0.astype(f32), gdn_chunk)
    o_a = rms_norm(o_a, g_o) * jax.nn.silu(z_a.astype(f32).reshape(bsz, t_len, A_HEADS, A_DV))
    o_a = o_a.reshape(bsz, t_len, A_WIDTH)

    q_b = rms_norm(q_b.reshape(bsz, t_len, B_HEADS, B_HD), g_qb)
    k_b = rms_norm(k_b.reshape(bsz, t_len, B_HEADS, B_HD), g_kb)
    v_b = v_b.reshape(bsz, t_len, B_HEADS, B_HD)
    k_i = rms_norm(k_i, g_ki)
    q_i = q_i.reshape(bsz, t_len, IDX_HEADS, IDX_HD)
    w_i = w_i * IDX_SCALE
    k_all = jnp.concatenate([past_k.astype(k_b.dtype), k_b], axis=1)
    v_all = jnp.concatenate([past_v.astype(v_b.dtype), v_b], axis=1)
    ki_all = jnp.concatenate([past_ki.astype(k_i.dtype), k_i], axis=1)
    q_pos = p_len + jnp.arange(t_len, dtype=jnp.int32)
    k_pos = jnp.arange(p_len + t_len, dtype=jnp.int32)
    n_sel = min(TOPK_MAX, (p_len + t_len) // 4)
    o_b = sparse_attention(q_b, q_i, w_i, q_pos, k_all, v_all, ki_all, k_pos, n_sel)
    o_b = o_b.reshape(bsz, t_len, B_WIDTH) * jax.nn.silu(z_b.astype(f32))

    q_m = rms_norm(q_m.reshape(bsz, t_len, M_HEADS, M_HD), g_qm).astype(f32)
    s_m = jnp.einsum('bqhd,bmhd->bhqm', q_m, mem_k.astype(f32)) * (M_HD ** -0.5)
    o_m = jnp.einsum('bhqm,bmhd->bqhd', jax.nn.softmax(s_m, axis=-1), mem_v.astype(f32))
    o_m = o_m.reshape(bsz, t_len, M_WIDTH) * jax.nn.silu(z_m.astype(f32))

    mix = jnp.concatenate([o_a, o_b, o_m], axis=-1).astype(x.dtype)
    y = x + mix @ w_out
    return y, conv_new, s_new.astype(x.dtype), k_b, v_b, k_i


def setup_inputs(seed: int = 0) -> dict:
    key = jax.random.key(seed)
    ks = jax.random.split(key, 32)
    f32 = jnp.float32

    def nrm(k, shape, scale=1.0):
        return jax.random.normal(k, shape, f32) * scale

    def gain(k, shape):
        return 1.0 + 0.02 * jax.random.normal(k, shape, f32)

    dt = jnp.exp(jax.random.uniform(ks[14], (DEPTH, A_HEADS), f32, math.log(1e-3), math.log(1e-1)))
    return {
        'x_prompt': nrm(ks[0], (BATCH, SEQ, D_MODEL)),
        'x_sample': nrm(ks[1], (DEC_BATCH, DEC_SEQ, D_MODEL)),
        'state_conv_A': nrm(ks[2], (DEPTH, DEC_BATCH, CONV_W - 1, 3 * A_WIDTH)),
        'state_ssm_A': nrm(ks[3], (DEPTH, DEC_BATCH, A_HEADS, A_DK, A_DV), 0.1),
        'cache_k_B': nrm(ks[4], (DEPTH, DEC_BATCH, PAST_LEN, B_HEADS, B_HD)),
        'cache_v_B': nrm(ks[5], (DEPTH, DEC_BATCH, PAST_LEN, B_HEADS, B_HD)),
        'cache_kidx_B': nrm(ks[6], (DEPTH, DEC_BATCH, PAST_LEN, IDX_HD)),
        'cache_mem_k': nrm(ks[7], (DEPTH, DEC_BATCH, N_MEM, M_HEADS, M_HD)),
        'cache_mem_v': nrm(ks[8], (DEPTH, DEC_BATCH, N_MEM, M_HEADS, M_HD)),
        'mem_prompt': nrm(ks[9], (BATCH, N_MEM, D_MODEL)),
        'g_in': gain(ks[10], (DEPTH, D_MODEL)),
        'w_in': nrm(ks[11], (DEPTH, D_MODEL, IN_WIDTH), D_MODEL ** -0.5),
        'conv_w_A': nrm(ks[12], (DEPTH, CONV_W, 3 * A_WIDTH), CONV_W ** -0.5),
        'a_log_A': jnp.log(jax.random.uniform(ks[13], (DEPTH, A_HEADS), f32, 1.0, 16.0)),
        'dt_bias_A': dt + jnp.log(-jnp.expm1(-dt)),
        'g_o_A': gain(ks[15], (DEPTH, A_DV)),
        'g_q_B': gain(ks[16], (DEPTH, B_HD)),
        'g_k_B': gain(ks[17], (DEPTH, B_HD)),
        'g_kidx_B': gain(ks[18], (DEPTH, IDX_HD)),
        'g_mem': gain(ks[19], (DEPTH, D_MODEL)),
        'w_mem_kv': nrm(ks[20], (DEPTH, D_MODEL, 2 * M_WIDTH), D_MODEL ** -0.5),
        'g_q_M': gain(ks[21], (DEPTH, M_HD)),
        'g_k_M': gain(ks[22], (DEPTH, M_HD)),
        'w_out': nrm(ks[23], (DEPTH, D_MIX, D_MODEL), D_MIX ** -0.5),
    }


def reference(x_prompt, x_sample, state_conv_A, state_ssm_A, cache_k_B, cache_v_B, cache_kidx_B,
              cache_mem_k, cache_mem_v, mem_prompt, g_in, w_in, conv_w_A, a_log_A, dt_bias_A, g_o_A,
              g_q_B, g_k_B, g_kidx_B, g_mem, w_mem_kv, g_q_M, g_k_M, w_out):
    bp = x_prompt.shape[0]
    dt_ = x_prompt.dtype
    zero_conv = jnp.zeros((bp, CONV_W - 1, 3 * A_WIDTH), dt_)
    zero_ssm = jnp.zeros((bp, A_HEADS, A_DK, A_DV), jnp.float32)
    empty_k = jnp.zeros((bp, 0, B_HEADS, B_HD), dt_)
    empty_ki = jnp.zeros((bp, 0, IDX_HD), dt_)
    yp, ys = x_prompt, x_sample
    pc, pS, pk, pv, pki, pmk, pmv = [], [], [], [], [], [], []
    sc, sS, sk, sv, ski = [], [], [], [], []
    for l in range(DEPTH):
        wts = (g_in[l], w_in[l], conv_w_A[l], a_log_A[l], dt_bias_A[l], g_o_A[l],
               g_q_B[l], g_k_B[l], g_kidx_B[l], g_q_M[l], w_out[l])
        mk, mv = memory_kv(mem_prompt, g_mem[l], w_mem_kv[l], g_k_M[l])
        yp, c1, s1, k1, v1, ki1 = layer(yp, mk, mv, zero_conv, zero_ssm, empty_k, empty_k, empty_ki,
                                        GDN_CHUNK, *wts)
        ys, c2, s2, k2, v2, ki2 = layer(ys, cache_mem_k[l], cache_mem_v[l], state_conv_A[l], state_ssm_A[l],
                                        cache_k_B[l], cache_v_B[l], cache_kidx_B[l], ys.shape[1], *wts)
        pc.append(c1); pS.append(s1); pk.append(k1); pv.append(v1); pki.append(ki1)
        pmk.append(mk); pmv.append(mv)
        sc.append(c2); sS.append(s2); sk.append(k2); sv.append(v2); ski.append(ki2)
    p_conv_A = jnp.stack(pc)
    p_ssm_A = jnp.stack(pS)
    p_k_B = jnp.stack(pk)
    p_v_B = jnp.stack(pv)
    p_kidx_B = jnp.stack(pki)
    p_mem_k = jnp.stack(pmk)
    p_mem_v = jnp.stack(pmv)
    s_conv_A = jnp.stack(sc)
    s_ssm_A = jnp.stack(sS)
    s_k_B = jnp.stack(sk)
    s_v_B = jnp.stack(sv)
    s_kidx_B = jnp.stack(ski)
    return (yp, ys, p_conv_A, p_ssm_A, p_k_B, p_v_B, p_kidx_B, p_mem_k, p_mem_v,
            s_conv_A, s_ssm_A, s_k_B, s_v_B, s_kidx_B)
```

```python
import numpy as np
import concourse.bass as bass
import concourse.mybir as mybir
from concourse.bass_utils import run_bass_kernel_spmd

F32 = mybir.dt.float32
BF16 = mybir.dt.bfloat16
ALU = mybir.AluOpType
AF = mybir.ActivationFunctionType
AX = mybir.AxisListType


def _region(ap):
    t = ap.tensor
    name = t.name
    esz = mybir.dt.size(ap.dtype)
    pat = tuple((st_ * esz, c_) for st_, c_ in ap.ap)
    off = ap.offset * esz
    space = str(ap.space)
    if 'DRAM' in space.upper() or 'Dram' in space or 'dram' in space:
        lo = off
        hi = off + sum((c - 1) * abs(s) for s, c in pat) + esz
        return (name, 0, 1, lo, hi)
    ps, pc = pat[0]
    if ps == 0:
        ps = 1 << 30
    p0 = off // ps
    fo = off % ps
    hi = fo + sum((c - 1) * abs(s) for s, c in pat[1:]) + esz
    if 'PSUM' in space.upper():
        return (name, (p0 // 32) * 32, ((p0 + pc + 31) // 32) * 32, 0, 1 << 30)
    return (name, p0, p0 + pc, fo, hi)


def _overlap(a, b):
    return a[1] < b[2] and b[1] < a[2] and a[3] < b[4] and b[3] < a[4]


def _contains(a, b):
    return a[1] <= b[1] and a[2] >= b[2] and a[3] <= b[3] and a[4] >= b[4]


class Op:
    __slots__ = ('eng', 'fn', 'reads', 'writes', 'dma', 'deps', 'signals', 'sem', 'val', 'idx', 'pe_acc')

    def __init__(self, eng, fn, reads, writes, dma=False):
        self.eng = eng
        self.fn = fn
        self.reads = reads
        self.writes = writes
        self.dma = dma
        self.deps = set()
        self.signals = False
        self.sem = None
        self.val = 0


class Prog:
    ENGS = ('pe', 'act', 'dve', 'pool', 'sp')

    def __init__(self, nc, n_dma_sems=12, same_engine_sync=True):
        self.nc = nc
        self.ops = []
        self.n_dma_sems = n_dma_sems
        self.same_engine_sync = same_engine_sync
        self.alias = {}

    def set_alias(self, name, group):
        self.alias[name] = group

    def add(self, eng, fn, reads, writes, dma=False):
        if getattr(self, 'dry', False):
            return None
        rr = [_region(a) for a in reads if a is not None and not isinstance(a, (int, float))]
        ww = [_region(a) for a in writes if a is not None]
        op = Op(eng, fn, rr, ww, dma)
        op.idx = len(self.ops)
        self.ops.append(op)
        return op

    def resolve(self):
        writers = {}
        readers = {}
        for op in self.ops:
            for r in op.reads:
                key = self.alias.get(r[0], r[0])
                full = key != r[0]
                for (wr, wi) in writers.get(key, ()):
                    if full or _overlap(r, wr):
                        op.deps.add(wi)
            for w in op.writes:
                key = self.alias.get(w[0], w[0])
                full = key != w[0]
                for (wr, wi) in writers.get(key, ()):
                    if full or _overlap(w, wr):
                        op.deps.add(wi)
                for (rr, ri) in readers.get(key, ()):
                    if full or _overlap(w, rr):
                        op.deps.add(ri)
            for w in op.writes:
                key = self.alias.get(w[0], w[0])
                full = key != w[0]
                if full:
                    writers[key] = [(w, op.idx)]
                    readers[key] = []
                else:
                    writers[key] = [(wr, wi) for (wr, wi) in writers.get(key, ()) if not _contains(w, wr)] + [(w, op.idx)]
                    readers[key] = [(rr, ri) for (rr, ri) in readers.get(key, ()) if not _contains(w, rr)]
            for r in op.reads:
                key = self.alias.get(r[0], r[0])
                lst = readers.setdefault(key, [])
                if not op.dma:
                    lst[:] = [(rr, ri) for (rr, ri) in lst
                              if not (rr == r and self.ops[ri].eng == op.eng and not self.ops[ri].dma)]
                lst.append((r, op.idx))
            op.deps.discard(op.idx)
        for op in self.ops:
            keep = set()
            for d in op.deps:
                o2 = self.ops[d]
                if not o2.dma and not op.dma and o2.eng == op.eng:
                    if op.eng == 'pe':
                        continue
                    if not self.same_engine_sync:
                        continue
                keep.add(d)
            op.deps = keep
            for d in keep:
                self.ops[d].signals = True
        for op in self.ops:
            if op.dma:
                op.signals = True

    def emit(self, final_wait_engine='sp'):
        nc = self.nc
        self.resolve()
        import contextlib
        with contextlib.ExitStack() as st:
            esem = {e: st.enter_context(nc.semaphore('sem_' + e)) for e in ('pe', 'act', 'dve', 'pool')}
            dsem = [st.enter_context(nc.semaphore('dsem%d' % i)) for i in range(self.n_dma_sems)]
            ecount = {e: 0 for e in esem}
            dcount = [0] * self.n_dma_sems
            half = self.n_dma_sems // 2
            kq = {'sp': 0, 'pool': 0, 'act': 0}
            for op in self.ops:
                if not op.signals:
                    continue
                if op.dma:
                    if op.eng == 'pool':
                        i = half + kq['pool'] % (self.n_dma_sems - half)
                        kq['pool'] += 1
                    else:
                        i = kq['sp'] % half
                        kq['sp'] += 1
                    dcount[i] += 16
                    op.sem = ('d', i)
                    op.val = dcount[i]
                else:
                    ecount[op.eng] += 1
                    op.sem = ('e', op.eng)
                    op.val = ecount[op.eng]
            final = {}
            for op in self.ops:
                if op.dma:
                    final[op.sem] = max(final.get(op.sem, 0), op.val)

            def semh(s):
                return esem[s[1]] if s[0] == 'e' else dsem[s[1]]

            per_eng = {e: [o for o in self.ops if o.eng == e] for e in self.ENGS}
            block = st.enter_context(nc.Block())

            def run(engname, eng):
                known = {}
                for op in per_eng[engname]:
                    need = {}
                    for d in op.deps:
                        o2 = self.ops[d]
                        need[o2.sem] = max(need.get(o2.sem, 0), o2.val)
                    if op.dma and op.val > 16:
                        need[op.sem] = max(need.get(op.sem, 0), op.val - 16)
                    for s, v in need.items():
                        if known.get(s, 0) >= v:
                            continue
                        eng.wait_ge(semh(s), v)
                        known[s] = v
                    ins = op.fn(eng)
                    if op.signals:
                        ins.then_inc(semh(op.sem), 16 if op.dma else 1)
                if engname == final_wait_engine:
                    for s, v in final.items():
                        if known.get(s, 0) < v:
                            eng.wait_ge(semh(s), v)

            @block.tensor
            def _(e):
                run('pe', e)

            @block.scalar
            def _(e):
                run('act', e)

            @block.vector
            def _(e):
                run('dve', e)

            @block.gpsimd
            def _(e):
                run('pool', e)

            @block.sync
            def _(e):
                run('sp', e)

    def dma(self, out, in_, eng='sp'):
        return self.add(eng, lambda e: e.dma_start(out=out, in_=in_), [in_], [out], dma=True)

    def matmul(self, out, lhsT, rhs, start=True, stop=True):
        rd = [lhsT, rhs] + ([] if start else [out])
        return self.add('pe', lambda e: e.matmul(out, lhsT, rhs, start=start, stop=stop), rd, [out])

    def transpose(self, out, in_, ident):
        return self.add('pe', lambda e: e.transpose(out, in_, ident), [in_, ident], [out])

    def act(self, out, in_, func, bias=None, scale=None, accum_out=None, eng='act'):
        kw = {}
        rd = [in_]
        if bias is not None:
            kw['bias'] = bias
            rd.append(bias)
        if scale is not None:
            kw['scale'] = scale
            rd.append(scale)
        wr = [out]
        if accum_out is not None:
            kw['accum_out'] = accum_out
            wr.append(accum_out)
        return self.add('act', lambda e: e.activation(out, in_, func, **kw), rd, wr)

    def tt(self, out, in0, in1, op, eng='dve'):
        return self.add(eng, lambda e: e.tensor_tensor(out, in0, in1, op), [in0, in1], [out])

    def ts(self, out, in0, s1, op0, s2=None, op1=None, accum_out=None, eng='dve'):
        rd = [in0, s1, s2]
        wr = [out, accum_out]

        def fn(e):
            kw = {}
            if accum_out is not None:
                kw['accum_out'] = accum_out
            if op1 is not None:
                return e.tensor_scalar(out, in0, s1, s2, op0, op1, **kw)
            return e.tensor_scalar(out, in0, s1, None, op0, **kw)
        return self.add(eng, fn, rd, wr)

    def stt(self, out, in0, scalar, in1, op0, op1, eng='dve'):
        return self.add(eng, lambda e: e.scalar_tensor_tensor(out, in0, scalar, in1, op0, op1),
                        [in0, scalar, in1], [out])

    def copy(self, out, in_, eng='dve'):
        if eng == 'act':
            return self.add('act', lambda e: e.activation(out, in_, AF.Copy), [in_], [out])
        return self.add(eng, lambda e: e.tensor_copy(out, in_), [in_], [out])

    def memset(self, ap, val, eng='dve'):
        return self.add(eng, lambda e: e.memset(ap, val), [], [ap])

    def reduce(self, out, in_, op, axis=AX.X, eng='dve'):
        return self.add(eng, lambda e: e.tensor_reduce(out, in_, axis, op), [in_], [out])

    def recip(self, out, in_):
        return self.add('dve', lambda e: e.reciprocal(out, in_), [in_], [out])


D = 1024
IN_W = 3888
SEQ = 2048
NT = SEQ // 128
PAST = 4096
EPS = 1e-6
C_ZA = 1536
C_TM = 2048
T_BA, T_AA, T_QB, T_KB, T_VB, T_ZB, T_KI, T_WI, T_QM, T_ZM = 0, 4, 8, 264, 520, 776, 1288, 1320, 1328, 1584
C_QI = 3080
TMW = IN_W - C_TM
IDX_SCALE = (8 ** -0.5) * (32 ** -0.5)
NBIS = 17
NEGBIG = -1.0e30
G_KB, G_KI, G_KM, G_QB, G_QM, G_DTB, G_ALOG = 0, 64, 96, 160, 224, 288, 292
GSM = 296


class _Stop(Exception):
    pass


def build_program(kstop=0):
    import contextlib

    def ck(n):
        if kstop == n:
            raise _Stop()
    nc = bass.Bass("TRN2", target_bir_lowering=False)

    def din(name, shape):
        return nc.dram_tensor(name, shape, F32, kind="ExternalInput").ap()

    def dout(name, shape):
        return nc.dram_tensor(name, shape, F32, kind="ExternalOutput").ap()

    x_p = din("x_p", [SEQ, D])
    x_s = din("x_s", [16, D])
    mem = din("mem", [256, D])
    w_in = din("w_in", [D, IN_W])
    w_mem = din("w_mem", [D, 512])
    w_out = din("w_out", [D, D])
    g_all = din("g_all", [16, 128])
    gsm_d = din("gsm", [1, GSM])
    g_o_d = din("g_o", [1, 128])
    convw_d = din("conv_w", [4, 1536])
    st_conv = din("st_conv", [3, 1536])
    st_ssm = din("st_ssm", [4, 128, 128])
    c_k = din("c_k", [PAST, 256])
    c_v = din("c_v", [PAST, 256])
    c_ki = din("c_ki", [PAST, 32])
    c_mk = din("c_mk", [256, 256])
    c_mv = din("c_mv", [256, 256])

    y_p = dout("y_p", [SEQ, D])
    y_s = dout("y_s", [16, D])
    p_conv = dout("p_conv", [3, 1536])
    p_ssm = dout("p_ssm", [4, 128, 128])
    p_k = dout("p_k", [SEQ, 256])
    p_v = dout("p_v", [SEQ, 256])
    p_ki = dout("p_ki", [SEQ, 32])
    p_mk = dout("p_mk", [256, 256])
    p_mv = dout("p_mv", [256, 256])
    s_conv = dout("s_conv", [3, 1536])
    s_ssm = dout("s_ssm", [4, 128, 128])
    s_k = dout("s_k", [16, 256])
    s_v = dout("s_v", [16, 256])
    s_ki = dout("s_ki", [16, 32])

    dbg_outs = [dout("dbg%d" % i, [128, 1024]) for i in range(6)] if kstop else []
    dbg_n = [0]

    with contextlib.ExitStack() as st:
        def sb(name, shape, dt=F32):
            return st.enter_context(nc.sbuf_tensor(name, shape, dt))

        def ps(name, shape, dt=F32):
            return st.enter_context(nc.psum_tensor(name, shape, dt))

        P = Prog(nc, n_dma_sems=16)

        W = sb("W", [128, 8, IN_W], BF16)
        Wo = sb("Wo", [128, 8, D], BF16)
        KbT = sb("KbT", [128, 2, 2048], BF16)
        Wm = KbT[:].rearrange("p a (b c) -> p (a b) c", c=512)
        Vaug = sb("Vaug", [128, 16, 260], BF16)
        kiT = sb("kiT", [32, 2048], BF16)
        scores = sb("scores", [128, 2048])
        scoresS = W[:].rearrange("p a b -> p (a b)").bitcast(F32)[:, 0:4224]
        id32 = sb("id32", [128, 128])
        idb = sb("idb", [128, 128], BF16)
        ones32 = sb("ones32", [128, 128])
        onesb = sb("onesb", [128, 128], BF16)
        negI = sb("negI", [128, 128], BF16)
        zerob = sb("zerob", [128, 128], BF16)
        blkP = sb("blkP", [128, 128])
        blkS = sb("blkS", [128, 128])
        triP = sb("triP", [128, 128])
        triS = sb("triS", [128, 128])
        lsP = sb("lsP", [128, 128])
        lsS = sb("lsS", [128, 128])
        admP = sb("admP", [128, 128])
        nadmP = sb("nadmP", [128, 128])
        nadmS = sb("nadmS", [128, 128])
        g16 = sb("g16", [16, 128])
        gcol = sb("gcol", [128, 16])
        gsm = sb("gsm_t", [128, GSM])
        negA = sb("negA", [128, 4])
        go1 = sb("go1", [1, 128])
        gocol = sb("gocol", [128, 1])
        cw4 = sb("cw4", [4, 1536])
        wc = sb("wc", [128, 12, 4])
        stc = sb("stc", [3, 1536])
        xt = sb("xt", [128, D])
        xs = sb("xs", [128, D], BF16)
        hT = sb("hT", [128, 8, 128], BF16)
        rx = sb("rx", [128, 1])
        tm = sb("tm", [128, TMW])
        raw = sb("raw", [128, 12, 131])
        acc = sb("acc", [128, 12, 128])
        vsb = sb("vsb", [128, 4, 128], BF16)
        zas = sb("zas", [128, 4, 128], BF16)
        zs = sb("zs", [128, 512], BF16)
        sqb = sb("sqb", [128, 8, 128], BF16)
        rbc = sb("rbc", [128, 8, 128])
        qn = sb("qn", [128, 4, 128], BF16)
        knT = sb("knT", [128, 4, 128], BF16)
        qiT = sb("qiT", [32, 8, 128], BF16)
        gt = sb("gt", [128, 28])
        egr = sb("egr", [128, 4, 128])
        DT = sb("DT", [128, 4, 128])
        Xm = sb("Xm", [128, 4, 128])
        Ym = sb("Ym", [128, 4, 128])
        Dg = Xm
        Dst = Ym
        Pm = sb("Pm", [128, 4, 128])
        Tt = sb("Tt", [128, 4, 128], BF16)
        bv = sb("bv", [128, 4, 128], BF16)
        kbg = sb("kbg", [128, 4, 128], BF16)
        kd = sb("kd", [128, 4, 128], BF16)
        nwk = sb("nwk", [128, 4, 128], BF16)
        qd = sb("qd", [128, 4, 128], BF16)
        qkT = sb("qkT", [128, 4, 128], BF16)
        usb = sb("usb", [128, 4, 128], BF16)
        S = sb("S", [128, 4, 128])
        Sb = sb("Sb", [128, 4, 128], BF16)
        mixT = sb("mixT", [128, 8, 128], BF16)
        mixBC = sb("mixBC", [128, 512], BF16)
        s5 = sb("s5", [128, 16])
        rbcf = rbc[:].rearrange("p a b -> p (a b)")
        sq = rbcf
        accf = acc[:].rearrange("p a b -> p (a b)")
        sqB = sb("sqB", [128, 512])
        nrm = sb("nrm", [128, 768], BF16)
        knf = sb("knf", [128, 256])
        kxn = sb("kxn", [128, 32])
        kxb = sb("kxb", [128, 32], BF16)
        QbT = sb("QbT", [128, 2, 128], BF16)
        QmT = sb("QmT", [128, 2, 128], BF16)
        MkT = sb("MkT", [128, 2, 256], BF16)
        MvA = sb("MvA", [128, 2, 260], BF16)
        wdiag = sb("wdiag", [128, 8, 128], BF16)
        wabs = sb("wabs", [128, 8])
        wsgn = sb("wsgn", [128, 8])
        rl = [sb("rl%d" % i, [128, 512]) for i in range(2)]
        kvc = rl
        up32 = rl[0][:, 0:128]
        lo32 = rl[0][:, 128:256]
        PT = [sb("PT%d" % i, [128, 4, 128], BF16) for i in range(2)]
        nmk = [sb("nmk%d" % i, [128, 128], BF16) for i in range(2)]
        bis = sb("bis", [128, 8])
        wtab = sb("wtab", [128, NBIS + 1])
        p2 = sb("p2", [128, NBIS + 1])
        rcp = sb("rcp", [128, 8])
        ob = rbcf[:, 768:1024]
        cv = accf
        kvb = sb("kvb", [128, 256], BF16)
        kic = sb("kic", [128, 32])
        yo = accf[:, 0:D]

        pb = [ps("pb%d" % i, [128, 512]) for i in range(8)]


        def aff(out, cmp, mult, pat):
            P.add('pool', lambda e: e.affine_select(out, out, [[pat, 128]], cmp, 0.0, base=0,
                                                    channel_multiplier=mult), [out], [out])
        P.memset(id32[:], 0.0)
        P.add('pool', lambda e: e.affine_select(id32[:], id32[:], [[-1, 128]], ALU.not_equal, 1.0,
                                                base=0, channel_multiplier=1), [id32[:]], [id32[:]])
        P.copy(idb[:], id32[:])
        P.ts(negI[:], id32[:], -30000.0, ALU.mult)
        P.memset(zerob[:], 0.0)
        for k_ in range(NBIS + 1):
            P.memset(p2[:, k_:k_ + 1], 2.0 ** -(k_ + 1), eng='pool')
        P.memset(ones32[:], 1.0)
        P.memset(onesb[:], 1.0)
        P.memset(up32, 1.0)
        aff(up32, ALU.is_ge, -1, 1)
        P.memset(lo32, 1.0)
        aff(lo32, ALU.is_gt, 1, -1)
        P.memset(blkP[:], 0.0)
        P.memset(blkP[0:64, 0:64], 1.0)
        P.memset(blkP[64:128, 64:128], 1.0)
        P.memset(blkS[:], 0.0)
        P.memset(blkS[0:16, 0:16], 1.0)
        P.tt(triP[:], up32, blkP[:], ALU.mult)
        P.tt(triS[:], up32, blkS[:], ALU.mult)
        P.tt(lsP[:], lo32, blkP[:], ALU.mult)
        P.tt(lsS[:], lo32, blkS[:], ALU.mult)
        P.memset(admP[:], 1.0)
        P.memset(admP[0:64, 64:128], 0.0)
        P.ts(nadmP[:], admP[:], -1.0, ALU.add, 1.0e30, ALU.mult)
        P.memset(nadmS[:], NEGBIG)
        P.memset(nadmS[:, 0:16], 0.0)

        P.dma(g16[:], g_all)
        P.dma(gsm[:], gsm_d.partition_broadcast(128))
        P.dma(go1[:], g_o_d)
        P.dma(cw4[:], convw_d)
        P.matmul(pb[0][:, 0:16], g16[0:16, :], id32[0:16, 0:16])
        P.copy(gcol[:], pb[0][:, 0:16])
        P.matmul(pb[0][:, 16:17], go1[0:1, :], id32[0:1, 0:1])
        P.copy(gocol[:], pb[0][:, 16:17])
        for c in range(12):
            P.matmul(pb[1][:, c * 4:(c + 1) * 4], cw4[0:4, c * 128:(c + 1) * 128], id32[0:4, 0:4])
        P.copy(wc[:], pb[1][:, 0:48].rearrange("p (c j) -> p c j", j=4))
        P.act(negA[:], gsm[:, G_ALOG:G_ALOG + 4], AF.Exp)
        P.ts(negA[:], negA[:], -1.0, ALU.mult)

        def cast_w(dst, src, gc, ncols):
            a = ncols * 9 // 25
            b = ncols * 17 // 25
            if gc is None:
                P.copy(dst[:, 0:a], src[:, 0:a])
                P.copy(dst[:, a:b], src[:, a:b], eng='act')
                P.copy(dst[:, b:ncols], src[:, b:ncols], eng='pool')
            else:
                P.ts(dst[:, 0:a], src[:, 0:a], gc, ALU.mult)
                P.act(dst[:, a:b], src[:, a:b], AF.Copy, scale=gc)
                P.ts(dst[:, b:ncols], src[:, b:ncols], gc, ALU.mult, 1.0, ALU.mult, eng='pool')

        CH = IN_W // 3
        slots = [scores[:, 0:CH], accf[:, 0:CH]]
        k = 0
        for kt in range(8):
            for hf in range(3):
                s_ = slots[k % 2]
                k += 1
                P.dma(s_, w_in[kt * 128:(kt + 1) * 128, hf * CH:(hf + 1) * CH])
                cast_w(W[:, kt, hf * CH:(hf + 1) * CH], s_, gcol[:, kt:kt + 1], CH)
        for kt in range(8):
            s_ = slots[k % 2][:, 0:512]
            k += 1
            P.dma(s_, w_mem[kt * 128:(kt + 1) * 128, :])
            cast_w(Wm[:, kt, :], s_, gcol[:, 8 + kt:9 + kt], 512)
        for kt in range(8):
            s_ = slots[k % 2][:, 0:1024]
            k += 1
            P.dma(s_, w_out[kt * 128:(kt + 1) * 128, :])
            cast_w(Wo[:, kt, :], s_, None, 1024)

        def load_h(x_src, nrows):
            if nrows < 128:
                P.memset(xt[:], 0.0)
            P.dma(xt[0:nrows, :], x_src)
            P.add('dve', lambda e: e.scalar_tensor_tensor(yo, xt[:], 1.0, xt[:], ALU.mult, ALU.mult,
                                                          accum_out=rx[:]), [xt[:]], [yo, rx[:]])
            P.ts(rx[:], rx[:], 1.0 / D, ALU.mult, EPS, ALU.add)
            P.act(rx[:], rx[:], AF.Ln)
            P.act(rx[:], rx[:], AF.Exp, scale=-0.5)
            P.ts(xs[:], xt[:], rx[:], ALU.mult)
            for half in range(2):
                for j in range(4):
                    kt = half * 4 + j
                    P.matmul(pb[half][:, j * 128:(j + 1) * 128], xs[:, kt * 128:(kt + 1) * 128], idb[:])
            P.copy(hT[:, 0:4, :], pb[0][:].rearrange("p (a b) -> p a b", a=4), eng='act')
            P.copy(hT[:, 4:8, :], pb[1][:].rearrange("p (a b) -> p a b", a=4), eng='act')

        def rstd_inplace(ap, scale, eps=EPS, post=None):
            P.ts(ap, ap, scale, ALU.mult, eps, ALU.add)
            P.act(ap, ap, AF.Ln)
            if post is None:
                P.act(ap, ap, AF.Exp, scale=-0.5)
            else:
                P.act(ap, ap, AF.Exp, scale=-0.5, bias=post)

        def transpose_bf(dst, src_list, pbank):
            for j, s_ in enumerate(src_list):
                P.matmul(pbank[:, j * 128:(j + 1) * 128], s_, idb[:])
            n = len(src_list)
            P.copy(dst, pbank[:, 0:n * 128].rearrange("p (a b) -> p a b", a=n), eng='act')

        def mem_tile(mt):
            r = slice(mt * 128, (mt + 1) * 128)
            load_h(mem[r, :], 128)
            for kt in range(8):
                P.matmul(pb[2][:], hT[:, kt, :], Wm[:, kt, :], start=(kt == 0), stop=(kt == 7))
            KV = kvc[mt]
            P.copy(KV[:], pb[2][:], eng='act')
            P.tt(sq[:, 0:256], KV[:, 0:256], KV[:, 0:256], ALU.mult)
            P.reduce(s5[:, 0:4], sq[:, 0:256].rearrange("p (h d) -> p h d", h=4), ALU.add)
            rstd_inplace(s5[:, 0:4], 1.0 / 64)
            k3 = KV[:, 0:256].rearrange("p (h d) -> p h d", h=4)
            n3 = knf[:].rearrange("p (h d) -> p h d", h=4)
            P.tt(n3, k3, s5[:, 0:4].unsqueeze(2).to_broadcast([128, 4, 64]), ALU.mult)
            P.tt(n3, n3, gsm[:, G_KM:G_KM + 64].unsqueeze(1).to_broadcast([128, 4, 64]), ALU.mult)
            P.dma(p_mk[r, :], knf[:], eng='pool')
            P.dma(p_mv[r, :], KV[:, 256:512], eng='pool')
            P.copy(kvb[:], knf[:])
            transpose_bf(MkT[:, :, r], [kvb[:, 0:128], kvb[:, 128:256]], pb[3])
            P.memset(MvA[:, mt, :].rearrange("p (h e) -> p h e", e=65)[:, :, 64:65], 1.0)
            P.copy(MvA[:, mt, :].rearrange("p (h e) -> p h e", e=65)[:, :, 0:64],
                   KV[:, 256:512].rearrange("p (h d) -> p h d", h=4), eng='pool')

        def layer_common(x_src, nrows):
            load_h(x_src, nrows)
            for c in range(16):
                bank = pb[2 + c // 4]
                for kt in range(8):
                    P.matmul(bank[:, (c % 4) * 128:(c % 4 + 1) * 128], W[:, kt, c * 128:(c + 1) * 128], hT[:, kt, :],
                             start=(kt == 0), stop=(kt == 7))
            for c in range(3):
                P.copy(raw[:, c * 4:(c + 1) * 4, 3:131], pb[2 + c][:].rearrange("p (a b) -> p a b", a=4), eng='act')
            for h in range(8):
                bank = pb[6 + h // 4]
                for kt in range(8):
                    P.matmul(bank[0:32, (h % 4) * 128:(h % 4 + 1) * 128], W[:, kt, C_QI + h * 32:C_QI + (h + 1) * 32],
                             hT[:, kt, :], start=(kt == 0), stop=(kt == 7))
            P.copy(qiT[:, 0:4, :], pb[6][0:32, :].rearrange("p (a b) -> p a b", a=4), eng='act')
            P.copy(qiT[:, 4:8, :], pb[7][0:32, :].rearrange("p (a b) -> p a b", a=4), eng='act')
            ck(10)
            for ci, c0 in enumerate(range(0, TMW, 512)):
                wd = min(512, TMW - c0)
                bank = pb[ci % 2]
                for kt in range(8):
                    P.matmul(bank[:, 0:wd], hT[:, kt, :], W[:, kt, C_TM + c0:C_TM + c0 + wd], start=(kt == 0), stop=(kt == 7))
                P.copy(tm[:, c0:c0 + wd], bank[:, 0:wd], eng='act')
            ck(12)

        def chainA(sample):
            tri, ls = (triS, lsS) if sample else (triP, lsP)
            for c in range(12):
                P.ts(acc[:, c, :], raw[:, c, 0:128], wc[:, c, 0:1], ALU.mult)
                for j in range(1, 4):
                    P.stt(acc[:, c, :], raw[:, c, j:j + 128], wc[:, c, j:j + 1], acc[:, c, :], ALU.mult, ALU.add)
                if c % 3 == 2:
                    yield
            P.copy(raw[:, :, 0:3], raw[:, :, 128:131], eng='pool')
            yield
            P.act(acc[:, 0:8, :], acc[:, 0:8, :], AF.Silu)
            P.act(vsb[:], acc[:, 8:12, :], AF.Silu)
            P.act(zas[:], pb[5][:].rearrange("p (a b) -> p a b", a=4), AF.Silu)
            P.act(zs[:, 0:256], tm[:, T_ZB:T_ZB + 256], AF.Silu)
            P.act(zs[:, 256:512], tm[:, T_ZM:T_ZM + 256], AF.Silu)
            yield
            P.tt(sqb[:], acc[:, 0:8, :], acc[:, 0:8, :], ALU.mult, eng='pool')
            for g_ in range(2):
                P.matmul(pb[2 + g_][:], onesb[:], sqb[:, g_ * 4:(g_ + 1) * 4, :].rearrange("p a b -> p (a b)"))
            P.act(rbc[:, 0:4, :], pb[2][:].rearrange("p (a b) -> p a b", a=4), AF.Ln, bias=EPS)
            P.act(rbc[:, 4:8, :], pb[3][:].rearrange("p (a b) -> p a b", a=4), AF.Ln, bias=EPS)
            P.act(rbc[:, 0:4, :], rbc[:, 0:4, :], AF.Exp, scale=-0.5, bias=float(np.log(128.0 ** -0.5)))
            P.act(rbc[:, 4:8, :], rbc[:, 4:8, :], AF.Exp, scale=-0.5)
            P.tt(qn[:], acc[:, 0:4, :], rbc[:, 0:4, :], ALU.mult)
            P.tt(knT[:], acc[:, 4:8, :], rbc[:, 4:8, :], ALU.mult)
            if kstop == 13:
                dbg(raw[:, 4:8, 3:131], 512, stage=None) if False else None
                dbg(acc[:, 4:8, :].rearrange("p a b -> p (a b)"), 512)
                dbg(rbc[:, 4:8, :].rearrange("p a b -> p (a b)"), 512)
                dbg(acc[:, 0:4, :].rearrange("p a b -> p (a b)"), 512)
                dbg(rbc[:, 0:4, :].rearrange("p a b -> p (a b)"), 512)
                dbg(knT[:].rearrange("p a b -> p (a b)"), 512, stage=tm)
                dbg(qn[:].rearrange("p a b -> p (a b)"), 512, stage=tm)
            ck(13)
            yield

            beta, gg, gc, nbeta, egc, ekd, c1 = (gt[:, 0:4], gt[:, 4:8], gt[:, 8:12], gt[:, 12:16],
                                                 gt[:, 16:20], gt[:, 20:24], gt[:, 24:28])
            P.act(beta, tm[:, T_BA:T_BA + 4], AF.Exp, scale=-1.0)
            P.ts(beta, beta, 1.0, ALU.add)
            P.recip(beta, beta)
            P.ts(nbeta, beta, -1.0, ALU.mult)
            P.tt(gg, tm[:, T_AA:T_AA + 4], gsm[:, G_DTB:G_DTB + 4], ALU.add)
            P.act(gg, gg, AF.Exp)
            P.act(gg, gg, AF.Ln, bias=1.0)
            P.tt(gg, gg, negA[:], ALU.mult)
            ck(131)
            yield
            blk = blkS if sample else blkP
            P.matmul(pb[4][:, 0:4], tri[:], gg)
            P.matmul(pb[4][:, 4:8], blk[:], gg)
            P.copy(gc, pb[4][:, 0:4])
            P.tt(ekd, pb[4][:, 4:8], gc, ALU.subtract)
            P.act(ekd, ekd, AF.Exp)
            P.act(egc, gc, AF.Exp)
            P.tt(c1, beta, egc, ALU.mult)
            ck(132)
            yield
            for h in range(4):
                P.ts(Dg[:, h, :], tri[:], gg[:, h:h + 1], ALU.mult)
                P.matmul(pb[5][:, h * 128:(h + 1) * 128], ones32[:], Dg[:, h, :])
            gcr = pb[5][:].rearrange("p (a b) -> p a b", a=4)
            ck(133)
            yield
            P.act(egr[:], gcr, AF.Exp)
            ck(134)
            yield
            P.copy(DT[:], gcr, eng='act')
            for h in range(4):
                P.ts(Dst[:, h, :], DT[:, h, :], gc[:, h:h + 1], ALU.subtract, 0.0, ALU.max)
                P.ts(DT[:, h, :], DT[:, h, :], gc[:, h:h + 1], ALU.subtract, 0.0, ALU.min)
            ck(135)
            yield
            P.act(Dst[:], Dst[:], AF.Exp, scale=-1.0)
            P.act(DT[:], DT[:], AF.Exp)
            ck(136)
            yield
            P.tt(Dst[:], Dst[:], ls[:].unsqueeze(1).to_broadcast([128, 4, 128]), ALU.mult)
            P.tt(DT[:], DT[:], tri[:].unsqueeze(1).to_broadcast([128, 4, 128]), ALU.mult)
            ck(14)
            yield

            for h in range(4):
                P.matmul(pb[2][:, h * 128:(h + 1) * 128], knT[:, h, :], idb[:])
                P.matmul(pb[3][:, h * 128:(h + 1) * 128], vsb[:, h, :], idb[:])
            ktm = pb[2][:].rearrange("p (a b) -> p a b", a=4)
            vtm = pb[3][:].rearrange("p (a b) -> p a b", a=4)
            P.tt(bv[:], vtm, beta.unsqueeze(2).to_broadcast([128, 4, 128]), ALU.mult)
            P.tt(kbg[:], ktm, c1.unsqueeze(2).to_broadcast([128, 4, 128]), ALU.mult)
            P.tt(kd[:], ktm, ekd.unsqueeze(2).to_broadcast([128, 4, 128]), ALU.mult)
            ck(15)
            yield

            for h in range(4):
                P.matmul(pb[2][:, h * 128:(h + 1) * 128], knT[:, h, :], knT[:, h, :])
            for h in range(4):
                P.ts(Dst[:, h, :], Dst[:, h, :], nbeta[:, h:h + 1], ALU.mult)
            P.tt(Xm[:], pb[2][:].rearrange("p (a b) -> p a b", a=4), Dst[:], ALU.mult)
            for h in range(4):
                P.matmul(pb[3][:, h * 128:(h + 1) * 128], Xm[:, h, :], id32[:])
            P.copy(Ym[:], pb[3][:].rearrange("p (a b) -> p a b", a=4), eng='act')
            P.tt(Pm[:], Ym[:], id32[:].unsqueeze(1).to_broadcast([128, 4, 128]), ALU.add)
            if kstop == 16:
                dbg(Xm[:].rearrange("p a b -> p (a b)"), 512)
                dbg(Ym[:].rearrange("p a b -> p (a b)"), 512)
                dbg(Pm[:].rearrange("p a b -> p (a b)"), 512)
                dbg(DT[:].rearrange("p a b -> p (a b)"), 512)
                dbg(gt[:, 0:12], 12)
                dbg(knT[:].rearrange("p a b -> p (a b)"), 512, stage=tm)
            ck(16)
            yield
            nlev = 3 if sample else 5
            for lv in range(1, nlev + 1):
                for h in range(4):
                    P.matmul(pb[2][:, h * 128:(h + 1) * 128], Ym[:, h, :], Xm[:, h, :])
                if lv < nlev:
                    for h in range(4):
                        P.matmul(pb[3][:, h * 128:(h + 1) * 128], Xm[:, h, :], Ym[:, h, :])
                P.copy(Xm[:], pb[2][:].rearrange("p (a b) -> p a b", a=4), eng='act')
                if lv < nlev:
                    P.copy(Ym[:], pb[3][:].rearrange("p (a b) -> p a b", a=4), eng='act')
                for h in range(4):
                    P.matmul(pb[4][:, h * 128:(h + 1) * 128], Xm[:, h, :], Pm[:, h, :])
                P.tt(Pm[:], Pm[:], pb[4][:].rearrange("p (a b) -> p a b", a=4), ALU.add)
                yield
            P.copy(Tt[:], Pm[:], eng='act')
            ck(17)
            yield

            for h in range(4):
                P.matmul(pb[2][:, h * 128:(h + 1) * 128], kbg[:, h, :], Tt[:, h, :])
            P.act(nwk[:], pb[2][:].rearrange("p (a b) -> p a b", a=4), AF.Copy, scale=-1.0)
            P.tt(qd[:], qn[:], egr[:], ALU.mult)
            for h in range(4):
                P.matmul(pb[3][:, h * 128:(h + 1) * 128], knT[:, h, :], qn[:, h, :])
            P.tt(qkT[:], pb[3][:].rearrange("p (a b) -> p a b", a=4), DT[:], ALU.mult)
            ck(18)
            yield

            chunks = [(0, 16)] if sample else [(0, 64), (64, 128)]
            obank = pb[4]
            for (r0, r1) in chunks:
                for h in range(4):
                    P.matmul(pb[5][:, h * 128:(h + 1) * 128], Tt[:, h, :], bv[:, h, :], start=True, stop=False)
                    P.matmul(pb[5][:, h * 128:(h + 1) * 128], nwk[:, h, :], Sb[:, h, :], start=False, stop=True)
                P.copy(usb[r0:r1, :, :], pb[5][r0:r1, :].rearrange("p (a b) -> p a b", a=4), eng='act')
                for h in range(4):
                    oc = obank[:, h * 128 + r0:h * 128 + r1]
                    P.matmul(oc, Sb[:, h, :], qd[:, h, r0:r1], start=True, stop=False)
                    P.matmul(oc, usb[r0:r1, h, :], qkT[r0:r1, h, r0:r1], start=False, stop=True)
                for h in range(4):
                    P.matmul(pb[2][:, h * 128:(h + 1) * 128], kd[r0:r1, h, :], usb[r0:r1, h, :])
                for h in range(4):
                    P.ts(S[:, h, :], S[:, h, :], egr[:, h, r1 - 1:r1], ALU.mult)
                P.tt(S[:], S[:], pb[2][:].rearrange("p (a b) -> p a b", a=4), ALU.add)
                P.copy(Sb[:], S[:], eng='act')
                yield
            o3 = obank[:].rearrange("p (a b) -> p a b", a=4)
            ck(19)
            yield
            P.act(sqb[:, 0:4, :], o3, AF.Square)
            P.matmul(pb[3][:], onesb[:], sqb[:, 0:4, :].rearrange("p a b -> p (a b)"))
            P.act(rbc[:, 0:4, :], pb[3][:].rearrange("p (a b) -> p a b", a=4), AF.Ln, scale=1.0 / 128, bias=EPS)
            P.act(rbc[:, 0:4, :], rbc[:, 0:4, :], AF.Exp, scale=-0.5)
            P.tt(rbc[:, 0:4, :], rbc[:, 0:4, :], o3, ALU.mult)
            P.stt(mixT[:, 0:4, :], rbc[:, 0:4, :], gocol[:, 0:1], zas[:], ALU.mult, ALU.mult)
            ck(20)
            yield


        def chainB_heads(nrows, outs):
            P.tt(sqB[:, 0:512], tm[:, T_QB:T_QB + 512], tm[:, T_QB:T_QB + 512], ALU.mult)
            P.reduce(s5[:, 0:8], sqB[:, 0:512].rearrange("p (h d) -> p h d", d=64), ALU.add)
            P.tt(sqB[:, 0:256], tm[:, T_QM:T_QM + 256], tm[:, T_QM:T_QM + 256], ALU.mult)
            P.reduce(s5[:, 8:12], sqB[:, 0:256].rearrange("p (h d) -> p h d", d=64), ALU.add)
            P.tt(sqB[:, 256:288], tm[:, T_KI:T_KI + 32], tm[:, T_KI:T_KI + 32], ALU.mult)
            P.reduce(s5[:, 12:13], sqB[:, 256:288], ALU.add)
            yield
            P.ts(s5[:, 12:13], s5[:, 12:13], 2.0, ALU.mult)
            rstd_inplace(s5[:, 0:13], 1.0 / 64)
            k3 = tm[:, T_KB:T_KB + 256].rearrange("p (h d) -> p h d", h=4)
            n3 = knf[:].rearrange("p (h d) -> p h d", h=4)
            P.tt(n3, k3, s5[:, 4:8].unsqueeze(2).to_broadcast([128, 4, 64]), ALU.mult)
            P.tt(n3, n3, gsm[:, G_KB:G_KB + 64].unsqueeze(1).to_broadcast([128, 4, 64]), ALU.mult)
            P.dma(outs["k"], knf[0:nrows, :], eng='pool')
            P.dma(outs["v"], tm[0:nrows, T_VB:T_VB + 256], eng='pool')
            P.copy(nrm[:, 256:512], knf[:], eng='pool')
            q3 = tm[:, T_QB:T_QB + 256].rearrange("p (h d) -> p h d", h=4)
            yield
            m3 = sqB[:, 0:256].rearrange("p (h d) -> p h d", h=4)
            P.tt(m3, q3, s5[:, 0:4].unsqueeze(2).to_broadcast([128, 4, 64]), ALU.mult)
            P.tt(nrm[:, 0:256].rearrange("p (h d) -> p h d", h=4), m3,
                 gsm[:, G_QB:G_QB + 64].unsqueeze(1).to_broadcast([128, 4, 64]), ALU.mult)
            q3 = tm[:, T_QM:T_QM + 256].rearrange("p (h d) -> p h d", h=4)
            m3 = sqB[:, 256:512].rearrange("p (h d) -> p h d", h=4)
            P.tt(m3, q3, s5[:, 8:12].unsqueeze(2).to_broadcast([128, 4, 64]), ALU.mult)
            P.tt(nrm[:, 512:768].rearrange("p (h d) -> p h d", h=4), m3,
                 gsm[:, G_QM:G_QM + 64].unsqueeze(1).to_broadcast([128, 4, 64]), ALU.mult)
            P.ts(kxn[:], tm[:, T_KI:T_KI + 32], s5[:, 12:13], ALU.mult)
            P.tt(kxn[:], kxn[:], gsm[:, G_KI:G_KI + 32], ALU.mult)
            P.dma(outs["ki"], kxn[0:nrows, :], eng='pool')
            P.copy(kxb[:], kxn[:], eng='pool')
            kslot = outs["kslot"]
            kc = slice(kslot * 128, (kslot + 1) * 128)
            yield
            transpose_bf(QbT[:], [nrm[:, 0:128], nrm[:, 128:256]], pb[0])
            transpose_bf(KbT[:, :, kc], [nrm[:, 256:384], nrm[:, 384:512]], pb[1])
            yield
            transpose_bf(QmT[:], [nrm[:, 512:640], nrm[:, 640:768]], pb[6])
            P.matmul(pb[7][0:32, 0:128], kxb[:], idb[:])
            P.copy(kiT[:, kc], pb[7][0:32, 0:128])
            yield
            va = Vaug[:, kslot, :].rearrange("p (h e) -> p h e", e=65)
            P.memset(va[:, :, 64:65], 1.0)
            P.copy(va[:, :, 0:64], tm[:, T_VB:T_VB + 256].rearrange("p (h d) -> p h d", h=4), eng='pool')
            P.ts(wabs[:], tm[:, T_WI:T_WI + 8], IDX_SCALE, ALU.mult)
            for h in range(8):
                P.ts(wdiag[:, h, :], id32[:], wabs[:, h:h + 1], ALU.mult, eng=('dve' if h % 2 == 0 else 'pool'))
            yield

        def conv_out(nrows, dst):
            for c in range(3):
                pc = pb[c % 2]
                for kt in range(8):
                    P.matmul(pc[:], hT[:, kt, :], W[:, kt, c * 512:(c + 1) * 512], start=(kt == 0), stop=(kt == 7))
                P.copy(cv[:, c * 512:(c + 1) * 512], pc[:], eng=('act' if c % 2 == 0 else 'dve'))
            P.dma(dst, cv[nrows - 3:nrows, :], eng='pool')

        def index_scores(sc, col0, ktiles, kbase):
            for g0 in range(0, ktiles, 4):
                nk = min(4, ktiles - g0) * 128
                kcs = slice((kbase + g0) * 128, (kbase + g0) * 128 + nk)
                dst = sc[:, col0 + g0 * 128:col0 + g0 * 128 + nk]

                def logits(h):
                    P.matmul(pb[h % 2][:, 0:nk], qiT[:, h, :], kiT[:, kcs])

                def relu_sum(h):
                    r_ = rl[h % 2][:].bitcast(BF16)
                    P.act(r_[:, 0:nk], pb[h % 2][:, 0:nk], AF.Relu)
                    return r_
                logits(0)
                for h in range(8):
                    r_ = relu_sum(h)
                    if h + 1 < 8:
                        logits(h + 1)
                    P.matmul(pb[6][:, 0:nk], wdiag[:, h, :], r_[:, 0:nk], start=(h == 0), stop=(h == 7))
                    if h % 2 == 1:
                        yield
                P.copy(dst, pb[6][:, 0:nk], eng='act')
                yield

        def bisect(sc, ncols, lo_cols):
            thr, w0, t_, cnt, hh = bis[:, 0:1], bis[:, 1:2], bis[:, 2:3], bis[:, 3:4], bis[:, 4:5]
            P.reduce(w0, sc[:, 0:ncols], ALU.max)
            P.reduce(thr, sc[:, 0:lo_cols], ALU.min)
            P.ts(thr, thr, -1.0, ALU.add)
            P.tt(w0, w0, thr, ALU.subtract)
            P.ts(wtab[:], p2[:], w0, ALU.mult)
            P.tt(t_, thr, wtab[:, 0:1], ALU.add)
            for k in range(NBIS):
                for ci, c0 in enumerate(range(0, ncols, 1024)):
                    wd = min(1024, ncols - c0)
                    P.ts(xs[:, 0:wd], sc[:, c0:c0 + wd], t_, ALU.is_gt, (None if ci == 0 else cnt), ALU.add,
                         accum_out=cnt)
                P.ts(hh, cnt, 255.5, ALU.is_ge, 0.5, ALU.subtract)
                P.stt(t_, hh, wtab[:, k:k + 1], t_, ALU.mult, ALU.add)
                yield
            P.tt(thr, t_, wtab[:, NBIS:NBIS + 1], ALU.subtract)

        def attend(qT, keys, obank_, first, last, mask_cols=None, sc=None):
            n = len(keys)

            def scores_t(i):
                kT, va = keys[i]
                bank = pb[i % 2]
                nm = None
                if mask_cols is not None:
                    nm = nmk[i % 2]
                    P.ts(nm[:], sc[:, mask_cols[i]:mask_cols[i] + 128], bis[:, 0:1], ALU.is_le)
                for h in range(4):
                    pr = slice((h % 2) * 64, (h % 2) * 64 + 64)
                    oc = bank[:, h * 128:(h + 1) * 128]
                    P.matmul(oc, kT[pr, h // 2, :], qT[pr, h // 2, :], start=True, stop=False)
                    P.matmul(oc, (nm[:] if nm is not None else zerob[:]), negI[:], start=False, stop=True)

            scores_t(0)
            for i in range(n):
                kT, va = keys[i]
                pt = PT[i % 2]
                P.act(pt[:], pb[i % 2][:].rearrange("p (a b) -> p a b", a=4), AF.Exp, scale=0.125)
                if i + 1 < n:
                    scores_t(i + 1)
                for h in range(4):
                    P.matmul(obank_[:, h * 65:(h + 1) * 65], pt[:, h, :], va[:, h, :],
                             start=(first and i == 0 and h == 0), stop=(last and i == n - 1 and h == 3))
                yield

        def finish_heads(obank_, zcols, dst_cols):
            P.copy(sqB[:, 0:260], obank_[:, 0:260], eng='act')
            o3 = sqB[:, 0:260].rearrange("p (h e) -> p h e", e=65)
            P.copy(rcp[:, 0:4], o3[:, :, 64])
            P.recip(rcp[:, 0:4], rcp[:, 0:4])
            P.tt(o3[:, :, 0:64], o3[:, :, 0:64], rcp[:, 0:4].unsqueeze(2).to_broadcast([128, 4, 64]), ALU.mult)
            P.tt(mixBC[:, dst_cols:dst_cols + 256].rearrange("p (h d) -> p h d", h=4), o3[:, :, 0:64],
                 zs[:, zcols:zcols + 256].rearrange("p (h d) -> p h d", h=4), ALU.mult)
            yield

        def out_proj(nrows, y_dst):
            transpose_bf(mixT[:, 4:8, :], [mixBC[:, j * 128:(j + 1) * 128] for j in range(4)], pb[5])
            for c in range(2):
                bank = pb[c]
                for kt in range(8):
                    P.matmul(bank[:], mixT[:, kt, :], Wo[:, kt, c * 512:(c + 1) * 512], start=(kt == 0), stop=(kt == 7))
                P.tt(yo[:, c * 512:(c + 1) * 512], bank[:], xt[:, c * 512:(c + 1) * 512], ALU.add)
            P.dma(y_dst, yo[0:nrows, :], eng='pool')

        def dbg(ap2d, ncols, stage=None):
            if not kstop:
                return
            i = dbg_n[0]
            dbg_n[0] += 1
            if stage is not None:
                P.copy(stage[:, 0:ncols], ap2d)
                P.dma(dbg_outs[i][:, 0:ncols], stage[:, 0:ncols], eng='pool')
            else:
                P.dma(dbg_outs[i][:, 0:ncols], ap2d, eng='pool')

        def drain(g):
            for _ in g:
                pass

        def count_steps(mk):
            P.dry = True
            n = 0
            for _ in mk():
                n += 1
            P.dry = False
            return n + 1

        def interleave(mka, mkb):
            na, nb = count_steps(mka), count_steps(mkb)
            ga, gb = mka(), mkb()
            ia = ib = 0
            da = db = False
            while not (da and db):
                pick_a = (not da) and (db or ia * nb <= ib * na)
                if pick_a:
                    try:
                        next(ga)
                        ia += 1
                    except StopIteration:
                        da = True
                else:
                    try:
                        next(gb)
                        ib += 1
                    except StopIteration:
                        db = True

        def prompt_chainB(tt, outs):
            yield from chainB_heads(128, outs)
            nk = tt + 1
            yield from index_scores(scores, 0, nk, 0)
            dcol = tt * 128
            P.stt(scores[:, dcol:dcol + 128], scores[:, dcol:dcol + 128], 1.0, admP[:], ALU.mult, ALU.mult)
            P.tt(scores[:, dcol:dcol + 128], scores[:, dcol:dcol + 128], nadmP[:], ALU.add)
            if tt >= 2:
                yield from bisect(scores, nk * 128, (nk - 1) * 128)
            else:
                P.memset(bis[:, 0:1], -1.0e29)
            keys = [(KbT[:, :, j * 128:(j + 1) * 128], Vaug[:, j, :].rearrange("p (h e) -> p h e", e=65)) for j in range(nk)]
            yield from attend(QbT, keys, pb[6], True, True, mask_cols=[j * 128 for j in range(nk)], sc=scores)
            yield from finish_heads(pb[6], 0, 0)
            mkeys = [(MkT[:, :, j * 128:(j + 1) * 128], MvA[:, j, :].rearrange("p (h e) -> p h e", e=65)) for j in range(2)]
            yield from attend(QmT, mkeys, pb[7], True, True)
            yield from finish_heads(pb[7], 256, 256)

        try:
            for mt in range(2):
                mem_tile(mt)
            ck(2)
            P.memset(raw[:, :, 0:3], 0.0)
            P.memset(S[:], 0.0)
            P.memset(Sb[:], 0.0)
            for tt in range(NT):
                r = slice(tt * 128, (tt + 1) * 128)
                outs = {"k": p_k[r, :], "v": p_v[r, :], "ki": p_ki[r, :], "kslot": tt}
                layer_common(x_p[r, :], 128)
                if kstop:
                    drain(chainA(False))
                    drain(prompt_chainB(tt, outs))
                else:
                    interleave(lambda: chainA(False), lambda tt=tt, outs=outs: prompt_chainB(tt, outs))
                if tt == NT - 1:
                    conv_out(128, p_conv)
                out_proj(128, y_p[r, :])
                ck(25)
            P.dma(p_ssm.rearrange("h k v -> k h v"), S[:], eng='pool')
            ck(30)

            P.dma(S[:], st_ssm.rearrange("h k v -> k h v"))
            P.copy(Sb[:], S[:])
            P.dma(stc[:], st_conv)
            for c in range(12):
                P.matmul(pb[2][:, c * 3:(c + 1) * 3], stc[0:3, c * 128:(c + 1) * 128], id32[0:3, 0:3])
            P.copy(raw[:, :, 0:3], pb[2][:, 0:36].rearrange("p (c j) -> p c j", j=3))
            for mt in range(2):
                r = slice(mt * 128, (mt + 1) * 128)
                KV = kvc[mt]
                P.dma(KV[:, 0:256], c_mk[r, :])
                P.dma(KV[:, 256:512], c_mv[r, :])
                P.copy(kvb[:], KV[:, 0:256])
                transpose_bf(MkT[:, :, r], [kvb[:, 0:128], kvb[:, 128:256]], pb[3])
                P.copy(MvA[:, mt, :].rearrange("p (h e) -> p h e", e=65)[:, :, 0:64],
                       KV[:, 256:512].rearrange("p (h d) -> p h d", h=4), eng='pool')
            outs = {"k": s_k, "v": s_v, "ki": s_ki, "kslot": 0}
            layer_common(x_s, 16)
            drain(chainA(True))
            drain(chainB_heads(16, outs))
            conv_out(16, s_conv)
            P.dma(s_ssm.rearrange("h k v -> k h v"), S[:], eng='pool')
            ck(31)
            Wf = W[:].rearrange("p a b -> p (a b)").bitcast(F32)
            Wb = W[:].rearrange("p a b -> p (a b)")
            kiS = Wf[:, 4224:5248].rearrange("p (t c) -> p t c", c=32)
            KS = Wf[:, 5248:9344].rearrange("p (t c) -> p t c", c=256)
            VS = Wf[:, 9344:13440].rearrange("p (t c) -> p t c", c=256)
            Kb16 = Wb[:, 26880:30976].rearrange("p (t c) -> p t c", c=256)
            kib = Wb[:, 18688:19712].rearrange("p (t c) -> p t c", c=32)
            def dma_tiles(dst, src_rows, ntile, step):
                v = src_rows.rearrange("(t p) c -> p t c", p=128)
                for t0 in range(0, ntile, step):
                    P.dma(dst[:, t0:t0 + step, :], v[:, t0:t0 + step, :])
            dma_tiles(kiS, c_ki, 32, 8)
            dma_tiles(KS, c_k[0:2048, :], 16, 8)
            drain(index_scores(scoresS, PAST, 1, 0))
            P.tt(scoresS[:, PAST:PAST + 128], scoresS[:, PAST:PAST + 128], nadmS[:], ALU.add)
            P.copy(kib, kiS)
            for g_ in range(2):
                for q4 in range(4):
                    bank = pb[2 + q4]
                    for j in range(4):
                        t_ = g_ * 16 + q4 * 4 + j
                        P.matmul(bank[0:32, j * 128:(j + 1) * 128], kib[:, t_, :], idb[:])
                    P.copy(kiT[:, q4 * 512:(q4 + 1) * 512], bank[0:32, :], eng='act')
                drain(index_scores(scoresS, g_ * 2048, 16, 0))
            dma_tiles(VS, c_v[0:2048, :], 16, 8)
            drain(bisect(scoresS, PAST + 128, PAST))
            ck(32)
            keys = [(KbT[:, :, 0:128], Vaug[:, 0, :].rearrange("p (h e) -> p h e", e=65))]
            drain(attend(QbT, keys, pb[6], True, False, mask_cols=[PAST], sc=scoresS))
            for g_ in range(2):
                if g_ == 1:
                    dma_tiles(KS, c_k[2048:4096, :], 16, 8)
                    dma_tiles(VS, c_v[2048:4096, :], 16, 8)
                P.copy(Kb16[:, 0:6, :], KS[:, 0:6, :])
                P.copy(Kb16[:, 6:11, :], KS[:, 6:11, :], eng='act')
                P.copy(Kb16[:, 11:16, :], KS[:, 11:16, :], eng='pool')
                for q2 in range(8):
                    bank = pb[2 + q2 % 4]
                    for pr_ in range(2):
                        for j in range(2):
                            t_ = q2 * 2 + j
                            P.matmul(bank[:, (pr_ * 2 + j) * 128:(pr_ * 2 + j + 1) * 128],
                                     Kb16[:, t_, pr_ * 128:(pr_ + 1) * 128], idb[:])
                    P.copy(KbT[:, :, q2 * 256:(q2 + 1) * 256], bank[:].rearrange("p (a b) -> p a b", a=2), eng='act')
                v4 = Vaug[:, 0:16, :].rearrange("p t (h e) -> p t h e", e=65)
                for t4 in range(4):
                    P.copy(v4[:, t4 * 4:(t4 + 1) * 4, :, 0:64].rearrange("p t h d -> p (t h) d"),
                           VS[:, t4 * 4:(t4 + 1) * 4, :].rearrange("p t (h d) -> p (t h) d", d=64),
                           eng=('pool' if t4 % 2 == 0 else 'dve'))
                keys = [(KbT[:, :, j * 128:(j + 1) * 128], Vaug[:, j, :].rearrange("p (h e) -> p h e", e=65)) for j in range(16)]
                drain(attend(QbT, keys, pb[6], False, g_ == 1, mask_cols=[g_ * 2048 + j * 128 for j in range(16)], sc=scoresS))
            drain(finish_heads(pb[6], 0, 0))
            mkeys = [(MkT[:, :, j * 128:(j + 1) * 128], MvA[:, j, :].rearrange("p (h e) -> p h e", e=65)) for j in range(2)]
            drain(attend(QmT, mkeys, pb[7], True, True))
            drain(finish_heads(pb[7], 256, 256))
            out_proj(16, y_s)
        except _Stop:
            pass
        P.emit(final_wait_engine='pool')
    return nc


_CACHE = {}


def _make_in_maps(inp):
    g_all = np.ascontiguousarray(np.concatenate([inp['g_in'][0].reshape(8, 128), inp['g_mem'][0].reshape(8, 128)], 0))
    gsm = np.ascontiguousarray(np.concatenate([
        inp['g_k_B'][0], inp['g_kidx_B'][0], inp['g_k_M'][0], inp['g_q_B'][0], inp['g_q_M'][0],
        inp['dt_bias_A'][0], inp['a_log_A'][0]]).reshape(1, GSM).astype(np.float32))
    maps = []
    for b in range(8):
        maps.append({
            "x_p": np.ascontiguousarray(inp['x_prompt'][b]),
            "x_s": np.ascontiguousarray(inp['x_sample'][b]),
            "mem": np.ascontiguousarray(inp['mem_prompt'][b]),
            "w_in": np.ascontiguousarray(inp['w_in'][0]),
            "w_mem": np.ascontiguousarray(inp['w_mem_kv'][0]),
            "w_out": np.ascontiguousarray(inp['w_out'][0]),
            "g_all": g_all,
            "gsm": gsm,
            "g_o": np.ascontiguousarray(inp['g_o_A'][0].reshape(1, 128)),
            "conv_w": np.ascontiguousarray(inp['conv_w_A'][0]),
            "st_conv": np.ascontiguousarray(inp['state_conv_A'][0, b]),
            "st_ssm": np.ascontiguousarray(inp['state_ssm_A'][0, b]),
            "c_k": np.ascontiguousarray(inp['cache_k_B'][0, b].reshape(PAST, 256)),
            "c_v": np.ascontiguousarray(inp['cache_v_B'][0, b].reshape(PAST, 256)),
            "c_ki": np.ascontiguousarray(inp['cache_kidx_B'][0, b]),
            "c_mk": np.ascontiguousarray(inp['cache_mem_k'][0, b].reshape(256, 256)),
            "c_mv": np.ascontiguousarray(inp['cache_mem_v'][0, b].reshape(256, 256)),
        })
    return maps


def _assemble(res):
    def stack(name, shape):
        return np.stack([np.asarray(r[name], dtype=np.float32).reshape(shape) for r in res], 0)
    outs = (
        stack("y_p", (SEQ, D)), stack("y_s", (16, D)),
        stack("p_conv", (3, 1536))[None],
        stack("p_ssm", (4, 128, 128))[None],
        stack("p_k", (SEQ, 4, 64))[None],
        stack("p_v", (SEQ, 4, 64))[None],
        stack("p_ki", (SEQ, 32))[None],
        stack("p_mk", (256, 4, 64))[None],
        stack("p_mv", (256, 4, 64))[None],
        stack("s_conv", (3, 1536))[None],
        stack("s_ssm", (4, 128, 128))[None],
        stack("s_k", (16, 4, 64))[None],
        stack("s_v", (16, 4, 64))[None],
        stack("s_ki", (16, 32))[None],
    )
    return outs


def kernel(**inputs):
    inp = {k: np.asarray(v) for k, v in inputs.items()}
    nc = build_program()
    in_maps = _make_in_maps(inp)
    res = run_bass_kernel_spmd(nc, in_maps, core_ids=list(range(8)))
    return _assemble(res.results)
```

```python
import numpy as np
import concourse.bass as bass
import concourse.mybir as mybir
from concourse.bass_utils import run_bass_kernel_spmd

F32 = mybir.dt.float32
BF16 = mybir.dt.bfloat16
ALU = mybir.AluOpType
AF = mybir.ActivationFunctionType
AX = mybir.AxisListType


def _region(ap):
    t = ap.tensor
    name = t.name
    esz = mybir.dt.size(ap.dtype)
    pat = tuple((st_ * esz, c_) for st_, c_ in ap.ap)
    off = ap.offset * esz
    space = str(ap.space)
    if 'DRAM' in space.upper() or 'Dram' in space or 'dram' in space:
        lo = off
        hi = off + sum((c - 1) * abs(s) for s, c in pat) + esz
        return (name, 0, 1, lo, hi)
    ps, pc = pat[0]
    if ps == 0:
        ps = 1 << 30
    p0 = off // ps
    fo = off % ps
    hi = fo + sum((c - 1) * abs(s) for s, c in pat[1:]) + esz
    if 'PSUM' in space.upper():
        return (name, (p0 // 32) * 32, ((p0 + pc + 31) // 32) * 32, 0, 1 << 30)
    return (name, p0, p0 + pc, fo, hi)


def _overlap(a, b):
    return a[1] < b[2] and b[1] < a[2] and a[3] < b[4] and b[3] < a[4]


def _contains(a, b):
    return a[1] <= b[1] and a[2] >= b[2] and a[3] <= b[3] and a[4] >= b[4]


class Op:
    __slots__ = ('eng', 'fn', 'reads', 'writes', 'dma', 'deps', 'signals', 'sem', 'val', 'idx', 'pe_acc')

    def __init__(self, eng, fn, reads, writes, dma=False):
        self.eng = eng
        self.fn = fn
        self.reads = reads
        self.writes = writes
        self.dma = dma
        self.deps = set()
        self.signals = False
        self.sem = None
        self.val = 0


class Prog:
    ENGS = ('pe', 'act', 'dve', 'pool', 'sp')

    def __init__(self, nc, n_dma_sems=12, same_engine_sync=True):
        self.nc = nc
        self.ops = []
        self.n_dma_sems = n_dma_sems
        self.same_engine_sync = same_engine_sync
        self.alias = {}

    def set_alias(self, name, group):
        self.alias[name] = group

    def add(self, eng, fn, reads, writes, dma=False):
        if getattr(self, 'dry', False):
            return None
        rr = [_region(a) for a in reads if a is not None and not isinstance(a, (int, float))]
        ww = [_region(a) for a in writes if a is not None]
        op = Op(eng, fn, rr, ww, dma)
        op.idx = len(self.ops)
        self.ops.append(op)
        return op

    def resolve(self):
        writers = {}
        readers = {}
        for op in self.ops:
            for r in op.reads:
                key = self.alias.get(r[0], r[0])
                full = key != r[0]
                for (wr, wi) in writers.get(key, ()):
                    if full or _overlap(r, wr):
                        op.deps.add(wi)
            for w in op.writes:
                key = self.alias.get(w[0], w[0])
                full = key != w[0]
                for (wr, wi) in writers.get(key, ()):
                    if full or _overlap(w, wr):
                        op.deps.add(wi)
                for (rr, ri) in readers.get(key, ()):
                    if full or _overlap(w, rr):
                        op.deps.add(ri)
            for w in op.writes:
                key = self.alias.get(w[0], w[0])
                full = key != w[0]
                if full:
                    writers[key] = [(w, op.idx)]
                    readers[key] = []
                else:
                    writers[key] = [(wr, wi) for (wr, wi) in writers.get(key, ()) if not _contains(w, wr)] + [(w, op.idx)]
                    readers[key] = [(rr, ri) for (rr, ri) in readers.get(key, ()) if not _contains(w, rr)]
            for r in op.reads:
                key = self.alias.get(r[0], r[0])
                lst = readers.setdefault(key, [])
                if not op.dma:
                    lst[:] = [(rr, ri) for (rr, ri) in lst
                              if not (rr == r and self.ops[ri].eng == op.eng and not self.ops[ri].dma)]
                lst.append((r, op.idx))
            op.deps.discard(op.idx)
        for op in self.ops:
            keep = set()
            for d in op.deps:
                o2 = self.ops[d]
                if not o2.dma and not op.dma and o2.eng == op.eng:
                    if op.eng == 'pe':
                        continue
                    if not self.same_engine_sync:
                        continue
                keep.add(d)
            op.deps = keep
            for d in keep:
                self.ops[d].signals = True
        for op in self.ops:
            if op.dma:
                op.signals = True

    def emit(self, final_wait_engine='sp'):
        nc = self.nc
        self.resolve()
        import contextlib
        with contextlib.ExitStack() as st:
            esem = {e: st.enter_context(nc.semaphore('sem_' + e)) for e in ('pe', 'act', 'dve', 'pool')}
            dsem = [st.enter_context(nc.semaphore('dsem%d' % i)) for i in range(self.n_dma_sems)]
            ecount = {e: 0 for e in esem}
            dcount = [0] * self.n_dma_sems
            half = self.n_dma_sems // 2
            kq = {'sp': 0, 'pool': 0, 'act': 0}
            for op in self.ops:
                if not op.signals:
                    continue
                if op.dma:
                    if op.eng == 'pool':
                        i = half + kq['pool'] % (self.n_dma_sems - half)
                        kq['pool'] += 1
                    else:
                        i = kq['sp'] % half
                        kq['sp'] += 1
                    dcount[i] += 16
                    op.sem = ('d', i)
                    op.val = dcount[i]
                else:
                    ecount[op.eng] += 1
                    op.sem = ('e', op.eng)
                    op.val = ecount[op.eng]
            final = {}
            for op in self.ops:
                if op.dma:
                    final[op.sem] = max(final.get(op.sem, 0), op.val)

            def semh(s):
                return esem[s[1]] if s[0] == 'e' else dsem[s[1]]

            per_eng = {e: [o for o in self.ops if o.eng == e] for e in self.ENGS}
            block = st.enter_context(nc.Block())

            def run(engname, eng):
                known = {}
                for op in per_eng[engname]:
                    need = {}
                    for d in op.deps:
                        o2 = self.ops[d]
                        need[o2.sem] = max(need.get(o2.sem, 0), o2.val)
                    if op.dma and op.val > 16:
                        need[op.sem] = max(need.get(op.sem, 0), op.val - 16)
                    for s, v in need.items():
                        if known.get(s, 0) >= v:
                            continue
                        eng.wait_ge(semh(s), v)
                        known[s] = v
                    ins = op.fn(eng)
                    if op.signals:
                        ins.then_inc(semh(op.sem), 16 if op.dma else 1)
                if engname == final_wait_engine:
                    for s, v in final.items():
                        if known.get(s, 0) < v:
                            eng.wait_ge(semh(s), v)

            @block.tensor
            def _(e):
                run('pe', e)

            @block.scalar
            def _(e):
                run('act', e)

            @block.vector
            def _(e):
                run('dve', e)

            @block.gpsimd
            def _(e):
                run('pool', e)

            @block.sync
            def _(e):
                run('sp', e)

    def dma(self, out, in_, eng='sp'):
        return self.add(eng, lambda e: e.dma_start(out=out, in_=in_), [in_], [out], dma=True)

    def matmul(self, out, lhsT, rhs, start=True, stop=True):
        rd = [lhsT, rhs] + ([] if start else [out])
        return self.add('pe', lambda e: e.matmul(out, lhsT, rhs, start=start, stop=stop), rd, [out])

    def transpose(self, out, in_, ident):
        return self.add('pe', lambda e: e.transpose(out, in_, ident), [in_, ident], [out])

    def act(self, out, in_, func, bias=None, scale=None, accum_out=None, eng='act'):
        kw = {}
        rd = [in_]
        if bias is not None:
            kw['bias'] = bias
            rd.append(bias)
        if scale is not None:
            kw['scale'] = scale
            rd.append(scale)
        wr = [out]
        if accum_out is not None:
            kw['accum_out'] = accum_out
            wr.append(accum_out)
        return self.add('act', lambda e: e.activation(out, in_, func, **kw), rd, wr)

    def tt(self, out, in0, in1, op, eng='dve'):
        return self.add(eng, lambda e: e.tensor_tensor(out, in0, in1, op), [in0, in1], [out])

    def ts(self, out, in0, s1, op0, s2=None, op1=None, accum_out=None, eng='dve'):
        rd = [in0, s1, s2]
        wr = [out, accum_out]

        def fn(e):
            kw = {}
            if accum_out is not None:
                kw['accum_out'] = accum_out
            if op1 is not None:
                return e.tensor_scalar(out, in0, s1, s2, op0, op1, **kw)
            return e.tensor_scalar(out, in0, s1, None, op0, **kw)
        return self.add(eng, fn, rd, wr)

    def stt(self, out, in0, scalar, in1, op0, op1, eng='dve'):
        return self.add(eng, lambda e: e.scalar_tensor_tensor(out, in0, scalar, in1, op0, op1),
                        [in0, scalar, in1], [out])

    def copy(self, out, in_, eng='dve'):
        if eng == 'act':
            return self.add('act', lambda e: e.activation(out, in_, AF.Copy), [in_], [out])
        return self.add(eng, lambda e: e.tensor_copy(out, in_), [in_], [out])

    def memset(self, ap, val, eng='dve'):
        return self.add(eng, lambda e: e.memset(ap, val), [], [ap])

    def reduce(self, out, in_, op, axis=AX.X, eng='dve'):
        return self.add(eng, lambda e: e.tensor_reduce(out, in_, axis, op), [in_], [out])

    def recip(self, out, in_):
        return self.add('dve', lambda e: e.reciprocal(out, in_), [in_], [out])


D = 1024
IN_W = 3888
SEQ = 2048
NT = SEQ // 128
PAST = 4096
EPS = 1e-6
C_ZA = 1536
C_TM = 2048
T_BA, T_AA, T_QB, T_KB, T_VB, T_ZB, T_KI, T_WI, T_QM, T_ZM = 0, 4, 8, 264, 520, 776, 1288, 1320, 1328, 1584
C_QI = 3080
TMW = IN_W - C_TM
IDX_SCALE = (8 ** -0.5) * (32 ** -0.5)
NBIS = 17
NEGBIG = -1.0e30
G_KB, G_KI, G_KM, G_QB, G_QM, G_DTB, G_ALOG = 0, 64, 96, 160, 224, 288, 292
GSM = 296


class _Stop(Exception):
    pass


def build_program(kstop=0):
    import contextlib

    def ck(n):
        if kstop == n:
            raise _Stop()
    nc = bass.Bass("TRN2", target_bir_lowering=False)

    def din(name, shape):
        return nc.dram_tensor(name, shape, F32, kind="ExternalInput").ap()

    def dout(name, shape):
        return nc.dram_tensor(name, shape, F32, kind="ExternalOutput").ap()

    x_p = din("x_p", [SEQ, D])
    x_s = din("x_s", [16, D])
    mem = din("mem", [256, D])
    w_in = din("w_in", [D, IN_W])
    w_mem = din("w_mem", [D, 512])
    w_out = din("w_out", [D, D])
    g_all = din("g_all", [16, 128])
    gsm_d = din("gsm", [1, GSM])
    g_o_d = din("g_o", [1, 128])
    convw_d = din("conv_w", [4, 1536])
    st_conv = din("st_conv", [3, 1536])
    st_ssm = din("st_ssm", [4, 128, 128])
    c_k = din("c_k", [PAST, 256])
    c_v = din("c_v", [PAST, 256])
    c_ki = din("c_ki", [PAST, 32])
    c_mk = din("c_mk", [256, 256])
    c_mv = din("c_mv", [256, 256])

    y_p = dout("y_p", [SEQ, D])
    y_s = dout("y_s", [16, D])
    p_conv = dout("p_conv", [3, 1536])
    p_ssm = dout("p_ssm", [4, 128, 128])
    p_k = dout("p_k", [SEQ, 256])
    p_v = dout("p_v", [SEQ, 256])
    p_ki = dout("p_ki", [SEQ, 32])
    p_mk = dout("p_mk", [256, 256])
    p_mv = dout("p_mv", [256, 256])
    s_conv = dout("s_conv", [3, 1536])
    s_ssm = dout("s_ssm", [4, 128, 128])
    s_k = dout("s_k", [16, 256])
    s_v = dout("s_v", [16, 256])
    s_ki = dout("s_ki", [16, 32])

    dbg_outs = [dout("dbg%d" % i, [128, 1024]) for i in range(6)] if kstop else []
    dbg_n = [0]

    with contextlib.ExitStack() as st:
        def sb(name, shape, dt=F32):
            return st.enter_context(nc.sbuf_tensor(name, shape, dt))

        def ps(name, shape, dt=F32):
            return st.enter_context(nc.psum_tensor(name, shape, dt))

        P = Prog(nc, n_dma_sems=16)

        W = sb("W", [128, 8, IN_W], BF16)
        Wo = sb("Wo", [128, 8, D], BF16)
        KbT = sb("KbT", [128, 2, 2048], BF16)
        Wm = KbT[:].rearrange("p a (b c) -> p (a b) c", c=512)
        Vaug = sb("Vaug", [128, 16, 260], BF16)
        kiT = sb("kiT", [32, 2048], BF16)
        scores = sb("scores", [128, 2048])
        scoresS = W[:].rearrange("p a b -> p (a b)").bitcast(F32)[:, 0:4224]
        id32 = sb("id32", [128, 128])
        idb = sb("idb", [128, 128], BF16)
        ones32 = sb("ones32", [128, 128])
        onesb = sb("onesb", [128, 128], BF16)
        negI = sb("negI", [128, 128], BF16)
        zerob = sb("zerob", [128, 128], BF16)
        blkP = sb("blkP", [128, 128])
        blkS = sb("blkS", [128, 128])
        triP = sb("triP", [128, 128])
        triS = sb("triS", [128, 128])
        lsP = sb("lsP", [128, 128])
        lsS = sb("lsS", [128, 128])
        admP = sb("admP", [128, 128])
        nadmP = sb("nadmP", [128, 128])
        nadmS = sb("nadmS", [128, 128])
        g16 = sb("g16", [16, 128])
        gcol = sb("gcol", [128, 16])
        gsm = sb("gsm_t", [128, GSM])
        negA = sb("negA", [128, 4])
        go1 = sb("go1", [1, 128])
        gocol = sb("gocol", [128, 1])
        cw4 = sb("cw4", [4, 1536])
        wc = sb("wc", [128, 12, 4])
        stc = sb("stc", [3, 1536])
        xt = sb("xt", [128, D])
        xs = sb("xs", [128, D], BF16)
        hT = sb("hT", [128, 8, 128], BF16)
        rx = sb("rx", [128, 1])
        tm = sb("tm", [128, TMW])
        raw = sb("raw", [128, 12, 131])
        acc = sb("acc", [128, 12, 128])
        vsb = sb("vsb", [128, 4, 128], BF16)
        zas = sb("zas", [128, 4, 128], BF16)
        zs = sb("zs", [128, 512], BF16)
        sqb = sb("sqb", [128, 8, 128], BF16)
        rbc = sb("rbc", [128, 8, 128])
        qn = sb("qn", [128, 4, 128], BF16)
        knT = sb("knT", [128, 4, 128], BF16)
        qiT = sb("qiT", [32, 8, 128], BF16)
        gt = sb("gt", [128, 28])
        egr = sb("egr", [128, 4, 128])
        DT = sb("DT", [128, 4, 128])
        Xm = sb("Xm", [128, 4, 128])
        Ym = sb("Ym", [128, 4, 128])
        Dg = Xm
        Dst = Ym
        Pm = sb("Pm", [128, 4, 128])
        Tt = sb("Tt", [128, 4, 128], BF16)
        bv = sb("bv", [128, 4, 128], BF16)
        kbg = sb("kbg", [128, 4, 128], BF16)
        kd = sb("kd", [128, 4, 128], BF16)
        nwk = sb("nwk", [128, 4, 128], BF16)
        qd = sb("qd", [128, 4, 128], BF16)
        qkT = sb("qkT", [128, 4, 128], BF16)
        usb = sb("usb", [128, 4, 128], BF16)
        S = sb("S", [128, 4, 128])
        Sb = sb("Sb", [128, 4, 128], BF16)
        mixT = sb("mixT", [128, 8, 128], BF16)
        mixBC = sb("mixBC", [128, 512], BF16)
        s5 = sb("s5", [128, 16])
        rbcf = rbc[:].rearrange("p a b -> p (a b)")
        sq = rbcf
        accf = acc[:].rearrange("p a b -> p (a b)")
        sqB = sb("sqB", [128, 512])
        nrm = sb("nrm", [128, 768], BF16)
        knf = sb("knf", [128, 256])
        kxn = sb("kxn", [128, 32])
        kxb = sb("kxb", [128, 32], BF16)
        QbT = sb("QbT", [128, 2, 128], BF16)
        QmT = sb("QmT", [128, 2, 128], BF16)
        MkT = sb("MkT", [128, 2, 256], BF16)
        MvA = sb("MvA", [128, 2, 260], BF16)
        wdiag = sb("wdiag", [128, 8, 128], BF16)
        wabs = sb("wabs", [128, 8])
        wsgn = sb("wsgn", [128, 8])
        rl = [sb("rl%d" % i, [128, 512]) for i in range(2)]
        kvc = rl
        up32 = rl[0][:, 0:128]
        lo32 = rl[0][:, 128:256]
        PT = [sb("PT%d" % i, [128, 4, 128], BF16) for i in range(2)]
        nmk = [sb("nmk%d" % i, [128, 128], BF16) for i in range(2)]
        bis = sb("bis", [128, 8])
        wtab = sb("wtab", [128, NBIS + 1])
        p2 = sb("p2", [128, NBIS + 1])
        rcp = sb("rcp", [128, 8])
        ob = rbcf[:, 768:1024]
        cv = accf
        kvb = sb("kvb", [128, 256], BF16)
        kic = sb("kic", [128, 32])
        yo = accf[:, 0:D]

        pb = [ps("pb%d" % i, [128, 512]) for i in range(8)]


        def aff(out, cmp, mult, pat):
            P.add('pool', lambda e: e.affine_select(out, out, [[pat, 128]], cmp, 0.0, base=0,
                                                    channel_multiplier=mult), [out], [out])
        P.memset(id32[:], 0.0)
        P.add('pool', lambda e: e.affine_select(id32[:], id32[:], [[-1, 128]], ALU.not_equal, 1.0,
                                                base=0, channel_multiplier=1), [id32[:]], [id32[:]])
        P.copy(idb[:], id32[:])
        P.ts(negI[:], id32[:], -30000.0, ALU.mult)
        P.memset(zerob[:], 0.0)
        for k_ in range(NBIS + 1):
            P.memset(p2[:, k_:k_ + 1], 2.0 ** -(k_ + 1), eng='pool')
        P.memset(ones32[:], 1.0)
        P.memset(onesb[:], 1.0)
        P.memset(up32, 1.0)
        aff(up32, ALU.is_ge, -1, 1)
        P.memset(lo32, 1.0)
        aff(lo32, ALU.is_gt, 1, -1)
        P.memset(blkP[:], 0.0)
        P.memset(blkP[0:64, 0:64], 1.0)
        P.memset(blkP[64:128, 64:128], 1.0)
        P.memset(blkS[:], 0.0)
        P.memset(blkS[0:16, 0:16], 1.0)
        P.tt(triP[:], up32, blkP[:], ALU.mult)
        P.tt(triS[:], up32, blkS[:], ALU.mult)
        P.tt(lsP[:], lo32, blkP[:], ALU.mult)
        P.tt(lsS[:], lo32, blkS[:], ALU.mult)
        P.memset(admP[:], 1.0)
        P.memset(admP[0:64, 64:128], 0.0)
        P.ts(nadmP[:], admP[:], -1.0, ALU.add, 1.0e30, ALU.mult)
        P.memset(nadmS[:], NEGBIG)
        P.memset(nadmS[:, 0:16], 0.0)

        P.dma(g16[:], g_all)
        P.dma(gsm[:], gsm_d.partition_broadcast(128))
        P.dma(go1[:], g_o_d)
        P.dma(cw4[:], convw_d)
        P.matmul(pb[0][:, 0:16], g16[0:16, :], id32[0:16, 0:16])
        P.copy(gcol[:], pb[0][:, 0:16])
        P.matmul(pb[0][:, 16:17], go1[0:1, :], id32[0:1, 0:1])
        P.copy(gocol[:], pb[0][:, 16:17])
        for c in range(12):
            P.matmul(pb[1][:, c * 4:(c + 1) * 4], cw4[0:4, c * 128:(c + 1) * 128], id32[0:4, 0:4])
        P.copy(wc[:], pb[1][:, 0:48].rearrange("p (c j) -> p c j", j=4))
        P.act(negA[:], gsm[:, G_ALOG:G_ALOG + 4], AF.Exp)
        P.ts(negA[:], negA[:], -1.0, ALU.mult)

        def cast_w(dst, src, gc, ncols):
            a = ncols * 9 // 25
            b = ncols * 17 // 25
            if gc is None:
                P.copy(dst[:, 0:a], src[:, 0:a])
                P.copy(dst[:, a:b], src[:, a:b], eng='act')
                P.copy(dst[:, b:ncols], src[:, b:ncols], eng='pool')
            else:
                P.ts(dst[:, 0:a], src[:, 0:a], gc, ALU.mult)
                P.act(dst[:, a:b], src[:, a:b], AF.Copy, scale=gc)
                P.ts(dst[:, b:ncols], src[:, b:ncols], gc, ALU.mult, 1.0, ALU.mult, eng='pool')

        CH = IN_W // 3
        slots = [scores[:, 0:CH], accf[:, 0:CH]]
        k = 0
        for kt in range(8):
            for hf in range(3):
                s_ = slots[k % 2]
                k += 1
                P.dma(s_, w_in[kt * 128:(kt + 1) * 128, hf * CH:(hf + 1) * CH])
                cast_w(W[:, kt, hf * CH:(hf + 1) * CH], s_, gcol[:, kt:kt + 1], CH)
        for kt in range(8):
            s_ = slots[k % 2][:, 0:512]
            k += 1
            P.dma(s_, w_mem[kt * 128:(kt + 1) * 128, :])
            cast_w(Wm[:, kt, :], s_, gcol[:, 8 + kt:9 + kt], 512)
        for kt in range(8):
            s_ = slots[k % 2][:, 0:1024]
            k += 1
            P.dma(s_, w_out[kt * 128:(kt + 1) * 128, :])
            cast_w(Wo[:, kt, :], s_, None, 1024)

        def load_h(x_src, nrows):
            if nrows < 128:
                P.memset(xt[:], 0.0)
            P.dma(xt[0:nrows, :], x_src)
            P.add('dve', lambda e: e.scalar_tensor_tensor(yo, xt[:], 1.0, xt[:], ALU.mult, ALU.mult,
                                                          accum_out=rx[:]), [xt[:]], [yo, rx[:]])
            P.ts(rx[:], rx[:], 1.0 / D, ALU.mult, EPS, ALU.add)
            P.act(rx[:], rx[:], AF.Ln)
            P.act(rx[:], rx[:], AF.Exp, scale=-0.5)
            P.ts(xs[:], xt[:], rx[:], ALU.mult)
            for half in range(2):
                for j in range(4):
                    kt = half * 4 + j
                    P.matmul(pb[half][:, j * 128:(j + 1) * 128], xs[:, kt * 128:(kt + 1) * 128], idb[:])
            P.copy(hT[:, 0:4, :], pb[0][:].rearrange("p (a b) -> p a b", a=4), eng='act')
            P.copy(hT[:, 4:8, :], pb[1][:].rearrange("p (a b) -> p a b", a=4), eng='act')

        def rstd_inplace(ap, scale, eps=EPS, post=None):
            P.ts(ap, ap, scale, ALU.mult, eps, ALU.add)
            P.act(ap, ap, AF.Ln)
            if post is None:
                P.act(ap, ap, AF.Exp, scale=-0.5)
            else:
                P.act(ap, ap, AF.Exp, scale=-0.5, bias=post)

        def transpose_bf(dst, src_list, pbank):
            for j, s_ in enumerate(src_list):
                P.matmul(pbank[:, j * 128:(j + 1) * 128], s_, idb[:])
            n = len(src_list)
            P.copy(dst, pbank[:, 0:n * 128].rearrange("p (a b) -> p a b", a=n), eng='act')

        def mem_tile(mt):
            r = slice(mt * 128, (mt + 1) * 128)
            load_h(mem[r, :], 128)
            for kt in range(8):
                P.matmul(pb[2][:], hT[:, kt, :], Wm[:, kt, :], start=(kt == 0), stop=(kt == 7))
            KV = kvc[mt]
            P.copy(KV[:], pb[2][:], eng='act')
            P.tt(sq[:, 0:256], KV[:, 0:256], KV[:, 0:256], ALU.mult)
            P.reduce(s5[:, 0:4], sq[:, 0:256].rearrange("p (h d) -> p h d", h=4), ALU.add)
            rstd_inplace(s5[:, 0:4], 1.0 / 64)
            k3 = KV[:, 0:256].rearrange("p (h d) -> p h d", h=4)
            n3 = knf[:].rearrange("p (h d) -> p h d", h=4)
            P.tt(n3, k3, s5[:, 0:4].unsqueeze(2).to_broadcast([128, 4, 64]), ALU.mult)
            P.tt(n3, n3, gsm[:, G_KM:G_KM + 64].unsqueeze(1).to_broadcast([128, 4, 64]), ALU.mult)
            P.dma(p_mk[r, :], knf[:], eng='pool')
            P.dma(p_mv[r, :], KV[:, 256:512], eng='pool')
            P.copy(kvb[:], knf[:])
            transpose_bf(MkT[:, :, r], [kvb[:, 0:128], kvb[:, 128:256]], pb[3])
            P.memset(MvA[:, mt, :].rearrange("p (h e) -> p h e", e=65)[:, :, 64:65], 1.0)
            P.copy(MvA[:, mt, :].rearrange("p (h e) -> p h e", e=65)[:, :, 0:64],
                   KV[:, 256:512].rearrange("p (h d) -> p h d", h=4), eng='pool')

        def layer_common(x_src, nrows):
            load_h(x_src, nrows)
            for c in range(16):
                bank = pb[2 + c // 4]
                for kt in range(8):
                    P.matmul(bank[:, (c % 4) * 128:(c % 4 + 1) * 128], W[:, kt, c * 128:(c + 1) * 128], hT[:, kt, :],
                             start=(kt == 0), stop=(kt == 7))
            for c in range(3):
                P.copy(raw[:, c * 4:(c + 1) * 4, 3:131], pb[2 + c][:].rearrange("p (a b) -> p a b", a=4), eng='act')
            for h in range(8):
                bank = pb[6 + h // 4]
                for kt in range(8):
                    P.matmul(bank[0:32, (h % 4) * 128:(h % 4 + 1) * 128], W[:, kt, C_QI + h * 32:C_QI + (h + 1) * 32],
                             hT[:, kt, :], start=(kt == 0), stop=(kt == 7))
            P.copy(qiT[:, 0:4, :], pb[6][0:32, :].rearrange("p (a b) -> p a b", a=4), eng='act')
            P.copy(qiT[:, 4:8, :], pb[7][0:32, :].rearrange("p (a b) -> p a b", a=4), eng='act')
            ck(10)
            for c in range(12):
                P.ts(acc[:, c, :], raw[:, c, 0:128], wc[:, c, 0:1], ALU.mult)
                for j in range(1, 4):
                    P.stt(acc[:, c, :], raw[:, c, j:j + 128], wc[:, c, j:j + 1], acc[:, c, :], ALU.mult, ALU.add)
            P.copy(raw[:, :, 0:3], raw[:, :, 128:131], eng='pool')
            ck(11)
            P.act(acc[:, 0:8, :], acc[:, 0:8, :], AF.Silu)
            P.act(vsb[:], acc[:, 8:12, :], AF.Silu)
            P.act(zas[:], pb[5][:].rearrange("p (a b) -> p a b", a=4), AF.Silu)
            for ci, c0 in enumerate(range(0, TMW, 512)):
                wd = min(512, TMW - c0)
                bank = pb[ci % 2]
                for kt in range(8):
                    P.matmul(bank[:, 0:wd], hT[:, kt, :], W[:, kt, C_TM + c0:C_TM + c0 + wd], start=(kt == 0), stop=(kt == 7))
                P.copy(tm[:, c0:c0 + wd], bank[:, 0:wd], eng='act')
            P.act(zs[:, 0:256], tm[:, T_ZB:T_ZB + 256], AF.Silu)
            P.act(zs[:, 256:512], tm[:, T_ZM:T_ZM + 256], AF.Silu)
            ck(12)

        def chainA(sample):
            tri, ls = (triS, lsS) if sample else (triP, lsP)
            P.tt(sqb[:], acc[:, 0:8, :], acc[:, 0:8, :], ALU.mult, eng='pool')
            for g_ in range(2):
                P.matmul(pb[2 + g_][:], onesb[:], sqb[:, g_ * 4:(g_ + 1) * 4, :].rearrange("p a b -> p (a b)"))
            P.act(rbc[:, 0:4, :], pb[2][:].rearrange("p (a b) -> p a b", a=4), AF.Ln, bias=EPS)
            P.act(rbc[:, 4:8, :], pb[3][:].rearrange("p (a b) -> p a b", a=4), AF.Ln, bias=EPS)
            P.act(rbc[:, 0:4, :], rbc[:, 0:4, :], AF.Exp, scale=-0.5, bias=float(np.log(128.0 ** -0.5)))
            P.act(rbc[:, 4:8, :], rbc[:, 4:8, :], AF.Exp, scale=-0.5)
            P.tt(qn[:], acc[:, 0:4, :], rbc[:, 0:4, :], ALU.mult)
            P.tt(knT[:], acc[:, 4:8, :], rbc[:, 4:8, :], ALU.mult)
            if kstop == 13:
                dbg(raw[:, 4:8, 3:131], 512, stage=None) if False else None
                dbg(acc[:, 4:8, :].rearrange("p a b -> p (a b)"), 512)
                dbg(rbc[:, 4:8, :].rearrange("p a b -> p (a b)"), 512)
                dbg(acc[:, 0:4, :].rearrange("p a b -> p (a b)"), 512)
                dbg(rbc[:, 0:4, :].rearrange("p a b -> p (a b)"), 512)
                dbg(knT[:].rearrange("p a b -> p (a b)"), 512, stage=tm)
                dbg(qn[:].rearrange("p a b -> p (a b)"), 512, stage=tm)
            ck(13)
            yield

            beta, gg, gc, nbeta, egc, ekd, c1 = (gt[:, 0:4], gt[:, 4:8], gt[:, 8:12], gt[:, 12:16],
                                                 gt[:, 16:20], gt[:, 20:24], gt[:, 24:28])
            P.act(beta, tm[:, T_BA:T_BA + 4], AF.Exp, scale=-1.0)
            P.ts(beta, beta, 1.0, ALU.add)
            P.recip(beta, beta)
            P.ts(nbeta, beta, -1.0, ALU.mult)
            P.tt(gg, tm[:, T_AA:T_AA + 4], gsm[:, G_DTB:G_DTB + 4], ALU.add)
            P.act(gg, gg, AF.Exp)
            P.act(gg, gg, AF.Ln, bias=1.0)
            P.tt(gg, gg, negA[:], ALU.mult)
            ck(131)
            yield
            blk = blkS if sample else blkP
            P.matmul(pb[4][:, 0:4], tri[:], gg)
            P.matmul(pb[4][:, 4:8], blk[:], gg)
            P.copy(gc, pb[4][:, 0:4])
            P.tt(ekd, pb[4][:, 4:8], gc, ALU.subtract)
            P.act(ekd, ekd, AF.Exp)
            P.act(egc, gc, AF.Exp)
            P.tt(c1, beta, egc, ALU.mult)
            ck(132)
            yield
            for h in range(4):
                P.ts(Dg[:, h, :], tri[:], gg[:, h:h + 1], ALU.mult)
                P.matmul(pb[5][:, h * 128:(h + 1) * 128], ones32[:], Dg[:, h, :])
            gcr = pb[5][:].rearrange("p (a b) -> p a b", a=4)
            ck(133)
            yield
            P.act(egr[:], gcr, AF.Exp)
            ck(134)
            yield
            P.copy(DT[:], gcr, eng='act')
            for h in range(4):
                P.ts(Dst[:, h, :], DT[:, h, :], gc[:, h:h + 1], ALU.subtract, 0.0, ALU.max)
                P.ts(DT[:, h, :], DT[:, h, :], gc[:, h:h + 1], ALU.subtract, 0.0, ALU.min)
            ck(135)
            yield
            P.act(Dst[:], Dst[:], AF.Exp, scale=-1.0)
            P.act(DT[:], DT[:], AF.Exp)
            ck(136)
            yield
            P.tt(Dst[:], Dst[:], ls[:].unsqueeze(1).to_broadcast([128, 4, 128]), ALU.mult)
            P.tt(DT[:], DT[:], tri[:].unsqueeze(1).to_broadcast([128, 4, 128]), ALU.mult)
            ck(14)
            yield

            for h in range(4):
                P.matmul(pb[2][:, h * 128:(h + 1) * 128], knT[:, h, :], idb[:])
                P.matmul(pb[3][:, h * 128:(h + 1) * 128], vsb[:, h, :], idb[:])
            ktm = pb[2][:].rearrange("p (a b) -> p a b", a=4)
            vtm = pb[3][:].rearrange("p (a b) -> p a b", a=4)
            P.tt(bv[:], vtm, beta.unsqueeze(2).to_broadcast([128, 4, 128]), ALU.mult)
            P.tt(kbg[:], ktm, c1.unsqueeze(2).to_broadcast([128, 4, 128]), ALU.mult)
            P.tt(kd[:], ktm, ekd.unsqueeze(2).to_broadcast([128, 4, 128]), ALU.mult)
            ck(15)
            yield

            for h in range(4):
                P.matmul(pb[2][:, h * 128:(h + 1) * 128], knT[:, h, :], knT[:, h, :])
            for h in range(4):
                P.ts(Dst[:, h, :], Dst[:, h, :], nbeta[:, h:h + 1], ALU.mult)
            P.tt(Xm[:], pb[2][:].rearrange("p (a b) -> p a b", a=4), Dst[:], ALU.mult)
            for h in range(4):
                P.matmul(pb[3][:, h * 128:(h + 1) * 128], Xm[:, h, :], id32[:])
            P.copy(Ym[:], pb[3][:].rearrange("p (a b) -> p a b", a=4), eng='act')
            P.tt(Pm[:], Ym[:], id32[:].unsqueeze(1).to_broadcast([128, 4, 128]), ALU.add)
            if kstop == 16:
                dbg(Xm[:].rearrange("p a b -> p (a b)"), 512)
                dbg(Ym[:].rearrange("p a b -> p (a b)"), 512)
                dbg(Pm[:].rearrange("p a b -> p (a b)"), 512)
                dbg(DT[:].rearrange("p a b -> p (a b)"), 512)
                dbg(gt[:, 0:12], 12)
                dbg(knT[:].rearrange("p a b -> p (a b)"), 512, stage=tm)
            ck(16)
            yield
            nlev = 3 if sample else 5
            for lv in range(1, nlev + 1):
                for h in range(4):
                    P.matmul(pb[2][:, h * 128:(h + 1) * 128], Ym[:, h, :], Xm[:, h, :])
                if lv < nlev:
                    for h in range(4):
                        P.matmul(pb[3][:, h * 128:(h + 1) * 128], Xm[:, h, :], Ym[:, h, :])
                P.copy(Xm[:], pb[2][:].rearrange("p (a b) -> p a b", a=4), eng='act')
                if lv < nlev:
                    P.copy(Ym[:], pb[3][:].rearrange("p (a b) -> p a b", a=4), eng='act')
                for h in range(4):
                    P.matmul(pb[4][:, h * 128:(h + 1) * 128], Xm[:, h, :], Pm[:, h, :])
                P.tt(Pm[:], Pm[:], pb[4][:].rearrange("p (a b) -> p a b", a=4), ALU.add)
                yield
            P.copy(Tt[:], Pm[:], eng='act')
            ck(17)
            yield

            for h in range(4):
                P.matmul(pb[2][:, h * 128:(h + 1) * 128], kbg[:, h, :], Tt[:, h, :])
            P.act(nwk[:], pb[2][:].rearrange("p (a b) -> p a b", a=4), AF.Copy, scale=-1.0)
            P.tt(qd[:], qn[:], egr[:], ALU.mult)
            for h in range(4):
                P.matmul(pb[3][:, h * 128:(h + 1) * 128], knT[:, h, :], qn[:, h, :])
            P.tt(qkT[:], pb[3][:].rearrange("p (a b) -> p a b", a=4), DT[:], ALU.mult)
            ck(18)
            yield

            chunks = [(0, 16)] if sample else [(0, 64), (64, 128)]
            obank = pb[4]
            for (r0, r1) in chunks:
                for h in range(4):
                    P.matmul(pb[5][:, h * 128:(h + 1) * 128], Tt[:, h, :], bv[:, h, :], start=True, stop=False)
                    P.matmul(pb[5][:, h * 128:(h + 1) * 128], nwk[:, h, :], Sb[:, h, :], start=False, stop=True)
                P.copy(usb[r0:r1, :, :], pb[5][r0:r1, :].rearrange("p (a b) -> p a b", a=4), eng='act')
                for h in range(4):
                    oc = obank[:, h * 128 + r0:h * 128 + r1]
                    P.matmul(oc, Sb[:, h, :], qd[:, h, r0:r1], start=True, stop=False)
                    P.matmul(oc, usb[r0:r1, h, :], qkT[r0:r1, h, r0:r1], start=False, stop=True)
                for h in range(4):
                    P.matmul(pb[2][:, h * 128:(h + 1) * 128], kd[r0:r1, h, :], usb[r0:r1, h, :])
                for h in range(4):
                    P.ts(S[:, h, :], S[:, h, :], egr[:, h, r1 - 1:r1], ALU.mult)
                P.tt(S[:], S[:], pb[2][:].rearrange("p (a b) -> p a b", a=4), ALU.add)
                P.copy(Sb[:], S[:], eng='act')
                yield
            o3 = obank[:].rearrange("p (a b) -> p a b", a=4)
            ck(19)
            yield
            P.act(sqb[:, 0:4, :], o3, AF.Square)
            P.matmul(pb[3][:], onesb[:], sqb[:, 0:4, :].rearrange("p a b -> p (a b)"))
            P.act(rbc[:, 0:4, :], pb[3][:].rearrange("p (a b) -> p a b", a=4), AF.Ln, scale=1.0 / 128, bias=EPS)
            P.act(rbc[:, 0:4, :], rbc[:, 0:4, :], AF.Exp, scale=-0.5)
            P.tt(rbc[:, 0:4, :], rbc[:, 0:4, :], o3, ALU.mult)
            P.stt(mixT[:, 0:4, :], rbc[:, 0:4, :], gocol[:, 0:1], zas[:], ALU.mult, ALU.mult)
            ck(20)
            yield


        def chainB_heads(nrows, outs):
            P.tt(sqB[:, 0:512], tm[:, T_QB:T_QB + 512], tm[:, T_QB:T_QB + 512], ALU.mult)
            P.reduce(s5[:, 0:8], sqB[:, 0:512].rearrange("p (h d) -> p h d", d=64), ALU.add)
            P.tt(sqB[:, 0:256], tm[:, T_QM:T_QM + 256], tm[:, T_QM:T_QM + 256], ALU.mult)
            P.reduce(s5[:, 8:12], sqB[:, 0:256].rearrange("p (h d) -> p h d", d=64), ALU.add)
            P.tt(sqB[:, 256:288], tm[:, T_KI:T_KI + 32], tm[:, T_KI:T_KI + 32], ALU.mult)
            P.reduce(s5[:, 12:13], sqB[:, 256:288], ALU.add)
            yield
            P.ts(s5[:, 12:13], s5[:, 12:13], 2.0, ALU.mult)
            rstd_inplace(s5[:, 0:13], 1.0 / 64)
            k3 = tm[:, T_KB:T_KB + 256].rearrange("p (h d) -> p h d", h=4)
            n3 = knf[:].rearrange("p (h d) -> p h d", h=4)
            P.tt(n3, k3, s5[:, 4:8].unsqueeze(2).to_broadcast([128, 4, 64]), ALU.mult)
            P.tt(n3, n3, gsm[:, G_KB:G_KB + 64].unsqueeze(1).to_broadcast([128, 4, 64]), ALU.mult)
            P.dma(outs["k"], knf[0:nrows, :], eng='pool')
            P.dma(outs["v"], tm[0:nrows, T_VB:T_VB + 256], eng='pool')
            P.copy(nrm[:, 256:512], knf[:], eng='pool')
            q3 = tm[:, T_QB:T_QB + 256].rearrange("p (h d) -> p h d", h=4)
            yield
            m3 = sqB[:, 0:256].rearrange("p (h d) -> p h d", h=4)
            P.tt(m3, q3, s5[:, 0:4].unsqueeze(2).to_broadcast([128, 4, 64]), ALU.mult)
            P.tt(nrm[:, 0:256].rearrange("p (h d) -> p h d", h=4), m3,
                 gsm[:, G_QB:G_QB + 64].unsqueeze(1).to_broadcast([128, 4, 64]), ALU.mult)
            q3 = tm[:, T_QM:T_QM + 256].rearrange("p (h d) -> p h d", h=4)
            m3 = sqB[:, 256:512].rearrange("p (h d) -> p h d", h=4)
            P.tt(m3, q3, s5[:, 8:12].unsqueeze(2).to_broadcast([128, 4, 64]), ALU.mult)
            P.tt(nrm[:, 512:768].rearrange("p (h d) -> p h d", h=4), m3,
                 gsm[:, G_QM:G_QM + 64].unsqueeze(1).to_broadcast([128, 4, 64]), ALU.mult)
            P.ts(kxn[:], tm[:, T_KI:T_KI + 32], s5[:, 12:13], ALU.mult)
            P.tt(kxn[:], kxn[:], gsm[:, G_KI:G_KI + 32], ALU.mult)
            P.dma(outs["ki"], kxn[0:nrows, :], eng='pool')
            P.copy(kxb[:], kxn[:], eng='pool')
            kslot = outs["kslot"]
            kc = slice(kslot * 128, (kslot + 1) * 128)
            yield
            transpose_bf(QbT[:], [nrm[:, 0:128], nrm[:, 128:256]], pb[0])
            transpose_bf(KbT[:, :, kc], [nrm[:, 256:384], nrm[:, 384:512]], pb[1])
            yield
            transpose_bf(QmT[:], [nrm[:, 512:640], nrm[:, 640:768]], pb[6])
            P.matmul(pb[7][0:32, 0:128], kxb[:], idb[:])
            P.copy(kiT[:, kc], pb[7][0:32, 0:128])
            yield
            va = Vaug[:, kslot, :].rearrange("p (h e) -> p h e", e=65)
            P.memset(va[:, :, 64:65], 1.0)
            P.copy(va[:, :, 0:64], tm[:, T_VB:T_VB + 256].rearrange("p (h d) -> p h d", h=4), eng='pool')
            P.ts(wabs[:], tm[:, T_WI:T_WI + 8], IDX_SCALE, ALU.mult)
            for h in range(8):
                P.ts(wdiag[:, h, :], id32[:], wabs[:, h:h + 1], ALU.mult, eng=('dve' if h % 2 == 0 else 'pool'))
            yield

        def conv_out(nrows, dst):
            for c in range(3):
                pc = pb[c % 2]
                for kt in range(8):
                    P.matmul(pc[:], hT[:, kt, :], W[:, kt, c * 512:(c + 1) * 512], start=(kt == 0), stop=(kt == 7))
                P.copy(cv[:, c * 512:(c + 1) * 512], pc[:], eng=('act' if c % 2 == 0 else 'dve'))
            P.dma(dst, cv[nrows - 3:nrows, :], eng='pool')

        def index_scores(sc, col0, ktiles, kbase, alt=False):
            for g0 in range(0, ktiles, 4):
                nk = min(4, ktiles - g0) * 128
                kcs = slice((kbase + g0) * 128, (kbase + g0) * 128 + nk)
                dst = sc[:, col0 + g0 * 128:col0 + g0 * 128 + nk]

                def logits(h):
                    P.matmul(pb[h % 2][:, 0:nk], qiT[:, h, :], kiT[:, kcs])

                def relu_sum(h):
                    r_ = rl[h % 2][:].bitcast(BF16)
                    if alt and h % 2 == 1:
                        P.ts(r_[:, 0:nk], pb[h % 2][:, 0:nk], 0.0, ALU.max)
                    else:
                        P.act(r_[:, 0:nk], pb[h % 2][:, 0:nk], AF.Relu)
                    return r_
                logits(0)
                for h in range(8):
                    r_ = relu_sum(h)
                    if h + 1 < 8:
                        logits(h + 1)
                    P.matmul(pb[6][:, 0:nk], wdiag[:, h, :], r_[:, 0:nk], start=(h == 0), stop=(h == 7))
                    if h % 2 == 1:
                        yield
                P.copy(dst, pb[6][:, 0:nk], eng='act')
                yield

        def bisect(sc, ncols, lo_cols):
            thr, w0, t_, cnt, hh = bis[:, 0:1], bis[:, 1:2], bis[:, 2:3], bis[:, 3:4], bis[:, 4:5]
            P.reduce(w0, sc[:, 0:ncols], ALU.max)
            P.reduce(thr, sc[:, 0:lo_cols], ALU.min)
            P.ts(thr, thr, -1.0, ALU.add)
            P.tt(w0, w0, thr, ALU.subtract)
            P.ts(wtab[:], p2[:], w0, ALU.mult)
            P.tt(t_, thr, wtab[:, 0:1], ALU.add)
            for k in range(NBIS):
                jk = xs[:].bitcast(mybir.dt.uint8)
                for ci, c0 in enumerate(range(0, ncols, 2048)):
                    wd = min(2048, ncols - c0)
                    P.ts(jk[:, 0:wd], sc[:, c0:c0 + wd], t_, ALU.is_gt, (None if ci == 0 else cnt), ALU.add,
                         accum_out=cnt)
                P.ts(hh, cnt, 255.5, ALU.is_ge, 0.5, ALU.subtract)
                P.stt(t_, hh, wtab[:, k:k + 1], t_, ALU.mult, ALU.add)
                yield
            P.tt(thr, t_, wtab[:, NBIS:NBIS + 1], ALU.subtract)

        def attend(qT, keys, obank_, first, last, mask_cols=None, sc=None):
            n = len(keys)

            def scores_t(i):
                kT, va = keys[i]
                bank = pb[i % 2]
                nm = None
                if mask_cols is not None:
                    nm = nmk[i % 2]
                    P.ts(nm[:], sc[:, mask_cols[i]:mask_cols[i] + 128], bis[:, 0:1], ALU.is_le)
                for h in range(4):
                    pr = slice((h % 2) * 64, (h % 2) * 64 + 64)
                    oc = bank[:, h * 128:(h + 1) * 128]
                    P.matmul(oc, kT[pr, h // 2, :], qT[pr, h // 2, :], start=True, stop=False)
                    P.matmul(oc, (nm[:] if nm is not None else zerob[:]), negI[:], start=False, stop=True)

            scores_t(0)
            for i in range(n):
                kT, va = keys[i]
                pt = PT[i % 2]
                P.act(pt[:], pb[i % 2][:].rearrange("p (a b) -> p a b", a=4), AF.Exp, scale=0.125)
                if i + 1 < n:
                    scores_t(i + 1)
                for h in range(4):
                    P.matmul(obank_[:, h * 65:(h + 1) * 65], pt[:, h, :], va[:, h, :],
                             start=(first and i == 0 and h == 0), stop=(last and i == n - 1 and h == 3))
                yield

        def finish_heads(obank_, zcols, dst_cols):
            P.copy(sqB[:, 0:260], obank_[:, 0:260], eng='act')
            o3 = sqB[:, 0:260].rearrange("p (h e) -> p h e", e=65)
            P.copy(rcp[:, 0:4], o3[:, :, 64])
            P.recip(rcp[:, 0:4], rcp[:, 0:4])
            P.tt(o3[:, :, 0:64], o3[:, :, 0:64], rcp[:, 0:4].unsqueeze(2).to_broadcast([128, 4, 64]), ALU.mult)
            P.tt(mixBC[:, dst_cols:dst_cols + 256].rearrange("p (h d) -> p h d", h=4), o3[:, :, 0:64],
                 zs[:, zcols:zcols + 256].rearrange("p (h d) -> p h d", h=4), ALU.mult)
            yield

        def out_proj(nrows, y_dst):
            transpose_bf(mixT[:, 4:8, :], [mixBC[:, j * 128:(j + 1) * 128] for j in range(4)], pb[5])
            for c in range(2):
                bank = pb[c]
                for kt in range(8):
                    P.matmul(bank[:], mixT[:, kt, :], Wo[:, kt, c * 512:(c + 1) * 512], start=(kt == 0), stop=(kt == 7))
                P.tt(yo[:, c * 512:(c + 1) * 512], bank[:], xt[:, c * 512:(c + 1) * 512], ALU.add)
            P.dma(y_dst, yo[0:nrows, :], eng='pool')

        def dbg(ap2d, ncols, stage=None):
            if not kstop:
                return
            i = dbg_n[0]
            dbg_n[0] += 1
            if stage is not None:
                P.copy(stage[:, 0:ncols], ap2d)
                P.dma(dbg_outs[i][:, 0:ncols], stage[:, 0:ncols], eng='pool')
            else:
                P.dma(dbg_outs[i][:, 0:ncols], ap2d, eng='pool')

        def drain(g):
            for _ in g:
                pass

        def count_steps(mk):
            P.dry = True
            n = 0
            for _ in mk():
                n += 1
            P.dry = False
            return n + 1

        def interleave(mka, mkb):
            na, nb = count_steps(mka), count_steps(mkb)
            ga, gb = mka(), mkb()
            ia = ib = 0
            da = db = False
            while not (da and db):
                pick_a = (not da) and (db or ia * nb <= ib * na)
                if pick_a:
                    try:
                        next(ga)
                        ia += 1
                    except StopIteration:
                        da = True
                else:
                    try:
                        next(gb)
                        ib += 1
                    except StopIteration:
                        db = True

        def prompt_chainB(tt, outs):
            yield from chainB_heads(128, outs)
            nk = tt + 1
            yield from index_scores(scores, 0, nk, 0)
            dcol = tt * 128
            P.stt(scores[:, dcol:dcol + 128], scores[:, dcol:dcol + 128], 1.0, admP[:], ALU.mult, ALU.mult)
            P.tt(scores[:, dcol:dcol + 128], scores[:, dcol:dcol + 128], nadmP[:], ALU.add)
            if tt >= 2:
                yield from bisect(scores, nk * 128, (nk - 1) * 128)
            else:
                P.memset(bis[:, 0:1], -1.0e29)
            keys = [(KbT[:, :, j * 128:(j + 1) * 128], Vaug[:, j, :].rearrange("p (h e) -> p h e", e=65)) for j in range(nk)]
            yield from attend(QbT, keys, pb[6], True, True, mask_cols=[j * 128 for j in range(nk)], sc=scores)
            yield from finish_heads(pb[6], 0, 0)
            mkeys = [(MkT[:, :, j * 128:(j + 1) * 128], MvA[:, j, :].rearrange("p (h e) -> p h e", e=65)) for j in range(2)]
            yield from attend(QmT, mkeys, pb[7], True, True)
            yield from finish_heads(pb[7], 256, 256)

        try:
            for mt in range(2):
                mem_tile(mt)
            ck(2)
            P.memset(raw[:, :, 0:3], 0.0)
            P.memset(S[:], 0.0)
            P.memset(Sb[:], 0.0)
            for tt in range(NT):
                r = slice(tt * 128, (tt + 1) * 128)
                outs = {"k": p_k[r, :], "v": p_v[r, :], "ki": p_ki[r, :], "kslot": tt}
                layer_common(x_p[r, :], 128)
                if kstop:
                    drain(chainA(False))
                    drain(prompt_chainB(tt, outs))
                else:
                    interleave(lambda: chainA(False), lambda tt=tt, outs=outs: prompt_chainB(tt, outs))
                if tt == NT - 1:
                    conv_out(128, p_conv)
                out_proj(128, y_p[r, :])
                ck(25)
            P.dma(p_ssm.rearrange("h k v -> k h v"), S[:], eng='pool')
            ck(30)

            P.dma(S[:], st_ssm.rearrange("h k v -> k h v"))
            P.copy(Sb[:], S[:])
            P.dma(stc[:], st_conv)
            for c in range(12):
                P.matmul(pb[2][:, c * 3:(c + 1) * 3], stc[0:3, c * 128:(c + 1) * 128], id32[0:3, 0:3])
            P.copy(raw[:, :, 0:3], pb[2][:, 0:36].rearrange("p (c j) -> p c j", j=3))
            for mt in range(2):
                r = slice(mt * 128, (mt + 1) * 128)
                KV = kvc[mt]
                P.dma(KV[:, 0:256], c_mk[r, :])
                P.dma(KV[:, 256:512], c_mv[r, :])
                P.copy(kvb[:], KV[:, 0:256])
                transpose_bf(MkT[:, :, r], [kvb[:, 0:128], kvb[:, 128:256]], pb[3])
                P.copy(MvA[:, mt, :].rearrange("p (h e) -> p h e", e=65)[:, :, 0:64],
                       KV[:, 256:512].rearrange("p (h d) -> p h d", h=4), eng='pool')
            outs = {"k": s_k, "v": s_v, "ki": s_ki, "kslot": 0}
            layer_common(x_s, 16)
            drain(chainA(True))
            drain(chainB_heads(16, outs))
            conv_out(16, s_conv)
            P.dma(s_ssm.rearrange("h k v -> k h v"), S[:], eng='pool')
            ck(31)
            Wf = W[:].rearrange("p a b -> p (a b)").bitcast(F32)
            Wb = W[:].rearrange("p a b -> p (a b)")
            kiS = Wf[:, 4224:5248].rearrange("p (t c) -> p t c", c=32)
            KS = Wf[:, 5248:9344].rearrange("p (t c) -> p t c", c=256)
            VS = Wf[:, 9344:13440].rearrange("p (t c) -> p t c", c=256)
            Kb16 = Wb[:, 26880:30976].rearrange("p (t c) -> p t c", c=256)
            kib = Wb[:, 18688:19712].rearrange("p (t c) -> p t c", c=32)
            def dma_tiles(dst, src_rows, ntile, step):
                v = src_rows.rearrange("(t p) c -> p t c", p=128)
                for t0 in range(0, ntile, step):
                    P.dma(dst[:, t0:t0 + step, :], v[:, t0:t0 + step, :])
            dma_tiles(kiS, c_ki, 32, 8)
            dma_tiles(KS, c_k[0:2048, :], 16, 8)
            drain(index_scores(scoresS, PAST, 1, 0, alt=True))
            P.tt(scoresS[:, PAST:PAST + 128], scoresS[:, PAST:PAST + 128], nadmS[:], ALU.add)
            P.copy(kib, kiS)
            for g_ in range(2):
                for q4 in range(4):
                    bank = pb[2 + q4]
                    for j in range(4):
                        t_ = g_ * 16 + q4 * 4 + j
                        P.matmul(bank[0:32, j * 128:(j + 1) * 128], kib[:, t_, :], idb[:])
                    P.copy(kiT[:, q4 * 512:(q4 + 1) * 512], bank[0:32, :], eng='act')
                drain(index_scores(scoresS, g_ * 2048, 16, 0, alt=True))
            dma_tiles(VS, c_v[0:2048, :], 16, 8)
            drain(bisect(scoresS, PAST + 128, PAST))
            ck(32)
            keys = [(KbT[:, :, 0:128], Vaug[:, 0, :].rearrange("p (h e) -> p h e", e=65))]
            drain(attend(QbT, keys, pb[6], True, False, mask_cols=[PAST], sc=scoresS))
            for g_ in range(2):
                if g_ == 1:
                    dma_tiles(KS, c_k[2048:4096, :], 16, 8)
                    dma_tiles(VS, c_v[2048:4096, :], 16, 8)
                P.copy(Kb16[:, 0:6, :], KS[:, 0:6, :])
                P.copy(Kb16[:, 6:11, :], KS[:, 6:11, :], eng='act')
                P.copy(Kb16[:, 11:16, :], KS[:, 11:16, :], eng='pool')
                for q2 in range(8):
                    bank = pb[2 + q2 % 4]
                    for pr_ in range(2):
                        for j in range(2):
                            t_ = q2 * 2 + j
                            P.matmul(bank[:, (pr_ * 2 + j) * 128:(pr_ * 2 + j + 1) * 128],
                                     Kb16[:, t_, pr_ * 128:(pr_ + 1) * 128], idb[:])
                    P.copy(KbT[:, :, q2 * 256:(q2 + 1) * 256], bank[:].rearrange("p (a b) -> p a b", a=2), eng='act')
                v4 = Vaug[:, 0:16, :].rearrange("p t (h e) -> p t h e", e=65)
                for t4 in range(4):
                    P.copy(v4[:, t4 * 4:(t4 + 1) * 4, :, 0:64].rearrange("p t h d -> p (t h) d"),
                           VS[:, t4 * 4:(t4 + 1) * 4, :].rearrange("p t (h d) -> p (t h) d", d=64),
                           eng=('pool' if t4 % 2 == 0 else 'dve'))
                keys = [(KbT[:, :, j * 128:(j + 1) * 128], Vaug[:, j, :].rearrange("p (h e) -> p h e", e=65)) for j in range(16)]
                drain(attend(QbT, keys, pb[6], False, g_ == 1, mask_cols=[g_ * 2048 + j * 128 for j in range(16)], sc=scoresS))
            drain(finish_heads(pb[6], 0, 0))
            mkeys = [(MkT[:, :, j * 128:(j + 1) * 128], MvA[:, j, :].rearrange("p (h e) -> p h e", e=65)) for j in range(2)]
            drain(attend(QmT, mkeys, pb[7], True, True))
            drain(finish_heads(pb[7], 256, 256))
            out_proj(16, y_s)
        except _Stop:
            pass
        P.emit(final_wait_engine='pool')
    return nc


_CACHE = {}


def _make_in_maps(inp):
    g_all = np.ascontiguousarray(np.concatenate([inp['g_in'][0].reshape(8, 128), inp['g_mem'][0].reshape(8, 128)], 0))
    gsm = np.ascontiguousarray(np.concatenate([
        inp['g_k_B'][0], inp['g_kidx_B'][0], inp['g_k_M'][0], inp['g_q_B'][0], inp['g_q_M'][0],
        inp['dt_bias_A'][0], inp['a_log_A'][0]]).reshape(1, GSM).astype(np.float32))
    maps = []
    for b in range(8):
        maps.append({
            "x_p": np.ascontiguousarray(inp['x_prompt'][b]),
            "x_s": np.ascontiguousarray(inp['x_sample'][b]),
            "mem": np.ascontiguousarray(inp['mem_prompt'][b]),
            "w_in": np.ascontiguousarray(inp['w_in'][0]),
            "w_mem": np.ascontiguousarray(inp['w_mem_kv'][0]),
            "w_out": np.ascontiguousarray(inp['w_out'][0]),
            "g_all": g_all,
            "gsm": gsm,
            "g_o": np.ascontiguousarray(inp['g_o_A'][0].reshape(1, 128)),
            "conv_w": np.ascontiguousarray(inp['conv_w_A'][0]),
            "st_conv": np.ascontiguousarray(inp['state_conv_A'][0, b]),
            "st_ssm": np.ascontiguousarray(inp['state_ssm_A'][0, b]),
            "c_k": np.ascontiguousarray(inp['cache_k_B'][0, b].reshape(PAST, 256)),
            "c_v": np.ascontiguousarray(inp['cache_v_B'][0, b].reshape(PAST, 256)),
            "c_ki": np.ascontiguousarray(inp['cache_kidx_B'][0, b]),
            "c_mk": np.ascontiguousarray(inp['cache_mem_k'][0, b].reshape(256, 256)),
            "c_mv": np.ascontiguousarray(inp['cache_mem_v'][0, b].reshape(256, 256)),
        })
    return maps


def _assemble(res):
    def stack(name, shape):
        return np.stack([np.asarray(r[name], dtype=np.float32).reshape(shape) for r in res], 0)
    outs = (
        stack("y_p", (SEQ, D)), stack("y_s", (16, D)),
        stack("p_conv", (3, 1536))[None],
        stack("p_ssm", (4, 128, 128))[None],
        stack("p_k", (SEQ, 4, 64))[None],
        stack("p_v", (SEQ, 4, 64))[None],
        stack("p_ki", (SEQ, 32))[None],
        stack("p_mk", (256, 4, 64))[None],
        stack("p_mv", (256, 4, 64))[None],
        stack("s_conv", (3, 1536))[None],
        stack("s_ssm", (4, 128, 128))[None],
        stack("s_k", (16, 4, 64))[None],
        stack("s_v", (16, 4, 64))[None],
        stack("s_ki", (16, 32))[None],
    )
    return outs


def kernel(**inputs):
    inp = {k: np.asarray(v) for k, v in inputs.items()}
    nc = build_program()
    in_maps = _make_in_maps(inp)
    res = run_bass_kernel_spmd(nc, in_maps, core_ids=list(range(8)))
    return _assemble(res.results)
```

```python
import numpy as np
import concourse.bass as bass
import concourse.mybir as mybir
from concourse.bass_utils import run_bass_kernel_spmd

F32 = mybir.dt.float32
BF16 = mybir.dt.bfloat16
ALU = mybir.AluOpType
AF = mybir.ActivationFunctionType
AX = mybir.AxisListType


def _region(ap):
    t = ap.tensor
    name = t.name
    esz = mybir.dt.size(ap.dtype)
    pat = tuple((st_ * esz, c_) for st_, c_ in ap.ap)
    off = ap.offset * esz
    space = str(ap.space)
    if 'DRAM' in space.upper() or 'Dram' in space or 'dram' in space:
        lo = off
        hi = off + sum((c - 1) * abs(s) for s, c in pat) + esz
        return (name, 0, 1, lo, hi)
    ps, pc = pat[0]
    if ps == 0:
        ps = 1 << 30
    p0 = off // ps
    fo = off % ps
    hi = fo + sum((c - 1) * abs(s) for s, c in pat[1:]) + esz
    if 'PSUM' in space.upper():
        return (name, (p0 // 32) * 32, ((p0 + pc + 31) // 32) * 32, 0, 1 << 30)
    return (name, p0, p0 + pc, fo, hi)


def _overlap(a, b):
    return a[1] < b[2] and b[1] < a[2] and a[3] < b[4] and b[3] < a[4]


def _contains(a, b):
    return a[1] <= b[1] and a[2] >= b[2] and a[3] <= b[3] and a[4] >= b[4]


class Op:
    __slots__ = ('eng', 'fn', 'reads', 'writes', 'dma', 'deps', 'signals', 'sem', 'val', 'idx', 'pe_acc')

    def __init__(self, eng, fn, reads, writes, dma=False):
        self.eng = eng
        self.fn = fn
        self.reads = reads
        self.writes = writes
        self.dma = dma
        self.deps = set()
        self.signals = False
        self.sem = None
        self.val = 0


class Prog:
    ENGS = ('pe', 'act', 'dve', 'pool', 'sp')

    def __init__(self, nc, n_dma_sems=12, same_engine_sync=True):
        self.nc = nc
        self.ops = []
        self.n_dma_sems = n_dma_sems
        self.same_engine_sync = same_engine_sync
        self.alias = {}

    def set_alias(self, name, group):
        self.alias[name] = group

    def add(self, eng, fn, reads, writes, dma=False):
        if getattr(self, 'dry', False):
            return None
        rr = [_region(a) for a in reads if a is not None and not isinstance(a, (int, float))]
        ww = [_region(a) for a in writes if a is not None]
        op = Op(eng, fn, rr, ww, dma)
        op.idx = len(self.ops)
        self.ops.append(op)
        return op

    def resolve(self):
        writers = {}
        readers = {}
        for op in self.ops:
            for r in op.reads:
                key = self.alias.get(r[0], r[0])
                full = key != r[0]
                for (wr, wi) in writers.get(key, ()):
                    if full or _overlap(r, wr):
                        op.deps.add(wi)
            for w in op.writes:
                key = self.alias.get(w[0], w[0])
                full = key != w[0]
                for (wr, wi) in writers.get(key, ()):
                    if full or _overlap(w, wr):
                        op.deps.add(wi)
                for (rr, ri) in readers.get(key, ()):
                    if full or _overlap(w, rr):
                        op.deps.add(ri)
            for w in op.writes:
                key = self.alias.get(w[0], w[0])
                full = key != w[0]
                if full:
                    writers[key] = [(w, op.idx)]
                    readers[key] = []
                else:
                    writers[key] = [(wr, wi) for (wr, wi) in writers.get(key, ()) if not _contains(w, wr)] + [(w, op.idx)]
                    readers[key] = [(rr, ri) for (rr, ri) in readers.get(key, ()) if not _contains(w, rr)]
            for r in op.reads:
                key = self.alias.get(r[0], r[0])
                lst = readers.setdefault(key, [])
                if not op.dma:
                    lst[:] = [(rr, ri) for (rr, ri) in lst
                              if not (rr == r and self.ops[ri].eng == op.eng and not self.ops[ri].dma)]
                lst.append((r, op.idx))
            op.deps.discard(op.idx)
        for op in self.ops:
            keep = set()
            for d in op.deps:
                o2 = self.ops[d]
                if not o2.dma and not op.dma and o2.eng == op.eng:
                    if op.eng == 'pe':
                        continue
                    if not self.same_engine_sync:
                        continue
                keep.add(d)
            op.deps = keep
            for d in keep:
                self.ops[d].signals = True
        for op in self.ops:
            if op.dma:
                op.signals = True

    def emit(self, final_wait_engine='sp'):
        nc = self.nc
        self.resolve()
        import contextlib
        with contextlib.ExitStack() as st:
            esem = {e: st.enter_context(nc.semaphore('sem_' + e)) for e in ('pe', 'act', 'dve', 'pool')}
            dsem = [st.enter_context(nc.semaphore('dsem%d' % i)) for i in range(self.n_dma_sems)]
            ecount = {e: 0 for e in esem}
            dcount = [0] * self.n_dma_sems
            half = self.n_dma_sems // 2
            kq = {'sp': 0, 'pool': 0, 'act': 0}
            for op in self.ops:
                if not op.signals:
                    continue
                if op.dma:
                    if op.eng == 'pool':
                        i = half + kq['pool'] % (self.n_dma_sems - half)
                        kq['pool'] += 1
                    else:
                        i = kq['sp'] % half
                        kq['sp'] += 1
                    dcount[i] += 16
                    op.sem = ('d', i)
                    op.val = dcount[i]
                else:
                    ecount[op.eng] += 1
                    op.sem = ('e', op.eng)
                    op.val = ecount[op.eng]
            final = {}
            for op in self.ops:
                if op.dma:
                    final[op.sem] = max(final.get(op.sem, 0), op.val)

            def semh(s):
                return esem[s[1]] if s[0] == 'e' else dsem[s[1]]

            per_eng = {e: [o for o in self.ops if o.eng == e] for e in self.ENGS}
            block = st.enter_context(nc.Block())

            def run(engname, eng):
                known = {}
                for op in per_eng[engname]:
                    need = {}
                    for d in op.deps:
                        o2 = self.ops[d]
                        need[o2.sem] = max(need.get(o2.sem, 0), o2.val)
                    if op.dma and op.val > 16:
                        need[op.sem] = max(need.get(op.sem, 0), op.val - 16)
                    for s, v in need.items():
                        if known.get(s, 0) >= v:
                            continue
                        eng.wait_ge(semh(s), v)
                        known[s] = v
                    ins = op.fn(eng)
                    if op.signals:
                        ins.then_inc(semh(op.sem), 16 if op.dma else 1)
                if engname == final_wait_engine:
                    for s, v in final.items():
                        if known.get(s, 0) < v:
                            eng.wait_ge(semh(s), v)

            @block.tensor
            def _(e):
                run('pe', e)

            @block.scalar
            def _(e):
                run('act', e)

            @block.vector
            def _(e):
                run('dve', e)

            @block.gpsimd
            def _(e):
                run('pool', e)

            @block.sync
            def _(e):
                run('sp', e)

    def dma(self, out, in_, eng='sp'):
        return self.add(eng, lambda e: e.dma_start(out=out, in_=in_), [in_], [out], dma=True)

    def matmul(self, out, lhsT, rhs, start=True, stop=True):
        rd = [lhsT, rhs] + ([] if start else [out])
        return self.add('pe', lambda e: e.matmul(out, lhsT, rhs, start=start, stop=stop), rd, [out])

    def transpose(self, out, in_, ident):
        return self.add('pe', lambda e: e.transpose(out, in_, ident), [in_, ident], [out])

    def act(self, out, in_, func, bias=None, scale=None, accum_out=None, eng='act'):
        kw = {}
        rd = [in_]
        if bias is not None:
            kw['bias'] = bias
            rd.append(bias)
        if scale is not None:
            kw['scale'] = scale
            rd.append(scale)
        wr = [out]
        if accum_out is not None:
            kw['accum_out'] = accum_out
            wr.append(accum_out)
        return self.add('act', lambda e: e.activation(out, in_, func, **kw), rd, wr)

    def tt(self, out, in0, in1, op, eng='dve'):
        return self.add(eng, lambda e: e.tensor_tensor(out, in0, in1, op), [in0, in1], [out])

    def ts(self, out, in0, s1, op0, s2=None, op1=None, accum_out=None, eng='dve'):
        rd = [in0, s1, s2]
        wr = [out, accum_out]

        def fn(e):
            kw = {}
            if accum_out is not None:
                kw['accum_out'] = accum_out
            if op1 is not None:
                return e.tensor_scalar(out, in0, s1, s2, op0, op1, **kw)
            return e.tensor_scalar(out, in0, s1, None, op0, **kw)
        return self.add(eng, fn, rd, wr)

    def stt(self, out, in0, scalar, in1, op0, op1, eng='dve'):
        return self.add(eng, lambda e: e.scalar_tensor_tensor(out, in0, scalar, in1, op0, op1),
                        [in0, scalar, in1], [out])

    def copy(self, out, in_, eng='dve'):
        if eng == 'act':
            return self.add('act', lambda e: e.activation(out, in_, AF.Copy), [in_], [out])
        return self.add(eng, lambda e: e.tensor_copy(out, in_), [in_], [out])

    def memset(self, ap, val, eng='dve'):
        return self.add(eng, lambda e: e.memset(ap, val), [], [ap])

    def reduce(self, out, in_, op, axis=AX.X, eng='dve'):
        return self.add(eng, lambda e: e.tensor_reduce(out, in_, axis, op), [in_], [out])

    def recip(self, out, in_):
        return self.add('dve', lambda e: e.reciprocal(out, in_), [in_], [out])


D = 1024
IN_W = 3888
SEQ = 2048
NT = SEQ // 128
PAST = 4096
EPS = 1e-6
C_ZA = 1536
C_TM = 2048
T_BA, T_AA, T_QB, T_KB, T_VB, T_ZB, T_KI, T_WI, T_QM, T_ZM = 0, 4, 8, 264, 520, 776, 1288, 1320, 1328, 1584
C_QI = 3080
TMW = IN_W - C_TM
IDX_SCALE = (8 ** -0.5) * (32 ** -0.5)
NBIS = 17
NEGBIG = -1.0e30
G_KB, G_KI, G_KM, G_QB, G_QM, G_DTB, G_ALOG = 0, 64, 96, 160, 224, 288, 292
GSM = 296


class _Stop(Exception):
    pass


def build_program(kstop=0):
    import contextlib

    def ck(n):
        if kstop == n:
            raise _Stop()
    nc = bass.Bass("TRN2", target_bir_lowering=False)

    def din(name, shape):
        return nc.dram_tensor(name, shape, F32, kind="ExternalInput").ap()

    def dout(name, shape):
        return nc.dram_tensor(name, shape, F32, kind="ExternalOutput").ap()

    x_p = din("x_p", [SEQ, D])
    x_s = din("x_s", [16, D])
    mem = din("mem", [256, D])
    w_in = din("w_in", [D, IN_W])
    w_mem = din("w_mem", [D, 512])
    w_out = din("w_out", [D, D])
    g_all = din("g_all", [16, 128])
    gsm_d = din("gsm", [1, GSM])
    g_o_d = din("g_o", [1, 128])
    convw_d = din("conv_w", [4, 1536])
    st_conv = din("st_conv", [3, 1536])
    st_ssm = din("st_ssm", [4, 128, 128])
    c_k = din("c_k", [PAST, 256])
    c_v = din("c_v", [PAST, 256])
    c_ki = din("c_ki", [PAST, 32])
    c_mk = din("c_mk", [256, 256])
    c_mv = din("c_mv", [256, 256])

    y_p = dout("y_p", [SEQ, D])
    y_s = dout("y_s", [16, D])
    p_conv = dout("p_conv", [3, 1536])
    p_ssm = dout("p_ssm", [4, 128, 128])
    p_k = dout("p_k", [SEQ, 256])
    p_v = dout("p_v", [SEQ, 256])
    p_ki = dout("p_ki", [SEQ, 32])
    p_mk = dout("p_mk", [256, 256])
    p_mv = dout("p_mv", [256, 256])
    s_conv = dout("s_conv", [3, 1536])
    s_ssm = dout("s_ssm", [4, 128, 128])
    s_k = dout("s_k", [16, 256])
    s_v = dout("s_v", [16, 256])
    s_ki = dout("s_ki", [16, 32])

    dbg_outs = [dout("dbg%d" % i, [128, 1024]) for i in range(6)] if kstop else []
    dbg_n = [0]

    with contextlib.ExitStack() as st:
        def sb(name, shape, dt=F32):
            return st.enter_context(nc.sbuf_tensor(name, shape, dt))

        def ps(name, shape, dt=F32):
            return st.enter_context(nc.psum_tensor(name, shape, dt))

        P = Prog(nc, n_dma_sems=16)

        W = sb("W", [128, 8, IN_W], BF16)
        Wo = sb("Wo", [128, 8, D], BF16)
        KbT = sb("KbT", [128, 2, 2048], BF16)
        Wm = KbT[:].rearrange("p a (b c) -> p (a b) c", c=512)
        Vaug = sb("Vaug", [128, 16, 260], BF16)
        kiT = sb("kiT", [32, 2048], BF16)
        scores = sb("scores", [128, 2048])
        g16 = scores[0:16, 1536:1664]
        go1 = scores[0:1, 1664:1792]
        cw4 = scores[0:4, 0:1536]
        stc = scores[0:3, 0:1536]
        scoresS = W[:].rearrange("p a b -> p (a b)").bitcast(F32)[:, 0:4224]
        id32 = sb("id32", [128, 128])
        idb = sb("idb", [128, 128], BF16)
        ones32 = sb("ones32", [128, 128])
        onesb = sb("onesb", [128, 128], BF16)
        negI = sb("negI", [128, 128], BF16)
        zerob = sb("zerob", [128, 128], BF16)
        blkP = sb("blkP", [128, 128])
        blkS = sb("blkS", [128, 128])
        triP = sb("triP", [128, 128])
        triS = sb("triS", [128, 128])
        lsP = sb("lsP", [128, 128])
        lsS = sb("lsS", [128, 128])
        admP = sb("admP", [128, 128])
        nadmP = sb("nadmP", [128, 128])
        nadmS = sb("nadmS", [128, 128])
        gcol = sb("gcol", [128, 16])
        gsm = sb("gsm_t", [128, GSM])
        negA = sb("negA", [128, 4])
        gocol = sb("gocol", [128, 1])
        wc = sb("wc", [128, 12, 4])
        xtb = [sb("xt%d" % i, [128, D]) for i in range(2)]
        jk8 = sb("jk8", [128, 2048], mybir.dt.uint8)
        xs = sb("xs", [128, D], BF16)
        hT = sb("hT", [128, 8, 128], BF16)
        rx = sb("rx", [128, 1])
        tm = sb("tm", [128, TMW])
        raw = sb("raw", [128, 12, 131])
        acc = sb("acc", [128, 12, 128])
        vsb = sb("vsb", [128, 4, 128], BF16)
        zas = sb("zas", [128, 4, 128], BF16)
        zs = sb("zs", [128, 512], BF16)
        sqb = sb("sqb", [128, 8, 128], BF16)
        rbc = sb("rbc", [128, 8, 128])
        qn = sb("qn", [128, 4, 128], BF16)
        knT = sb("knT", [128, 4, 128], BF16)
        qiT = sb("qiT", [32, 8, 128], BF16)
        gt = sb("gt", [128, 28])
        egr = sb("egr", [128, 4, 128])
        DT = sb("DT", [128, 4, 128])
        Xm = sb("Xm", [128, 4, 128])
        Ym = sb("Ym", [128, 4, 128])
        Dg = Xm
        Dst = Ym
        Pm = sb("Pm", [128, 4, 128])
        Tt = sb("Tt", [128, 4, 128], BF16)
        bv = sb("bv", [128, 4, 128], BF16)
        kbg = sb("kbg", [128, 4, 128], BF16)
        kd = sb("kd", [128, 4, 128], BF16)
        nwk = sb("nwk", [128, 4, 128], BF16)
        qd = sb("qd", [128, 4, 128], BF16)
        qkT = sb("qkT", [128, 4, 128], BF16)
        usb = sb("usb", [128, 4, 128], BF16)
        S = sb("S", [128, 4, 128])
        Sb = sb("Sb", [128, 4, 128], BF16)
        mixT = sb("mixT", [128, 8, 128], BF16)
        mixBC = sb("mixBC", [128, 512], BF16)
        s5 = sb("s5", [128, 16])
        rbcf = rbc[:].rearrange("p a b -> p (a b)")
        sq = rbcf
        accf = acc[:].rearrange("p a b -> p (a b)")
        sqB = sb("sqB", [128, 512])
        nrm = sb("nrm", [128, 768], BF16)
        knf = sb("knf", [128, 256])
        kxn = sb("kxn", [128, 32])
        kxb = sb("kxb", [128, 32], BF16)
        QbT = sb("QbT", [128, 2, 128], BF16)
        QmT = sb("QmT", [128, 2, 128], BF16)
        MkT = sb("MkT", [128, 2, 256], BF16)
        MvA = sb("MvA", [128, 2, 260], BF16)
        wdiag = sb("wdiag", [128, 8, 128], BF16)
        wabs = sb("wabs", [128, 8])
        wsgn = sb("wsgn", [128, 8])
        rl = [sb("rl%d" % i, [128, 512]) for i in range(2)]
        kvc = rl
        up32 = rl[0][:, 0:128]
        lo32 = rl[0][:, 128:256]
        PT = [sb("PT%d" % i, [128, 4, 128], BF16) for i in range(2)]
        nmk = [sb("nmk%d" % i, [128, 128], BF16) for i in range(2)]
        bis = sb("bis", [128, 8])
        wtab = sb("wtab", [128, NBIS + 1])
        p2 = sb("p2", [128, NBIS + 1])
        rcp = sb("rcp", [128, 8])
        ob = rbcf[:, 768:1024]
        cv = accf
        kvb = sb("kvb", [128, 256], BF16)
        kic = sb("kic", [128, 32])
        yo = accf[:, 0:D]

        pb = [ps("pb%d" % i, [128, 512]) for i in range(8)]


        def aff(out, cmp, mult, pat):
            P.add('pool', lambda e: e.affine_select(out, out, [[pat, 128]], cmp, 0.0, base=0,
                                                    channel_multiplier=mult), [out], [out])
        P.memset(id32[:], 0.0)
        P.add('pool', lambda e: e.affine_select(id32[:], id32[:], [[-1, 128]], ALU.not_equal, 1.0,
                                                base=0, channel_multiplier=1), [id32[:]], [id32[:]])
        P.copy(idb[:], id32[:])
        P.ts(negI[:], id32[:], -30000.0, ALU.mult)
        P.memset(zerob[:], 0.0)
        for k_ in range(NBIS + 1):
            P.memset(p2[:, k_:k_ + 1], 2.0 ** -(k_ + 1), eng='pool')
        P.memset(ones32[:], 1.0)
        P.memset(onesb[:], 1.0)
        P.memset(up32, 1.0)
        aff(up32, ALU.is_ge, -1, 1)
        P.memset(lo32, 1.0)
        aff(lo32, ALU.is_gt, 1, -1)
        P.memset(blkP[:], 0.0)
        P.memset(blkP[0:64, 0:64], 1.0)
        P.memset(blkP[64:128, 64:128], 1.0)
        P.memset(blkS[:], 0.0)
        P.memset(blkS[0:16, 0:16], 1.0)
        P.tt(triP[:], up32, blkP[:], ALU.mult)
        P.tt(triS[:], up32, blkS[:], ALU.mult)
        P.tt(lsP[:], lo32, blkP[:], ALU.mult)
        P.tt(lsS[:], lo32, blkS[:], ALU.mult)
        P.memset(admP[:], 1.0)
        P.memset(admP[0:64, 64:128], 0.0)
        P.ts(nadmP[:], admP[:], -1.0, ALU.add, 1.0e30, ALU.mult)
        P.memset(nadmS[:], NEGBIG)
        P.memset(nadmS[:, 0:16], 0.0)

        P.dma(g16, g_all)
        P.dma(gsm[:], gsm_d.partition_broadcast(128))
        P.dma(go1, g_o_d)
        P.dma(cw4, convw_d)
        P.matmul(pb[0][:, 0:16], g16, id32[0:16, 0:16])
        P.copy(gcol[:], pb[0][:, 0:16])
        P.matmul(pb[0][:, 16:17], go1, id32[0:1, 0:1])
        P.copy(gocol[:], pb[0][:, 16:17])
        for c in range(12):
            P.matmul(pb[1][:, c * 4:(c + 1) * 4], cw4[0:4, c * 128:(c + 1) * 128], id32[0:4, 0:4])
        P.copy(wc[:], pb[1][:, 0:48].rearrange("p (c j) -> p c j", j=4))
        P.act(negA[:], gsm[:, G_ALOG:G_ALOG + 4], AF.Exp)
        P.ts(negA[:], negA[:], -1.0, ALU.mult)

        def cast_w(dst, src, gc, ncols):
            a = ncols * 9 // 25
            b = ncols * 17 // 25
            if gc is None:
                P.copy(dst[:, 0:a], src[:, 0:a])
                P.copy(dst[:, a:b], src[:, a:b], eng='act')
                P.copy(dst[:, b:ncols], src[:, b:ncols], eng='pool')
            else:
                P.ts(dst[:, 0:a], src[:, 0:a], gc, ALU.mult)
                P.act(dst[:, a:b], src[:, a:b], AF.Copy, scale=gc)
                P.ts(dst[:, b:ncols], src[:, b:ncols], gc, ALU.mult, 1.0, ALU.mult, eng='pool')

        CH = IN_W // 3
        slots = [scores[:, 0:CH], accf[:, 0:CH]]
        k = 0
        for kt in range(8):
            for hf in range(3):
                s_ = slots[k % 2]
                k += 1
                P.dma(s_, w_in[kt * 128:(kt + 1) * 128, hf * CH:(hf + 1) * CH])
                cast_w(W[:, kt, hf * CH:(hf + 1) * CH], s_, gcol[:, kt:kt + 1], CH)
        for kt in range(8):
            s_ = slots[k % 2][:, 0:512]
            k += 1
            P.dma(s_, w_mem[kt * 128:(kt + 1) * 128, :])
            cast_w(Wm[:, kt, :], s_, gcol[:, 8 + kt:9 + kt], 512)
        for kt in range(8):
            s_ = slots[k % 2][:, 0:1024]
            k += 1
            P.dma(s_, w_out[kt * 128:(kt + 1) * 128, :])
            cast_w(Wo[:, kt, :], s_, None, 1024)

        def load_h(x_src, nrows, xt, banks):
            if nrows < 128:
                P.memset(xt[:], 0.0)
            P.dma(xt[0:nrows, :], x_src)
            P.add('dve', lambda e: e.scalar_tensor_tensor(yo, xt[:], 1.0, xt[:], ALU.mult, ALU.mult,
                                                          accum_out=rx[:]), [xt[:]], [yo, rx[:]])
            P.ts(rx[:], rx[:], 1.0 / D, ALU.mult, EPS, ALU.add)
            P.act(rx[:], rx[:], AF.Ln)
            P.act(rx[:], rx[:], AF.Exp, scale=-0.5)
            P.ts(xs[:], xt[:], rx[:], ALU.mult)
            for half in range(2):
                for j in range(4):
                    kt = half * 4 + j
                    P.matmul(banks[half][:, j * 128:(j + 1) * 128], xs[:, kt * 128:(kt + 1) * 128], idb[:])
            P.copy(hT[:, 0:4, :], banks[0][:].rearrange("p (a b) -> p a b", a=4), eng='act')
            P.copy(hT[:, 4:8, :], banks[1][:].rearrange("p (a b) -> p a b", a=4), eng='act')

        def rstd_inplace(ap, scale, eps=EPS, post=None):
            P.ts(ap, ap, scale, ALU.mult, eps, ALU.add)
            P.act(ap, ap, AF.Ln)
            if post is None:
                P.act(ap, ap, AF.Exp, scale=-0.5)
            else:
                P.act(ap, ap, AF.Exp, scale=-0.5, bias=post)

        def transpose_bf(dst, src_list, pbank):
            for j, s_ in enumerate(src_list):
                P.matmul(pbank[:, j * 128:(j + 1) * 128], s_, idb[:])
            n = len(src_list)
            P.copy(dst, pbank[:, 0:n * 128].rearrange("p (a b) -> p a b", a=n), eng='act')

        def mem_tile(mt):
            r = slice(mt * 128, (mt + 1) * 128)
            load_h(mem[r, :], 128, xtb[0], (pb[0], pb[1]))
            for kt in range(8):
                P.matmul(pb[2][:], hT[:, kt, :], Wm[:, kt, :], start=(kt == 0), stop=(kt == 7))
            KV = kvc[mt]
            P.copy(KV[:], pb[2][:], eng='act')
            P.tt(sq[:, 0:256], KV[:, 0:256], KV[:, 0:256], ALU.mult)
            P.reduce(s5[:, 0:4], sq[:, 0:256].rearrange("p (h d) -> p h d", h=4), ALU.add)
            rstd_inplace(s5[:, 0:4], 1.0 / 64)
            k3 = KV[:, 0:256].rearrange("p (h d) -> p h d", h=4)
            n3 = knf[:].rearrange("p (h d) -> p h d", h=4)
            P.tt(n3, k3, s5[:, 0:4].unsqueeze(2).to_broadcast([128, 4, 64]), ALU.mult)
            P.tt(n3, n3, gsm[:, G_KM:G_KM + 64].unsqueeze(1).to_broadcast([128, 4, 64]), ALU.mult)
            P.dma(p_mk[r, :], knf[:], eng='pool')
            P.dma(p_mv[r, :], KV[:, 256:512], eng='pool')
            P.copy(kvb[:], knf[:])
            transpose_bf(MkT[:, :, r], [kvb[:, 0:128], kvb[:, 128:256]], pb[3])
            P.memset(MvA[:, mt, :].rearrange("p (h e) -> p h e", e=65)[:, :, 64:65], 1.0)
            P.copy(MvA[:, mt, :].rearrange("p (h e) -> p h e", e=65)[:, :, 0:64],
                   KV[:, 256:512].rearrange("p (h d) -> p h d", h=4), eng='pool')

        def layer_common(x_src, nrows, xt, preloaded=False):
            if not preloaded:
                load_h(x_src, nrows, xt, (pb[0], pb[1]))
            for c in range(16):
                bank = pb[2 + c // 4]
                for kt in range(8):
                    P.matmul(bank[:, (c % 4) * 128:(c % 4 + 1) * 128], W[:, kt, c * 128:(c + 1) * 128], hT[:, kt, :],
                             start=(kt == 0), stop=(kt == 7))
            for c in range(3):
                P.copy(raw[:, c * 4:(c + 1) * 4, 3:131], pb[2 + c][:].rearrange("p (a b) -> p a b", a=4), eng='act')
            for h in range(8):
                bank = pb[6 + h // 4]
                for kt in range(8):
                    P.matmul(bank[0:32, (h % 4) * 128:(h % 4 + 1) * 128], W[:, kt, C_QI + h * 32:C_QI + (h + 1) * 32],
                             hT[:, kt, :], start=(kt == 0), stop=(kt == 7))
            P.copy(qiT[:, 0:4, :], pb[6][0:32, :].rearrange("p (a b) -> p a b", a=4), eng='act')
            P.copy(qiT[:, 4:8, :], pb[7][0:32, :].rearrange("p (a b) -> p a b", a=4), eng='act')
            ck(10)
            for c in range(12):
                P.ts(acc[:, c, :], raw[:, c, 0:128], wc[:, c, 0:1], ALU.mult)
                for j in range(1, 4):
                    P.stt(acc[:, c, :], raw[:, c, j:j + 128], wc[:, c, j:j + 1], acc[:, c, :], ALU.mult, ALU.add)
            P.copy(raw[:, :, 0:3], raw[:, :, 128:131], eng='pool')
            ck(11)
            P.act(acc[:, 0:8, :], acc[:, 0:8, :], AF.Silu)
            P.act(vsb[:], acc[:, 8:12, :], AF.Silu)
            P.act(zas[:], pb[5][:].rearrange("p (a b) -> p a b", a=4), AF.Silu)
            for ci, c0 in enumerate(range(0, TMW, 512)):
                wd = min(512, TMW - c0)
                bank = pb[ci % 2]
                for kt in range(8):
                    P.matmul(bank[:, 0:wd], hT[:, kt, :], W[:, kt, C_TM + c0:C_TM + c0 + wd], start=(kt == 0), stop=(kt == 7))
                P.copy(tm[:, c0:c0 + wd], bank[:, 0:wd], eng='act')
            P.act(zs[:, 0:256], tm[:, T_ZB:T_ZB + 256], AF.Silu)
            P.act(zs[:, 256:512], tm[:, T_ZM:T_ZM + 256], AF.Silu)
            ck(12)

        def chainA(sample, tail=None):
            tri, ls = (triS, lsS) if sample else (triP, lsP)
            P.tt(sqb[:], acc[:, 0:8, :], acc[:, 0:8, :], ALU.mult, eng='pool')
            for g_ in range(2):
                P.matmul(pb[2 + g_][:], onesb[:], sqb[:, g_ * 4:(g_ + 1) * 4, :].rearrange("p a b -> p (a b)"))
            P.act(rbc[:, 0:4, :], pb[2][:].rearrange("p (a b) -> p a b", a=4), AF.Ln, bias=EPS)
            P.act(rbc[:, 4:8, :], pb[3][:].rearrange("p (a b) -> p a b", a=4), AF.Ln, bias=EPS)
            P.act(rbc[:, 0:4, :], rbc[:, 0:4, :], AF.Exp, scale=-0.5, bias=float(np.log(128.0 ** -0.5)))
            P.act(rbc[:, 4:8, :], rbc[:, 4:8, :], AF.Exp, scale=-0.5)
            P.tt(qn[:], acc[:, 0:4, :], rbc[:, 0:4, :], ALU.mult)
            P.tt(knT[:], acc[:, 4:8, :], rbc[:, 4:8, :], ALU.mult)
            if kstop == 13:
                dbg(raw[:, 4:8, 3:131], 512, stage=None) if False else None
                dbg(acc[:, 4:8, :].rearrange("p a b -> p (a b)"), 512)
                dbg(rbc[:, 4:8, :].rearrange("p a b -> p (a b)"), 512)
                dbg(acc[:, 0:4, :].rearrange("p a b -> p (a b)"), 512)
                dbg(rbc[:, 0:4, :].rearrange("p a b -> p (a b)"), 512)
                dbg(knT[:].rearrange("p a b -> p (a b)"), 512, stage=tm)
                dbg(qn[:].rearrange("p a b -> p (a b)"), 512, stage=tm)
            ck(13)
            yield

            beta, gg, gc, nbeta, egc, ekd, c1 = (gt[:, 0:4], gt[:, 4:8], gt[:, 8:12], gt[:, 12:16],
                                                 gt[:, 16:20], gt[:, 20:24], gt[:, 24:28])
            P.act(beta, tm[:, T_BA:T_BA + 4], AF.Exp, scale=-1.0)
            P.ts(beta, beta, 1.0, ALU.add)
            P.recip(beta, beta)
            P.ts(nbeta, beta, -1.0, ALU.mult)
            P.tt(gg, tm[:, T_AA:T_AA + 4], gsm[:, G_DTB:G_DTB + 4], ALU.add)
            P.act(gg, gg, AF.Exp)
            P.act(gg, gg, AF.Ln, bias=1.0)
            P.tt(gg, gg, negA[:], ALU.mult)
            ck(131)
            yield
            blk = blkS if sample else blkP
            P.matmul(pb[4][:, 0:4], tri[:], gg)
            P.matmul(pb[4][:, 4:8], blk[:], gg)
            P.copy(gc, pb[4][:, 0:4])
            P.tt(ekd, pb[4][:, 4:8], gc, ALU.subtract)
            P.act(ekd, ekd, AF.Exp)
            P.act(egc, gc, AF.Exp)
            P.tt(c1, beta, egc, ALU.mult)
            ck(132)
            yield
            for h in range(4):
                P.ts(Dg[:, h, :], tri[:], gg[:, h:h + 1], ALU.mult)
                P.matmul(pb[5][:, h * 128:(h + 1) * 128], ones32[:], Dg[:, h, :])
            gcr = pb[5][:].rearrange("p (a b) -> p a b", a=4)
            ck(133)
            yield
            P.act(egr[:], gcr, AF.Exp)
            ck(134)
            yield
            P.copy(DT[:], gcr, eng='act')
            for h in range(4):
                P.ts(Dst[:, h, :], DT[:, h, :], gc[:, h:h + 1], ALU.subtract, 0.0, ALU.max)
                P.ts(DT[:, h, :], DT[:, h, :], gc[:, h:h + 1], ALU.subtract, 0.0, ALU.min)
            ck(135)
            yield
            P.act(Dst[:], Dst[:], AF.Exp, scale=-1.0)
            P.act(DT[:], DT[:], AF.Exp)
            ck(136)
            yield
            P.tt(Dst[:], Dst[:], ls[:].unsqueeze(1).to_broadcast([128, 4, 128]), ALU.mult)
            P.tt(DT[:], DT[:], tri[:].unsqueeze(1).to_broadcast([128, 4, 128]), ALU.mult)
            ck(14)
            yield

            for h in range(4):
                P.matmul(pb[2][:, h * 128:(h + 1) * 128], knT[:, h, :], idb[:])
                P.matmul(pb[3][:, h * 128:(h + 1) * 128], vsb[:, h, :], idb[:])
            ktm = pb[2][:].rearrange("p (a b) -> p a b", a=4)
            vtm = pb[3][:].rearrange("p (a b) -> p a b", a=4)
            P.tt(bv[:], vtm, beta.unsqueeze(2).to_broadcast([128, 4, 128]), ALU.mult)
            P.tt(kbg[:], ktm, c1.unsqueeze(2).to_broadcast([128, 4, 128]), ALU.mult)
            P.tt(kd[:], ktm, ekd.unsqueeze(2).to_broadcast([128, 4, 128]), ALU.mult)
            ck(15)
            yield

            for h in range(4):
                P.matmul(pb[2][:, h * 128:(h + 1) * 128], knT[:, h, :], knT[:, h, :])
            for h in range(4):
                P.ts(Dst[:, h, :], Dst[:, h, :], nbeta[:, h:h + 1], ALU.mult)
            P.tt(Xm[:], pb[2][:].rearrange("p (a b) -> p a b", a=4), Dst[:], ALU.mult)
            for h in range(4):
                P.matmul(pb[3][:, h * 128:(h + 1) * 128], Xm[:, h, :], id32[:])
            P.copy(Ym[:], pb[3][:].rearrange("p (a b) -> p a b", a=4), eng='act')
            P.tt(Pm[:], Ym[:], id32[:].unsqueeze(1).to_broadcast([128, 4, 128]), ALU.add)
            if kstop == 16:
                dbg(Xm[:].rearrange("p a b -> p (a b)"), 512)
                dbg(Ym[:].rearrange("p a b -> p (a b)"), 512)
                dbg(Pm[:].rearrange("p a b -> p (a b)"), 512)
                dbg(DT[:].rearrange("p a b -> p (a b)"), 512)
                dbg(gt[:, 0:12], 12)
                dbg(knT[:].rearrange("p a b -> p (a b)"), 512, stage=tm)
            ck(16)
            yield
            nlev = 3 if sample else 5
            for lv in range(1, nlev + 1):
                for h in range(4):
                    P.matmul(pb[2][:, h * 128:(h + 1) * 128], Ym[:, h, :], Xm[:, h, :])
                if lv < nlev:
                    for h in range(4):
                        P.matmul(pb[3][:, h * 128:(h + 1) * 128], Xm[:, h, :], Ym[:, h, :])
                P.copy(Xm[:], pb[2][:].rearrange("p (a b) -> p a b", a=4), eng='act')
                if lv < nlev:
                    P.copy(Ym[:], pb[3][:].rearrange("p (a b) -> p a b", a=4), eng='act')
                for h in range(4):
                    P.matmul(pb[4][:, h * 128:(h + 1) * 128], Xm[:, h, :], Pm[:, h, :])
                P.tt(Pm[:], Pm[:], pb[4][:].rearrange("p (a b) -> p a b", a=4), ALU.add)
                yield
            P.copy(Tt[:], Pm[:], eng='act')
            ck(17)
            yield

            for h in range(4):
                P.matmul(pb[2][:, h * 128:(h + 1) * 128], kbg[:, h, :], Tt[:, h, :])
            P.act(nwk[:], pb[2][:].rearrange("p (a b) -> p a b", a=4), AF.Copy, scale=-1.0)
            P.tt(qd[:], qn[:], egr[:], ALU.mult)
            for h in range(4):
                P.matmul(pb[3][:, h * 128:(h + 1) * 128], knT[:, h, :], qn[:, h, :])
            P.tt(qkT[:], pb[3][:].rearrange("p (a b) -> p a b", a=4), DT[:], ALU.mult)
            ck(18)
            yield

            chunks = [(0, 16)] if sample else [(0, 64), (64, 128)]
            obank = pb[4]
            for (r0, r1) in chunks:
                for h in range(4):
                    P.matmul(pb[5][:, h * 128:(h + 1) * 128], Tt[:, h, :], bv[:, h, :], start=True, stop=False)
                    P.matmul(pb[5][:, h * 128:(h + 1) * 128], nwk[:, h, :], Sb[:, h, :], start=False, stop=True)
                P.copy(usb[r0:r1, :, :], pb[5][r0:r1, :].rearrange("p (a b) -> p a b", a=4), eng='act')
                for h in range(4):
                    oc = obank[:, h * 128 + r0:h * 128 + r1]
                    P.matmul(oc, Sb[:, h, :], qd[:, h, r0:r1], start=True, stop=False)
                    P.matmul(oc, usb[r0:r1, h, :], qkT[r0:r1, h, r0:r1], start=False, stop=True)
                for h in range(4):
                    P.matmul(pb[2][:, h * 128:(h + 1) * 128], kd[r0:r1, h, :], usb[r0:r1, h, :])
                for h in range(4):
                    P.ts(S[:, h, :], S[:, h, :], egr[:, h, r1 - 1:r1], ALU.mult)
                P.tt(S[:], S[:], pb[2][:].rearrange("p (a b) -> p a b", a=4), ALU.add)
                P.copy(Sb[:], S[:], eng='act')
                yield
            o3 = obank[:].rearrange("p (a b) -> p a b", a=4)
            ck(19)
            yield
            P.act(sqb[:, 0:4, :], o3, AF.Square)
            P.matmul(pb[3][:], onesb[:], sqb[:, 0:4, :].rearrange("p a b -> p (a b)"))
            P.act(rbc[:, 0:4, :], pb[3][:].rearrange("p (a b) -> p a b", a=4), AF.Ln, scale=1.0 / 128, bias=EPS)
            P.act(rbc[:, 0:4, :], rbc[:, 0:4, :], AF.Exp, scale=-0.5)
            P.tt(rbc[:, 0:4, :], rbc[:, 0:4, :], o3, ALU.mult)
            P.stt(mixT[:, 0:4, :], rbc[:, 0:4, :], gocol[:, 0:1], zas[:], ALU.mult, ALU.mult)
            ck(20)
            yield
            if tail is not None:
                tail()
                yield


        def chainB_heads(nrows, outs):
            P.tt(sqB[:, 0:512], tm[:, T_QB:T_QB + 512], tm[:, T_QB:T_QB + 512], ALU.mult)
            P.reduce(s5[:, 0:8], sqB[:, 0:512].rearrange("p (h d) -> p h d", d=64), ALU.add)
            P.tt(sqB[:, 0:256], tm[:, T_QM:T_QM + 256], tm[:, T_QM:T_QM + 256], ALU.mult)
            P.reduce(s5[:, 8:12], sqB[:, 0:256].rearrange("p (h d) -> p h d", d=64), ALU.add)
            P.tt(sqB[:, 256:288], tm[:, T_KI:T_KI + 32], tm[:, T_KI:T_KI + 32], ALU.mult)
            P.reduce(s5[:, 12:13], sqB[:, 256:288], ALU.add)
            yield
            P.ts(s5[:, 12:13], s5[:, 12:13], 2.0, ALU.mult)
            rstd_inplace(s5[:, 0:13], 1.0 / 64)
            k3 = tm[:, T_KB:T_KB + 256].rearrange("p (h d) -> p h d", h=4)
            n3 = knf[:].rearrange("p (h d) -> p h d", h=4)
            P.tt(n3, k3, s5[:, 4:8].unsqueeze(2).to_broadcast([128, 4, 64]), ALU.mult)
            P.tt(n3, n3, gsm[:, G_KB:G_KB + 64].unsqueeze(1).to_broadcast([128, 4, 64]), ALU.mult)
            P.dma(outs["k"], knf[0:nrows, :], eng='pool')
            P.dma(outs["v"], tm[0:nrows, T_VB:T_VB + 256], eng='pool')
            P.copy(nrm[:, 256:512], knf[:], eng='pool')
            q3 = tm[:, T_QB:T_QB + 256].rearrange("p (h d) -> p h d", h=4)
            yield
            m3 = sqB[:, 0:256].rearrange("p (h d) -> p h d", h=4)
            P.tt(m3, q3, s5[:, 0:4].unsqueeze(2).to_broadcast([128, 4, 64]), ALU.mult)
            P.tt(nrm[:, 0:256].rearrange("p (h d) -> p h d", h=4), m3,
                 gsm[:, G_QB:G_QB + 64].unsqueeze(1).to_broadcast([128, 4, 64]), ALU.mult)
            q3 = tm[:, T_QM:T_QM + 256].rearrange("p (h d) -> p h d", h=4)
            m3 = sqB[:, 256:512].rearrange("p (h d) -> p h d", h=4)
            P.tt(m3, q3, s5[:, 8:12].unsqueeze(2).to_broadcast([128, 4, 64]), ALU.mult)
            P.tt(nrm[:, 512:768].rearrange("p (h d) -> p h d", h=4), m3,
                 gsm[:, G_QM:G_QM + 64].unsqueeze(1).to_broadcast([128, 4, 64]), ALU.mult)
            P.ts(kxn[:], tm[:, T_KI:T_KI + 32], s5[:, 12:13], ALU.mult)
            P.tt(kxn[:], kxn[:], gsm[:, G_KI:G_KI + 32], ALU.mult)
            P.dma(outs["ki"], kxn[0:nrows, :], eng='pool')
            P.copy(kxb[:], kxn[:], eng='pool')
            kslot = outs["kslot"]
            kc = slice(kslot * 128, (kslot + 1) * 128)
            yield
            transpose_bf(QbT[:], [nrm[:, 0:128], nrm[:, 128:256]], pb[0])
            transpose_bf(KbT[:, :, kc], [nrm[:, 256:384], nrm[:, 384:512]], pb[1])
            yield
            transpose_bf(QmT[:], [nrm[:, 512:640], nrm[:, 640:768]], pb[6])
            P.matmul(pb[7][0:32, 0:128], kxb[:], idb[:])
            P.copy(kiT[:, kc], pb[7][0:32, 0:128])
            yield
            va = Vaug[:, kslot, :].rearrange("p (h e) -> p h e", e=65)
            P.memset(va[:, :, 64:65], 1.0)
            P.copy(va[:, :, 0:64], tm[:, T_VB:T_VB + 256].rearrange("p (h d) -> p h d", h=4), eng='pool')
            P.ts(wabs[:], tm[:, T_WI:T_WI + 8], IDX_SCALE, ALU.mult)
            for h in range(8):
                P.ts(wdiag[:, h, :], id32[:], wabs[:, h:h + 1], ALU.mult, eng=('dve' if h % 2 == 0 else 'pool'))
            yield

        def conv_out(nrows, dst):
            for c in range(3):
                pc = pb[c % 2]
                for kt in range(8):
                    P.matmul(pc[:], hT[:, kt, :], W[:, kt, c * 512:(c + 1) * 512], start=(kt == 0), stop=(kt == 7))
                P.copy(cv[:, c * 512:(c + 1) * 512], pc[:], eng=('act' if c % 2 == 0 else 'dve'))
            P.dma(dst, cv[nrows - 3:nrows, :], eng='pool')

        def index_scores(sc, col0, ktiles, kbase, alt=False):
            for g0 in range(0, ktiles, 4):
                nk = min(4, ktiles - g0) * 128
                kcs = slice((kbase + g0) * 128, (kbase + g0) * 128 + nk)
                dst = sc[:, col0 + g0 * 128:col0 + g0 * 128 + nk]

                def logits(h):
                    P.matmul(pb[h % 2][:, 0:nk], qiT[:, h, :], kiT[:, kcs])

                def relu_sum(h):
                    r_ = rl[h % 2][:].bitcast(BF16)
                    if alt and h % 2 == 1:
                        P.ts(r_[:, 0:nk], pb[h % 2][:, 0:nk], 0.0, ALU.max)
                    else:
                        P.act(r_[:, 0:nk], pb[h % 2][:, 0:nk], AF.Relu)
                    return r_
                logits(0)
                for h in range(8):
                    r_ = relu_sum(h)
                    if h + 1 < 8:
                        logits(h + 1)
                    P.matmul(pb[6][:, 0:nk], wdiag[:, h, :], r_[:, 0:nk], start=(h == 0), stop=(h == 7))
                    if h % 2 == 1:
                        yield
                P.copy(dst, pb[6][:, 0:nk], eng='act')
                yield

        def bisect(sc, ncols, lo_cols):
            thr, w0, t_, cnt, hh = bis[:, 0:1], bis[:, 1:2], bis[:, 2:3], bis[:, 3:4], bis[:, 4:5]
            P.reduce(w0, sc[:, 0:ncols], ALU.max)
            P.reduce(thr, sc[:, 0:lo_cols], ALU.min)
            P.ts(thr, thr, -1.0, ALU.add)
            P.tt(w0, w0, thr, ALU.subtract)
            P.ts(wtab[:], p2[:], w0, ALU.mult)
            P.tt(t_, thr, wtab[:, 0:1], ALU.add)
            for k in range(NBIS):
                jk = jk8
                for ci, c0 in enumerate(range(0, ncols, 2048)):
                    wd = min(2048, ncols - c0)
                    P.ts(jk[:, 0:wd], sc[:, c0:c0 + wd], t_, ALU.is_gt, (None if ci == 0 else cnt), ALU.add,
                         accum_out=cnt)
                P.ts(hh, cnt, 255.5, ALU.is_ge, 0.5, ALU.subtract)
                P.stt(t_, hh, wtab[:, k:k + 1], t_, ALU.mult, ALU.add)
                yield
            P.tt(thr, t_, wtab[:, NBIS:NBIS + 1], ALU.subtract)

        def attend(qT, keys, obank_, first, last, mask_cols=None, sc=None):
            n = len(keys)

            def scores_t(i):
                kT, va = keys[i]
                bank = pb[i % 2]
                nm = None
                if mask_cols is not None:
                    nm = nmk[i % 2]
                    P.ts(nm[:], sc[:, mask_cols[i]:mask_cols[i] + 128], bis[:, 0:1], ALU.is_le)
                for h in range(4):
                    pr = slice((h % 2) * 64, (h % 2) * 64 + 64)
                    oc = bank[:, h * 128:(h + 1) * 128]
                    P.matmul(oc, kT[pr, h // 2, :], qT[pr, h // 2, :], start=True, stop=False)
                    P.matmul(oc, (nm[:] if nm is not None else zerob[:]), negI[:], start=False, stop=True)

            scores_t(0)
            for i in range(n):
                kT, va = keys[i]
                pt = PT[i % 2]
                P.act(pt[:], pb[i % 2][:].rearrange("p (a b) -> p a b", a=4), AF.Exp, scale=0.125)
                if i + 1 < n:
                    scores_t(i + 1)
                for h in range(4):
                    P.matmul(obank_[:, h * 65:(h + 1) * 65], pt[:, h, :], va[:, h, :],
                             start=(first and i == 0 and h == 0), stop=(last and i == n - 1 and h == 3))
                yield

        def finish_heads(obank_, zcols, dst_cols):
            P.copy(sqB[:, 0:260], obank_[:, 0:260], eng='act')
            o3 = sqB[:, 0:260].rearrange("p (h e) -> p h e", e=65)
            P.copy(rcp[:, 0:4], o3[:, :, 64])
            P.recip(rcp[:, 0:4], rcp[:, 0:4])
            P.tt(o3[:, :, 0:64], o3[:, :, 0:64], rcp[:, 0:4].unsqueeze(2).to_broadcast([128, 4, 64]), ALU.mult)
            P.tt(mixBC[:, dst_cols:dst_cols + 256].rearrange("p (h d) -> p h d", h=4), o3[:, :, 0:64],
                 zs[:, zcols:zcols + 256].rearrange("p (h d) -> p h d", h=4), ALU.mult)
            yield

        def mix_bc_T(bank):
            transpose_bf(mixT[:, 4:8, :], [mixBC[:, j * 128:(j + 1) * 128] for j in range(4)], bank)

        def out_proj(nrows, y_dst, xt):
            for c in range(2):
                bank = pb[c]
                for kt in range(8):
                    P.matmul(bank[:], mixT[:, kt, :], Wo[:, kt, c * 512:(c + 1) * 512], start=(kt == 0), stop=(kt == 7))
                P.tt(yo[:, c * 512:(c + 1) * 512], bank[:], xt[:, c * 512:(c + 1) * 512], ALU.add)
            P.dma(y_dst, yo[0:nrows, :], eng='pool')

        def dbg(ap2d, ncols, stage=None):
            if not kstop:
                return
            i = dbg_n[0]
            dbg_n[0] += 1
            if stage is not None:
                P.copy(stage[:, 0:ncols], ap2d)
                P.dma(dbg_outs[i][:, 0:ncols], stage[:, 0:ncols], eng='pool')
            else:
                P.dma(dbg_outs[i][:, 0:ncols], ap2d, eng='pool')

        def drain(g):
            for _ in g:
                pass

        def count_steps(mk):
            P.dry = True
            n = 0
            for _ in mk():
                n += 1
            P.dry = False
            return n + 1

        def interleave(mka, mkb):
            na, nb = count_steps(mka), count_steps(mkb)
            ga, gb = mka(), mkb()
            ia = ib = 0
            da = db = False
            while not (da and db):
                pick_a = (not da) and (db or ia * nb <= ib * na)
                if pick_a:
                    try:
                        next(ga)
                        ia += 1
                    except StopIteration:
                        da = True
                else:
                    try:
                        next(gb)
                        ib += 1
                    except StopIteration:
                        db = True

        def prompt_chainB(tt, outs):
            yield from chainB_heads(128, outs)
            nk = tt + 1
            yield from index_scores(scores, 0, nk, 0)
            dcol = tt * 128
            P.stt(scores[:, dcol:dcol + 128], scores[:, dcol:dcol + 128], 1.0, admP[:], ALU.mult, ALU.mult)
            P.tt(scores[:, dcol:dcol + 128], scores[:, dcol:dcol + 128], nadmP[:], ALU.add)
            if tt >= 2:
                yield from bisect(scores, nk * 128, (nk - 1) * 128)
            else:
                P.memset(bis[:, 0:1], -1.0e29)
            keys = [(KbT[:, :, j * 128:(j + 1) * 128], Vaug[:, j, :].rearrange("p (h e) -> p h e", e=65)) for j in range(nk)]
            yield from attend(QbT, keys, pb[6], True, True, mask_cols=[j * 128 for j in range(nk)], sc=scores)
            yield from finish_heads(pb[6], 0, 0)
            mkeys = [(MkT[:, :, j * 128:(j + 1) * 128], MvA[:, j, :].rearrange("p (h e) -> p h e", e=65)) for j in range(2)]
            yield from attend(QmT, mkeys, pb[7], True, True)
            yield from finish_heads(pb[7], 256, 256)
            mix_bc_T(pb[0])
            yield

        try:
            for mt in range(2):
                mem_tile(mt)
            ck(2)
            P.memset(raw[:, :, 0:3], 0.0)
            P.memset(S[:], 0.0)
            P.memset(Sb[:], 0.0)
            for tt in range(NT):
                r = slice(tt * 128, (tt + 1) * 128)
                outs = {"k": p_k[r, :], "v": p_v[r, :], "ki": p_ki[r, :], "kslot": tt}
                xt = xtb[tt % 2]
                layer_common(x_p[r, :], 128, xt, preloaded=(tt > 0 and not kstop))
                tail = None
                if tt + 1 < NT and not kstop:
                    r2 = slice((tt + 1) * 128, (tt + 2) * 128)
                    tail = (lambda r2=r2, tt=tt: load_h(x_p[r2, :], 128, xtb[(tt + 1) % 2], (pb[2], pb[3])))
                if kstop:
                    drain(chainA(False))
                    drain(prompt_chainB(tt, outs))
                else:
                    interleave(lambda tail=tail: chainA(False, tail), lambda tt=tt, outs=outs: prompt_chainB(tt, outs))
                if tt == NT - 1:
                    conv_out(128, p_conv)
                out_proj(128, y_p[r, :], xt)
                ck(25)
            P.dma(p_ssm.rearrange("h k v -> k h v"), S[:], eng='pool')
            ck(30)

            P.dma(S[:], st_ssm.rearrange("h k v -> k h v"))
            P.copy(Sb[:], S[:])
            P.dma(stc, st_conv)
            for c in range(12):
                P.matmul(pb[2][:, c * 3:(c + 1) * 3], stc[0:3, c * 128:(c + 1) * 128], id32[0:3, 0:3])
            P.copy(raw[:, :, 0:3], pb[2][:, 0:36].rearrange("p (c j) -> p c j", j=3))
            for mt in range(2):
                r = slice(mt * 128, (mt + 1) * 128)
                KV = kvc[mt]
                P.dma(KV[:, 0:256], c_mk[r, :])
                P.dma(KV[:, 256:512], c_mv[r, :])
                P.copy(kvb[:], KV[:, 0:256])
                transpose_bf(MkT[:, :, r], [kvb[:, 0:128], kvb[:, 128:256]], pb[3])
                P.copy(MvA[:, mt, :].rearrange("p (h e) -> p h e", e=65)[:, :, 0:64],
                       KV[:, 256:512].rearrange("p (h d) -> p h d", h=4), eng='pool')
            outs = {"k": s_k, "v": s_v, "ki": s_ki, "kslot": 0}
            layer_common(x_s, 16, xtb[0])
            drain(chainA(True))
            drain(chainB_heads(16, outs))
            conv_out(16, s_conv)
            P.dma(s_ssm.rearrange("h k v -> k h v"), S[:], eng='pool')
            ck(31)
            Wf = W[:].rearrange("p a b -> p (a b)").bitcast(F32)
            Wb = W[:].rearrange("p a b -> p (a b)")
            kiS = Wf[:, 4224:5248].rearrange("p (t c) -> p t c", c=32)
            KS = Wf[:, 5248:9344].rearrange("p (t c) -> p t c", c=256)
            VS = Wf[:, 9344:13440].rearrange("p (t c) -> p t c", c=256)
            Kb16 = Wb[:, 26880:30976].rearrange("p (t c) -> p t c", c=256)
            kib = Wb[:, 18688:19712].rearrange("p (t c) -> p t c", c=32)
            def dma_tiles(dst, src_rows, ntile, step):
                v = src_rows.rearrange("(t p) c -> p t c", p=128)
                for t0 in range(0, ntile, step):
                    P.dma(dst[:, t0:t0 + step, :], v[:, t0:t0 + step, :])
            dma_tiles(kiS, c_ki, 32, 8)
            dma_tiles(KS, c_k[0:2048, :], 16, 8)
            drain(index_scores(scoresS, PAST, 1, 0, alt=True))
            P.tt(scoresS[:, PAST:PAST + 128], scoresS[:, PAST:PAST + 128], nadmS[:], ALU.add)
            P.copy(kib, kiS)
            for g_ in range(2):
                for q4 in range(4):
                    bank = pb[2 + q4]
                    for j in range(4):
                        t_ = g_ * 16 + q4 * 4 + j
                        P.matmul(bank[0:32, j * 128:(j + 1) * 128], kib[:, t_, :], idb[:])
                    P.copy(kiT[:, q4 * 512:(q4 + 1) * 512], bank[0:32, :], eng='act')
                drain(index_scores(scoresS, g_ * 2048, 16, 0, alt=True))
            dma_tiles(VS, c_v[0:2048, :], 16, 8)
            drain(bisect(scoresS, PAST + 128, PAST))
            ck(32)
            keys = [(KbT[:, :, 0:128], Vaug[:, 0, :].rearrange("p (h e) -> p h e", e=65))]
            drain(attend(QbT, keys, pb[6], True, False, mask_cols=[PAST], sc=scoresS))
            for g_ in range(2):
                if g_ == 1:
                    dma_tiles(KS, c_k[2048:4096, :], 16, 8)
                    dma_tiles(VS, c_v[2048:4096, :], 16, 8)
                P.copy(Kb16[:, 0:6, :], KS[:, 0:6, :])
                P.copy(Kb16[:, 6:11, :], KS[:, 6:11, :], eng='act')
                P.copy(Kb16[:, 11:16, :], KS[:, 11:16, :], eng='pool')
                for q2 in range(8):
                    bank = pb[2 + q2 % 4]
                    for pr_ in range(2):
                        for j in range(2):
                            t_ = q2 * 2 + j
                            P.matmul(bank[:, (pr_ * 2 + j) * 128:(pr_ * 2 + j + 1) * 128],
                                     Kb16[:, t_, pr_ * 128:(pr_ + 1) * 128], idb[:])
                    P.copy(KbT[:, :, q2 * 256:(q2 + 1) * 256], bank[:].rearrange("p (a b) -> p a b", a=2), eng='act')
                v4 = Vaug[:, 0:16, :].rearrange("p t (h e) -> p t h e", e=65)
                for t4 in range(4):
                    P.copy(v4[:, t4 * 4:(t4 + 1) * 4, :, 0:64].rearrange("p t h d -> p (t h) d"),
                           VS[:, t4 * 4:(t4 + 1) * 4, :].rearrange("p t (h d) -> p (t h) d", d=64),
                           eng=('pool' if t4 % 2 == 0 else 'dve'))
                keys = [(KbT[:, :, j * 128:(j + 1) * 128], Vaug[:, j, :].rearrange("p (h e) -> p h e", e=65)) for j in range(16)]
                drain(attend(QbT, keys, pb[6], False, g_ == 1, mask_cols=[g_ * 2048 + j * 128 for j in range(16)], sc=scoresS))
            drain(finish_heads(pb[6], 0, 0))
            mkeys = [(MkT[:, :, j * 128:(j + 1) * 128], MvA[:, j, :].rearrange("p (h e) -> p h e", e=65)) for j in range(2)]
            drain(attend(QmT, mkeys, pb[7], True, True))
            drain(finish_heads(pb[7], 256, 256))
            mix_bc_T(pb[0])
            out_proj(16, y_s, xtb[0])
        except _Stop:
            pass
        P.emit(final_wait_engine='pool')
    return nc


_CACHE = {}


def _make_in_maps(inp):
    g_all = np.ascontiguousarray(np.concatenate([inp['g_in'][0].reshape(8, 128), inp['g_mem'][0].reshape(8, 128)], 0))
    gsm = np.ascontiguousarray(np.concatenate([
        inp['g_k_B'][0], inp['g_kidx_B'][0], inp['g_k_M'][0], inp['g_q_B'][0], inp['g_q_M'][0],
        inp['dt_bias_A'][0], inp['a_log_A'][0]]).reshape(1, GSM).astype(np.float32))
    maps = []
    for b in range(8):
        maps.append({
            "x_p": np.ascontiguousarray(inp['x_prompt'][b]),
            "x_s": np.ascontiguousarray(inp['x_sample'][b]),
            "mem": np.ascontiguousarray(inp['mem_prompt'][b]),
            "w_in": np.ascontiguousarray(inp['w_in'][0]),
            "w_mem": np.ascontiguousarray(inp['w_mem_kv'][0]),
            "w_out": np.ascontiguousarray(inp['w_out'][0]),
            "g_all": g_all,
            "gsm": gsm,
            "g_o": np.ascontiguousarray(inp['g_o_A'][0].reshape(1, 128)),
            "conv_w": np.ascontiguousarray(inp['conv_w_A'][0]),
            "st_conv": np.ascontiguousarray(inp['state_conv_A'][0, b]),
            "st_ssm": np.ascontiguousarray(inp['state_ssm_A'][0, b]),
            "c_k": np.ascontiguousarray(inp['cache_k_B'][0, b].reshape(PAST, 256)),
            "c_v": np.ascontiguousarray(inp['cache_v_B'][0, b].reshape(PAST, 256)),
            "c_ki": np.ascontiguousarray(inp['cache_kidx_B'][0, b]),
            "c_mk": np.ascontiguousarray(inp['cache_mem_k'][0, b].reshape(256, 256)),
            "c_mv": np.ascontiguousarray(inp['cache_mem_v'][0, b].reshape(256, 256)),
        })
    return maps


def _assemble(res):
    def stack(name, shape):
        return np.stack([np.asarray(r[name], dtype=np.float32).reshape(shape) for r in res], 0)
    outs = (
        stack("y_p", (SEQ, D)), stack("y_s", (16, D)),
        stack("p_conv", (3, 1536))[None],
        stack("p_ssm", (4, 128, 128))[None],
        stack("p_k", (SEQ, 4, 64))[None],
        stack("p_v", (SEQ, 4, 64))[None],
        stack("p_ki", (SEQ, 32))[None],
        stack("p_mk", (256, 4, 64))[None],
        stack("p_mv", (256, 4, 64))[None],
        stack("s_conv", (3, 1536))[None],
        stack("s_ssm", (4, 128, 128))[None],
        stack("s_k", (16, 4, 64))[None],
        stack("s_v", (16, 4, 64))[None],
        stack("s_ki", (16, 32))[None],
    )
    return outs


def kernel(**inputs):
    inp = {k: np.asarray(v) for k, v in inputs.items()}
    nc = build_program()
    in_maps = _make_in_maps(inp)
    res = run_bass_kernel_spmd(nc, in_maps, core_ids=list(range(8)))
    return _assemble(res.results)
```

```python
import numpy as np
import concourse.bass as bass
import concourse.mybir as mybir
from concourse.bass_utils import run_bass_kernel_spmd

F32 = mybir.dt.float32
BF16 = mybir.dt.bfloat16
ALU = mybir.AluOpType
AF = mybir.ActivationFunctionType
AX = mybir.AxisListType


def _region(ap):
    t = ap.tensor
    name = t.name
    esz = mybir.dt.size(ap.dtype)
    pat = tuple((st_ * esz, c_) for st_, c_ in ap.ap)
    off = ap.offset * esz
    space = str(ap.space)
    if 'DRAM' in space.upper() or 'Dram' in space or 'dram' in space:
        lo = off
        hi = off + sum((c - 1) * abs(s) for s, c in pat) + esz
        return (name, 0, 1, lo, hi)
    ps, pc = pat[0]
    if ps == 0:
        ps = 1 << 30
    p0 = off // ps
    fo = off % ps
    hi = fo + sum((c - 1) * abs(s) for s, c in pat[1:]) + esz
    if 'PSUM' in space.upper():
        return (name, (p0 // 32) * 32, ((p0 + pc + 31) // 32) * 32, 0, 1 << 30)
    return (name, p0, p0 + pc, fo, hi)


def _overlap(a, b):
    return a[1] < b[2] and b[1] < a[2] and a[3] < b[4] and b[3] < a[4]


def _contains(a, b):
    return a[1] <= b[1] and a[2] >= b[2] and a[3] <= b[3] and a[4] >= b[4]


class Op:
    __slots__ = ('eng', 'fn', 'reads', 'writes', 'dma', 'deps', 'signals', 'sem', 'val', 'idx', 'pe_acc')

    def __init__(self, eng, fn, reads, writes, dma=False):
        self.eng = eng
        self.fn = fn
        self.reads = reads
        self.writes = writes
        self.dma = dma
        self.deps = set()
        self.signals = False
        self.sem = None
        self.val = 0


class Prog:
    ENGS = ('pe', 'act', 'dve', 'pool', 'sp')

    def __init__(self, nc, n_dma_sems=12, same_engine_sync=True):
        self.nc = nc
        self.ops = []
        self.n_dma_sems = n_dma_sems
        self.same_engine_sync = same_engine_sync
        self.alias = {}

    def set_alias(self, name, group):
        self.alias[name] = group

    def add(self, eng, fn, reads, writes, dma=False):
        if getattr(self, 'dry', False):
            return None
        rr = [_region(a) for a in reads if a is not None and not isinstance(a, (int, float))]
        ww = [_region(a) for a in writes if a is not None]
        op = Op(eng, fn, rr, ww, dma)
        op.idx = len(self.ops)
        self.ops.append(op)
        return op

    def resolve(self):
        writers = {}
        readers = {}
        for op in self.ops:
            for r in op.reads:
                key = self.alias.get(r[0], r[0])
                full = key != r[0]
                for (wr, wi) in writers.get(key, ()):
                    if full or _overlap(r, wr):
                        op.deps.add(wi)
            for w in op.writes:
                key = self.alias.get(w[0], w[0])
                full = key != w[0]
                for (wr, wi) in writers.get(key, ()):
                    if full or _overlap(w, wr):
                        op.deps.add(wi)
                for (rr, ri) in readers.get(key, ()):
                    if full or _overlap(w, rr):
                        op.deps.add(ri)
            for w in op.writes:
                key = self.alias.get(w[0], w[0])
                full = key != w[0]
                if full:
                    writers[key] = [(w, op.idx)]
                    readers[key] = []
                else:
                    writers[key] = [(wr, wi) for (wr, wi) in writers.get(key, ()) if not _contains(w, wr)] + [(w, op.idx)]
                    readers[key] = [(rr, ri) for (rr, ri) in readers.get(key, ()) if not _contains(w, rr)]
            for r in op.reads:
                key = self.alias.get(r[0], r[0])
                lst = readers.setdefault(key, [])
                if not op.dma:
                    lst[:] = [(rr, ri) for (rr, ri) in lst
                              if not (rr == r and self.ops[ri].eng == op.eng and not self.ops[ri].dma)]
                lst.append((r, op.idx))
            op.deps.discard(op.idx)
        for op in self.ops:
            keep = set()
            for d in op.deps:
                o2 = self.ops[d]
                if not o2.dma and not op.dma and o2.eng == op.eng:
                    if op.eng == 'pe':
                        continue
                    if not self.same_engine_sync:
                        continue
                keep.add(d)
            op.deps = keep
            for d in keep:
                self.ops[d].signals = True
        for op in self.ops:
            if op.dma:
                op.signals = True

    def emit(self, final_wait_engine='sp'):
        nc = self.nc
        self.resolve()
        import contextlib
        with contextlib.ExitStack() as st:
            esem = {e: st.enter_context(nc.semaphore('sem_' + e)) for e in ('pe', 'act', 'dve', 'pool')}
            dsem = [st.enter_context(nc.semaphore('dsem%d' % i)) for i in range(self.n_dma_sems)]
            ecount = {e: 0 for e in esem}
            dcount = [0] * self.n_dma_sems
            half = self.n_dma_sems // 2
            kq = {'sp': 0, 'pool': 0, 'act': 0}
            for op in self.ops:
                if not op.signals:
                    continue
                if op.dma:
                    if op.eng == 'pool':
                        i = half + kq['pool'] % (self.n_dma_sems - half)
                        kq['pool'] += 1
                    else:
                        i = kq['sp'] % half
                        kq['sp'] += 1
                    dcount[i] += 16
                    op.sem = ('d', i)
                    op.val = dcount[i]
                else:
                    ecount[op.eng] += 1
                    op.sem = ('e', op.eng)
                    op.val = ecount[op.eng]
            final = {}
            for op in self.ops:
                if op.dma:
                    final[op.sem] = max(final.get(op.sem, 0), op.val)

            def semh(s):
                return esem[s[1]] if s[0] == 'e' else dsem[s[1]]

            per_eng = {e: [o for o in self.ops if o.eng == e] for e in self.ENGS}
            block = st.enter_context(nc.Block())

            def run(engname, eng):
                known = {}
                for op in per_eng[engname]:
                    need = {}
                    for d in op.deps:
                        o2 = self.ops[d]
                        need[o2.sem] = max(need.get(o2.sem, 0), o2.val)
                    if op.dma and op.val > 16:
                        need[op.sem] = max(need.get(op.sem, 0), op.val - 16)
                    for s, v in need.items():
                        if known.get(s, 0) >= v:
                            continue
                        eng.wait_ge(semh(s), v)
                        known[s] = v
                    ins = op.fn(eng)
                    if op.signals:
                        ins.then_inc(semh(op.sem), 16 if op.dma else 1)
                if engname == final_wait_engine:
                    for s, v in final.items():
                        if known.get(s, 0) < v:
                            eng.wait_ge(semh(s), v)

            @block.tensor
            def _(e):
                run('pe', e)

            @block.scalar
            def _(e):
                run('act', e)

            @block.vector
            def _(e):
                run('dve', e)

            @block.gpsimd
            def _(e):
                run('pool', e)

            @block.sync
            def _(e):
                run('sp', e)

    def dma(self, out, in_, eng='sp'):
        return self.add(eng, lambda e: e.dma_start(out=out, in_=in_), [in_], [out], dma=True)

    def matmul(self, out, lhsT, rhs, start=True, stop=True):
        rd = [lhsT, rhs] + ([] if start else [out])
        return self.add('pe', lambda e: e.matmul(out, lhsT, rhs, start=start, stop=stop), rd, [out])

    def transpose(self, out, in_, ident):
        return self.add('pe', lambda e: e.transpose(out, in_, ident), [in_, ident], [out])

    def act(self, out, in_, func, bias=None, scale=None, accum_out=None, eng='act'):
        kw = {}
        rd = [in_]
        if bias is not None:
            kw['bias'] = bias
            rd.append(bias)
        if scale is not None:
            kw['scale'] = scale
            rd.append(scale)
        wr = [out]
        if accum_out is not None:
            kw['accum_out'] = accum_out
            wr.append(accum_out)
        return self.add('act', lambda e: e.activation(out, in_, func, **kw), rd, wr)

    def tt(self, out, in0, in1, op, eng='dve'):
        return self.add(eng, lambda e: e.tensor_tensor(out, in0, in1, op), [in0, in1], [out])

    def ts(self, out, in0, s1, op0, s2=None, op1=None, accum_out=None, eng='dve'):
        rd = [in0, s1, s2]
        wr = [out, accum_out]

        def fn(e):
            kw = {}
            if accum_out is not None:
                kw['accum_out'] = accum_out
            if op1 is not None:
                return e.tensor_scalar(out, in0, s1, s2, op0, op1, **kw)
            return e.tensor_scalar(out, in0, s1, None, op0, **kw)
        return self.add(eng, fn, rd, wr)

    def stt(self, out, in0, scalar, in1, op0, op1, eng='dve'):
        return self.add(eng, lambda e: e.scalar_tensor_tensor(out, in0, scalar, in1, op0, op1),
                        [in0, scalar, in1], [out])

    def copy(self, out, in_, eng='dve'):
        if eng == 'act':
            return self.add('act', lambda e: e.activation(out, in_, AF.Copy), [in_], [out])
        return self.add(eng, lambda e: e.tensor_copy(out, in_), [in_], [out])

    def memset(self, ap, val, eng='dve'):
        return self.add(eng, lambda e: e.memset(ap, val), [], [ap])

    def reduce(self, out, in_, op, axis=AX.X, eng='dve'):
        return self.add(eng, lambda e: e.tensor_reduce(out, in_, axis, op), [in_], [out])

    def recip(self, out, in_):
        return self.add('dve', lambda e: e.reciprocal(out, in_), [in_], [out])


D = 1024
IN_W = 3888
SEQ = 2048
NT = SEQ // 128
PAST = 4096
EPS = 1e-6
C_ZA = 1536
C_TM = 2048
T_BA, T_AA, T_QB, T_KB, T_VB, T_ZB, T_KI, T_WI, T_QM, T_ZM = 0, 4, 8, 264, 520, 776, 1288, 1320, 1328, 1584
C_QI = 3080
TMW = IN_W - C_TM
IDX_SCALE = (8 ** -0.5) * (32 ** -0.5)
NBIS = 17
NEGBIG = -1.0e30
G_KB, G_KI, G_KM, G_QB, G_QM, G_DTB, G_ALOG = 0, 64, 96, 160, 224, 288, 292
GSM = 296


class _Stop(Exception):
    pass


def build_program(kstop=0):
    import contextlib

    def ck(n):
        if kstop == n:
            raise _Stop()
    nc = bass.Bass("TRN2", target_bir_lowering=False)

    def din(name, shape):
        return nc.dram_tensor(name, shape, F32, kind="ExternalInput").ap()

    def dout(name, shape):
        return nc.dram_tensor(name, shape, F32, kind="ExternalOutput").ap()

    x_p = din("x_p", [SEQ, D])
    x_s = din("x_s", [16, D])
    mem = din("mem", [256, D])
    w_in = din("w_in", [D, IN_W])
    w_mem = din("w_mem", [D, 512])
    w_out = din("w_out", [D, D])
    g_all = din("g_all", [16, 128])
    gsm_d = din("gsm", [1, GSM])
    g_o_d = din("g_o", [1, 128])
    convw_d = din("conv_w", [4, 1536])
    st_conv = din("st_conv", [3, 1536])
    st_ssm = din("st_ssm", [4, 128, 128])
    c_k = din("c_k", [PAST, 256])
    c_v = din("c_v", [PAST, 256])
    c_ki = din("c_ki", [PAST, 32])
    c_mk = din("c_mk", [256, 256])
    c_mv = din("c_mv", [256, 256])

    y_p = dout("y_p", [SEQ, D])
    y_s = dout("y_s", [16, D])
    p_conv = dout("p_conv", [3, 1536])
    p_ssm = dout("p_ssm", [4, 128, 128])
    p_k = dout("p_k", [SEQ, 256])
    p_v = dout("p_v", [SEQ, 256])
    p_ki = dout("p_ki", [SEQ, 32])
    p_mk = dout("p_mk", [256, 256])
    p_mv = dout("p_mv", [256, 256])
    s_conv = dout("s_conv", [3, 1536])
    s_ssm = dout("s_ssm", [4, 128, 128])
    s_k = dout("s_k", [16, 256])
    s_v = dout("s_v", [16, 256])
    s_ki = dout("s_ki", [16, 32])

    dbg_outs = [dout("dbg%d" % i, [128, 1024]) for i in range(6)] if kstop else []
    dbg_n = [0]

    with contextlib.ExitStack() as st:
        def sb(name, shape, dt=F32):
            return st.enter_context(nc.sbuf_tensor(name, shape, dt))

        def ps(name, shape, dt=F32):
            return st.enter_context(nc.psum_tensor(name, shape, dt))

        P = Prog(nc, n_dma_sems=16)

        W = sb("W", [128, 8, IN_W], BF16)
        Wo = sb("Wo", [128, 8, D], BF16)
        KbT = sb("KbT", [128, 2, 2048], BF16)
        Wm = KbT[:].rearrange("p a (b c) -> p (a b) c", c=512)
        Vaug = sb("Vaug", [128, 16, 260], BF16)
        kiT = sb("kiT", [32, 2048], BF16)
        scores = sb("scores", [128, 2048])
        g16 = scores[0:16, 1536:1664]
        go1 = scores[0:1, 1664:1792]
        cw4 = scores[0:4, 0:1536]
        stc = scores[0:3, 0:1536]
        scoresS = W[:].rearrange("p a b -> p (a b)").bitcast(F32)[:, 0:4224]
        id32 = sb("id32", [128, 128])
        idb = sb("idb", [128, 128], BF16)
        ones32 = sb("ones32", [128, 128])
        onesb = sb("onesb", [128, 128], BF16)
        negI = sb("negI", [128, 128], BF16)
        zerob = sb("zerob", [128, 128], BF16)
        blkP = sb("blkP", [128, 128])
        blkS = sb("blkS", [128, 128])
        triP = sb("triP", [128, 128])
        triS = sb("triS", [128, 128])
        lsP = sb("lsP", [128, 128])
        lsS = sb("lsS", [128, 128])
        admP = sb("admP", [128, 128])
        nadmP = sb("nadmP", [128, 128])
        nadmS = sb("nadmS", [128, 128])
        gcol = sb("gcol", [128, 16])
        gsm = sb("gsm_t", [128, GSM])
        negA = sb("negA", [128, 4])
        gocol = sb("gocol", [128, 1])
        wc = sb("wc", [128, 12, 4])
        xtb = [sb("xt%d" % i, [128, D]) for i in range(2)]
        jk8 = sb("jk8", [128, 2048], mybir.dt.uint8)
        xs = sb("xs", [128, D], BF16)
        hT = sb("hT", [128, 8, 128], BF16)
        rx = sb("rx", [128, 1])
        tm = sb("tm", [128, TMW])
        raw = sb("raw", [128, 12, 131])
        acc = sb("acc", [128, 12, 128])
        vsb = sb("vsb", [128, 4, 128], BF16)
        zas = sb("zas", [128, 4, 128], BF16)
        zs = sb("zs", [128, 512], BF16)
        sqb = sb("sqb", [128, 8, 128], BF16)
        rbc = sb("rbc", [128, 8, 128])
        qn = sb("qn", [128, 4, 128], BF16)
        knT = sb("knT", [128, 4, 128], BF16)
        qiT = sb("qiT", [32, 8, 128], BF16)
        gt = sb("gt", [128, 28])
        egr = sb("egr", [128, 4, 128])
        DT = sb("DT", [128, 4, 128])
        Xm = sb("Xm", [128, 4, 128])
        Ym = sb("Ym", [128, 4, 128])
        Dg = Xm
        Dst = Ym
        Pm = sb("Pm", [128, 4, 128])
        Tt = sb("Tt", [128, 4, 128], BF16)
        bv = sb("bv", [128, 4, 128], BF16)
        kbg = sb("kbg", [128, 4, 128], BF16)
        kd = sb("kd", [128, 4, 128], BF16)
        nwk = sb("nwk", [128, 4, 128], BF16)
        qd = sb("qd", [128, 4, 128], BF16)
        qkT = sb("qkT", [128, 4, 128], BF16)
        usb = sb("usb", [128, 4, 128], BF16)
        S = sb("S", [128, 4, 128])
        Sb = sb("Sb", [128, 4, 128], BF16)
        mixT = sb("mixT", [128, 8, 128], BF16)
        mixBC = sb("mixBC", [128, 512], BF16)
        s5 = sb("s5", [128, 16])
        rbcf = rbc[:].rearrange("p a b -> p (a b)")
        sq = rbcf
        accf = acc[:].rearrange("p a b -> p (a b)")
        sqB = sb("sqB", [128, 512])
        nrm = sb("nrm", [128, 768], BF16)
        knf = sb("knf", [128, 256])
        kxn = sb("kxn", [128, 32])
        kxb = sb("kxb", [128, 32], BF16)
        QbT = sb("QbT", [128, 2, 128], BF16)
        QmT = sb("QmT", [128, 2, 128], BF16)
        MkT = sb("MkT", [128, 2, 256], BF16)
        MvA = sb("MvA", [128, 2, 260], BF16)
        wdiag = sb("wdiag", [128, 8, 128], BF16)
        wabs = sb("wabs", [128, 8])
        wsgn = sb("wsgn", [128, 8])
        rl = [sb("rl%d" % i, [128, 512]) for i in range(2)]
        kvc = rl
        up32 = rl[0][:, 0:128]
        lo32 = rl[0][:, 128:256]
        PT = [sb("PT%d" % i, [128, 4, 128], BF16) for i in range(2)]
        nmk = [sb("nmk%d" % i, [128, 128], BF16) for i in range(2)]
        bis = sb("bis", [128, 8])
        wtab = sb("wtab", [128, NBIS + 1])
        p2 = sb("p2", [128, NBIS + 1])
        rcp = sb("rcp", [128, 8])
        ob = rbcf[:, 768:1024]
        cv = accf
        kvb = sb("kvb", [128, 256], BF16)
        kic = sb("kic", [128, 32])
        yo = accf[:, 0:D]

        pb = [ps("pb%d" % i, [128, 512]) for i in range(8)]


        def aff(out, cmp, mult, pat):
            P.add('pool', lambda e: e.affine_select(out, out, [[pat, 128]], cmp, 0.0, base=0,
                                                    channel_multiplier=mult), [out], [out])
        P.memset(id32[:], 0.0)
        P.add('pool', lambda e: e.affine_select(id32[:], id32[:], [[-1, 128]], ALU.not_equal, 1.0,
                                                base=0, channel_multiplier=1), [id32[:]], [id32[:]])
        P.copy(idb[:], id32[:])
        P.ts(negI[:], id32[:], -30000.0, ALU.mult)
        P.memset(zerob[:], 0.0)
        for k_ in range(NBIS + 1):
            P.memset(p2[:, k_:k_ + 1], 2.0 ** -(k_ + 1), eng='pool')
        P.memset(ones32[:], 1.0)
        P.memset(onesb[:], 1.0)
        P.memset(up32, 1.0)
        aff(up32, ALU.is_ge, -1, 1)
        P.memset(lo32, 1.0)
        aff(lo32, ALU.is_gt, 1, -1)
        P.memset(blkP[:], 0.0)
        P.memset(blkP[0:64, 0:64], 1.0)
        P.memset(blkP[64:128, 64:128], 1.0)
        P.memset(blkS[:], 0.0)
        P.memset(blkS[0:16, 0:16], 1.0)
        P.tt(triP[:], up32, blkP[:], ALU.mult)
        P.tt(triS[:], up32, blkS[:], ALU.mult)
        P.tt(lsP[:], lo32, blkP[:], ALU.mult)
        P.tt(lsS[:], lo32, blkS[:], ALU.mult)
        P.memset(admP[:], 1.0)
        P.memset(admP[0:64, 64:128], 0.0)
        P.ts(nadmP[:], admP[:], -1.0, ALU.add, 1.0e30, ALU.mult)
        P.memset(nadmS[:], NEGBIG)
        P.memset(nadmS[:, 0:16], 0.0)

        P.dma(g16, g_all)
        P.dma(gsm[:], gsm_d.partition_broadcast(128))
        P.dma(go1, g_o_d)
        P.dma(cw4, convw_d)
        P.matmul(pb[0][:, 0:16], g16, id32[0:16, 0:16])
        P.copy(gcol[:], pb[0][:, 0:16])
        P.matmul(pb[0][:, 16:17], go1, id32[0:1, 0:1])
        P.copy(gocol[:], pb[0][:, 16:17])
        for c in range(12):
            P.matmul(pb[1][:, c * 4:(c + 1) * 4], cw4[0:4, c * 128:(c + 1) * 128], id32[0:4, 0:4])
        P.copy(wc[:], pb[1][:, 0:48].rearrange("p (c j) -> p c j", j=4))
        P.act(negA[:], gsm[:, G_ALOG:G_ALOG + 4], AF.Exp)
        P.ts(negA[:], negA[:], -1.0, ALU.mult)

        def cast_w(dst, src, gc, ncols):
            a = ncols * 9 // 25
            b = ncols * 17 // 25
            if gc is None:
                P.copy(dst[:, 0:a], src[:, 0:a])
                P.copy(dst[:, a:b], src[:, a:b], eng='act')
                P.copy(dst[:, b:ncols], src[:, b:ncols], eng='pool')
            else:
                P.ts(dst[:, 0:a], src[:, 0:a], gc, ALU.mult)
                P.act(dst[:, a:b], src[:, a:b], AF.Copy, scale=gc)
                P.ts(dst[:, b:ncols], src[:, b:ncols], gc, ALU.mult, 1.0, ALU.mult, eng='pool')

        CH = IN_W // 3
        slots = [scores[:, 0:CH], accf[:, 0:CH]]
        k = 0
        for kt in range(8):
            for hf in range(3):
                s_ = slots[k % 2]
                k += 1
                P.dma(s_, w_in[kt * 128:(kt + 1) * 128, hf * CH:(hf + 1) * CH])
                cast_w(W[:, kt, hf * CH:(hf + 1) * CH], s_, gcol[:, kt:kt + 1], CH)
        for kt in range(8):
            s_ = slots[k % 2][:, 0:512]
            k += 1
            P.dma(s_, w_mem[kt * 128:(kt + 1) * 128, :])
            cast_w(Wm[:, kt, :], s_, gcol[:, 8 + kt:9 + kt], 512)
        for kt in range(8):
            s_ = slots[k % 2][:, 0:1024]
            k += 1
            P.dma(s_, w_out[kt * 128:(kt + 1) * 128, :])
            cast_w(Wo[:, kt, :], s_, None, 1024)

        def load_h(x_src, nrows, xt, banks):
            if nrows < 128:
                P.memset(xt[:], 0.0)
            P.dma(xt[0:nrows, :], x_src)
            P.add('dve', lambda e: e.scalar_tensor_tensor(yo, xt[:], 1.0, xt[:], ALU.mult, ALU.mult,
                                                          accum_out=rx[:]), [xt[:]], [yo, rx[:]])
            P.ts(rx[:], rx[:], 1.0 / D, ALU.mult, EPS, ALU.add)
            P.act(rx[:], rx[:], AF.Ln)
            P.act(rx[:], rx[:], AF.Exp, scale=-0.5)
            P.ts(xs[:], xt[:], rx[:], ALU.mult)
            for half in range(2):
                for j in range(4):
                    kt = half * 4 + j
                    P.matmul(banks[half][:, j * 128:(j + 1) * 128], xs[:, kt * 128:(kt + 1) * 128], idb[:])
            P.copy(hT[:, 0:4, :], banks[0][:].rearrange("p (a b) -> p a b", a=4), eng='act')
            P.copy(hT[:, 4:8, :], banks[1][:].rearrange("p (a b) -> p a b", a=4), eng='act')

        def rstd_inplace(ap, scale, eps=EPS, post=None):
            P.ts(ap, ap, scale, ALU.mult, eps, ALU.add)
            P.act(ap, ap, AF.Ln)
            if post is None:
                P.act(ap, ap, AF.Exp, scale=-0.5)
            else:
                P.act(ap, ap, AF.Exp, scale=-0.5, bias=post)

        def transpose_bf(dst, src_list, pbank):
            for j, s_ in enumerate(src_list):
                P.matmul(pbank[:, j * 128:(j + 1) * 128], s_, idb[:])
            n = len(src_list)
            P.copy(dst, pbank[:, 0:n * 128].rearrange("p (a b) -> p a b", a=n), eng='act')

        def mem_tile(mt):
            r = slice(mt * 128, (mt + 1) * 128)
            load_h(mem[r, :], 128, xtb[0], (pb[0], pb[1]))
            for kt in range(8):
                P.matmul(pb[2][:], hT[:, kt, :], Wm[:, kt, :], start=(kt == 0), stop=(kt == 7))
            KV = kvc[mt]
            P.copy(KV[:], pb[2][:], eng='act')
            P.tt(sq[:, 0:256], KV[:, 0:256], KV[:, 0:256], ALU.mult)
            P.reduce(s5[:, 0:4], sq[:, 0:256].rearrange("p (h d) -> p h d", h=4), ALU.add)
            rstd_inplace(s5[:, 0:4], 1.0 / 64)
            k3 = KV[:, 0:256].rearrange("p (h d) -> p h d", h=4)
            n3 = knf[:].rearrange("p (h d) -> p h d", h=4)
            P.tt(n3, k3, s5[:, 0:4].unsqueeze(2).to_broadcast([128, 4, 64]), ALU.mult)
            P.tt(n3, n3, gsm[:, G_KM:G_KM + 64].unsqueeze(1).to_broadcast([128, 4, 64]), ALU.mult)
            P.dma(p_mk[r, :], knf[:], eng='pool')
            P.dma(p_mv[r, :], KV[:, 256:512], eng='pool')
            P.copy(kvb[:], knf[:])
            transpose_bf(MkT[:, :, r], [kvb[:, 0:128], kvb[:, 128:256]], pb[3])
            P.memset(MvA[:, mt, :].rearrange("p (h e) -> p h e", e=65)[:, :, 64:65], 1.0)
            P.copy(MvA[:, mt, :].rearrange("p (h e) -> p h e", e=65)[:, :, 0:64],
                   KV[:, 256:512].rearrange("p (h d) -> p h d", h=4), eng='pool')

        def layer_common(x_src, nrows, xt, preloaded=False):
            if not preloaded:
                load_h(x_src, nrows, xt, (pb[0], pb[1]))
            for c in range(16):
                bank = pb[2 + c // 4]
                for kt in range(8):
                    P.matmul(bank[:, (c % 4) * 128:(c % 4 + 1) * 128], W[:, kt, c * 128:(c + 1) * 128], hT[:, kt, :],
                             start=(kt == 0), stop=(kt == 7))
            for c in range(3):
                P.copy(raw[:, c * 4:(c + 1) * 4, 3:131], pb[2 + c][:].rearrange("p (a b) -> p a b", a=4), eng='act')
            for h in range(8):
                bank = pb[6 + h // 4]
                for kt in range(8):
                    P.matmul(bank[0:32, (h % 4) * 128:(h % 4 + 1) * 128], W[:, kt, C_QI + h * 32:C_QI + (h + 1) * 32],
                             hT[:, kt, :], start=(kt == 0), stop=(kt == 7))
            P.copy(qiT[:, 0:4, :], pb[6][0:32, :].rearrange("p (a b) -> p a b", a=4), eng='act')
            P.copy(qiT[:, 4:8, :], pb[7][0:32, :].rearrange("p (a b) -> p a b", a=4), eng='act')
            ck(10)
            P.copy(zas[:], pb[5][:].rearrange("p (a b) -> p a b", a=4), eng='act')
            for ci, c0 in enumerate(range(0, TMW, 512)):
                wd = min(512, TMW - c0)
                bank = pb[ci % 2]
                for kt in range(8):
                    P.matmul(bank[:, 0:wd], hT[:, kt, :], W[:, kt, C_TM + c0:C_TM + c0 + wd], start=(kt == 0), stop=(kt == 7))
                P.copy(tm[:, c0:c0 + wd], bank[:, 0:wd], eng='act')
            ck(12)

        def chainA(sample, tail=None):
            tri, ls = (triS, lsS) if sample else (triP, lsP)
            beta, gg, gc, nbeta, egc, ekd, c1 = (gt[:, 0:4], gt[:, 4:8], gt[:, 8:12], gt[:, 12:16],
                                                 gt[:, 16:20], gt[:, 20:24], gt[:, 24:28])

            def a1():
                for c in range(12):
                    P.ts(acc[:, c, :], raw[:, c, 0:128], wc[:, c, 0:1], ALU.mult)
                    for j in range(1, 4):
                        P.stt(acc[:, c, :], raw[:, c, j:j + 128], wc[:, c, j:j + 1], acc[:, c, :], ALU.mult, ALU.add)
                    if c % 3 == 2:
                        yield
                P.copy(raw[:, :, 0:3], raw[:, :, 128:131], eng='pool')
                P.act(acc[:, 0:8, :], acc[:, 0:8, :], AF.Silu)
                P.act(vsb[:], acc[:, 8:12, :], AF.Silu)
                P.act(zas[:], zas[:], AF.Silu)
                P.act(zs[:, 0:256], tm[:, T_ZB:T_ZB + 256], AF.Silu)
                P.act(zs[:, 256:512], tm[:, T_ZM:T_ZM + 256], AF.Silu)
                yield
                P.tt(sqb[:], acc[:, 0:8, :], acc[:, 0:8, :], ALU.mult, eng='pool')
                for g_ in range(2):
                    P.matmul(pb[2 + g_][:], onesb[:], sqb[:, g_ * 4:(g_ + 1) * 4, :].rearrange("p a b -> p (a b)"))
                P.act(rbc[:, 0:4, :], pb[2][:].rearrange("p (a b) -> p a b", a=4), AF.Ln, bias=EPS)
                P.act(rbc[:, 4:8, :], pb[3][:].rearrange("p (a b) -> p a b", a=4), AF.Ln, bias=EPS)
                P.act(rbc[:, 0:4, :], rbc[:, 0:4, :], AF.Exp, scale=-0.5, bias=float(np.log(128.0 ** -0.5)))
                P.act(rbc[:, 4:8, :], rbc[:, 4:8, :], AF.Exp, scale=-0.5)
                P.tt(qn[:], acc[:, 0:4, :], rbc[:, 0:4, :], ALU.mult)
                P.tt(knT[:], acc[:, 4:8, :], rbc[:, 4:8, :], ALU.mult)
                if kstop == 13:
                    dbg(raw[:, 4:8, 3:131], 512, stage=None) if False else None
                    dbg(acc[:, 4:8, :].rearrange("p a b -> p (a b)"), 512)
                    dbg(rbc[:, 4:8, :].rearrange("p a b -> p (a b)"), 512)
                    dbg(acc[:, 0:4, :].rearrange("p a b -> p (a b)"), 512)
                    dbg(rbc[:, 0:4, :].rearrange("p a b -> p (a b)"), 512)
                    dbg(knT[:].rearrange("p a b -> p (a b)"), 512, stage=tm)
                    dbg(qn[:].rearrange("p a b -> p (a b)"), 512, stage=tm)
                ck(13)
                yield

                for h in range(4):
                    P.matmul(pb[2][:, h * 128:(h + 1) * 128], knT[:, h, :], idb[:])
                    P.matmul(pb[3][:, h * 128:(h + 1) * 128], vsb[:, h, :], idb[:])

            def a2():
                P.act(beta, tm[:, T_BA:T_BA + 4], AF.Exp, scale=-1.0)
                P.ts(beta, beta, 1.0, ALU.add)
                P.recip(beta, beta)
                P.ts(nbeta, beta, -1.0, ALU.mult)
                P.tt(gg, tm[:, T_AA:T_AA + 4], gsm[:, G_DTB:G_DTB + 4], ALU.add)
                P.act(gg, gg, AF.Exp)
                P.act(gg, gg, AF.Ln, bias=1.0)
                P.tt(gg, gg, negA[:], ALU.mult)
                ck(131)
                yield
                blk = blkS if sample else blkP
                P.matmul(pb[4][:, 0:4], tri[:], gg)
                P.matmul(pb[4][:, 4:8], blk[:], gg)
                P.copy(gc, pb[4][:, 0:4])
                P.tt(ekd, pb[4][:, 4:8], gc, ALU.subtract)
                P.act(ekd, ekd, AF.Exp)
                P.act(egc, gc, AF.Exp)
                P.tt(c1, beta, egc, ALU.mult)
                ck(132)
                yield
                for h in range(4):
                    P.ts(Dg[:, h, :], tri[:], gg[:, h:h + 1], ALU.mult)
                    P.matmul(pb[5][:, h * 128:(h + 1) * 128], ones32[:], Dg[:, h, :])
                gcr = pb[5][:].rearrange("p (a b) -> p a b", a=4)
                ck(133)
                yield
                P.act(egr[:], gcr, AF.Exp)
                ck(134)
                yield
                P.copy(DT[:], gcr, eng='act')
                for h in range(4):
                    P.ts(Dst[:, h, :], DT[:, h, :], gc[:, h:h + 1], ALU.subtract, 0.0, ALU.max)
                    P.ts(DT[:, h, :], DT[:, h, :], gc[:, h:h + 1], ALU.subtract, 0.0, ALU.min)
                ck(135)
                yield
                P.act(Dst[:], Dst[:], AF.Exp, scale=-1.0)
                P.act(DT[:], DT[:], AF.Exp)
                ck(136)
                yield
                P.tt(Dst[:], Dst[:], ls[:].unsqueeze(1).to_broadcast([128, 4, 128]), ALU.mult)
                P.tt(DT[:], DT[:], tri[:].unsqueeze(1).to_broadcast([128, 4, 128]), ALU.mult)
                ck(14)
                yield


            def mix(g1, g2):
                d1 = d2 = False
                while not (d1 and d2):
                    if not d1:
                        try:
                            next(g1)
                            yield
                        except StopIteration:
                            d1 = True
                    if not d2:
                        try:
                            next(g2)
                            yield
                        except StopIteration:
                            d2 = True
            yield from mix(a1(), a2())
            ktm = pb[2][:].rearrange("p (a b) -> p a b", a=4)
            vtm = pb[3][:].rearrange("p (a b) -> p a b", a=4)
            P.tt(bv[:], vtm, beta.unsqueeze(2).to_broadcast([128, 4, 128]), ALU.mult)
            P.tt(kbg[:], ktm, c1.unsqueeze(2).to_broadcast([128, 4, 128]), ALU.mult)
            P.tt(kd[:], ktm, ekd.unsqueeze(2).to_broadcast([128, 4, 128]), ALU.mult)
            ck(15)
            yield

            P.tt(qd[:], qn[:], egr[:], ALU.mult)
            for h in range(4):
                P.matmul(pb[5][:, h * 128:(h + 1) * 128], knT[:, h, :], qn[:, h, :])
            P.tt(qkT[:], pb[5][:].rearrange("p (a b) -> p a b", a=4), DT[:], ALU.mult)
            yield
            for h in range(4):
                P.matmul(pb[2][:, h * 128:(h + 1) * 128], knT[:, h, :], knT[:, h, :])
            for h in range(4):
                P.ts(Dst[:, h, :], Dst[:, h, :], nbeta[:, h:h + 1], ALU.mult)
            P.tt(Xm[:], pb[2][:].rearrange("p (a b) -> p a b", a=4), Dst[:], ALU.mult)
            for h in range(4):
                P.matmul(pb[3][:, h * 128:(h + 1) * 128], Xm[:, h, :], id32[:])
            P.copy(Ym[:], pb[3][:].rearrange("p (a b) -> p a b", a=4), eng='act')
            P.tt(Pm[:], Ym[:], id32[:].unsqueeze(1).to_broadcast([128, 4, 128]), ALU.add)
            if kstop == 16:
                dbg(Xm[:].rearrange("p a b -> p (a b)"), 512)
                dbg(Ym[:].rearrange("p a b -> p (a b)"), 512)
                dbg(Pm[:].rearrange("p a b -> p (a b)"), 512)
                dbg(DT[:].rearrange("p a b -> p (a b)"), 512)
                dbg(gt[:, 0:12], 12)
                dbg(knT[:].rearrange("p a b -> p (a b)"), 512, stage=tm)
            ck(16)
            yield
            nlev = 3 if sample else 5
            for lv in range(1, nlev + 1):
                for h in range(4):
                    P.matmul(pb[2][:, h * 128:(h + 1) * 128], Ym[:, h, :], Xm[:, h, :])
                if lv < nlev:
                    for h in range(4):
                        P.matmul(pb[3][:, h * 128:(h + 1) * 128], Xm[:, h, :], Ym[:, h, :])
                P.copy(Xm[:], pb[2][:].rearrange("p (a b) -> p a b", a=4), eng='act')
                if lv < nlev:
                    P.copy(Ym[:], pb[3][:].rearrange("p (a b) -> p a b", a=4), eng='act')
                for h in range(4):
                    P.matmul(pb[4][:, h * 128:(h + 1) * 128], Xm[:, h, :], Pm[:, h, :])
                P.tt(Pm[:], Pm[:], pb[4][:].rearrange("p (a b) -> p a b", a=4), ALU.add)
                yield
            P.copy(Tt[:], Pm[:], eng='act')
            ck(17)
            yield

            for h in range(4):
                P.matmul(pb[2][:, h * 128:(h + 1) * 128], kbg[:, h, :], Tt[:, h, :])
            P.act(nwk[:], pb[2][:].rearrange("p (a b) -> p a b", a=4), AF.Copy, scale=-1.0)
            ck(18)
            yield

            chunks = [(0, 16)] if sample else [(0, 64), (64, 128)]
            obank = pb[4]
            for (r0, r1) in chunks:
                for h in range(4):
                    P.matmul(pb[5][:, h * 128:(h + 1) * 128], Tt[:, h, :], bv[:, h, :], start=True, stop=False)
                    P.matmul(pb[5][:, h * 128:(h + 1) * 128], nwk[:, h, :], Sb[:, h, :], start=False, stop=True)
                P.copy(usb[r0:r1, :, :], pb[5][r0:r1, :].rearrange("p (a b) -> p a b", a=4), eng='act')
                for h in range(4):
                    oc = obank[:, h * 128 + r0:h * 128 + r1]
                    P.matmul(oc, Sb[:, h, :], qd[:, h, r0:r1], start=True, stop=False)
                    P.matmul(oc, usb[r0:r1, h, :], qkT[r0:r1, h, r0:r1], start=False, stop=True)
                for h in range(4):
                    P.matmul(pb[2][:, h * 128:(h + 1) * 128], kd[r0:r1, h, :], usb[r0:r1, h, :])
                for h in range(4):
                    P.ts(S[:, h, :], S[:, h, :], egr[:, h, r1 - 1:r1], ALU.mult)
                P.tt(S[:], S[:], pb[2][:].rearrange("p (a b) -> p a b", a=4), ALU.add)
                P.copy(Sb[:], S[:], eng='act')
                yield
            o3 = obank[:].rearrange("p (a b) -> p a b", a=4)
            ck(19)
            yield
            P.act(sqb[:, 0:4, :], o3, AF.Square)
            P.matmul(pb[3][:], onesb[:], sqb[:, 0:4, :].rearrange("p a b -> p (a b)"))
            P.act(rbc[:, 0:4, :], pb[3][:].rearrange("p (a b) -> p a b", a=4), AF.Ln, scale=1.0 / 128, bias=EPS)
            P.act(rbc[:, 0:4, :], rbc[:, 0:4, :], AF.Exp, scale=-0.5)
            P.tt(rbc[:, 0:4, :], rbc[:, 0:4, :], o3, ALU.mult)
            P.stt(mixT[:, 0:4, :], rbc[:, 0:4, :], gocol[:, 0:1], zas[:], ALU.mult, ALU.mult)
            ck(20)
            yield
            if tail is not None:
                tail()
                yield


        def chainB_heads(nrows, outs):
            P.tt(sqB[:, 0:512], tm[:, T_QB:T_QB + 512], tm[:, T_QB:T_QB + 512], ALU.mult)
            P.reduce(s5[:, 0:8], sqB[:, 0:512].rearrange("p (h d) -> p h d", d=64), ALU.add)
            P.tt(sqB[:, 0:256], tm[:, T_QM:T_QM + 256], tm[:, T_QM:T_QM + 256], ALU.mult)
            P.reduce(s5[:, 8:12], sqB[:, 0:256].rearrange("p (h d) -> p h d", d=64), ALU.add)
            P.tt(sqB[:, 256:288], tm[:, T_KI:T_KI + 32], tm[:, T_KI:T_KI + 32], ALU.mult)
            P.reduce(s5[:, 12:13], sqB[:, 256:288], ALU.add)
            yield
            P.ts(s5[:, 12:13], s5[:, 12:13], 2.0, ALU.mult)
            rstd_inplace(s5[:, 0:13], 1.0 / 64)
            k3 = tm[:, T_KB:T_KB + 256].rearrange("p (h d) -> p h d", h=4)
            n3 = knf[:].rearrange("p (h d) -> p h d", h=4)
            P.tt(n3, k3, s5[:, 4:8].unsqueeze(2).to_broadcast([128, 4, 64]), ALU.mult)
            P.tt(n3, n3, gsm[:, G_KB:G_KB + 64].unsqueeze(1).to_broadcast([128, 4, 64]), ALU.mult)
            P.dma(outs["k"], knf[0:nrows, :], eng='pool')
            P.dma(outs["v"], tm[0:nrows, T_VB:T_VB + 256], eng='pool')
            P.copy(nrm[:, 256:512], knf[:], eng='pool')
            q3 = tm[:, T_QB:T_QB + 256].rearrange("p (h d) -> p h d", h=4)
            yield
            m3 = sqB[:, 0:256].rearrange("p (h d) -> p h d", h=4)
            P.tt(m3, q3, s5[:, 0:4].unsqueeze(2).to_broadcast([128, 4, 64]), ALU.mult)
            P.tt(nrm[:, 0:256].rearrange("p (h d) -> p h d", h=4), m3,
                 gsm[:, G_QB:G_QB + 64].unsqueeze(1).to_broadcast([128, 4, 64]), ALU.mult)
            q3 = tm[:, T_QM:T_QM + 256].rearrange("p (h d) -> p h d", h=4)
            m3 = sqB[:, 256:512].rearrange("p (h d) -> p h d", h=4)
            P.tt(m3, q3, s5[:, 8:12].unsqueeze(2).to_broadcast([128, 4, 64]), ALU.mult)
            P.tt(nrm[:, 512:768].rearrange("p (h d) -> p h d", h=4), m3,
                 gsm[:, G_QM:G_QM + 64].unsqueeze(1).to_broadcast([128, 4, 64]), ALU.mult)
            P.ts(kxn[:], tm[:, T_KI:T_KI + 32], s5[:, 12:13], ALU.mult)
            P.tt(kxn[:], kxn[:], gsm[:, G_KI:G_KI + 32], ALU.mult)
            P.dma(outs["ki"], kxn[0:nrows, :], eng='pool')
            P.copy(kxb[:], kxn[:], eng='pool')
            kslot = outs["kslot"]
            kc = slice(kslot * 128, (kslot + 1) * 128)
            yield
            transpose_bf(QbT[:], [nrm[:, 0:128], nrm[:, 128:256]], pb[0])
            transpose_bf(KbT[:, :, kc], [nrm[:, 256:384], nrm[:, 384:512]], pb[1])
            yield
            transpose_bf(QmT[:], [nrm[:, 512:640], nrm[:, 640:768]], pb[6])
            P.matmul(pb[7][0:32, 0:128], kxb[:], idb[:])
            P.copy(kiT[:, kc], pb[7][0:32, 0:128])
            yield
            va = Vaug[:, kslot, :].rearrange("p (h e) -> p h e", e=65)
            P.memset(va[:, :, 64:65], 1.0)
            P.copy(va[:, :, 0:64], tm[:, T_VB:T_VB + 256].rearrange("p (h d) -> p h d", h=4), eng='pool')
            P.ts(wabs[:], tm[:, T_WI:T_WI + 8], IDX_SCALE, ALU.mult)
            for h in range(8):
                P.ts(wdiag[:, h, :], id32[:], wabs[:, h:h + 1], ALU.mult, eng=('dve' if h % 2 == 0 else 'pool'))
            yield

        def conv_out(nrows, dst):
            for c in range(3):
                pc = pb[c % 2]
                for kt in range(8):
                    P.matmul(pc[:], hT[:, kt, :], W[:, kt, c * 512:(c + 1) * 512], start=(kt == 0), stop=(kt == 7))
                P.copy(cv[:, c * 512:(c + 1) * 512], pc[:], eng=('act' if c % 2 == 0 else 'dve'))
            P.dma(dst, cv[nrows - 3:nrows, :], eng='pool')

        def index_scores(sc, col0, ktiles, kbase, alt=False):
            for g0 in range(0, ktiles, 4):
                nk = min(4, ktiles - g0) * 128
                kcs = slice((kbase + g0) * 128, (kbase + g0) * 128 + nk)
                dst = sc[:, col0 + g0 * 128:col0 + g0 * 128 + nk]

                def logits(h):
                    P.matmul(pb[h % 2][:, 0:nk], qiT[:, h, :], kiT[:, kcs])

                def relu_sum(h):
                    r_ = rl[h % 2][:].bitcast(BF16)
                    if alt and h % 2 == 1:
                        P.ts(r_[:, 0:nk], pb[h % 2][:, 0:nk], 0.0, ALU.max)
                    else:
                        P.act(r_[:, 0:nk], pb[h % 2][:, 0:nk], AF.Relu)
                    return r_
                logits(0)
                for h in range(8):
                    r_ = relu_sum(h)
                    if h + 1 < 8:
                        logits(h + 1)
                    P.matmul(pb[6][:, 0:nk], wdiag[:, h, :], r_[:, 0:nk], start=(h == 0), stop=(h == 7))
                    if h % 2 == 1:
                        yield
                P.copy(dst, pb[6][:, 0:nk], eng='act')
                yield

        def bisect(sc, ncols, lo_cols):
            thr, w0, t_, cnt, hh = bis[:, 0:1], bis[:, 1:2], bis[:, 2:3], bis[:, 3:4], bis[:, 4:5]
            P.reduce(w0, sc[:, 0:ncols], ALU.max)
            P.reduce(thr, sc[:, 0:lo_cols], ALU.min)
            P.ts(thr, thr, -1.0, ALU.add)
            P.tt(w0, w0, thr, ALU.subtract)
            P.ts(wtab[:], p2[:], w0, ALU.mult)
            P.tt(t_, thr, wtab[:, 0:1], ALU.add)
            for k in range(NBIS):
                jk = jk8
                for ci, c0 in enumerate(range(0, ncols, 2048)):
                    wd = min(2048, ncols - c0)
                    P.ts(jk[:, 0:wd], sc[:, c0:c0 + wd], t_, ALU.is_gt, (None if ci == 0 else cnt), ALU.add,
                         accum_out=cnt)
                P.ts(hh, cnt, 255.5, ALU.is_ge, 0.5, ALU.subtract)
                P.stt(t_, hh, wtab[:, k:k + 1], t_, ALU.mult, ALU.add)
                yield
            P.tt(thr, t_, wtab[:, NBIS:NBIS + 1], ALU.subtract)

        def attend(qT, keys, obank_, first, last, mask_cols=None, sc=None):
            n = len(keys)

            def scores_t(i):
                kT, va = keys[i]
                bank = pb[i % 2]
                nm = None
                if mask_cols is not None:
                    nm = nmk[i % 2]
                    P.ts(nm[:], sc[:, mask_cols[i]:mask_cols[i] + 128], bis[:, 0:1], ALU.is_le)
                for h in range(4):
                    pr = slice((h % 2) * 64, (h % 2) * 64 + 64)
                    oc = bank[:, h * 128:(h + 1) * 128]
                    P.matmul(oc, kT[pr, h // 2, :], qT[pr, h // 2, :], start=True, stop=False)
                    P.matmul(oc, (nm[:] if nm is not None else zerob[:]), negI[:], start=False, stop=True)

            scores_t(0)
            for i in range(n):
                kT, va = keys[i]
                pt = PT[i % 2]
                P.act(pt[:], pb[i % 2][:].rearrange("p (a b) -> p a b", a=4), AF.Exp, scale=0.125)
                if i + 1 < n:
                    scores_t(i + 1)
                for h in range(4):
                    P.matmul(obank_[:, h * 65:(h + 1) * 65], pt[:, h, :], va[:, h, :],
                             start=(first and i == 0 and h == 0), stop=(last and i == n - 1 and h == 3))
                yield

        def finish_heads(obank_, zcols, dst_cols):
            P.copy(sqB[:, 0:260], obank_[:, 0:260], eng='act')
            o3 = sqB[:, 0:260].rearrange("p (h e) -> p h e", e=65)
            P.copy(rcp[:, 0:4], o3[:, :, 64])
            P.recip(rcp[:, 0:4], rcp[:, 0:4])
            P.tt(o3[:, :, 0:64], o3[:, :, 0:64], rcp[:, 0:4].unsqueeze(2).to_broadcast([128, 4, 64]), ALU.mult)
            P.tt(mixBC[:, dst_cols:dst_cols + 256].rearrange("p (h d) -> p h d", h=4), o3[:, :, 0:64],
                 zs[:, zcols:zcols + 256].rearrange("p (h d) -> p h d", h=4), ALU.mult)
            yield

        def mix_bc_T(bank):
            transpose_bf(mixT[:, 4:8, :], [mixBC[:, j * 128:(j + 1) * 128] for j in range(4)], bank)

        def out_proj(nrows, y_dst, xt):
            for c in range(2):
                bank = pb[c]
                for kt in range(8):
                    P.matmul(bank[:], mixT[:, kt, :], Wo[:, kt, c * 512:(c + 1) * 512], start=(kt == 0), stop=(kt == 7))
                P.tt(yo[:, c * 512:(c + 1) * 512], bank[:], xt[:, c * 512:(c + 1) * 512], ALU.add)
            P.dma(y_dst, yo[0:nrows, :], eng='pool')

        def dbg(ap2d, ncols, stage=None):
            if not kstop:
                return
            i = dbg_n[0]
            dbg_n[0] += 1
            if stage is not None:
                P.copy(stage[:, 0:ncols], ap2d)
                P.dma(dbg_outs[i][:, 0:ncols], stage[:, 0:ncols], eng='pool')
            else:
                P.dma(dbg_outs[i][:, 0:ncols], ap2d, eng='pool')

        def drain(g):
            for _ in g:
                pass

        def count_steps(mk):
            P.dry = True
            n = 0
            for _ in mk():
                n += 1
            P.dry = False
            return n + 1

        def interleave(mka, mkb):
            na, nb = count_steps(mka), count_steps(mkb)
            ga, gb = mka(), mkb()
            ia = ib = 0
            da = db = False
            while not (da and db):
                pick_a = (not da) and (db or ia * nb <= ib * na)
                if pick_a:
                    try:
                        next(ga)
                        ia += 1
                    except StopIteration:
                        da = True
                else:
                    try:
                        next(gb)
                        ib += 1
                    except StopIteration:
                        db = True

        def prompt_chainB(tt, outs):
            yield from chainB_heads(128, outs)
            nk = tt + 1
            yield from index_scores(scores, 0, nk, 0)
            dcol = tt * 128
            P.stt(scores[:, dcol:dcol + 128], scores[:, dcol:dcol + 128], 1.0, admP[:], ALU.mult, ALU.mult)
            P.tt(scores[:, dcol:dcol + 128], scores[:, dcol:dcol + 128], nadmP[:], ALU.add)
            if tt >= 2:
                yield from bisect(scores, nk * 128, (nk - 1) * 128)
            else:
                P.memset(bis[:, 0:1], -1.0e29)
            keys = [(KbT[:, :, j * 128:(j + 1) * 128], Vaug[:, j, :].rearrange("p (h e) -> p h e", e=65)) for j in range(nk)]
            yield from attend(QbT, keys, pb[6], True, True, mask_cols=[j * 128 for j in range(nk)], sc=scores)
            yield from finish_heads(pb[6], 0, 0)
            mkeys = [(MkT[:, :, j * 128:(j + 1) * 128], MvA[:, j, :].rearrange("p (h e) -> p h e", e=65)) for j in range(2)]
            yield from attend(QmT, mkeys, pb[7], True, True)
            yield from finish_heads(pb[7], 256, 256)
            mix_bc_T(pb[0])
            yield

        try:
            for mt in range(2):
                mem_tile(mt)
            ck(2)
            P.memset(raw[:, :, 0:3], 0.0)
            P.memset(S[:], 0.0)
            P.memset(Sb[:], 0.0)
            for tt in range(NT):
                r = slice(tt * 128, (tt + 1) * 128)
                outs = {"k": p_k[r, :], "v": p_v[r, :], "ki": p_ki[r, :], "kslot": tt}
                xt = xtb[tt % 2]
                layer_common(x_p[r, :], 128, xt, preloaded=(tt > 0 and not kstop))
                tail = None
                if tt + 1 < NT and not kstop:
                    r2 = slice((tt + 1) * 128, (tt + 2) * 128)
                    tail = (lambda r2=r2, tt=tt: load_h(x_p[r2, :], 128, xtb[(tt + 1) % 2], (pb[2], pb[3])))
                if kstop:
                    drain(chainA(False))
                    drain(prompt_chainB(tt, outs))
                else:
                    interleave(lambda tail=tail: chainA(False, tail), lambda tt=tt, outs=outs: prompt_chainB(tt, outs))
                if tt == NT - 1:
                    conv_out(128, p_conv)
                out_proj(128, y_p[r, :], xt)
                ck(25)
            P.dma(p_ssm.rearrange("h k v -> k h v"), S[:], eng='pool')
            ck(30)

            P.dma(S[:], st_ssm.rearrange("h k v -> k h v"))
            P.copy(Sb[:], S[:])
            P.dma(stc, st_conv)
            for c in range(12):
                P.matmul(pb[2][:, c * 3:(c + 1) * 3], stc[0:3, c * 128:(c + 1) * 128], id32[0:3, 0:3])
            P.copy(raw[:, :, 0:3], pb[2][:, 0:36].rearrange("p (c j) -> p c j", j=3))
            for mt in range(2):
                r = slice(mt * 128, (mt + 1) * 128)
                KV = kvc[mt]
                P.dma(KV[:, 0:256], c_mk[r, :])
                P.dma(KV[:, 256:512], c_mv[r, :])
                P.copy(kvb[:], KV[:, 0:256])
                transpose_bf(MkT[:, :, r], [kvb[:, 0:128], kvb[:, 128:256]], pb[3])
                P.copy(MvA[:, mt, :].rearrange("p (h e) -> p h e", e=65)[:, :, 0:64],
                       KV[:, 256:512].rearrange("p (h d) -> p h d", h=4), eng='pool')
            outs = {"k": s_k, "v": s_v, "ki": s_ki, "kslot": 0}
            layer_common(x_s, 16, xtb[0])
            if kstop:
                drain(chainA(True))
                drain(chainB_heads(16, outs))
            else:
                interleave(lambda: chainA(True), lambda outs=outs: chainB_heads(16, outs))
            conv_out(16, s_conv)
            P.dma(s_ssm.rearrange("h k v -> k h v"), S[:], eng='pool')
            ck(31)
            Wf = W[:].rearrange("p a b -> p (a b)").bitcast(F32)
            Wb = W[:].rearrange("p a b -> p (a b)")
            kiS = Wf[:, 4224:5248].rearrange("p (t c) -> p t c", c=32)
            KS = Wf[:, 5248:9344].rearrange("p (t c) -> p t c", c=256)
            VS = Wf[:, 9344:13440].rearrange("p (t c) -> p t c", c=256)
            Kb16 = Wb[:, 26880:30976].rearrange("p (t c) -> p t c", c=256)
            kib = Wb[:, 18688:19712].rearrange("p (t c) -> p t c", c=32)
            def dma_tiles(dst, src_rows, ntile, step):
                v = src_rows.rearrange("(t p) c -> p t c", p=128)
                for t0 in range(0, ntile, step):
                    P.dma(dst[:, t0:t0 + step, :], v[:, t0:t0 + step, :])
            dma_tiles(kiS, c_ki, 32, 8)
            dma_tiles(KS, c_k[0:2048, :], 16, 8)
            drain(index_scores(scoresS, PAST, 1, 0, alt=True))
            P.tt(scoresS[:, PAST:PAST + 128], scoresS[:, PAST:PAST + 128], nadmS[:], ALU.add)
            P.copy(kib, kiS)
            for g_ in range(2):
                for q4 in range(4):
                    bank = pb[2 + q4]
                    for j in range(4):
                        t_ = g_ * 16 + q4 * 4 + j
                        P.matmul(bank[0:32, j * 128:(j + 1) * 128], kib[:, t_, :], idb[:])
                    P.copy(kiT[:, q4 * 512:(q4 + 1) * 512], bank[0:32, :], eng='act')
                drain(index_scores(scoresS, g_ * 2048, 16, 0, alt=True))
            dma_tiles(VS, c_v[0:2048, :], 16, 8)
            drain(bisect(scoresS, PAST + 128, PAST))
            ck(32)
            keys = [(KbT[:, :, 0:128], Vaug[:, 0, :].rearrange("p (h e) -> p h e", e=65))]
            drain(attend(QbT, keys, pb[6], True, False, mask_cols=[PAST], sc=scoresS))
            for g_ in range(2):
                if g_ == 1:
                    dma_tiles(KS, c_k[2048:4096, :], 16, 8)
                    dma_tiles(VS, c_v[2048:4096, :], 16, 8)
                P.copy(Kb16[:, 0:6, :], KS[:, 0:6, :])
                P.copy(Kb16[:, 6:11, :], KS[:, 6:11, :], eng='act')
                P.copy(Kb16[:, 11:16, :], KS[:, 11:16, :], eng='pool')
                for q2 in range(8):
                    bank = pb[2 + q2 % 4]
                    for pr_ in range(2):
                        for j in range(2):
                            t_ = q2 * 2 + j
                            P.matmul(bank[:, (pr_ * 2 + j) * 128:(pr_ * 2 + j + 1) * 128],
                                     Kb16[:, t_, pr_ * 128:(pr_ + 1) * 128], idb[:])
                    P.copy(KbT[:, :, q2 * 256:(q2 + 1) * 256], bank[:].rearrange("p (a b) -> p a b", a=2), eng='act')
                v4 = Vaug[:, 0:16, :].rearrange("p t (h e) -> p t h e", e=65)
                for t4 in range(4):
                    P.copy(v4[:, t4 * 4:(t4 + 1) * 4, :, 0:64].rearrange("p t h d -> p (t h) d"),
                           VS[:, t4 * 4:(t4 + 1) * 4, :].rearrange("p t (h d) -> p (t h) d", d=64),
                           eng=('pool' if t4 % 2 == 0 else 'dve'))
                keys = [(KbT[:, :, j * 128:(j + 1) * 128], Vaug[:, j, :].rearrange("p (h e) -> p h e", e=65)) for j in range(16)]
                drain(attend(QbT, keys, pb[6], False, g_ == 1, mask_cols=[g_ * 2048 + j * 128 for j in range(16)], sc=scoresS))
            drain(finish_heads(pb[6], 0, 0))
            mkeys = [(MkT[:, :, j * 128:(j + 1) * 128], MvA[:, j, :].rearrange("p (h e) -> p h e", e=65)) for j in range(2)]
            drain(attend(QmT, mkeys, pb[7], True, True))
            drain(finish_heads(pb[7], 256, 256))
            mix_bc_T(pb[0])
            out_proj(16, y_s, xtb[0])
        except _Stop:
            pass
        P.emit(final_wait_engine='pool')
    return nc


_CACHE = {}


def _make_in_maps(inp):
    g_all = np.ascontiguousarray(np.concatenate([inp['g_in'][0].reshape(8, 128), inp['g_mem'][0].reshape(8, 128)], 0))
    gsm = np.ascontiguousarray(np.concatenate([
        inp['g_k_B'][0], inp['g_kidx_B'][0], inp['g_k_M'][0], inp['g_q_B'][0], inp['g_q_M'][0],
        inp['dt_bias_A'][0], inp['a_log_A'][0]]).reshape(1, GSM).astype(np.float32))
    maps = []
    for b in range(8):
        maps.append({
            "x_p": np.ascontiguousarray(inp['x_prompt'][b]),
            "x_s": np.ascontiguousarray(inp['x_sample'][b]),
            "mem": np.ascontiguousarray(inp['mem_prompt'][b]),
            "w_in": np.ascontiguousarray(inp['w_in'][0]),
            "w_mem": np.ascontiguousarray(inp['w_mem_kv'][0]),
            "w_out": np.ascontiguousarray(inp['w_out'][0]),
            "g_all": g_all,
            "gsm": gsm,
            "g_o": np.ascontiguousarray(inp['g_o_A'][0].reshape(1, 128)),
            "conv_w": np.ascontiguousarray(inp['conv_w_A'][0]),
            "st_conv": np.ascontiguousarray(inp['state_conv_A'][0, b]),
            "st_ssm": np.ascontiguousarray(inp['state_ssm_A'][0, b]),
            "c_k": np.ascontiguousarray(inp['cache_k_B'][0, b].reshape(PAST, 256)),
            "c_v": np.ascontiguousarray(inp['cache_v_B'][0, b].reshape(PAST, 256)),
            "c_ki": np.ascontiguousarray(inp['cache_kidx_B'][0, b]),
            "c_mk": np.ascontiguousarray(inp['cache_mem_k'][0, b].reshape(256, 256)),
            "c_mv": np.ascontiguousarray(inp['cache_mem_v'][0, b].reshape(256, 256)),
        })
    return maps


def _assemble(res):
    def stack(name, shape):
        return np.stack([np.asarray(r[name], dtype=np.float32).reshape(shape) for r in res], 0)
    outs = (
        stack("y_p", (SEQ, D)), stack("y_s", (16, D)),
        stack("p_conv", (3, 1536))[None],
        stack("p_ssm", (4, 128, 128))[None],
        stack("p_k", (SEQ, 4, 64))[None],
        stack("p_v", (SEQ, 4, 64))[None],
        stack("p_ki", (SEQ, 32))[None],
        stack("p_mk", (256, 4, 64))[None],
        stack("p_mv", (256, 4, 64))[None],
        stack("s_conv", (3, 1536))[None],
        stack("s_ssm", (4, 128, 128))[None],
        stack("s_k", (16, 4, 64))[None],
        stack("s_v", (16, 4, 64))[None],
        stack("s_ki", (16, 32))[None],
    )
    return outs


def kernel(**inputs):
    inp = {k: np.asarray(v) for k, v in inputs.items()}
    nc = build_program()
    in_maps = _make_in_maps(inp)
    res = run_bass_kernel_spmd(nc, in_maps, core_ids=list(range(8)))
    return _assemble(res.results)
```

```python
import numpy as np
import concourse.bass as bass
import concourse.mybir as mybir
from concourse.bass_utils import run_bass_kernel_spmd

F32 = mybir.dt.float32
BF16 = mybir.dt.bfloat16
ALU = mybir.AluOpType
AF = mybir.ActivationFunctionType
AX = mybir.AxisListType


def _region(ap):
    t = ap.tensor
    name = t.name
    esz = mybir.dt.size(ap.dtype)
    pat = tuple((st_ * esz, c_) for st_, c_ in ap.ap)
    off = ap.offset * esz
    space = str(ap.space)
    if 'DRAM' in space.upper() or 'Dram' in space or 'dram' in space:
        lo = off
        hi = off + sum((c - 1) * abs(s) for s, c in pat) + esz
        return (name, 0, 1, lo, hi)
    ps, pc = pat[0]
    if ps == 0:
        ps = 1 << 30
    p0 = off // ps
    fo = off % ps
    hi = fo + sum((c - 1) * abs(s) for s, c in pat[1:]) + esz
    if 'PSUM' in space.upper():
        return (name, (p0 // 32) * 32, ((p0 + pc + 31) // 32) * 32, 0, 1 << 30)
    return (name, p0, p0 + pc, fo, hi)


def _overlap(a, b):
    return a[1] < b[2] and b[1] < a[2] and a[3] < b[4] and b[3] < a[4]


def _contains(a, b):
    return a[1] <= b[1] and a[2] >= b[2] and a[3] <= b[3] and a[4] >= b[4]


class Op:
    __slots__ = ('eng', 'fn', 'reads', 'writes', 'dma', 'deps', 'signals', 'sem', 'val', 'idx', 'pe_acc')

    def __init__(self, eng, fn, reads, writes, dma=False):
        self.eng = eng
        self.fn = fn
        self.reads = reads
        self.writes = writes
        self.dma = dma
        self.deps = set()
        self.signals = False
        self.sem = None
        self.val = 0


class Prog:
    ENGS = ('pe', 'act', 'dve', 'pool', 'sp')

    def __init__(self, nc, n_dma_sems=12, same_engine_sync=True):
        self.nc = nc
        self.ops = []
        self.n_dma_sems = n_dma_sems
        self.same_engine_sync = same_engine_sync
        self.alias = {}

    def set_alias(self, name, group):
        self.alias[name] = group

    def add(self, eng, fn, reads, writes, dma=False):
        if getattr(self, 'dry', False):
            return None
        rr = [_region(a) for a in reads if a is not None and not isinstance(a, (int, float))]
        ww = [_region(a) for a in writes if a is not None]
        op = Op(eng, fn, rr, ww, dma)
        op.idx = len(self.ops)
        self.ops.append(op)
        return op

    def resolve(self):
        writers = {}
        readers = {}
        for op in self.ops:
            for r in op.reads:
                key = self.alias.get(r[0], r[0])
                full = key != r[0]
                for (wr, wi) in writers.get(key, ()):
                    if full or _overlap(r, wr):
                        op.deps.add(wi)
            for w in op.writes:
                key = self.alias.get(w[0], w[0])
                full = key != w[0]
                for (wr, wi) in writers.get(key, ()):
                    if full or _overlap(w, wr):
                        op.deps.add(wi)
                for (rr, ri) in readers.get(key, ()):
                    if full or _overlap(w, rr):
                        op.deps.add(ri)
            for w in op.writes:
                key = self.alias.get(w[0], w[0])
                full = key != w[0]
                if full:
                    writers[key] = [(w, op.idx)]
                    readers[key] = []
                else:
                    writers[key] = [(wr, wi) for (wr, wi) in writers.get(key, ()) if not _contains(w, wr)] + [(w, op.idx)]
                    readers[key] = [(rr, ri) for (rr, ri) in readers.get(key, ()) if not _contains(w, rr)]
            for r in op.reads:
                key = self.alias.get(r[0], r[0])
                lst = readers.setdefault(key, [])
                if not op.dma:
                    lst[:] = [(rr, ri) for (rr, ri) in lst
                              if not (rr == r and self.ops[ri].eng == op.eng and not self.ops[ri].dma)]
                lst.append((r, op.idx))
            op.deps.discard(op.idx)
        for op in self.ops:
            keep = set()
            for d in op.deps:
                o2 = self.ops[d]
                if not o2.dma and not op.dma and o2.eng == op.eng:
                    if op.eng == 'pe':
                        continue
                    if not self.same_engine_sync:
                        continue
                keep.add(d)
            op.deps = keep
            for d in keep:
                self.ops[d].signals = True
        for op in self.ops:
            if op.dma:
                op.signals = True

    def emit(self, final_wait_engine='sp'):
        nc = self.nc
        self.resolve()
        import contextlib
        with contextlib.ExitStack() as st:
            esem = {e: st.enter_context(nc.semaphore('sem_' + e)) for e in ('pe', 'act', 'dve', 'pool')}
            dsem = [st.enter_context(nc.semaphore('dsem%d' % i)) for i in range(self.n_dma_sems)]
            ecount = {e: 0 for e in esem}
            dcount = [0] * self.n_dma_sems
            half = self.n_dma_sems // 2
            kq = {'sp': 0, 'pool': 0, 'act': 0}
            for op in self.ops:
                if not op.signals:
                    continue
                if op.dma:
                    if op.eng == 'pool':
                        i = half + kq['pool'] % (self.n_dma_sems - half)
                        kq['pool'] += 1
                    else:
                        i = kq['sp'] % half
                        kq['sp'] += 1
                    dcount[i] += 16
                    op.sem = ('d', i)
                    op.val = dcount[i]
                else:
                    ecount[op.eng] += 1
                    op.sem = ('e', op.eng)
                    op.val = ecount[op.eng]
            final = {}
            for op in self.ops:
                if op.dma:
                    final[op.sem] = max(final.get(op.sem, 0), op.val)

            def semh(s):
                return esem[s[1]] if s[0] == 'e' else dsem[s[1]]

            per_eng = {e: [o for o in self.ops if o.eng == e] for e in self.ENGS}
            block = st.enter_context(nc.Block())

            def run(engname, eng):
                known = {}
                for op in per_eng[engname]:
                    need = {}
                    for d in op.deps:
                        o2 = self.ops[d]
                        need[o2.sem] = max(need.get(o2.sem, 0), o2.val)
                    if op.dma and op.val > 16:
                        need[op.sem] = max(need.get(op.sem, 0), op.val - 16)
                    for s, v in need.items():
                        if known.get(s, 0) >= v:
                            continue
                        eng.wait_ge(semh(s), v)
                        known[s] = v
                    ins = op.fn(eng)
                    if op.signals:
                        ins.then_inc(semh(op.sem), 16 if op.dma else 1)
                if engname == final_wait_engine:
                    for s, v in final.items():
                        if known.get(s, 0) < v:
                            eng.wait_ge(semh(s), v)

            @block.tensor
            def _(e):
                run('pe', e)

            @block.scalar
            def _(e):
                run('act', e)

            @block.vector
            def _(e):
                run('dve', e)

            @block.gpsimd
            def _(e):
                run('pool', e)

            @block.sync
            def _(e):
                run('sp', e)

    def dma(self, out, in_, eng='sp'):
        return self.add(eng, lambda e: e.dma_start(out=out, in_=in_), [in_], [out], dma=True)

    def matmul(self, out, lhsT, rhs, start=True, stop=True):
        rd = [lhsT, rhs] + ([] if start else [out])
        return self.add('pe', lambda e: e.matmul(out, lhsT, rhs, start=start, stop=stop), rd, [out])

    def transpose(self, out, in_, ident):
        return self.add('pe', lambda e: e.transpose(out, in_, ident), [in_, ident], [out])

    def act(self, out, in_, func, bias=None, scale=None, accum_out=None, eng='act'):
        kw = {}
        rd = [in_]
        if bias is not None:
            kw['bias'] = bias
            rd.append(bias)
        if scale is not None:
            kw['scale'] = scale
            rd.append(scale)
        wr = [out]
        if accum_out is not None:
            kw['accum_out'] = accum_out
            wr.append(accum_out)
        return self.add('act', lambda e: e.activation(out, in_, func, **kw), rd, wr)

    def tt(self, out, in0, in1, op, eng='dve'):
        return self.add(eng, lambda e: e.tensor_tensor(out, in0, in1, op), [in0, in1], [out])

    def ts(self, out, in0, s1, op0, s2=None, op1=None, accum_out=None, eng='dve'):
        rd = [in0, s1, s2]
        wr = [out, accum_out]

        def fn(e):
            kw = {}
            if accum_out is not None:
                kw['accum_out'] = accum_out
            if op1 is not None:
                return e.tensor_scalar(out, in0, s1, s2, op0, op1, **kw)
            return e.tensor_scalar(out, in0, s1, None, op0, **kw)
        return self.add(eng, fn, rd, wr)

    def stt(self, out, in0, scalar, in1, op0, op1, eng='dve'):
        return self.add(eng, lambda e: e.scalar_tensor_tensor(out, in0, scalar, in1, op0, op1),
                        [in0, scalar, in1], [out])

    def copy(self, out, in_, eng='dve'):
        if eng == 'act':
            return self.add('act', lambda e: e.activation(out, in_, AF.Copy), [in_], [out])
        return self.add(eng, lambda e: e.tensor_copy(out, in_), [in_], [out])

    def memset(self, ap, val, eng='dve'):
        return self.add(eng, lambda e: e.memset(ap, val), [], [ap])

    def reduce(self, out, in_, op, axis=AX.X, eng='dve'):
        return self.add(eng, lambda e: e.tensor_reduce(out, in_, axis, op), [in_], [out])

    def recip(self, out, in_):
        return self.add('dve', lambda e: e.reciprocal(out, in_), [in_], [out])


D = 1024
IN_W = 3888
SEQ = 2048
NT = SEQ // 128
PAST = 4096
EPS = 1e-6
C_ZA = 1536
C_TM = 2048
T_BA, T_AA, T_QB, T_KB, T_VB, T_ZB, T_KI, T_WI, T_QM, T_ZM = 0, 4, 8, 264, 520, 776, 1288, 1320, 1328, 1584
C_QI = 3080
TMW = IN_W - C_TM
IDX_SCALE = (8 ** -0.5) * (32 ** -0.5)
NBIS = 17
NEGBIG = -1.0e30
G_KB, G_KI, G_KM, G_QB, G_QM, G_DTB, G_ALOG = 0, 64, 96, 160, 224, 288, 292
GSM = 296


class _Stop(Exception):
    pass


def build_program(kstop=0):
    import contextlib

    def ck(n):
        if kstop == n:
            raise _Stop()
    nc = bass.Bass("TRN2", target_bir_lowering=False)

    def din(name, shape):
        return nc.dram_tensor(name, shape, F32, kind="ExternalInput").ap()

    def dout(name, shape):
        return nc.dram_tensor(name, shape, F32, kind="ExternalOutput").ap()

    x_p = din("x_p", [SEQ, D])
    x_s = din("x_s", [16, D])
    mem = din("mem", [256, D])
    w_in = din("w_in", [D, IN_W])
    w_mem = din("w_mem", [D, 512])
    w_out = din("w_out", [D, D])
    g_all = din("g_all", [16, 128])
    gsm_d = din("gsm", [1, GSM])
    g_o_d = din("g_o", [1, 128])
    convw_d = din("conv_w", [4, 1536])
    st_conv = din("st_conv", [3, 1536])
    st_ssm = din("st_ssm", [4, 128, 128])
    c_k = din("c_k", [PAST, 256])
    c_v = din("c_v", [PAST, 256])
    c_ki = din("c_ki", [PAST, 32])
    c_mk = din("c_mk", [256, 256])
    c_mv = din("c_mv", [256, 256])

    y_p = dout("y_p", [SEQ, D])
    y_s = dout("y_s", [16, D])
    p_conv = dout("p_conv", [3, 1536])
    p_ssm = dout("p_ssm", [4, 128, 128])
    p_k = dout("p_k", [SEQ, 256])
    p_v = dout("p_v", [SEQ, 256])
    p_ki = dout("p_ki", [SEQ, 32])
    p_mk = dout("p_mk", [256, 256])
    p_mv = dout("p_mv", [256, 256])
    s_conv = dout("s_conv", [3, 1536])
    s_ssm = dout("s_ssm", [4, 128, 128])
    s_k = dout("s_k", [16, 256])
    s_v = dout("s_v", [16, 256])
    s_ki = dout("s_ki", [16, 32])

    dbg_outs = [dout("dbg%d" % i, [128, 1024]) for i in range(6)] if kstop else []
    dbg_n = [0]

    with contextlib.ExitStack() as st:
        def sb(name, shape, dt=F32):
            return st.enter_context(nc.sbuf_tensor(name, shape, dt))

        def ps(name, shape, dt=F32):
            return st.enter_context(nc.psum_tensor(name, shape, dt))

        P = Prog(nc, n_dma_sems=16)

        W = sb("W", [128, 8, IN_W], BF16)
        Wo = sb("Wo", [128, 8, D], BF16)
        KbT = sb("KbT", [128, 2, 2048], BF16)
        Wm = KbT[:].rearrange("p a (b c) -> p (a b) c", c=512)
        Vaug = sb("Vaug", [128, 16, 260], BF16)
        kiT = sb("kiT", [32, 2048], BF16)
        scores = sb("scores", [128, 2048])
        g16 = scores[0:16, 1536:1664]
        go1 = scores[0:1, 1664:1792]
        cw4 = scores[0:4, 0:1536]
        stc = scores[0:3, 0:1536]
        scoresS = W[:].rearrange("p a b -> p (a b)").bitcast(F32)[:, 0:4224]
        id32 = sb("id32", [128, 128])
        idb = sb("idb", [128, 128], BF16)
        ones32 = sb("ones32", [128, 128])
        onesb = sb("onesb", [128, 128], BF16)
        negI = sb("negI", [128, 128], BF16)
        zerob = sb("zerob", [128, 128], BF16)
        blkP = sb("blkP", [128, 128])
        blkS = sb("blkS", [128, 128])
        triP = sb("triP", [128, 128])
        triS = sb("triS", [128, 128])
        lsP = sb("lsP", [128, 128])
        lsS = sb("lsS", [128, 128])
        admP = sb("admP", [128, 128])
        nadmP = sb("nadmP", [128, 128])
        nadmS = sb("nadmS", [128, 128])
        gcol = sb("gcol", [128, 16])
        gsm = sb("gsm_t", [128, GSM])
        negA = sb("negA", [128, 4])
        gocol = sb("gocol", [128, 1])
        wc = sb("wc", [128, 12, 4])
        xtb = [sb("xt%d" % i, [128, D]) for i in range(2)]
        jk8 = sb("jk8", [128, 2048], mybir.dt.uint8)
        xs = sb("xs", [128, D], BF16)
        hT = sb("hT", [128, 8, 128], BF16)
        rx = sb("rx", [128, 1])
        tm = sb("tm", [128, TMW])
        raw = sb("raw", [128, 12, 131])
        acc = sb("acc", [128, 12, 128])
        vsb = sb("vsb", [128, 4, 128], BF16)
        zas = sb("zas", [128, 4, 128], BF16)
        zs = sb("zs", [128, 512], BF16)
        sqb = sb("sqb", [128, 8, 128], BF16)
        rbc = sb("rbc", [128, 8, 128])
        qn = sb("qn", [128, 4, 128], BF16)
        knT = sb("knT", [128, 4, 128], BF16)
        qiT = sb("qiT", [32, 8, 128], BF16)
        gt = sb("gt", [128, 28])
        egr = sb("egr", [128, 4, 128])
        DT = sb("DT", [128, 4, 128])
        Xm = sb("Xm", [128, 4, 128])
        Ym = sb("Ym", [128, 4, 128])
        Dg = Xm
        Dst = Ym
        Pm = sb("Pm", [128, 4, 128])
        Tt = sb("Tt", [128, 4, 128], BF16)
        bv = sb("bv", [128, 4, 128], BF16)
        kbg = sb("kbg", [128, 4, 128], BF16)
        kd = sb("kd", [128, 4, 128], BF16)
        nwk = sb("nwk", [128, 4, 128], BF16)
        qd = sb("qd", [128, 4, 128], BF16)
        qkT = sb("qkT", [128, 4, 128], BF16)
        usb = sb("usb", [128, 4, 128], BF16)
        S = sb("S", [128, 4, 128])
        Sb = sb("Sb", [128, 4, 128], BF16)
        mixT = sb("mixT", [128, 8, 128], BF16)
        mixBC = sb("mixBC", [128, 512], BF16)
        s5 = sb("s5", [128, 16])
        rbcf = rbc[:].rearrange("p a b -> p (a b)")
        sq = rbcf
        accf = acc[:].rearrange("p a b -> p (a b)")
        sqB = sb("sqB", [128, 512])
        nrm = sb("nrm", [128, 768], BF16)
        knf = sb("knf", [128, 256])
        kxn = sb("kxn", [128, 32])
        kxb = sb("kxb", [128, 32], BF16)
        QbT = sb("QbT", [128, 2, 128], BF16)
        QmT = sb("QmT", [128, 2, 128], BF16)
        MkT = sb("MkT", [128, 2, 256], BF16)
        MvA = sb("MvA", [128, 2, 260], BF16)
        wdiag = sb("wdiag", [128, 8, 128], BF16)
        wabs = sb("wabs", [128, 8])
        wsgn = sb("wsgn", [128, 8])
        rl = [sb("rl%d" % i, [128, 512]) for i in range(2)]
        kvc = rl
        up32 = rl[0][:, 0:128]
        lo32 = rl[0][:, 128:256]
        PT = [sb("PT%d" % i, [128, 4, 128], BF16) for i in range(2)]
        nmk = [sb("nmk%d" % i, [128, 128], BF16) for i in range(2)]
        bis = sb("bis", [128, 8])
        wtab = sb("wtab", [128, NBIS + 1])
        p2 = sb("p2", [128, NBIS + 1])
        rcp = sb("rcp", [128, 8])
        ob = rbcf[:, 768:1024]
        cv = accf
        kvb = sb("kvb", [128, 256], BF16)
        kic = sb("kic", [128, 32])
        yo = accf[:, 0:D]

        pb = [ps("pb%d" % i, [128, 512]) for i in range(8)]


        def aff(out, cmp, mult, pat):
            P.add('pool', lambda e: e.affine_select(out, out, [[pat, 128]], cmp, 0.0, base=0,
                                                    channel_multiplier=mult), [out], [out])
        P.memset(id32[:], 0.0)
        P.add('pool', lambda e: e.affine_select(id32[:], id32[:], [[-1, 128]], ALU.not_equal, 1.0,
                                                base=0, channel_multiplier=1), [id32[:]], [id32[:]])
        P.copy(idb[:], id32[:])
        P.ts(negI[:], id32[:], -30000.0, ALU.mult)
        P.memset(zerob[:], 0.0)
        for k_ in range(NBIS + 1):
            P.memset(p2[:, k_:k_ + 1], 2.0 ** -(k_ + 1), eng='pool')
        P.memset(ones32[:], 1.0)
        P.memset(onesb[:], 1.0)
        P.memset(up32, 1.0)
        aff(up32, ALU.is_ge, -1, 1)
        P.memset(lo32, 1.0)
        aff(lo32, ALU.is_gt, 1, -1)
        P.memset(blkP[:], 0.0)
        P.memset(blkP[0:64, 0:64], 1.0)
        P.memset(blkP[64:128, 64:128], 1.0)
        P.memset(blkS[:], 0.0)
        P.memset(blkS[0:16, 0:16], 1.0)
        P.tt(triP[:], up32, blkP[:], ALU.mult)
        P.tt(triS[:], up32, blkS[:], ALU.mult)
        P.tt(lsP[:], lo32, blkP[:], ALU.mult)
        P.tt(lsS[:], lo32, blkS[:], ALU.mult)
        P.memset(admP[:], 1.0)
        P.memset(admP[0:64, 64:128], 0.0)
        P.ts(nadmP[:], admP[:], -1.0, ALU.add, 1.0e30, ALU.mult)
        P.memset(nadmS[:], NEGBIG)
        P.memset(nadmS[:, 0:16], 0.0)

        P.dma(g16, g_all)
        P.dma(gsm[:], gsm_d.partition_broadcast(128))
        P.dma(go1, g_o_d)
        P.dma(cw4, convw_d)
        P.matmul(pb[0][:, 0:16], g16, id32[0:16, 0:16])
        P.copy(gcol[:], pb[0][:, 0:16])
        P.matmul(pb[0][:, 16:17], go1, id32[0:1, 0:1])
        P.copy(gocol[:], pb[0][:, 16:17])
        for c in range(12):
            P.matmul(pb[1][:, c * 4:(c + 1) * 4], cw4[0:4, c * 128:(c + 1) * 128], id32[0:4, 0:4])
        P.copy(wc[:], pb[1][:, 0:48].rearrange("p (c j) -> p c j", j=4))
        P.act(negA[:], gsm[:, G_ALOG:G_ALOG + 4], AF.Exp)
        P.ts(negA[:], negA[:], -1.0, ALU.mult)

        def cast_w(dst, src, gc, ncols):
            a = ncols * 9 // 25
            b = ncols * 17 // 25
            if gc is None:
                P.copy(dst[:, 0:a], src[:, 0:a])
                P.copy(dst[:, a:b], src[:, a:b], eng='act')
                P.copy(dst[:, b:ncols], src[:, b:ncols], eng='pool')
            else:
                P.ts(dst[:, 0:a], src[:, 0:a], gc, ALU.mult)
                P.act(dst[:, a:b], src[:, a:b], AF.Copy, scale=gc)
                P.ts(dst[:, b:ncols], src[:, b:ncols], gc, ALU.mult, 1.0, ALU.mult, eng='pool')

        CH = IN_W // 3
        stg3 = sb("stg3", [128, IN_W // 3])
        slots = [scores[:, 0:CH], accf[:, 0:CH], stg3[:, 0:CH]]
        k = 0
        for kt in range(8):
            for hf in range(3):
                s_ = slots[k % 3]
                k += 1
                P.dma(s_, w_in[kt * 128:(kt + 1) * 128, hf * CH:(hf + 1) * CH])
                cast_w(W[:, kt, hf * CH:(hf + 1) * CH], s_, gcol[:, kt:kt + 1], CH)
        for kt in range(8):
            s_ = slots[k % 3][:, 0:512]
            k += 1
            P.dma(s_, w_mem[kt * 128:(kt + 1) * 128, :])
            cast_w(Wm[:, kt, :], s_, gcol[:, 8 + kt:9 + kt], 512)
        for kt in range(8):
            s_ = slots[k % 3][:, 0:1024]
            k += 1
            P.dma(s_, w_out[kt * 128:(kt + 1) * 128, :])
            cast_w(Wo[:, kt, :], s_, None, 1024)

        def load_h(x_src, nrows, xt, banks):
            if nrows < 128:
                P.memset(xt[:], 0.0)
            P.dma(xt[0:nrows, :], x_src)
            P.add('dve', lambda e: e.scalar_tensor_tensor(yo, xt[:], 1.0, xt[:], ALU.mult, ALU.mult,
                                                          accum_out=rx[:]), [xt[:]], [yo, rx[:]])
            P.ts(rx[:], rx[:], 1.0 / D, ALU.mult, EPS, ALU.add)
            P.act(rx[:], rx[:], AF.Ln)
            P.act(rx[:], rx[:], AF.Exp, scale=-0.5)
            P.ts(xs[:], xt[:], rx[:], ALU.mult)
            for half in range(2):
                for j in range(4):
                    kt = half * 4 + j
                    P.matmul(banks[half][:, j * 128:(j + 1) * 128], xs[:, kt * 128:(kt + 1) * 128], idb[:])
            P.copy(hT[:, 0:4, :], banks[0][:].rearrange("p (a b) -> p a b", a=4), eng='act')
            P.copy(hT[:, 4:8, :], banks[1][:].rearrange("p (a b) -> p a b", a=4), eng='act')

        def rstd_inplace(ap, scale, eps=EPS, post=None):
            P.ts(ap, ap, scale, ALU.mult, eps, ALU.add)
            P.act(ap, ap, AF.Ln)
            if post is None:
                P.act(ap, ap, AF.Exp, scale=-0.5)
            else:
                P.act(ap, ap, AF.Exp, scale=-0.5, bias=post)

        def transpose_bf(dst, src_list, pbank):
            for j, s_ in enumerate(src_list):
                P.matmul(pbank[:, j * 128:(j + 1) * 128], s_, idb[:])
            n = len(src_list)
            P.copy(dst, pbank[:, 0:n * 128].rearrange("p (a b) -> p a b", a=n), eng='act')

        def mem_tile(mt):
            r = slice(mt * 128, (mt + 1) * 128)
            load_h(mem[r, :], 128, xtb[0], (pb[0], pb[1]))
            for kt in range(8):
                P.matmul(pb[2][:], hT[:, kt, :], Wm[:, kt, :], start=(kt == 0), stop=(kt == 7))
            KV = kvc[mt]
            P.copy(KV[:], pb[2][:], eng='act')
            P.tt(sq[:, 0:256], KV[:, 0:256], KV[:, 0:256], ALU.mult)
            P.reduce(s5[:, 0:4], sq[:, 0:256].rearrange("p (h d) -> p h d", h=4), ALU.add)
            rstd_inplace(s5[:, 0:4], 1.0 / 64)
            k3 = KV[:, 0:256].rearrange("p (h d) -> p h d", h=4)
            n3 = knf[:].rearrange("p (h d) -> p h d", h=4)
            P.tt(n3, k3, s5[:, 0:4].unsqueeze(2).to_broadcast([128, 4, 64]), ALU.mult)
            P.tt(n3, n3, gsm[:, G_KM:G_KM + 64].unsqueeze(1).to_broadcast([128, 4, 64]), ALU.mult)
            P.dma(p_mk[r, :], knf[:], eng='pool')
            P.dma(p_mv[r, :], KV[:, 256:512], eng='pool')
            P.copy(kvb[:], knf[:])
            transpose_bf(MkT[:, :, r], [kvb[:, 0:128], kvb[:, 128:256]], pb[3])
            P.memset(MvA[:, mt, :].rearrange("p (h e) -> p h e", e=65)[:, :, 64:65], 1.0)
            P.copy(MvA[:, mt, :].rearrange("p (h e) -> p h e", e=65)[:, :, 0:64],
                   KV[:, 256:512].rearrange("p (h d) -> p h d", h=4), eng='pool')

        def layer_common(x_src, nrows, xt, preloaded=False):
            if not preloaded:
                load_h(x_src, nrows, xt, (pb[0], pb[1]))
            for c in range(16):
                bank = pb[2 + c // 4]
                for kt in range(8):
                    P.matmul(bank[:, (c % 4) * 128:(c % 4 + 1) * 128], W[:, kt, c * 128:(c + 1) * 128], hT[:, kt, :],
                             start=(kt == 0), stop=(kt == 7))
            for c in range(3):
                P.copy(raw[:, c * 4:(c + 1) * 4, 3:131], pb[2 + c][:].rearrange("p (a b) -> p a b", a=4), eng='act')
            for h in range(8):
                bank = pb[6 + h // 4]
                for kt in range(8):
                    P.matmul(bank[0:32, (h % 4) * 128:(h % 4 + 1) * 128], W[:, kt, C_QI + h * 32:C_QI + (h + 1) * 32],
                             hT[:, kt, :], start=(kt == 0), stop=(kt == 7))
            P.copy(qiT[:, 0:4, :], pb[6][0:32, :].rearrange("p (a b) -> p a b", a=4), eng='act')
            P.copy(qiT[:, 4:8, :], pb[7][0:32, :].rearrange("p (a b) -> p a b", a=4), eng='act')
            ck(10)
            P.copy(zas[:], pb[5][:].rearrange("p (a b) -> p a b", a=4), eng='act')
            for ci, c0 in enumerate(range(0, TMW, 512)):
                wd = min(512, TMW - c0)
                bank = pb[ci % 2]
                for kt in range(8):
                    P.matmul(bank[:, 0:wd], hT[:, kt, :], W[:, kt, C_TM + c0:C_TM + c0 + wd], start=(kt == 0), stop=(kt == 7))
                P.copy(tm[:, c0:c0 + wd], bank[:, 0:wd], eng='act')
            ck(12)

        def chainA(sample, tail=None):
            tri, ls = (triS, lsS) if sample else (triP, lsP)
            beta, gg, gc, nbeta, egc, ekd, c1 = (gt[:, 0:4], gt[:, 4:8], gt[:, 8:12], gt[:, 12:16],
                                                 gt[:, 16:20], gt[:, 20:24], gt[:, 24:28])

            def a1():
                for c in range(12):
                    P.ts(acc[:, c, :], raw[:, c, 0:128], wc[:, c, 0:1], ALU.mult)
                    for j in range(1, 4):
                        P.stt(acc[:, c, :], raw[:, c, j:j + 128], wc[:, c, j:j + 1], acc[:, c, :], ALU.mult, ALU.add)
                    if c % 3 == 2:
                        yield
                P.copy(raw[:, :, 0:3], raw[:, :, 128:131], eng='pool')
                P.act(acc[:, 0:8, :], acc[:, 0:8, :], AF.Silu)
                P.act(vsb[:], acc[:, 8:12, :], AF.Silu)
                P.act(zas[:], zas[:], AF.Silu)
                P.act(zs[:, 0:256], tm[:, T_ZB:T_ZB + 256], AF.Silu)
                P.act(zs[:, 256:512], tm[:, T_ZM:T_ZM + 256], AF.Silu)
                yield
                P.tt(sqb[:], acc[:, 0:8, :], acc[:, 0:8, :], ALU.mult, eng='pool')
                for g_ in range(2):
                    P.matmul(pb[2 + g_][:], onesb[:], sqb[:, g_ * 4:(g_ + 1) * 4, :].rearrange("p a b -> p (a b)"))
                P.act(rbc[:, 0:4, :], pb[2][:].rearrange("p (a b) -> p a b", a=4), AF.Ln, bias=EPS)
                P.act(rbc[:, 4:8, :], pb[3][:].rearrange("p (a b) -> p a b", a=4), AF.Ln, bias=EPS)
                P.act(rbc[:, 0:4, :], rbc[:, 0:4, :], AF.Exp, scale=-0.5, bias=float(np.log(128.0 ** -0.5)))
                P.act(rbc[:, 4:8, :], rbc[:, 4:8, :], AF.Exp, scale=-0.5)
                P.tt(qn[:], acc[:, 0:4, :], rbc[:, 0:4, :], ALU.mult)
                P.tt(knT[:], acc[:, 4:8, :], rbc[:, 4:8, :], ALU.mult)
                if kstop == 13:
                    dbg(raw[:, 4:8, 3:131], 512, stage=None) if False else None
                    dbg(acc[:, 4:8, :].rearrange("p a b -> p (a b)"), 512)
                    dbg(rbc[:, 4:8, :].rearrange("p a b -> p (a b)"), 512)
                    dbg(acc[:, 0:4, :].rearrange("p a b -> p (a b)"), 512)
                    dbg(rbc[:, 0:4, :].rearrange("p a b -> p (a b)"), 512)
                    dbg(knT[:].rearrange("p a b -> p (a b)"), 512, stage=tm)
                    dbg(qn[:].rearrange("p a b -> p (a b)"), 512, stage=tm)
                ck(13)
                yield

                for h in range(4):
                    P.matmul(pb[2][:, h * 128:(h + 1) * 128], knT[:, h, :], idb[:])
                    P.matmul(pb[3][:, h * 128:(h + 1) * 128], vsb[:, h, :], idb[:])

            def a2():
                P.act(beta, tm[:, T_BA:T_BA + 4], AF.Exp, scale=-1.0)
                P.ts(beta, beta, 1.0, ALU.add)
                P.recip(beta, beta)
                P.ts(nbeta, beta, -1.0, ALU.mult)
                P.tt(gg, tm[:, T_AA:T_AA + 4], gsm[:, G_DTB:G_DTB + 4], ALU.add)
                P.act(gg, gg, AF.Exp)
                P.act(gg, gg, AF.Ln, bias=1.0)
                P.tt(gg, gg, negA[:], ALU.mult)
                ck(131)
                yield
                blk = blkS if sample else blkP
                P.matmul(pb[4][:, 0:4], tri[:], gg)
                P.matmul(pb[4][:, 4:8], blk[:], gg)
                P.copy(gc, pb[4][:, 0:4])
                P.tt(ekd, pb[4][:, 4:8], gc, ALU.subtract)
                P.act(ekd, ekd, AF.Exp)
                P.act(egc, gc, AF.Exp)
                P.tt(c1, beta, egc, ALU.mult)
                ck(132)
                yield
                for h in range(4):
                    P.ts(Dg[:, h, :], tri[:], gg[:, h:h + 1], ALU.mult)
                    P.matmul(pb[5][:, h * 128:(h + 1) * 128], ones32[:], Dg[:, h, :])
                gcr = pb[5][:].rearrange("p (a b) -> p a b", a=4)
                ck(133)
                yield
                P.act(egr[:], gcr, AF.Exp)
                ck(134)
                yield
                P.copy(DT[:], gcr, eng='act')
                for h in range(4):
                    P.ts(Dst[:, h, :], DT[:, h, :], gc[:, h:h + 1], ALU.subtract, 0.0, ALU.max)
                    P.ts(DT[:, h, :], DT[:, h, :], gc[:, h:h + 1], ALU.subtract, 0.0, ALU.min)
                ck(135)
                yield
                P.act(Dst[:], Dst[:], AF.Exp, scale=-1.0)
                P.act(DT[:], DT[:], AF.Exp)
                ck(136)
                yield
                P.tt(Dst[:], Dst[:], ls[:].unsqueeze(1).to_broadcast([128, 4, 128]), ALU.mult)
                P.tt(DT[:], DT[:], tri[:].unsqueeze(1).to_broadcast([128, 4, 128]), ALU.mult)
                ck(14)
                yield


            def mix(g1, g2):
                d1 = d2 = False
                while not (d1 and d2):
                    if not d1:
                        try:
                            next(g1)
                            yield
                        except StopIteration:
                            d1 = True
                    if not d2:
                        try:
                            next(g2)
                            yield
                        except StopIteration:
                            d2 = True
            yield from mix(a1(), a2())
            ktm = pb[2][:].rearrange("p (a b) -> p a b", a=4)
            vtm = pb[3][:].rearrange("p (a b) -> p a b", a=4)
            P.tt(bv[:], vtm, beta.unsqueeze(2).to_broadcast([128, 4, 128]), ALU.mult)
            P.tt(kbg[:], ktm, c1.unsqueeze(2).to_broadcast([128, 4, 128]), ALU.mult)
            P.tt(kd[:], ktm, ekd.unsqueeze(2).to_broadcast([128, 4, 128]), ALU.mult)
            ck(15)
            yield

            for h in range(4):
                P.matmul(pb[2][:, h * 128:(h + 1) * 128], knT[:, h, :], knT[:, h, :])
            for h in range(4):
                P.ts(Dst[:, h, :], Dst[:, h, :], nbeta[:, h:h + 1], ALU.mult)
            P.tt(Xm[:], pb[2][:].rearrange("p (a b) -> p a b", a=4), Dst[:], ALU.mult)
            for h in range(4):
                P.matmul(pb[3][:, h * 128:(h + 1) * 128], Xm[:, h, :], id32[:])
            P.copy(Ym[:], pb[3][:].rearrange("p (a b) -> p a b", a=4), eng='act')
            P.tt(Pm[:], Ym[:], id32[:].unsqueeze(1).to_broadcast([128, 4, 128]), ALU.add)
            if kstop == 16:
                dbg(Xm[:].rearrange("p a b -> p (a b)"), 512)
                dbg(Ym[:].rearrange("p a b -> p (a b)"), 512)
                dbg(Pm[:].rearrange("p a b -> p (a b)"), 512)
                dbg(DT[:].rearrange("p a b -> p (a b)"), 512)
                dbg(gt[:, 0:12], 12)
                dbg(knT[:].rearrange("p a b -> p (a b)"), 512, stage=tm)
            ck(16)
            yield
            nlev = 3 if sample else 5
            for lv in range(1, nlev + 1):
                for h in range(4):
                    P.matmul(pb[2][:, h * 128:(h + 1) * 128], Ym[:, h, :], Xm[:, h, :])
                if lv < nlev:
                    for h in range(4):
                        P.matmul(pb[3][:, h * 128:(h + 1) * 128], Xm[:, h, :], Ym[:, h, :])
                P.copy(Xm[:], pb[2][:].rearrange("p (a b) -> p a b", a=4), eng='act')
                if lv < nlev:
                    P.copy(Ym[:], pb[3][:].rearrange("p (a b) -> p a b", a=4), eng='act')
                for h in range(4):
                    P.matmul(pb[4][:, h * 128:(h + 1) * 128], Xm[:, h, :], Pm[:, h, :])
                P.tt(Pm[:], Pm[:], pb[4][:].rearrange("p (a b) -> p a b", a=4), ALU.add)
                yield
            P.copy(Tt[:], Pm[:], eng='act')
            ck(17)
            yield

            for h in range(4):
                P.matmul(pb[2][:, h * 128:(h + 1) * 128], kbg[:, h, :], Tt[:, h, :])
            P.act(nwk[:], pb[2][:].rearrange("p (a b) -> p a b", a=4), AF.Copy, scale=-1.0)
            P.tt(qd[:], qn[:], egr[:], ALU.mult)
            for h in range(4):
                P.matmul(pb[3][:, h * 128:(h + 1) * 128], knT[:, h, :], qn[:, h, :])
            P.tt(qkT[:], pb[3][:].rearrange("p (a b) -> p a b", a=4), DT[:], ALU.mult)
            ck(18)
            yield

            chunks = [(0, 16)] if sample else [(0, 64), (64, 128)]
            obank = pb[4]
            for (r0, r1) in chunks:
                for h in range(4):
                    P.matmul(pb[5][:, h * 128:(h + 1) * 128], Tt[:, h, :], bv[:, h, :], start=True, stop=False)
                    P.matmul(pb[5][:, h * 128:(h + 1) * 128], nwk[:, h, :], Sb[:, h, :], start=False, stop=True)
                P.copy(usb[r0:r1, :, :], pb[5][r0:r1, :].rearrange("p (a b) -> p a b", a=4), eng='act')
                for h in range(4):
                    oc = obank[:, h * 128 + r0:h * 128 + r1]
                    P.matmul(oc, Sb[:, h, :], qd[:, h, r0:r1], start=True, stop=False)
                    P.matmul(oc, usb[r0:r1, h, :], qkT[r0:r1, h, r0:r1], start=False, stop=True)
                for h in range(4):
                    P.matmul(pb[2][:, h * 128:(h + 1) * 128], kd[r0:r1, h, :], usb[r0:r1, h, :])
                for h in range(4):
                    P.ts(S[:, h, :], S[:, h, :], egr[:, h, r1 - 1:r1], ALU.mult)
                P.tt(S[:], S[:], pb[2][:].rearrange("p (a b) -> p a b", a=4), ALU.add)
                P.copy(Sb[:], S[:], eng='act')
                yield
            o3 = obank[:].rearrange("p (a b) -> p a b", a=4)
            ck(19)
            yield
            P.act(sqb[:, 0:4, :], o3, AF.Square)
            P.matmul(pb[3][:], onesb[:], sqb[:, 0:4, :].rearrange("p a b -> p (a b)"))
            P.act(rbc[:, 0:4, :], pb[3][:].rearrange("p (a b) -> p a b", a=4), AF.Ln, scale=1.0 / 128, bias=EPS)
            P.act(rbc[:, 0:4, :], rbc[:, 0:4, :], AF.Exp, scale=-0.5)
            P.tt(rbc[:, 0:4, :], rbc[:, 0:4, :], o3, ALU.mult)
            P.stt(mixT[:, 0:4, :], rbc[:, 0:4, :], gocol[:, 0:1], zas[:], ALU.mult, ALU.mult)
            ck(20)
            yield
            if tail is not None:
                tail()
                yield


        def chainB_heads(nrows, outs):
            P.tt(sqB[:, 0:512], tm[:, T_QB:T_QB + 512], tm[:, T_QB:T_QB + 512], ALU.mult)
            P.reduce(s5[:, 0:8], sqB[:, 0:512].rearrange("p (h d) -> p h d", d=64), ALU.add)
            P.tt(sqB[:, 0:256], tm[:, T_QM:T_QM + 256], tm[:, T_QM:T_QM + 256], ALU.mult)
            P.reduce(s5[:, 8:12], sqB[:, 0:256].rearrange("p (h d) -> p h d", d=64), ALU.add)
            P.tt(sqB[:, 256:288], tm[:, T_KI:T_KI + 32], tm[:, T_KI:T_KI + 32], ALU.mult)
            P.reduce(s5[:, 12:13], sqB[:, 256:288], ALU.add)
            yield
            P.ts(s5[:, 12:13], s5[:, 12:13], 2.0, ALU.mult)
            rstd_inplace(s5[:, 0:13], 1.0 / 64)
            k3 = tm[:, T_KB:T_KB + 256].rearrange("p (h d) -> p h d", h=4)
            n3 = knf[:].rearrange("p (h d) -> p h d", h=4)
            P.tt(n3, k3, s5[:, 4:8].unsqueeze(2).to_broadcast([128, 4, 64]), ALU.mult)
            P.tt(n3, n3, gsm[:, G_KB:G_KB + 64].unsqueeze(1).to_broadcast([128, 4, 64]), ALU.mult)
            P.dma(outs["k"], knf[0:nrows, :], eng='pool')
            P.dma(outs["v"], tm[0:nrows, T_VB:T_VB + 256], eng='pool')
            P.copy(nrm[:, 256:512], knf[:], eng='pool')
            q3 = tm[:, T_QB:T_QB + 256].rearrange("p (h d) -> p h d", h=4)
            yield
            m3 = sqB[:, 0:256].rearrange("p (h d) -> p h d", h=4)
            P.tt(m3, q3, s5[:, 0:4].unsqueeze(2).to_broadcast([128, 4, 64]), ALU.mult)
            P.tt(nrm[:, 0:256].rearrange("p (h d) -> p h d", h=4), m3,
                 gsm[:, G_QB:G_QB + 64].unsqueeze(1).to_broadcast([128, 4, 64]), ALU.mult)
            q3 = tm[:, T_QM:T_QM + 256].rearrange("p (h d) -> p h d", h=4)
            m3 = sqB[:, 256:512].rearrange("p (h d) -> p h d", h=4)
            P.tt(m3, q3, s5[:, 8:12].unsqueeze(2).to_broadcast([128, 4, 64]), ALU.mult)
            P.tt(nrm[:, 512:768].rearrange("p (h d) -> p h d", h=4), m3,
                 gsm[:, G_QM:G_QM + 64].unsqueeze(1).to_broadcast([128, 4, 64]), ALU.mult)
            P.ts(kxn[:], tm[:, T_KI:T_KI + 32], s5[:, 12:13], ALU.mult)
            P.tt(kxn[:], kxn[:], gsm[:, G_KI:G_KI + 32], ALU.mult)
            P.dma(outs["ki"], kxn[0:nrows, :], eng='pool')
            P.copy(kxb[:], kxn[:], eng='pool')
            kslot = outs["kslot"]
            kc = slice(kslot * 128, (kslot + 1) * 128)
            yield
            transpose_bf(QbT[:], [nrm[:, 0:128], nrm[:, 128:256]], pb[0])
            transpose_bf(KbT[:, :, kc], [nrm[:, 256:384], nrm[:, 384:512]], pb[1])
            yield
            transpose_bf(QmT[:], [nrm[:, 512:640], nrm[:, 640:768]], pb[6])
            P.matmul(pb[7][0:32, 0:128], kxb[:], idb[:])
            P.copy(kiT[:, kc], pb[7][0:32, 0:128])
            yield
            va = Vaug[:, kslot, :].rearrange("p (h e) -> p h e", e=65)
            P.memset(va[:, :, 64:65], 1.0)
            P.copy(va[:, :, 0:64], tm[:, T_VB:T_VB + 256].rearrange("p (h d) -> p h d", h=4), eng='pool')
            P.ts(wabs[:], tm[:, T_WI:T_WI + 8], IDX_SCALE, ALU.mult)
            for h in range(8):
                P.ts(wdiag[:, h, :], id32[:], wabs[:, h:h + 1], ALU.mult, eng=('dve' if h % 2 == 0 else 'pool'))
            yield

        def conv_out(nrows, dst):
            for c in range(3):
                pc = pb[c % 2]
                for kt in range(8):
                    P.matmul(pc[:], hT[:, kt, :], W[:, kt, c * 512:(c + 1) * 512], start=(kt == 0), stop=(kt == 7))
                P.copy(cv[:, c * 512:(c + 1) * 512], pc[:], eng=('act' if c % 2 == 0 else 'dve'))
            P.dma(dst, cv[nrows - 3:nrows, :], eng='pool')

        def index_scores(sc, col0, ktiles, kbase, alt=False):
            for g0 in range(0, ktiles, 4):
                nk = min(4, ktiles - g0) * 128
                kcs = slice((kbase + g0) * 128, (kbase + g0) * 128 + nk)
                dst = sc[:, col0 + g0 * 128:col0 + g0 * 128 + nk]

                def logits(h):
                    P.matmul(pb[h % 2][:, 0:nk], qiT[:, h, :], kiT[:, kcs])

                def relu_sum(h):
                    r_ = rl[h % 2][:].bitcast(BF16)
                    if alt and h % 2 == 1:
                        P.ts(r_[:, 0:nk], pb[h % 2][:, 0:nk], 0.0, ALU.max)
                    else:
                        P.act(r_[:, 0:nk], pb[h % 2][:, 0:nk], AF.Relu)
                    return r_
                logits(0)
                for h in range(8):
                    r_ = relu_sum(h)
                    if h + 1 < 8:
                        logits(h + 1)
                    P.matmul(pb[6][:, 0:nk], wdiag[:, h, :], r_[:, 0:nk], start=(h == 0), stop=(h == 7))
                    if h % 2 == 1:
                        yield
                P.copy(dst, pb[6][:, 0:nk], eng='act')
                yield

        def bisect(sc, ncols, lo_cols):
            thr, w0, t_, cnt, hh = bis[:, 0:1], bis[:, 1:2], bis[:, 2:3], bis[:, 3:4], bis[:, 4:5]
            P.reduce(w0, sc[:, 0:ncols], ALU.max)
            P.reduce(thr, sc[:, 0:lo_cols], ALU.min)
            P.ts(thr, thr, -1.0, ALU.add)
            P.tt(w0, w0, thr, ALU.subtract)
            P.ts(wtab[:], p2[:], w0, ALU.mult)
            P.tt(t_, thr, wtab[:, 0:1], ALU.add)
            for k in range(NBIS):
                jk = jk8
                for ci, c0 in enumerate(range(0, ncols, 2048)):
                    wd = min(2048, ncols - c0)
                    P.ts(jk[:, 0:wd], sc[:, c0:c0 + wd], t_, ALU.is_gt, (None if ci == 0 else cnt), ALU.add,
                         accum_out=cnt)
                P.ts(hh, cnt, 255.5, ALU.is_ge, 0.5, ALU.subtract)
                P.stt(t_, hh, wtab[:, k:k + 1], t_, ALU.mult, ALU.add)
                yield
            P.tt(thr, t_, wtab[:, NBIS:NBIS + 1], ALU.subtract)

        def attend(qT, keys, obank_, first, last, mask_cols=None, sc=None):
            n = len(keys)

            def scores_t(i):
                kT, va = keys[i]
                bank = pb[i % 2]
                nm = None
                if mask_cols is not None:
                    nm = nmk[i % 2]
                    P.ts(nm[:], sc[:, mask_cols[i]:mask_cols[i] + 128], bis[:, 0:1], ALU.is_le)
                for h in range(4):
                    pr = slice((h % 2) * 64, (h % 2) * 64 + 64)
                    oc = bank[:, h * 128:(h + 1) * 128]
                    P.matmul(oc, kT[pr, h // 2, :], qT[pr, h // 2, :], start=True, stop=False)
                    P.matmul(oc, (nm[:] if nm is not None else zerob[:]), negI[:], start=False, stop=True)

            scores_t(0)
            for i in range(n):
                kT, va = keys[i]
                pt = PT[i % 2]
                P.act(pt[:], pb[i % 2][:].rearrange("p (a b) -> p a b", a=4), AF.Exp, scale=0.125)
                if i + 1 < n:
                    scores_t(i + 1)
                for h in range(4):
                    P.matmul(obank_[:, h * 65:(h + 1) * 65], pt[:, h, :], va[:, h, :],
                             start=(first and i == 0 and h == 0), stop=(last and i == n - 1 and h == 3))
                yield

        def finish_heads(obank_, zcols, dst_cols):
            P.copy(sqB[:, 0:260], obank_[:, 0:260], eng='act')
            o3 = sqB[:, 0:260].rearrange("p (h e) -> p h e", e=65)
            P.copy(rcp[:, 0:4], o3[:, :, 64])
            P.recip(rcp[:, 0:4], rcp[:, 0:4])
            P.tt(o3[:, :, 0:64], o3[:, :, 0:64], rcp[:, 0:4].unsqueeze(2).to_broadcast([128, 4, 64]), ALU.mult)
            P.tt(mixBC[:, dst_cols:dst_cols + 256].rearrange("p (h d) -> p h d", h=4), o3[:, :, 0:64],
                 zs[:, zcols:zcols + 256].rearrange("p (h d) -> p h d", h=4), ALU.mult)
            yield

        def mix_bc_T(bank):
            transpose_bf(mixT[:, 4:8, :], [mixBC[:, j * 128:(j + 1) * 128] for j in range(4)], bank)

        def out_proj(nrows, y_dst, xt):
            for c in range(2):
                bank = pb[c]
                for kt in range(8):
                    P.matmul(bank[:], mixT[:, kt, :], Wo[:, kt, c * 512:(c + 1) * 512], start=(kt == 0), stop=(kt == 7))
                P.tt(yo[:, c * 512:(c + 1) * 512], bank[:], xt[:, c * 512:(c + 1) * 512], ALU.add)
            P.dma(y_dst, yo[0:nrows, :], eng='pool')

        def dbg(ap2d, ncols, stage=None):
            if not kstop:
                return
            i = dbg_n[0]
            dbg_n[0] += 1
            if stage is not None:
                P.copy(stage[:, 0:ncols], ap2d)
                P.dma(dbg_outs[i][:, 0:ncols], stage[:, 0:ncols], eng='pool')
            else:
                P.dma(dbg_outs[i][:, 0:ncols], ap2d, eng='pool')

        def drain(g):
            for _ in g:
                pass

        def count_steps(mk):
            P.dry = True
            n = 0
            for _ in mk():
                n += 1
            P.dry = False
            return n + 1

        def interleave(mka, mkb):
            na, nb = count_steps(mka), count_steps(mkb)
            ga, gb = mka(), mkb()
            ia = ib = 0
            da = db = False
            while not (da and db):
                pick_a = (not da) and (db or ia * nb <= ib * na)
                if pick_a:
                    try:
                        next(ga)
                        ia += 1
                    except StopIteration:
                        da = True
                else:
                    try:
                        next(gb)
                        ib += 1
                    except StopIteration:
                        db = True

        def prompt_chainB(tt, outs):
            yield from chainB_heads(128, outs)
            nk = tt + 1
            yield from index_scores(scores, 0, nk, 0)
            dcol = tt * 128
            P.stt(scores[:, dcol:dcol + 128], scores[:, dcol:dcol + 128], 1.0, admP[:], ALU.mult, ALU.mult)
            P.tt(scores[:, dcol:dcol + 128], scores[:, dcol:dcol + 128], nadmP[:], ALU.add)
            if tt >= 2:
                yield from bisect(scores, nk * 128, (nk - 1) * 128)
            else:
                P.memset(bis[:, 0:1], -1.0e29)
            keys = [(KbT[:, :, j * 128:(j + 1) * 128], Vaug[:, j, :].rearrange("p (h e) -> p h e", e=65)) for j in range(nk)]
            yield from attend(QbT, keys, pb[6], True, True, mask_cols=[j * 128 for j in range(nk)], sc=scores)
            yield from finish_heads(pb[6], 0, 0)
            mkeys = [(MkT[:, :, j * 128:(j + 1) * 128], MvA[:, j, :].rearrange("p (h e) -> p h e", e=65)) for j in range(2)]
            yield from attend(QmT, mkeys, pb[7], True, True)
            yield from finish_heads(pb[7], 256, 256)
            mix_bc_T(pb[0])
            yield

        try:
            for mt in range(2):
                mem_tile(mt)
            ck(2)
            P.memset(raw[:, :, 0:3], 0.0)
            P.memset(S[:], 0.0)
            P.memset(Sb[:], 0.0)
            for tt in range(NT):
                r = slice(tt * 128, (tt + 1) * 128)
                outs = {"k": p_k[r, :], "v": p_v[r, :], "ki": p_ki[r, :], "kslot": tt}
                xt = xtb[tt % 2]
                layer_common(x_p[r, :], 128, xt, preloaded=(tt > 0 and not kstop))
                tail = None
                if tt + 1 < NT and not kstop:
                    r2 = slice((tt + 1) * 128, (tt + 2) * 128)
                    tail = (lambda r2=r2, tt=tt: load_h(x_p[r2, :], 128, xtb[(tt + 1) % 2], (pb[2], pb[3])))
                if kstop:
                    drain(chainA(False))
                    drain(prompt_chainB(tt, outs))
                else:
                    interleave(lambda tail=tail: chainA(False, tail), lambda tt=tt, outs=outs: prompt_chainB(tt, outs))
                if tt == NT - 1:
                    conv_out(128, p_conv)
                out_proj(128, y_p[r, :], xt)
                ck(25)
            P.dma(p_ssm.rearrange("h k v -> k h v"), S[:], eng='pool')
            ck(30)

            P.dma(S[:], st_ssm.rearrange("h k v -> k h v"))
            P.copy(Sb[:], S[:])
            P.dma(stc, st_conv)
            for c in range(12):
                P.matmul(pb[2][:, c * 3:(c + 1) * 3], stc[0:3, c * 128:(c + 1) * 128], id32[0:3, 0:3])
            P.copy(raw[:, :, 0:3], pb[2][:, 0:36].rearrange("p (c j) -> p c j", j=3))
            for mt in range(2):
                r = slice(mt * 128, (mt + 1) * 128)
                KV = kvc[mt]
                P.dma(KV[:, 0:256], c_mk[r, :])
                P.dma(KV[:, 256:512], c_mv[r, :])
                P.copy(kvb[:], KV[:, 0:256])
                transpose_bf(MkT[:, :, r], [kvb[:, 0:128], kvb[:, 128:256]], pb[3])
                P.copy(MvA[:, mt, :].rearrange("p (h e) -> p h e", e=65)[:, :, 0:64],
                       KV[:, 256:512].rearrange("p (h d) -> p h d", h=4), eng='pool')
            outs = {"k": s_k, "v": s_v, "ki": s_ki, "kslot": 0}
            layer_common(x_s, 16, xtb[0])
            drain(chainA(True))
            drain(chainB_heads(16, outs))
            conv_out(16, s_conv)
            P.dma(s_ssm.rearrange("h k v -> k h v"), S[:], eng='pool')
            ck(31)
            Wf = W[:].rearrange("p a b -> p (a b)").bitcast(F32)
            Wb = W[:].rearrange("p a b -> p (a b)")
            kiS = Wf[:, 4224:5248].rearrange("p (t c) -> p t c", c=32)
            KS = Wf[:, 5248:9344].rearrange("p (t c) -> p t c", c=256)
            VS = Wf[:, 9344:13440].rearrange("p (t c) -> p t c", c=256)
            Kb16 = Wb[:, 26880:30976].rearrange("p (t c) -> p t c", c=256)
            kib = Wb[:, 18688:19712].rearrange("p (t c) -> p t c", c=32)
            def dma_tiles(dst, src_rows, ntile, step):
                v = src_rows.rearrange("(t p) c -> p t c", p=128)
                for t0 in range(0, ntile, step):
                    P.dma(dst[:, t0:t0 + step, :], v[:, t0:t0 + step, :])
            dma_tiles(kiS, c_ki, 32, 8)
            dma_tiles(KS, c_k[0:2048, :], 16, 8)
            drain(index_scores(scoresS, PAST, 1, 0, alt=True))
            P.tt(scoresS[:, PAST:PAST + 128], scoresS[:, PAST:PAST + 128], nadmS[:], ALU.add)
            P.copy(kib, kiS)
            for g_ in range(2):
                for q4 in range(4):
                    bank = pb[2 + q4]
                    for j in range(4):
                        t_ = g_ * 16 + q4 * 4 + j
                        P.matmul(bank[0:32, j * 128:(j + 1) * 128], kib[:, t_, :], idb[:])
                    P.copy(kiT[:, q4 * 512:(q4 + 1) * 512], bank[0:32, :], eng='act')
                drain(index_scores(scoresS, g_ * 2048, 16, 0, alt=True))
            dma_tiles(VS, c_v[0:2048, :], 16, 8)
            drain(bisect(scoresS, PAST + 128, PAST))
            ck(32)
            keys = [(KbT[:, :, 0:128], Vaug[:, 0, :].rearrange("p (h e) -> p h e", e=65))]
            drain(attend(QbT, keys, pb[6], True, False, mask_cols=[PAST], sc=scoresS))
            for g_ in range(2):
                if g_ == 1:
                    dma_tiles(KS, c_k[2048:4096, :], 16, 8)
                    dma_tiles(VS, c_v[2048:4096, :], 16, 8)
                P.copy(Kb16[:, 0:6, :], KS[:, 0:6, :])
                P.copy(Kb16[:, 6:11, :], KS[:, 6:11, :], eng='act')
                P.copy(Kb16[:, 11:16, :], KS[:, 11:16, :], eng='pool')
                for q2 in range(8):
                    bank = pb[2 + q2 % 4]
                    for pr_ in range(2):
                        for j in range(2):
                            t_ = q2 * 2 + j
                            P.matmul(bank[:, (pr_ * 2 + j) * 128:(pr_ * 2 + j + 1) * 128],
                                     Kb16[:, t_, pr_ * 128:(pr_ + 1) * 128], idb[:])
                    P.copy(KbT[:, :, q2 * 256:(q2 + 1) * 256], bank[:].rearrange("p (a b) -> p a b", a=2), eng='act')
                v4 = Vaug[:, 0:16, :].rearrange("p t (h e) -> p t h e", e=65)
                for t4 in range(4):
                    P.copy(v4[:, t4 * 4:(t4 + 1) * 4, :, 0:64].rearrange("p t h d -> p (t h) d"),
                           VS[:, t4 * 4:(t4 + 1) * 4, :].rearrange("p t (h d) -> p (t h) d", d=64),
                           eng=('pool' if t4 % 2 == 0 else 'dve'))
                keys = [(KbT[:, :, j * 128:(j + 1) * 128], Vaug[:, j, :].rearrange("p (h e) -> p h e", e=65)) for j in range(16)]
                drain(attend(QbT, keys, pb[6], False, g_ == 1, mask_cols=[g_ * 2048 + j * 128 for j in range(16)], sc=scoresS))
            drain(finish_heads(pb[6], 0, 0))
            mkeys = [(MkT[:, :, j * 128:(j + 1) * 128], MvA[:, j, :].rearrange("p (h e) -> p h e", e=65)) for j in range(2)]
            drain(attend(QmT, mkeys, pb[7], True, True))
            drain(finish_heads(pb[7], 256, 256))
            mix_bc_T(pb[0])
            out_proj(16, y_s, xtb[0])
        except _Stop:
            pass
        P.emit(final_wait_engine='pool')
    return nc


_CACHE = {}


def _make_in_maps(inp):
    g_all = np.ascontiguousarray(np.concatenate([inp['g_in'][0].reshape(8, 128), inp['g_mem'][0].reshape(8, 128)], 0))
    gsm = np.ascontiguousarray(np.concatenate([
        inp['g_k_B'][0], inp['g_kidx_B'][0], inp['g_k_M'][0], inp['g_q_B'][0], inp['g_q_M'][0],
        inp['dt_bias_A'][0], inp['a_log_A'][0]]).reshape(1, GSM).astype(np.float32))
    maps = []
    for b in range(8):
        maps.append({
            "x_p": np.ascontiguousarray(inp['x_prompt'][b]),
            "x_s": np.ascontiguousarray(inp['x_sample'][b]),
            "mem": np.ascontiguousarray(inp['mem_prompt'][b]),
            "w_in": np.ascontiguousarray(inp['w_in'][0]),
            "w_mem": np.ascontiguousarray(inp['w_mem_kv'][0]),
            "w_out": np.ascontiguousarray(inp['w_out'][0]),
            "g_all": g_all,
            "gsm": gsm,
            "g_o": np.ascontiguousarray(inp['g_o_A'][0].reshape(1, 128)),
            "conv_w": np.ascontiguousarray(inp['conv_w_A'][0]),
            "st_conv": np.ascontiguousarray(inp['state_conv_A'][0, b]),
            "st_ssm": np.ascontiguousarray(inp['state_ssm_A'][0, b]),
            "c_k": np.ascontiguousarray(inp['cache_k_B'][0, b].reshape(PAST, 256)),
            "c_v": np.ascontiguousarray(inp['cache_v_B'][0, b].reshape(PAST, 256)),
            "c_ki": np.ascontiguousarray(inp['cache_kidx_B'][0, b]),
            "c_mk": np.ascontiguousarray(inp['cache_mem_k'][0, b].reshape(256, 256)),
            "c_mv": np.ascontiguousarray(inp['cache_mem_v'][0, b].reshape(256, 256)),
        })
    return maps


def _assemble(res):
    def stack(name, shape):
        return np.stack([np.asarray(r[name], dtype=np.float32).reshape(shape) for r in res], 0)
    outs = (
        stack("y_p", (SEQ, D)), stack("y_s", (16, D)),
        stack("p_conv", (3, 1536))[None],
        stack("p_ssm", (4, 128, 128))[None],
        stack("p_k", (SEQ, 4, 64))[None],
        stack("p_v", (SEQ, 4, 64))[None],
        stack("p_ki", (SEQ, 32))[None],
        stack("p_mk", (256, 4, 64))[None],
        stack("p_mv", (256, 4, 64))[None],
        stack("s_conv", (3, 1536))[None],
        stack("s_ssm", (4, 128, 128))[None],
        stack("s_k", (16, 4, 64))[None],
        stack("s_v", (16, 4, 64))[None],
        stack("s_ki", (16, 32))[None],
    )
    return outs


def kernel(**inputs):
    inp = {k: np.asarray(v) for k, v in inputs.items()}
    nc = build_program()
    in_maps = _make_in_maps(inp)
    res = run_bass_kernel_spmd(nc, in_maps, core_ids=list(range(8)))
    return _assemble(res.results)
```

```python
import numpy as np
import concourse.bass as bass
import concourse.mybir as mybir
from concourse.bass_utils import run_bass_kernel_spmd

F32 = mybir.dt.float32
BF16 = mybir.dt.bfloat16
ALU = mybir.AluOpType
AF = mybir.ActivationFunctionType
AX = mybir.AxisListType


def _region(ap):
    t = ap.tensor
    name = t.name
    esz = mybir.dt.size(ap.dtype)
    pat = tuple((st_ * esz, c_) for st_, c_ in ap.ap)
    off = ap.offset * esz
    space = str(ap.space)
    if 'DRAM' in space.upper() or 'Dram' in space or 'dram' in space:
        lo = off
        hi = off + sum((c - 1) * abs(s) for s, c in pat) + esz
        return (name, 0, 1, lo, hi)
    ps, pc = pat[0]
    if ps == 0:
        ps = 1 << 30
    p0 = off // ps
    fo = off % ps
    hi = fo + sum((c - 1) * abs(s) for s, c in pat[1:]) + esz
    if 'PSUM' in space.upper():
        return (name, (p0 // 32) * 32, ((p0 + pc + 31) // 32) * 32, 0, 1 << 30)
    return (name, p0, p0 + pc, fo, hi)


def _overlap(a, b):
    return a[1] < b[2] and b[1] < a[2] and a[3] < b[4] and b[3] < a[4]


def _contains(a, b):
    return a[1] <= b[1] and a[2] >= b[2] and a[3] <= b[3] and a[4] >= b[4]


class Op:
    __slots__ = ('eng', 'fn', 'reads', 'writes', 'dma', 'deps', 'signals', 'sem', 'val', 'idx', 'pe_acc')

    def __init__(self, eng, fn, reads, writes, dma=False):
        self.eng = eng
        self.fn = fn
        self.reads = reads
        self.writes = writes
        self.dma = dma
        self.deps = set()
        self.signals = False
        self.sem = None
        self.val = 0


class Prog:
    ENGS = ('pe', 'act', 'dve', 'pool', 'sp')

    def __init__(self, nc, n_dma_sems=12, same_engine_sync=True):
        self.nc = nc
        self.ops = []
        self.n_dma_sems = n_dma_sems
        self.same_engine_sync = same_engine_sync
        self.alias = {}

    def set_alias(self, name, group):
        self.alias[name] = group

    def add(self, eng, fn, reads, writes, dma=False):
        if getattr(self, 'dry', False):
            return None
        rr = [_region(a) for a in reads if a is not None and not isinstance(a, (int, float))]
        ww = [_region(a) for a in writes if a is not None]
        op = Op(eng, fn, rr, ww, dma)
        op.idx = len(self.ops)
        self.ops.append(op)
        return op

    def resolve(self):
        writers = {}
        readers = {}
        for op in self.ops:
            for r in op.reads:
                key = self.alias.get(r[0], r[0])
                full = key != r[0]
                for (wr, wi) in writers.get(key, ()):
                    if full or _overlap(r, wr):
                        op.deps.add(wi)
            for w in op.writes:
                key = self.alias.get(w[0], w[0])
                full = key != w[0]
                for (wr, wi) in writers.get(key, ()):
                    if full or _overlap(w, wr):
                        op.deps.add(wi)
                for (rr, ri) in readers.get(key, ()):
                    if full or _overlap(w, rr):
                        op.deps.add(ri)
            for w in op.writes:
                key = self.alias.get(w[0], w[0])
                full = key != w[0]
                if full:
                    writers[key] = [(w, op.idx)]
                    readers[key] = []
                else:
                    writers[key] = [(wr, wi) for (wr, wi) in writers.get(key, ()) if not _contains(w, wr)] + [(w, op.idx)]
                    readers[key] = [(rr, ri) for (rr, ri) in readers.get(key, ()) if not _contains(w, rr)]
            for r in op.reads:
                key = self.alias.get(r[0], r[0])
                lst = readers.setdefault(key, [])
                if not op.dma:
                    lst[:] = [(rr, ri) for (rr, ri) in lst
                              if not (rr == r and self.ops[ri].eng == op.eng and not self.ops[ri].dma)]
                lst.append((r, op.idx))
            op.deps.discard(op.idx)
        for op in self.ops:
            keep = set()
            for d in op.deps:
                o2 = self.ops[d]
                if not o2.dma and not op.dma and o2.eng == op.eng:
                    if op.eng == 'pe':
                        continue
                    if not self.same_engine_sync:
                        continue
                keep.add(d)
            op.deps = keep
            for d in keep:
                self.ops[d].signals = True
        for op in self.ops:
            if op.dma:
                op.signals = True

    def emit(self, final_wait_engine='sp'):
        nc = self.nc
        self.resolve()
        import contextlib
        with contextlib.ExitStack() as st:
            esem = {e: st.enter_context(nc.semaphore('sem_' + e)) for e in ('pe', 'act', 'dve', 'pool')}
            dsem = [st.enter_context(nc.semaphore('dsem%d' % i)) for i in range(self.n_dma_sems)]
            ecount = {e: 0 for e in esem}
            dcount = [0] * self.n_dma_sems
            half = self.n_dma_sems // 2
            kq = {'sp': 0, 'pool': 0, 'act': 0}
            for op in self.ops:
                if not op.signals:
                    continue
                if op.dma:
                    if op.eng == 'pool':
                        i = half + kq['pool'] % (self.n_dma_sems - half)
                        kq['pool'] += 1
                    else:
                        i = kq['sp'] % half
                        kq['sp'] += 1
                    dcount[i] += 16
                    op.sem = ('d', i)
                    op.val = dcount[i]
                else:
                    ecount[op.eng] += 1
                    op.sem = ('e', op.eng)
                    op.val = ecount[op.eng]
            final = {}
            for op in self.ops:
                if op.dma:
                    final[op.sem] = max(final.get(op.sem, 0), op.val)

            def semh(s):
                return esem[s[1]] if s[0] == 'e' else dsem[s[1]]

            per_eng = {e: [o for o in self.ops if o.eng == e] for e in self.ENGS}
            block = st.enter_context(nc.Block())

            def run(engname, eng):
                known = {}
                for op in per_eng[engname]:
                    need = {}
                    for d in op.deps:
                        o2 = self.ops[d]
                        need[o2.sem] = max(need.get(o2.sem, 0), o2.val)
                    if op.dma and op.val > 16:
                        need[op.sem] = max(need.get(op.sem, 0), op.val - 16)
                    for s, v in need.items():
                        if known.get(s, 0) >= v:
                            continue
                        eng.wait_ge(semh(s), v)
                        known[s] = v
                    ins = op.fn(eng)
                    if op.signals:
                        ins.then_inc(semh(op.sem), 16 if op.dma else 1)
                if engname == final_wait_engine:
                    for s, v in final.items():
                        if known.get(s, 0) < v:
                            eng.wait_ge(semh(s), v)

            @block.tensor
            def _(e):
                run('pe', e)

            @block.scalar
            def _(e):
                run('act', e)

            @block.vector
            def _(e):
                run('dve', e)

            @block.gpsimd
            def _(e):
                run('pool', e)

            @block.sync
            def _(e):
                run('sp', e)

    def dma(self, out, in_, eng='sp'):
        return self.add(eng, lambda e: e.dma_start(out=out, in_=in_), [in_], [out], dma=True)

    def matmul(self, out, lhsT, rhs, start=True, stop=True):
        rd = [lhsT, rhs] + ([] if start else [out])
        return self.add('pe', lambda e: e.matmul(out, lhsT, rhs, start=start, stop=stop), rd, [out])

    def transpose(self, out, in_, ident):
        return self.add('pe', lambda e: e.transpose(out, in_, ident), [in_, ident], [out])

    def act(self, out, in_, func, bias=None, scale=None, accum_out=None, eng='act'):
        kw = {}
        rd = [in_]
        if bias is not None:
            kw['bias'] = bias
            rd.append(bias)
        if scale is not None:
            kw['scale'] = scale
            rd.append(scale)
        wr = [out]
        if accum_out is not None:
            kw['accum_out'] = accum_out
            wr.append(accum_out)
        return self.add('act', lambda e: e.activation(out, in_, func, **kw), rd, wr)

    def tt(self, out, in0, in1, op, eng='dve'):
        return self.add(eng, lambda e: e.tensor_tensor(out, in0, in1, op), [in0, in1], [out])

    def ts(self, out, in0, s1, op0, s2=None, op1=None, accum_out=None, eng='dve'):
        rd = [in0, s1, s2]
        wr = [out, accum_out]

        def fn(e):
            kw = {}
            if accum_out is not None:
                kw['accum_out'] = accum_out
            if op1 is not None:
                return e.tensor_scalar(out, in0, s1, s2, op0, op1, **kw)
            return e.tensor_scalar(out, in0, s1, None, op0, **kw)
        return self.add(eng, fn, rd, wr)

    def stt(self, out, in0, scalar, in1, op0, op1, eng='dve'):
        return self.add(eng, lambda e: e.scalar_tensor_tensor(out, in0, scalar, in1, op0, op1),
                        [in0, scalar, in1], [out])

    def copy(self, out, in_, eng='dve'):
        if eng == 'act':
            return self.add('act', lambda e: e.activation(out, in_, AF.Copy), [in_], [out])
        return self.add(eng, lambda e: e.tensor_copy(out, in_), [in_], [out])

    def memset(self, ap, val, eng='dve'):
        return self.add(eng, lambda e: e.memset(ap, val), [], [ap])

    def reduce(self, out, in_, op, axis=AX.X, eng='dve'):
        return self.add(eng, lambda e: e.tensor_reduce(out, in_, axis, op), [in_], [out])

    def recip(self, out, in_):
        return self.add('dve', lambda e: e.reciprocal(out, in_), [in_], [out])


D = 1024
IN_W = 3888
SEQ = 2048
NT = SEQ // 128
PAST = 4096
EPS = 1e-6
C_ZA = 1536
C_TM = 2048
T_BA, T_AA, T_QB, T_KB, T_VB, T_ZB, T_KI, T_WI, T_QM, T_ZM = 0, 4, 8, 264, 520, 776, 1288, 1320, 1328, 1584
C_QI = 3080
TMW = IN_W - C_TM
IDX_SCALE = (8 ** -0.5) * (32 ** -0.5)
NBIS = 17
NEGBIG = -1.0e30
G_KB, G_KI, G_KM, G_QB, G_QM, G_DTB, G_ALOG = 0, 64, 96, 160, 224, 288, 292
GSM = 296


class _Stop(Exception):
    pass


def build_program(kstop=0):
    import contextlib

    def ck(n):
        if kstop == n:
            raise _Stop()
    nc = bass.Bass("TRN2", target_bir_lowering=False)

    def din(name, shape):
        return nc.dram_tensor(name, shape, F32, kind="ExternalInput").ap()

    def dout(name, shape):
        return nc.dram_tensor(name, shape, F32, kind="ExternalOutput").ap()

    x_p = din("x_p", [SEQ, D])
    x_s = din("x_s", [16, D])
    mem = din("mem", [256, D])
    w_in = din("w_in", [D, IN_W])
    w_mem = din("w_mem", [D, 512])
    w_out = din("w_out", [D, D])
    g_all = din("g_all", [16, 128])
    gsm_d = din("gsm", [1, GSM])
    g_o_d = din("g_o", [1, 128])
    convw_d = din("conv_w", [4, 1536])
    st_conv = din("st_conv", [3, 1536])
    st_ssm = din("st_ssm", [4, 128, 128])
    c_k = din("c_k", [PAST, 256])
    c_v = din("c_v", [PAST, 256])
    c_ki = din("c_ki", [PAST, 32])
    c_mk = din("c_mk", [256, 256])
    c_mv = din("c_mv", [256, 256])

    y_p = dout("y_p", [SEQ, D])
    y_s = dout("y_s", [16, D])
    p_conv = dout("p_conv", [3, 1536])
    p_ssm = dout("p_ssm", [4, 128, 128])
    p_k = dout("p_k", [SEQ, 256])
    p_v = dout("p_v", [SEQ, 256])
    p_ki = dout("p_ki", [SEQ, 32])
    p_mk = dout("p_mk", [256, 256])
    p_mv = dout("p_mv", [256, 256])
    s_conv = dout("s_conv", [3, 1536])
    s_ssm = dout("s_ssm", [4, 128, 128])
    s_k = dout("s_k", [16, 256])
    s_v = dout("s_v", [16, 256])
    s_ki = dout("s_ki", [16, 32])

    dbg_outs = [dout("dbg%d" % i, [128, 1024]) for i in range(6)] if kstop else []
    dbg_n = [0]

    with contextlib.ExitStack() as st:
        def sb(name, shape, dt=F32):
            return st.enter_context(nc.sbuf_tensor(name, shape, dt))

        def ps(name, shape, dt=F32):
            return st.enter_context(nc.psum_tensor(name, shape, dt))

        P = Prog(nc, n_dma_sems=16)

        W = sb("W", [128, 8, IN_W], BF16)
        Wo = sb("Wo", [128, 8, D], BF16)
        KbT = sb("KbT", [128, 2, 2048], BF16)
        Wm = KbT[:].rearrange("p a (b c) -> p (a b) c", c=512)
        Vaug = sb("Vaug", [128, 16, 260], BF16)
        kiT = sb("kiT", [32, 2048], BF16)
        scores = sb("scores", [128, 2048])
        g16 = scores[0:16, 1536:1664]
        go1 = scores[0:1, 1664:1792]
        cw4 = scores[0:4, 0:1536]
        stc = scores[0:3, 0:1536]
        scoresS = W[:].rearrange("p a b -> p (a b)").bitcast(F32)[:, 0:4224]
        id32 = sb("id32", [128, 128])
        idb = sb("idb", [128, 128], BF16)
        ones32 = sb("ones32", [128, 128])
        onesb = sb("onesb", [128, 128], BF16)
        negI = sb("negI", [128, 128], BF16)
        zerob = sb("zerob", [128, 128], BF16)
        blkP = sb("blkP", [128, 128])
        blkS = sb("blkS", [128, 128])
        triP = sb("triP", [128, 128])
        triS = sb("triS", [128, 128])
        lsP = sb("lsP", [128, 128])
        lsS = sb("lsS", [128, 128])
        admP = sb("admP", [128, 128])
        nadmP = sb("nadmP", [128, 128])
        nadmS = sb("nadmS", [128, 128])
        gcol = sb("gcol", [128, 16])
        gsm = sb("gsm_t", [128, GSM])
        negA = sb("negA", [128, 4])
        gocol = sb("gocol", [128, 1])
        wc = sb("wc", [128, 12, 4])
        xtb = [sb("xt%d" % i, [128, D]) for i in range(2)]
        jk8 = sb("jk8", [128, 2048], mybir.dt.uint8)
        xs = sb("xs", [128, D], BF16)
        hT = sb("hT", [128, 8, 128], BF16)
        rx = sb("rx", [128, 1])
        tm = sb("tm", [128, TMW])
        raw = sb("raw", [128, 12, 131])
        acc = sb("acc", [128, 12, 128])
        vsb = sb("vsb", [128, 4, 128], BF16)
        zas = sb("zas", [128, 4, 128], BF16)
        zs = sb("zs", [128, 512], BF16)
        sqb = sb("sqb", [128, 8, 128], BF16)
        rbc = sb("rbc", [128, 8, 128])
        qn = sb("qn", [128, 4, 128], BF16)
        knT = sb("knT", [128, 4, 128], BF16)
        qiT = sb("qiT", [32, 8, 128], BF16)
        gt = sb("gt", [128, 28])
        egr = sb("egr", [128, 4, 128])
        DT = sb("DT", [128, 4, 128])
        Xm = sb("Xm", [128, 4, 128])
        Ym = sb("Ym", [128, 4, 128])
        Dg = Xm
        Dst = Ym
        Pm = sb("Pm", [128, 4, 128])
        Tt = sb("Tt", [128, 4, 128], BF16)
        bv = sb("bv", [128, 4, 128], BF16)
        kbg = sb("kbg", [128, 4, 128], BF16)
        kd = sb("kd", [128, 4, 128], BF16)
        nwk = sb("nwk", [128, 4, 128], BF16)
        qd = sb("qd", [128, 4, 128], BF16)
        qkT = sb("qkT", [128, 4, 128], BF16)
        usb = sb("usb", [128, 4, 128], BF16)
        S = sb("S", [128, 4, 128])
        Sb = sb("Sb", [128, 4, 128], BF16)
        mixT = sb("mixT", [128, 8, 128], BF16)
        mixBC = sb("mixBC", [128, 512], BF16)
        s5 = sb("s5", [128, 16])
        rbcf = rbc[:].rearrange("p a b -> p (a b)")
        sq = rbcf
        accf = acc[:].rearrange("p a b -> p (a b)")
        sqB = sb("sqB", [128, 512])
        nrm = sb("nrm", [128, 768], BF16)
        knf = sb("knf", [128, 256])
        kxn = sb("kxn", [128, 32])
        kxb = sb("kxb", [128, 32], BF16)
        QbT = sb("QbT", [128, 2, 128], BF16)
        QmT = sb("QmT", [128, 2, 128], BF16)
        MkT = sb("MkT", [128, 2, 256], BF16)
        MvA = sb("MvA", [128, 2, 260], BF16)
        wdiag = sb("wdiag", [128, 8, 128], BF16)
        wabs = sb("wabs", [128, 8])
        wsgn = sb("wsgn", [128, 8])
        rl = [sb("rl%d" % i, [128, 512]) for i in range(2)]
        kvc = rl
        up32 = rl[0][:, 0:128]
        lo32 = rl[0][:, 128:256]
        PT = [sb("PT%d" % i, [128, 4, 128], BF16) for i in range(2)]
        nmk = [sb("nmk%d" % i, [128, 128], BF16) for i in range(2)]
        bis = sb("bis", [128, 8])
        wtab = sb("wtab", [128, NBIS + 1])
        p2 = sb("p2", [128, NBIS + 1])
        rcp = sb("rcp", [128, 8])
        ob = rbcf[:, 768:1024]
        cv = accf
        kvb = sb("kvb", [128, 256], BF16)
        kic = sb("kic", [128, 32])
        yo = accf[:, 0:D]

        pb = [ps("pb%d" % i, [128, 512]) for i in range(8)]


        def aff(out, cmp, mult, pat):
            P.add('pool', lambda e: e.affine_select(out, out, [[pat, 128]], cmp, 0.0, base=0,
                                                    channel_multiplier=mult), [out], [out])
        P.memset(id32[:], 0.0)
        P.add('pool', lambda e: e.affine_select(id32[:], id32[:], [[-1, 128]], ALU.not_equal, 1.0,
                                                base=0, channel_multiplier=1), [id32[:]], [id32[:]])
        P.copy(idb[:], id32[:])
        P.ts(negI[:], id32[:], -30000.0, ALU.mult)
        P.memset(zerob[:], 0.0)
        for k_ in range(NBIS + 1):
            P.memset(p2[:, k_:k_ + 1], 2.0 ** -(k_ + 1), eng='pool')
        P.memset(ones32[:], 1.0)
        P.memset(onesb[:], 1.0)
        P.memset(up32, 1.0)
        aff(up32, ALU.is_ge, -1, 1)
        P.memset(lo32, 1.0)
        aff(lo32, ALU.is_gt, 1, -1)
        P.memset(blkP[:], 0.0)
        P.memset(blkP[0:64, 0:64], 1.0)
        P.memset(blkP[64:128, 64:128], 1.0)
        P.memset(blkS[:], 0.0)
        P.memset(blkS[0:16, 0:16], 1.0)
        P.tt(triP[:], up32, blkP[:], ALU.mult)
        P.tt(triS[:], up32, blkS[:], ALU.mult)
        P.tt(lsP[:], lo32, blkP[:], ALU.mult)
        P.tt(lsS[:], lo32, blkS[:], ALU.mult)
        P.memset(admP[:], 1.0)
        P.memset(admP[0:64, 64:128], 0.0)
        P.ts(nadmP[:], admP[:], -1.0, ALU.add, 1.0e30, ALU.mult)
        P.memset(nadmS[:], NEGBIG)
        P.memset(nadmS[:, 0:16], 0.0)

        P.dma(g16, g_all)
        P.dma(gsm[:], gsm_d.partition_broadcast(128))
        P.dma(go1, g_o_d)
        P.dma(cw4, convw_d)
        P.matmul(pb[0][:, 0:16], g16, id32[0:16, 0:16])
        P.copy(gcol[:], pb[0][:, 0:16])
        P.matmul(pb[0][:, 16:17], go1, id32[0:1, 0:1])
        P.copy(gocol[:], pb[0][:, 16:17])
        for c in range(12):
            P.matmul(pb[1][:, c * 4:(c + 1) * 4], cw4[0:4, c * 128:(c + 1) * 128], id32[0:4, 0:4])
        P.copy(wc[:], pb[1][:, 0:48].rearrange("p (c j) -> p c j", j=4))
        P.act(negA[:], gsm[:, G_ALOG:G_ALOG + 4], AF.Exp)
        P.ts(negA[:], negA[:], -1.0, ALU.mult)

        def cast_w(dst, src, gc, ncols):
            a = ncols * 9 // 25
            b = ncols * 17 // 25
            if gc is None:
                P.copy(dst[:, 0:a], src[:, 0:a])
                P.copy(dst[:, a:b], src[:, a:b], eng='act')
                P.copy(dst[:, b:ncols], src[:, b:ncols], eng='pool')
            else:
                P.ts(dst[:, 0:a], src[:, 0:a], gc, ALU.mult)
                P.act(dst[:, a:b], src[:, a:b], AF.Copy, scale=gc)
                P.ts(dst[:, b:ncols], src[:, b:ncols], gc, ALU.mult, 1.0, ALU.mult, eng='pool')

        CH = IN_W // 3
        stg3 = sb("stg3", [128, IN_W // 3])
        slots = [scores[:, 0:CH], accf[:, 0:CH], stg3[:, 0:CH]]
        k = 0
        for kt in range(8):
            for hf in range(3):
                s_ = slots[k % 3]
                k += 1
                P.dma(s_, w_in[kt * 128:(kt + 1) * 128, hf * CH:(hf + 1) * CH])
                cast_w(W[:, kt, hf * CH:(hf + 1) * CH], s_, gcol[:, kt:kt + 1], CH)
        for kt in range(8):
            s_ = slots[k % 3][:, 0:512]
            k += 1
            P.dma(s_, w_mem[kt * 128:(kt + 1) * 128, :])
            cast_w(Wm[:, kt, :], s_, gcol[:, 8 + kt:9 + kt], 512)
        for kt in range(8):
            s_ = slots[k % 3][:, 0:1024]
            k += 1
            P.dma(s_, w_out[kt * 128:(kt + 1) * 128, :])
            cast_w(Wo[:, kt, :], s_, None, 1024)

        def load_h(x_src, nrows, xt, banks):
            if nrows < 128:
                P.memset(xt[:], 0.0)
            P.dma(xt[0:nrows, :], x_src)
            P.add('dve', lambda e: e.scalar_tensor_tensor(yo, xt[:], 1.0, xt[:], ALU.mult, ALU.mult,
                                                          accum_out=rx[:]), [xt[:]], [yo, rx[:]])
            P.ts(rx[:], rx[:], 1.0 / D, ALU.mult, EPS, ALU.add)
            P.act(rx[:], rx[:], AF.Ln)
            P.act(rx[:], rx[:], AF.Exp, scale=-0.5)
            P.ts(xs[:], xt[:], rx[:], ALU.mult)
            for half in range(2):
                for j in range(4):
                    kt = half * 4 + j
                    P.matmul(banks[half][:, j * 128:(j + 1) * 128], xs[:, kt * 128:(kt + 1) * 128], idb[:])
            P.copy(hT[:, 0:4, :], banks[0][:].rearrange("p (a b) -> p a b", a=4), eng='act')
            P.copy(hT[:, 4:8, :], banks[1][:].rearrange("p (a b) -> p a b", a=4), eng='act')

        def rstd_inplace(ap, scale, eps=EPS, post=None):
            P.ts(ap, ap, scale, ALU.mult, eps, ALU.add)
            P.act(ap, ap, AF.Ln)
            if post is None:
                P.act(ap, ap, AF.Exp, scale=-0.5)
            else:
                P.act(ap, ap, AF.Exp, scale=-0.5, bias=post)

        def transpose_bf(dst, src_list, pbank):
            for j, s_ in enumerate(src_list):
                P.matmul(pbank[:, j * 128:(j + 1) * 128], s_, idb[:])
            n = len(src_list)
            P.copy(dst, pbank[:, 0:n * 128].rearrange("p (a b) -> p a b", a=n), eng='act')

        def mem_tile(mt):
            r = slice(mt * 128, (mt + 1) * 128)
            load_h(mem[r, :], 128, xtb[0], (pb[0], pb[1]))
            for kt in range(8):
                P.matmul(pb[2][:], hT[:, kt, :], Wm[:, kt, :], start=(kt == 0), stop=(kt == 7))
            KV = kvc[mt]
            P.copy(KV[:], pb[2][:], eng='act')
            P.tt(sq[:, 0:256], KV[:, 0:256], KV[:, 0:256], ALU.mult)
            P.reduce(s5[:, 0:4], sq[:, 0:256].rearrange("p (h d) -> p h d", h=4), ALU.add)
            rstd_inplace(s5[:, 0:4], 1.0 / 64)
            k3 = KV[:, 0:256].rearrange("p (h d) -> p h d", h=4)
            n3 = knf[:].rearrange("p (h d) -> p h d", h=4)
            P.tt(n3, k3, s5[:, 0:4].unsqueeze(2).to_broadcast([128, 4, 64]), ALU.mult)
            P.tt(n3, n3, gsm[:, G_KM:G_KM + 64].unsqueeze(1).to_broadcast([128, 4, 64]), ALU.mult)
            P.dma(p_mk[r, :], knf[:], eng='pool')
            P.dma(p_mv[r, :], KV[:, 256:512], eng='pool')
            P.copy(kvb[:], knf[:])
            transpose_bf(MkT[:, :, r], [kvb[:, 0:128], kvb[:, 128:256]], pb[3])
            P.memset(MvA[:, mt, :].rearrange("p (h e) -> p h e", e=65)[:, :, 64:65], 1.0)
            P.copy(MvA[:, mt, :].rearrange("p (h e) -> p h e", e=65)[:, :, 0:64],
                   KV[:, 256:512].rearrange("p (h d) -> p h d", h=4), eng='pool')

        def layer_common(x_src, nrows, xt, preloaded=False):
            if not preloaded:
                load_h(x_src, nrows, xt, (pb[0], pb[1]))
            for c in range(16):
                bank = pb[2 + c // 4]
                for kt in range(8):
                    P.matmul(bank[:, (c % 4) * 128:(c % 4 + 1) * 128], W[:, kt, c * 128:(c + 1) * 128], hT[:, kt, :],
                             start=(kt == 0), stop=(kt == 7))
            for c in range(3):
                P.copy(raw[:, c * 4:(c + 1) * 4, 3:131], pb[2 + c][:].rearrange("p (a b) -> p a b", a=4), eng='act')
            for h in range(8):
                bank = pb[6 + h // 4]
                for kt in range(8):
                    P.matmul(bank[0:32, (h % 4) * 128:(h % 4 + 1) * 128], W[:, kt, C_QI + h * 32:C_QI + (h + 1) * 32],
                             hT[:, kt, :], start=(kt == 0), stop=(kt == 7))
            P.copy(qiT[:, 0:4, :], pb[6][0:32, :].rearrange("p (a b) -> p a b", a=4), eng='act')
            P.copy(qiT[:, 4:8, :], pb[7][0:32, :].rearrange("p (a b) -> p a b", a=4), eng='act')
            ck(10)
            P.copy(zas[:], pb[5][:].rearrange("p (a b) -> p a b", a=4), eng='act')
            for ci, c0 in enumerate(range(0, TMW, 512)):
                wd = min(512, TMW - c0)
                bank = pb[ci % 2]
                for kt in range(8):
                    P.matmul(bank[:, 0:wd], hT[:, kt, :], W[:, kt, C_TM + c0:C_TM + c0 + wd], start=(kt == 0), stop=(kt == 7))
                P.copy(tm[:, c0:c0 + wd], bank[:, 0:wd], eng='act')
            ck(12)

        def chainA(sample, tail=None):
            tri, ls = (triS, lsS) if sample else (triP, lsP)
            beta, gg, gc, nbeta, egc, ekd, c1 = (gt[:, 0:4], gt[:, 4:8], gt[:, 8:12], gt[:, 12:16],
                                                 gt[:, 16:20], gt[:, 20:24], gt[:, 24:28])

            def a1():
                for c in range(12):
                    P.ts(acc[:, c, :], raw[:, c, 0:128], wc[:, c, 0:1], ALU.mult)
                    for j in range(1, 4):
                        P.stt(acc[:, c, :], raw[:, c, j:j + 128], wc[:, c, j:j + 1], acc[:, c, :], ALU.mult, ALU.add)
                    if c % 3 == 2:
                        yield
                P.copy(raw[:, :, 0:3], raw[:, :, 128:131], eng='pool')
                P.act(acc[:, 0:8, :], acc[:, 0:8, :], AF.Silu)
                P.act(vsb[:], acc[:, 8:12, :], AF.Silu)
                P.act(zas[:], zas[:], AF.Silu)
                P.act(zs[:, 0:256], tm[:, T_ZB:T_ZB + 256], AF.Silu)
                P.act(zs[:, 256:512], tm[:, T_ZM:T_ZM + 256], AF.Silu)
                yield
                P.tt(sqb[:], acc[:, 0:8, :], acc[:, 0:8, :], ALU.mult, eng='pool')
                for g_ in range(2):
                    P.matmul(pb[2 + g_][:], onesb[:], sqb[:, g_ * 4:(g_ + 1) * 4, :].rearrange("p a b -> p (a b)"))
                P.act(rbc[:, 0:4, :], pb[2][:].rearrange("p (a b) -> p a b", a=4), AF.Ln, bias=EPS)
                P.act(rbc[:, 4:8, :], pb[3][:].rearrange("p (a b) -> p a b", a=4), AF.Ln, bias=EPS)
                P.act(rbc[:, 0:4, :], rbc[:, 0:4, :], AF.Exp, scale=-0.5, bias=float(np.log(128.0 ** -0.5)))
                P.act(rbc[:, 4:8, :], rbc[:, 4:8, :], AF.Exp, scale=-0.5)
                P.tt(qn[:], acc[:, 0:4, :], rbc[:, 0:4, :], ALU.mult)
                P.tt(knT[:], acc[:, 4:8, :], rbc[:, 4:8, :], ALU.mult)
                if kstop == 13:
                    dbg(raw[:, 4:8, 3:131], 512, stage=None) if False else None
                    dbg(acc[:, 4:8, :].rearrange("p a b -> p (a b)"), 512)
                    dbg(rbc[:, 4:8, :].rearrange("p a b -> p (a b)"), 512)
                    dbg(acc[:, 0:4, :].rearrange("p a b -> p (a b)"), 512)
                    dbg(rbc[:, 0:4, :].rearrange("p a b -> p (a b)"), 512)
                    dbg(knT[:].rearrange("p a b -> p (a b)"), 512, stage=tm)
                    dbg(qn[:].rearrange("p a b -> p (a b)"), 512, stage=tm)
                ck(13)
                yield

                for h in range(4):
                    P.matmul(pb[2][:, h * 128:(h + 1) * 128], knT[:, h, :], idb[:])
                    P.matmul(pb[3][:, h * 128:(h + 1) * 128], vsb[:, h, :], idb[:])

            def a2():
                P.act(beta, tm[:, T_BA:T_BA + 4], AF.Exp, scale=-1.0)
                P.ts(beta, beta, 1.0, ALU.add)
                P.recip(beta, beta)
                P.ts(nbeta, beta, -1.0, ALU.mult)
                P.tt(gg, tm[:, T_AA:T_AA + 4], gsm[:, G_DTB:G_DTB + 4], ALU.add)
                P.act(gg, gg, AF.Exp)
                P.act(gg, gg, AF.Ln, bias=1.0)
                P.tt(gg, gg, negA[:], ALU.mult)
                ck(131)
                yield
                blk = blkS if sample else blkP
                P.matmul(pb[4][:, 0:4], tri[:], gg)
                P.matmul(pb[4][:, 4:8], blk[:], gg)
                P.copy(gc, pb[4][:, 0:4])
                P.tt(ekd, pb[4][:, 4:8], gc, ALU.subtract)
                P.act(ekd, ekd, AF.Exp)
                P.act(egc, gc, AF.Exp)
                P.tt(c1, beta, egc, ALU.mult)
                ck(132)
                yield
                for h in range(4):
                    P.ts(Dg[:, h, :], tri[:], gg[:, h:h + 1], ALU.mult)
                    P.matmul(pb[5][:, h * 128:(h + 1) * 128], ones32[:], Dg[:, h, :])
                gcr = pb[5][:].rearrange("p (a b) -> p a b", a=4)
                ck(133)
                yield
                P.act(egr[:], gcr, AF.Exp)
                ck(134)
                yield
                P.copy(DT[:], gcr, eng='act')
                for h in range(4):
                    P.ts(Dst[:, h, :], DT[:, h, :], gc[:, h:h + 1], ALU.subtract, 0.0, ALU.max)
                    P.ts(DT[:, h, :], DT[:, h, :], gc[:, h:h + 1], ALU.subtract, 0.0, ALU.min)
                ck(135)
                yield
                P.act(Dst[:], Dst[:], AF.Exp, scale=-1.0)
                P.act(DT[:], DT[:], AF.Exp)
                ck(136)
                yield
                P.tt(Dst[:], Dst[:], ls[:].unsqueeze(1).to_broadcast([128, 4, 128]), ALU.mult)
                P.tt(DT[:], DT[:], tri[:].unsqueeze(1).to_broadcast([128, 4, 128]), ALU.mult)
                ck(14)
                yield


            def mix(g1, g2):
                d1 = d2 = False
                while not (d1 and d2):
                    if not d1:
                        try:
                            next(g1)
                            yield
                        except StopIteration:
                            d1 = True
                    if not d2:
                        try:
                            next(g2)
                            yield
                        except StopIteration:
                            d2 = True
            yield from mix(a1(), a2())
            ktm = pb[2][:].rearrange("p (a b) -> p a b", a=4)
            vtm = pb[3][:].rearrange("p (a b) -> p a b", a=4)
            P.tt(bv[:], vtm, beta.unsqueeze(2).to_broadcast([128, 4, 128]), ALU.mult)
            P.tt(kbg[:], ktm, c1.unsqueeze(2).to_broadcast([128, 4, 128]), ALU.mult)
            P.tt(kd[:], ktm, ekd.unsqueeze(2).to_broadcast([128, 4, 128]), ALU.mult)
            ck(15)
            yield

            for h in range(4):
                P.matmul(pb[2][:, h * 128:(h + 1) * 128], knT[:, h, :], knT[:, h, :])
            for h in range(4):
                P.ts(Dst[:, h, :], Dst[:, h, :], nbeta[:, h:h + 1], ALU.mult)
            P.tt(Xm[:], pb[2][:].rearrange("p (a b) -> p a b", a=4), Dst[:], ALU.mult)
            for h in range(4):
                P.matmul(pb[3][:, h * 128:(h + 1) * 128], Xm[:, h, :], id32[:])
            P.copy(Ym[:], pb[3][:].rearrange("p (a b) -> p a b", a=4), eng='act')
            P.tt(Pm[:], Ym[:], id32[:].unsqueeze(1).to_broadcast([128, 4, 128]), ALU.add)
            if kstop == 16:
                dbg(Xm[:].rearrange("p a b -> p (a b)"), 512)
                dbg(Ym[:].rearrange("p a b -> p (a b)"), 512)
                dbg(Pm[:].rearrange("p a b -> p (a b)"), 512)
                dbg(DT[:].rearrange("p a b -> p (a b)"), 512)
                dbg(gt[:, 0:12], 12)
                dbg(knT[:].rearrange("p a b -> p (a b)"), 512, stage=tm)
            ck(16)
            yield
            nlev = 3 if sample else 5
            for lv in range(1, nlev + 1):
                for h in range(4):
                    P.matmul(pb[2][:, h * 128:(h + 1) * 128], Ym[:, h, :], Xm[:, h, :])
                if lv < nlev:
                    for h in range(4):
                        P.matmul(pb[3][:, h * 128:(h + 1) * 128], Xm[:, h, :], Ym[:, h, :])
                P.copy(Xm[:], pb[2][:].rearrange("p (a b) -> p a b", a=4), eng='act')
                if lv < nlev:
                    P.copy(Ym[:], pb[3][:].rearrange("p (a b) -> p a b", a=4), eng='act')
                for h in range(4):
                    P.matmul(pb[4][:, h * 128:(h + 1) * 128], Xm[:, h, :], Pm[:, h, :])
                P.tt(Pm[:], Pm[:], pb[4][:].rearrange("p (a b) -> p a b", a=4), ALU.add)
                yield
            P.copy(Tt[:], Pm[:], eng='act')
            ck(17)
            yield

            for h in range(4):
                P.matmul(pb[2][:, h * 128:(h + 1) * 128], kbg[:, h, :], Tt[:, h, :])
            P.act(nwk[:], pb[2][:].rearrange("p (a b) -> p a b", a=4), AF.Copy, scale=-1.0)
            P.tt(qd[:], qn[:], egr[:], ALU.mult)
            for h in range(4):
                P.matmul(pb[3][:, h * 128:(h + 1) * 128], knT[:, h, :], qn[:, h, :])
            P.tt(qkT[:], pb[3][:].rearrange("p (a b) -> p a b", a=4), DT[:], ALU.mult)
            ck(18)
            yield

            chunks = [(0, 16)] if sample else [(0, 64), (64, 128)]
            obank = pb[4]
            for (r0, r1) in chunks:
                for h in range(4):
                    P.matmul(pb[5][:, h * 128:(h + 1) * 128], Tt[:, h, :], bv[:, h, :], start=True, stop=False)
                    P.matmul(pb[5][:, h * 128:(h + 1) * 128], nwk[:, h, :], Sb[:, h, :], start=False, stop=True)
                P.copy(usb[r0:r1, :, :], pb[5][r0:r1, :].rearrange("p (a b) -> p a b", a=4), eng='act')
                for h in range(4):
                    oc = obank[:, h * 128 + r0:h * 128 + r1]
                    P.matmul(oc, Sb[:, h, :], qd[:, h, r0:r1], start=True, stop=False)
                    P.matmul(oc, usb[r0:r1, h, :], qkT[r0:r1, h, r0:r1], start=False, stop=True)
                for h in range(4):
                    P.matmul(pb[2][:, h * 128:(h + 1) * 128], kd[r0:r1, h, :], usb[r0:r1, h, :])
                for h in range(4):
                    P.ts(S[:, h, :], S[:, h, :], egr[:, h, r1 - 1:r1], ALU.mult)
                P.tt(S[:], S[:], pb[2][:].rearrange("p (a b) -> p a b", a=4), ALU.add)
                P.copy(Sb[:], S[:], eng='act')
                yield
            o3 = obank[:].rearrange("p (a b) -> p a b", a=4)
            ck(19)
            yield
            P.act(sqb[:, 0:4, :], o3, AF.Square)
            P.matmul(pb[3][:], onesb[:], sqb[:, 0:4, :].rearrange("p a b -> p (a b)"))
            P.act(rbc[:, 0:4, :], pb[3][:].rearrange("p (a b) -> p a b", a=4), AF.Ln, scale=1.0 / 128, bias=EPS)
            P.act(rbc[:, 0:4, :], rbc[:, 0:4, :], AF.Exp, scale=-0.5)
            P.tt(rbc[:, 0:4, :], rbc[:, 0:4, :], o3, ALU.mult)
            P.stt(mixT[:, 0:4, :], rbc[:, 0:4, :], gocol[:, 0:1], zas[:], ALU.mult, ALU.mult)
            ck(20)
            yield
            if tail is not None:
                tail()
                yield


        def chainB_heads(nrows, outs):
            P.tt(sqB[:, 0:512], tm[:, T_QB:T_QB + 512], tm[:, T_QB:T_QB + 512], ALU.mult)
            P.reduce(s5[:, 0:8], sqB[:, 0:512].rearrange("p (h d) -> p h d", d=64), ALU.add)
            P.tt(sqB[:, 0:256], tm[:, T_QM:T_QM + 256], tm[:, T_QM:T_QM + 256], ALU.mult)
            P.reduce(s5[:, 8:12], sqB[:, 0:256].rearrange("p (h d) -> p h d", d=64), ALU.add)
            P.tt(sqB[:, 256:288], tm[:, T_KI:T_KI + 32], tm[:, T_KI:T_KI + 32], ALU.mult)
            P.reduce(s5[:, 12:13], sqB[:, 256:288], ALU.add)
            yield
            P.ts(s5[:, 12:13], s5[:, 12:13], 2.0, ALU.mult)
            rstd_inplace(s5[:, 0:13], 1.0 / 64)
            k3 = tm[:, T_KB:T_KB + 256].rearrange("p (h d) -> p h d", h=4)
            n3 = knf[:].rearrange("p (h d) -> p h d", h=4)
            P.tt(n3, k3, s5[:, 4:8].unsqueeze(2).to_broadcast([128, 4, 64]), ALU.mult)
            P.tt(n3, n3, gsm[:, G_KB:G_KB + 64].unsqueeze(1).to_broadcast([128, 4, 64]), ALU.mult)
            P.dma(outs["k"], knf[0:nrows, :], eng='pool')
            P.dma(outs["v"], tm[0:nrows, T_VB:T_VB + 256], eng='pool')
            P.copy(nrm[:, 256:512], knf[:], eng='pool')
            q3 = tm[:, T_QB:T_QB + 256].rearrange("p (h d) -> p h d", h=4)
            yield
            m3 = sqB[:, 0:256].rearrange("p (h d) -> p h d", h=4)
            P.tt(m3, q3, s5[:, 0:4].unsqueeze(2).to_broadcast([128, 4, 64]), ALU.mult)
            P.tt(nrm[:, 0:256].rearrange("p (h d) -> p h d", h=4), m3,
                 gsm[:, G_QB:G_QB + 64].unsqueeze(1).to_broadcast([128, 4, 64]), ALU.mult)
            q3 = tm[:, T_QM:T_QM + 256].rearrange("p (h d) -> p h d", h=4)
            m3 = sqB[:, 256:512].rearrange("p (h d) -> p h d", h=4)
            P.tt(m3, q3, s5[:, 8:12].unsqueeze(2).to_broadcast([128, 4, 64]), ALU.mult)
            P.tt(nrm[:, 512:768].rearrange("p (h d) -> p h d", h=4), m3,
                 gsm[:, G_QM:G_QM + 64].unsqueeze(1).to_broadcast([128, 4, 64]), ALU.mult)
            P.ts(kxn[:], tm[:, T_KI:T_KI + 32], s5[:, 12:13], ALU.mult)
            P.tt(kxn[:], kxn[:], gsm[:, G_KI:G_KI + 32], ALU.mult)
            P.dma(outs["ki"], kxn[0:nrows, :], eng='pool')
            P.copy(kxb[:], kxn[:], eng='pool')
            kslot = outs["kslot"]
            kc = slice(kslot * 128, (kslot + 1) * 128)
            yield
            transpose_bf(QbT[:], [nrm[:, 0:128], nrm[:, 128:256]], pb[0])
            transpose_bf(KbT[:, :, kc], [nrm[:, 256:384], nrm[:, 384:512]], pb[1])
            yield
            transpose_bf(QmT[:], [nrm[:, 512:640], nrm[:, 640:768]], pb[6])
            P.matmul(pb[7][0:32, 0:128], kxb[:], idb[:])
            P.copy(kiT[:, kc], pb[7][0:32, 0:128])
            yield
            va = Vaug[:, kslot, :].rearrange("p (h e) -> p h e", e=65)
            P.memset(va[:, :, 64:65], 1.0)
            P.copy(va[:, :, 0:64], tm[:, T_VB:T_VB + 256].rearrange("p (h d) -> p h d", h=4), eng='pool')
            P.ts(wabs[:], tm[:, T_WI:T_WI + 8], IDX_SCALE, ALU.mult)
            for h in range(8):
                P.ts(wdiag[:, h, :], id32[:], wabs[:, h:h + 1], ALU.mult, eng=('dve' if h % 2 == 0 else 'pool'))
            yield

        def conv_out(nrows, dst):
            for c in range(3):
                pc = pb[c % 2]
                for kt in range(8):
                    P.matmul(pc[:], hT[:, kt, :], W[:, kt, c * 512:(c + 1) * 512], start=(kt == 0), stop=(kt == 7))
                P.copy(cv[:, c * 512:(c + 1) * 512], pc[:], eng=('act' if c % 2 == 0 else 'dve'))
            P.dma(dst, cv[nrows - 3:nrows, :], eng='pool')

        def index_scores(sc, col0, ktiles, kbase, alt=False):
            for g0 in range(0, ktiles, 4):
                nk = min(4, ktiles - g0) * 128
                kcs = slice((kbase + g0) * 128, (kbase + g0) * 128 + nk)
                dst = sc[:, col0 + g0 * 128:col0 + g0 * 128 + nk]

                def logits(h):
                    P.matmul(pb[h % 2][:, 0:nk], qiT[:, h, :], kiT[:, kcs])

                def relu_sum(h):
                    r_ = rl[h % 2][:].bitcast(BF16)
                    if alt and h % 2 == 1:
                        P.ts(r_[:, 0:nk], pb[h % 2][:, 0:nk], 0.0, ALU.max)
                    else:
                        P.act(r_[:, 0:nk], pb[h % 2][:, 0:nk], AF.Relu)
                    return r_
                logits(0)
                for h in range(8):
                    r_ = relu_sum(h)
                    if h + 1 < 8:
                        logits(h + 1)
                    P.matmul(pb[6][:, 0:nk], wdiag[:, h, :], r_[:, 0:nk], start=(h == 0), stop=(h == 7))
                    if h % 2 == 1:
                        yield
                P.copy(dst, pb[6][:, 0:nk], eng='act')
                yield

        def bisect(sc, ncols, lo_cols):
            thr, w0, t_, cnt, hh = bis[:, 0:1], bis[:, 1:2], bis[:, 2:3], bis[:, 3:4], bis[:, 4:5]
            P.reduce(w0, sc[:, 0:ncols], ALU.max)
            P.reduce(thr, sc[:, 0:lo_cols], ALU.min)
            P.ts(thr, thr, -1.0, ALU.add)
            P.tt(w0, w0, thr, ALU.subtract)
            P.ts(wtab[:], p2[:], w0, ALU.mult)
            P.tt(t_, thr, wtab[:, 0:1], ALU.add)
            for k in range(NBIS):
                jk = jk8
                for ci, c0 in enumerate(range(0, ncols, 2048)):
                    wd = min(2048, ncols - c0)
                    P.ts(jk[:, 0:wd], sc[:, c0:c0 + wd], t_, ALU.is_gt, (None if ci == 0 else cnt), ALU.add,
                         accum_out=cnt)
                P.ts(hh, cnt, 255.5, ALU.is_ge, 0.5, ALU.subtract)
                P.stt(t_, hh, wtab[:, k:k + 1], t_, ALU.mult, ALU.add)
                yield
            P.tt(thr, t_, wtab[:, NBIS:NBIS + 1], ALU.subtract)

        def attend(qT, keys, obank_, first, last, mask_cols=None, sc=None):
            n = len(keys)

            def scores_t(i):
                kT, va = keys[i]
                bank = pb[i % 2]
                nm = None
                if mask_cols is not None:
                    nm = nmk[i % 2]
                    P.ts(nm[:], sc[:, mask_cols[i]:mask_cols[i] + 128], bis[:, 0:1], ALU.is_le)
                for h in range(4):
                    pr = slice((h % 2) * 64, (h % 2) * 64 + 64)
                    oc = bank[:, h * 128:(h + 1) * 128]
                    P.matmul(oc, kT[pr, h // 2, :], qT[pr, h // 2, :], start=True, stop=False)
                    P.matmul(oc, (nm[:] if nm is not None else zerob[:]), negI[:], start=False, stop=True)

            scores_t(0)
            for i in range(n):
                kT, va = keys[i]
                pt = PT[i % 2]
                P.act(pt[:], pb[i % 2][:].rearrange("p (a b) -> p a b", a=4), AF.Exp, scale=0.125)
                if i + 1 < n:
                    scores_t(i + 1)
                for h in range(4):
                    P.matmul(obank_[:, h * 65:(h + 1) * 65], pt[:, h, :], va[:, h, :],
                             start=(first and i == 0 and h == 0), stop=(last and i == n - 1 and h == 3))
                yield

        def finish_heads(obank_, zcols, dst_cols):
            P.copy(sqB[:, 0:260], obank_[:, 0:260], eng='act')
            o3 = sqB[:, 0:260].rearrange("p (h e) -> p h e", e=65)
            P.copy(rcp[:, 0:4], o3[:, :, 64])
            P.recip(rcp[:, 0:4], rcp[:, 0:4])
            P.tt(o3[:, :, 0:64], o3[:, :, 0:64], rcp[:, 0:4].unsqueeze(2).to_broadcast([128, 4, 64]), ALU.mult)
            P.tt(mixBC[:, dst_cols:dst_cols + 256].rearrange("p (h d) -> p h d", h=4), o3[:, :, 0:64],
                 zs[:, zcols:zcols + 256].rearrange("p (h d) -> p h d", h=4), ALU.mult)
            yield

        def mix_bc_T(bank):
            transpose_bf(mixT[:, 4:8, :], [mixBC[:, j * 128:(j + 1) * 128] for j in range(4)], bank)

        def out_proj(nrows, y_dst, xt):
            for c in range(2):
                bank = pb[c]
                for kt in range(8):
                    P.matmul(bank[:], mixT[:, kt, :], Wo[:, kt, c * 512:(c + 1) * 512], start=(kt == 0), stop=(kt == 7))
                P.tt(yo[:, c * 512:(c + 1) * 512], bank[:], xt[:, c * 512:(c + 1) * 512], ALU.add)
            P.dma(y_dst, yo[0:nrows, :], eng='pool')

        def dbg(ap2d, ncols, stage=None):
            if not kstop:
                return
            i = dbg_n[0]
            dbg_n[0] += 1
            if stage is not None:
                P.copy(stage[:, 0:ncols], ap2d)
                P.dma(dbg_outs[i][:, 0:ncols], stage[:, 0:ncols], eng='pool')
            else:
                P.dma(dbg_outs[i][:, 0:ncols], ap2d, eng='pool')

        def drain(g):
            for _ in g:
                pass

        def count_steps(mk):
            P.dry = True
            n = 0
            for _ in mk():
                n += 1
            P.dry = False
            return n + 1

        def interleave(mka, mkb):
            na, nb = count_steps(mka), count_steps(mkb)
            ga, gb = mka(), mkb()
            ia = ib = 0
            da = db = False
            while not (da and db):
                pick_a = (not da) and (db or ia * nb <= ib * na)
                if pick_a:
                    try:
                        next(ga)
                        ia += 1
                    except StopIteration:
                        da = True
                else:
                    try:
                        next(gb)
                        ib += 1
                    except StopIteration:
                        db = True

        def mixg(g1, g2):
            d1 = d2 = False
            while not (d1 and d2):
                if not d1:
                    try:
                        next(g1)
                        yield
                    except StopIteration:
                        d1 = True
                if not d2:
                    try:
                        next(g2)
                        yield
                    except StopIteration:
                        d2 = True

        def prompt_chainB(tt, outs):
            yield from chainB_heads(128, outs)
            nk = tt + 1
            yield from index_scores(scores, 0, nk, 0)
            dcol = tt * 128
            P.stt(scores[:, dcol:dcol + 128], scores[:, dcol:dcol + 128], 1.0, admP[:], ALU.mult, ALU.mult)
            P.tt(scores[:, dcol:dcol + 128], scores[:, dcol:dcol + 128], nadmP[:], ALU.add)
            mkeys = [(MkT[:, :, j * 128:(j + 1) * 128], MvA[:, j, :].rearrange("p (h e) -> p h e", e=65)) for j in range(2)]

            def cpart():
                yield from attend(QmT, mkeys, pb[7], True, True)
                yield from finish_heads(pb[7], 256, 256)
            if tt >= 2:
                yield from mixg(bisect(scores, nk * 128, (nk - 1) * 128), cpart())
            else:
                P.memset(bis[:, 0:1], -1.0e29)
                yield from cpart()
            keys = [(KbT[:, :, j * 128:(j + 1) * 128], Vaug[:, j, :].rearrange("p (h e) -> p h e", e=65)) for j in range(nk)]
            yield from attend(QbT, keys, pb[6], True, True, mask_cols=[j * 128 for j in range(nk)], sc=scores)
            yield from finish_heads(pb[6], 0, 0)
            mix_bc_T(pb[0])
            yield

        try:
            for mt in range(2):
                mem_tile(mt)
            ck(2)
            P.memset(raw[:, :, 0:3], 0.0)
            P.memset(S[:], 0.0)
            P.memset(Sb[:], 0.0)
            for tt in range(NT):
                r = slice(tt * 128, (tt + 1) * 128)
                outs = {"k": p_k[r, :], "v": p_v[r, :], "ki": p_ki[r, :], "kslot": tt}
                xt = xtb[tt % 2]
                layer_common(x_p[r, :], 128, xt, preloaded=(tt > 0 and not kstop))
                tail = None
                if tt + 1 < NT and not kstop:
                    r2 = slice((tt + 1) * 128, (tt + 2) * 128)
                    tail = (lambda r2=r2, tt=tt: load_h(x_p[r2, :], 128, xtb[(tt + 1) % 2], (pb[2], pb[3])))
                if kstop:
                    drain(chainA(False))
                    drain(prompt_chainB(tt, outs))
                else:
                    interleave(lambda tail=tail: chainA(False, tail), lambda tt=tt, outs=outs: prompt_chainB(tt, outs))
                if tt == NT - 1:
                    conv_out(128, p_conv)
                out_proj(128, y_p[r, :], xt)
                ck(25)
            P.dma(p_ssm.rearrange("h k v -> k h v"), S[:], eng='pool')
            ck(30)

            P.dma(S[:], st_ssm.rearrange("h k v -> k h v"))
            P.copy(Sb[:], S[:])
            P.dma(stc, st_conv)
            for c in range(12):
                P.matmul(pb[2][:, c * 3:(c + 1) * 3], stc[0:3, c * 128:(c + 1) * 128], id32[0:3, 0:3])
            P.copy(raw[:, :, 0:3], pb[2][:, 0:36].rearrange("p (c j) -> p c j", j=3))
            for mt in range(2):
                r = slice(mt * 128, (mt + 1) * 128)
                KV = kvc[mt]
                P.dma(KV[:, 0:256], c_mk[r, :])
                P.dma(KV[:, 256:512], c_mv[r, :])
                P.copy(kvb[:], KV[:, 0:256])
                transpose_bf(MkT[:, :, r], [kvb[:, 0:128], kvb[:, 128:256]], pb[3])
                P.copy(MvA[:, mt, :].rearrange("p (h e) -> p h e", e=65)[:, :, 0:64],
                       KV[:, 256:512].rearrange("p (h d) -> p h d", h=4), eng='pool')
            outs = {"k": s_k, "v": s_v, "ki": s_ki, "kslot": 0}
            layer_common(x_s, 16, xtb[0])
            drain(chainA(True))
            drain(chainB_heads(16, outs))
            conv_out(16, s_conv)
            P.dma(s_ssm.rearrange("h k v -> k h v"), S[:], eng='pool')
            ck(31)
            Wf = W[:].rearrange("p a b -> p (a b)").bitcast(F32)
            Wb = W[:].rearrange("p a b -> p (a b)")
            kiS = Wf[:, 4224:5248].rearrange("p (t c) -> p t c", c=32)
            KS = Wf[:, 5248:9344].rearrange("p (t c) -> p t c", c=256)
            VS = Wf[:, 9344:13440].rearrange("p (t c) -> p t c", c=256)
            Kb16 = Wb[:, 26880:30976].rearrange("p (t c) -> p t c", c=256)
            kib = Wb[:, 18688:19712].rearrange("p (t c) -> p t c", c=32)
            def dma_tiles(dst, src_rows, ntile, step):
                v = src_rows.rearrange("(t p) c -> p t c", p=128)
                for t0 in range(0, ntile, step):
                    P.dma(dst[:, t0:t0 + step, :], v[:, t0:t0 + step, :])
            dma_tiles(kiS, c_ki, 32, 8)
            dma_tiles(KS, c_k[0:2048, :], 16, 8)
            drain(index_scores(scoresS, PAST, 1, 0, alt=True))
            P.tt(scoresS[:, PAST:PAST + 128], scoresS[:, PAST:PAST + 128], nadmS[:], ALU.add)
            P.copy(kib, kiS)
            for g_ in range(2):
                for q4 in range(4):
                    bank = pb[2 + q4]
                    for j in range(4):
                        t_ = g_ * 16 + q4 * 4 + j
                        P.matmul(bank[0:32, j * 128:(j + 1) * 128], kib[:, t_, :], idb[:])
                    P.copy(kiT[:, q4 * 512:(q4 + 1) * 512], bank[0:32, :], eng='act')
                drain(index_scores(scoresS, g_ * 2048, 16, 0, alt=True))
            dma_tiles(VS, c_v[0:2048, :], 16, 8)
            drain(bisect(scoresS, PAST + 128, PAST))
            ck(32)
            keys = [(KbT[:, :, 0:128], Vaug[:, 0, :].rearrange("p (h e) -> p h e", e=65))]
            drain(attend(QbT, keys, pb[6], True, False, mask_cols=[PAST], sc=scoresS))
            for g_ in range(2):
                if g_ == 1:
                    dma_tiles(KS, c_k[2048:4096, :], 16, 8)
                    dma_tiles(VS, c_v[2048:4096, :], 16, 8)
                P.copy(Kb16[:, 0:6, :], KS[:, 0:6, :])
                P.copy(Kb16[:, 6:11, :], KS[:, 6:11, :], eng='act')
                P.copy(Kb16[:, 11:16, :], KS[:, 11:16, :], eng='pool')
                for q2 in range(8):
                    bank = pb[2 + q2 % 4]
                    for pr_ in range(2):
                        for j in range(2):
                            t_ = q2 * 2 + j
                            P.matmul(bank[:, (pr_ * 2 + j) * 128:(pr_ * 2 + j + 1) * 128],
                                     Kb16[:, t_, pr_ * 128:(pr_ + 1) * 128], idb[:])
                    P.copy(KbT[:, :, q2 * 256:(q2 + 1) * 256], bank[:].rearrange("p (a b) -> p a b", a=2), eng='act')
                v4 = Vaug[:, 0:16, :].rearrange("p t (h e) -> p t h e", e=65)
                for t4 in range(4):
                    P.copy(v4[:, t4 * 4:(t4 + 1) * 4, :, 0:64].rearrange("p t h d -> p (t h) d"),
                           VS[:, t4 * 4:(t4 + 1) * 4, :].rearrange("p t (h d) -> p (t h) d", d=64),
                           eng=('pool' if t4 % 2 == 0 else 'dve'))
                keys = [(KbT[:, :, j * 128:(j + 1) * 128], Vaug[:, j, :].rearrange("p (h e) -> p h e", e=65)) for j in range(16)]
                drain(attend(QbT, keys, pb[6], False, g_ == 1, mask_cols=[g_ * 2048 + j * 128 for j in range(16)], sc=scoresS))
            drain(finish_heads(pb[6], 0, 0))
            mkeys = [(MkT[:, :, j * 128:(j + 1) * 128], MvA[:, j, :].rearrange("p (h e) -> p h e", e=65)) for j in range(2)]
            drain(attend(QmT, mkeys, pb[7], True, True))
            drain(finish_heads(pb[7], 256, 256))
            mix_bc_T(pb[0])
            out_proj(16, y_s, xtb[0])
        except _Stop:
            pass
        P.emit(final_wait_engine='pool')
    return nc


_CACHE = {}


def _make_in_maps(inp):
    g_all = np.ascontiguousarray(np.concatenate([inp['g_in'][0].reshape(8, 128), inp['g_mem'][0].reshape(8, 128)], 0))
    gsm = np.ascontiguousarray(np.concatenate([
        inp['g_k_B'][0], inp['g_kidx_B'][0], inp['g_k_M'][0], inp['g_q_B'][0], inp['g_q_M'][0],
        inp['dt_bias_A'][0], inp['a_log_A'][0]]).reshape(1, GSM).astype(np.float32))
    maps = []
    for b in range(8):
        maps.append({
            "x_p": np.ascontiguousarray(inp['x_prompt'][b]),
            "x_s": np.ascontiguousarray(inp['x_sample'][b]),
            "mem": np.ascontiguousarray(inp['mem_prompt'][b]),
            "w_in": np.ascontiguousarray(inp['w_in'][0]),
            "w_mem": np.ascontiguousarray(inp['w_mem_kv'][0]),
            "w_out": np.ascontiguousarray(inp['w_out'][0]),
            "g_all": g_all,
            "gsm": gsm,
            "g_o": np.ascontiguousarray(inp['g_o_A'][0].reshape(1, 128)),
            "conv_w": np.ascontiguousarray(inp['conv_w_A'][0]),
            "st_conv": np.ascontiguousarray(inp['state_conv_A'][0, b]),
            "st_ssm": np.ascontiguousarray(inp['state_ssm_A'][0, b]),
            "c_k": np.ascontiguousarray(inp['cache_k_B'][0, b].reshape(PAST, 256)),
            "c_v": np.ascontiguousarray(inp['cache_v_B'][0, b].reshape(PAST, 256)),
            "c_ki": np.ascontiguousarray(inp['cache_kidx_B'][0, b]),
            "c_mk": np.ascontiguousarray(inp['cache_mem_k'][0, b].reshape(256, 256)),
            "c_mv": np.ascontiguousarray(inp['cache_mem_v'][0, b].reshape(256, 256)),
        })
    return maps


def _assemble(res):
    def stack(name, shape):
        return np.stack([np.asarray(r[name], dtype=np.float32).reshape(shape) for r in res], 0)
    outs = (
        stack("y_p", (SEQ, D)), stack("y_s", (16, D)),
        stack("p_conv", (3, 1536))[None],
        stack("p_ssm", (4, 128, 128))[None],
        stack("p_k", (SEQ, 4, 64))[None],
        stack("p_v", (SEQ, 4, 64))[None],
        stack("p_ki", (SEQ, 32))[None],
        stack("p_mk", (256, 4, 64))[None],
        stack("p_mv", (256, 4, 64))[None],
        stack("s_conv", (3, 1536))[None],
        stack("s_ssm", (4, 128, 128))[None],
        stack("s_k", (16, 4, 64))[None],
        stack("s_v", (16, 4, 64))[None],
        stack("s_ki", (16, 32))[None],
    )
    return outs


def kernel(**inputs):
    inp = {k: np.asarray(v) for k, v in inputs.items()}
    nc = build_program()
    in_maps = _make_in_maps(inp)
    res = run_bass_kernel_spmd(nc, in_maps, core_ids=list(range(8)))
    return _assemble(res.results)
```

```python
import numpy as np
import concourse.bass as bass
import concourse.mybir as mybir
from concourse.bass_utils import run_bass_kernel_spmd

F32 = mybir.dt.float32
BF16 = mybir.dt.bfloat16
ALU = mybir.AluOpType
AF = mybir.ActivationFunctionType
AX = mybir.AxisListType


def _region(ap):
    t = ap.tensor
    name = t.name
    esz = mybir.dt.size(ap.dtype)
    pat = tuple((st_ * esz, c_) for st_, c_ in ap.ap)
    off = ap.offset * esz
    space = str(ap.space)
    if 'DRAM' in space.upper() or 'Dram' in space or 'dram' in space:
        lo = off
        hi = off + sum((c - 1) * abs(s) for s, c in pat) + esz
        return (name, 0, 1, lo, hi)
    ps, pc = pat[0]
    if ps == 0:
        ps = 1 << 30
    p0 = off // ps
    fo = off % ps
    hi = fo + sum((c - 1) * abs(s) for s, c in pat[1:]) + esz
    if 'PSUM' in space.upper():
        return (name, (p0 // 32) * 32, ((p0 + pc + 31) // 32) * 32, 0, 1 << 30)
    return (name, p0, p0 + pc, fo, hi)


def _overlap(a, b):
    return a[1] < b[2] and b[1] < a[2] and a[3] < b[4] and b[3] < a[4]


def _contains(a, b):
    return a[1] <= b[1] and a[2] >= b[2] and a[3] <= b[3] and a[4] >= b[4]


class Op:
    __slots__ = ('eng', 'fn', 'reads', 'writes', 'dma', 'deps', 'signals', 'sem', 'val', 'idx', 'pe_acc')

    def __init__(self, eng, fn, reads, writes, dma=False):
        self.eng = eng
        self.fn = fn
        self.reads = reads
        self.writes = writes
        self.dma = dma
        self.deps = set()
        self.signals = False
        self.sem = None
        self.val = 0


class Prog:
    ENGS = ('pe', 'act', 'dve', 'pool', 'sp')

    def __init__(self, nc, n_dma_sems=12, same_engine_sync=True):
        self.nc = nc
        self.ops = []
        self.n_dma_sems = n_dma_sems
        self.same_engine_sync = same_engine_sync
        self.alias = {}

    def set_alias(self, name, group):
        self.alias[name] = group

    def add(self, eng, fn, reads, writes, dma=False):
        if getattr(self, 'dry', False):
            return None
        rr = [_region(a) for a in reads if a is not None and not isinstance(a, (int, float))]
        ww = [_region(a) for a in writes if a is not None]
        op = Op(eng, fn, rr, ww, dma)
        op.idx = len(self.ops)
        self.ops.append(op)
        return op

    def resolve(self):
        writers = {}
        readers = {}
        for op in self.ops:
            for r in op.reads:
                key = self.alias.get(r[0], r[0])
                full = key != r[0]
                for (wr, wi) in writers.get(key, ()):
                    if full or _overlap(r, wr):
                        op.deps.add(wi)
            for w in op.writes:
                key = self.alias.get(w[0], w[0])
                full = key != w[0]
                for (wr, wi) in writers.get(key, ()):
                    if full or _overlap(w, wr):
                        op.deps.add(wi)
                for (rr, ri) in readers.get(key, ()):
                    if full or _overlap(w, rr):
                        op.deps.add(ri)
            for w in op.writes:
                key = self.alias.get(w[0], w[0])
                full = key != w[0]
                if full:
                    writers[key] = [(w, op.idx)]
                    readers[key] = []
                else:
                    writers[key] = [(wr, wi) for (wr, wi) in writers.get(key, ()) if not _contains(w, wr)] + [(w, op.idx)]
                    readers[key] = [(rr, ri) for (rr, ri) in readers.get(key, ()) if not _contains(w, rr)]
            for r in op.reads:
                key = self.alias.get(r[0], r[0])
                lst = readers.setdefault(key, [])
                if not op.dma:
                    lst[:] = [(rr, ri) for (rr, ri) in lst
                              if not (rr == r and self.ops[ri].eng == op.eng and not self.ops[ri].dma)]
                lst.append((r, op.idx))
            op.deps.discard(op.idx)
        for op in self.ops:
            keep = set()
            for d in op.deps:
                o2 = self.ops[d]
                if not o2.dma and not op.dma and o2.eng == op.eng:
                    if op.eng == 'pe':
                        continue
                    if not self.same_engine_sync:
                        continue
                keep.add(d)
            op.deps = keep
            for d in keep:
                self.ops[d].signals = True
        for op in self.ops:
            if op.dma:
                op.signals = True

    def emit(self, final_wait_engine='sp'):
        nc = self.nc
        self.resolve()
        import contextlib
        with contextlib.ExitStack() as st:
            esem = {e: st.enter_context(nc.semaphore('sem_' + e)) for e in ('pe', 'act', 'dve', 'pool')}
            dsem = [st.enter_context(nc.semaphore('dsem%d' % i)) for i in range(self.n_dma_sems)]
            ecount = {e: 0 for e in esem}
            dcount = [0] * self.n_dma_sems
            half = self.n_dma_sems // 2
            kq = {'sp': 0, 'pool': 0, 'act': 0}
            for op in self.ops:
                if not op.signals:
                    continue
                if op.dma:
                    if op.eng == 'pool':
                        i = half + kq['pool'] % (self.n_dma_sems - half)
                        kq['pool'] += 1
                    else:
                        i = kq['sp'] % half
                        kq['sp'] += 1
                    dcount[i] += 16
                    op.sem = ('d', i)
                    op.val = dcount[i]
                else:
                    ecount[op.eng] += 1
                    op.sem = ('e', op.eng)
                    op.val = ecount[op.eng]
            final = {}
            for op in self.ops:
                if op.dma:
                    final[op.sem] = max(final.get(op.sem, 0), op.val)

            def semh(s):
                return esem[s[1]] if s[0] == 'e' else dsem[s[1]]

            per_eng = {e: [o for o in self.ops if o.eng == e] for e in self.ENGS}
            block = st.enter_context(nc.Block())

            def run(engname, eng):
                known = {}
                for op in per_eng[engname]:
                    need = {}
                    for d in op.deps:
                        o2 = self.ops[d]
                        need[o2.sem] = max(need.get(o2.sem, 0), o2.val)
                    if op.dma and op.val > 16:
                        need[op.sem] = max(need.get(op.sem, 0), op.val - 16)
                    for s, v in need.items():
                        if known.get(s, 0) >= v:
                            continue
                        eng.wait_ge(semh(s), v)
                        known[s] = v
                    ins = op.fn(eng)
                    if op.signals:
                        ins.then_inc(semh(op.sem), 16 if op.dma else 1)
                if engname == final_wait_engine:
                    for s, v in final.items():
                        if known.get(s, 0) < v:
                            eng.wait_ge(semh(s), v)

            @block.tensor
            def _(e):
                run('pe', e)

            @block.scalar
            def _(e):
                run('act', e)

            @block.vector
            def _(e):
                run('dve', e)

            @block.gpsimd
            def _(e):
                run('pool', e)

            @block.sync
            def _(e):
                run('sp', e)

    def dma(self, out, in_, eng='sp'):
        return self.add(eng, lambda e: e.dma_start(out=out, in_=in_), [in_], [out], dma=True)

    def matmul(self, out, lhsT, rhs, start=True, stop=True):
        rd = [lhsT, rhs] + ([] if start else [out])
        return self.add('pe', lambda e: e.matmul(out, lhsT, rhs, start=start, stop=stop), rd, [out])

    def transpose(self, out, in_, ident):
        return self.add('pe', lambda e: e.transpose(out, in_, ident), [in_, ident], [out])

    def act(self, out, in_, func, bias=None, scale=None, accum_out=None, eng='act'):
        kw = {}
        rd = [in_]
        if bias is not None:
            kw['bias'] = bias
            rd.append(bias)
        if scale is not None:
            kw['scale'] = scale
            rd.append(scale)
        wr = [out]
        if accum_out is not None:
            kw['accum_out'] = accum_out
            wr.append(accum_out)
        return self.add('act', lambda e: e.activation(out, in_, func, **kw), rd, wr)

    def tt(self, out, in0, in1, op, eng='dve'):
        return self.add(eng, lambda e: e.tensor_tensor(out, in0, in1, op), [in0, in1], [out])

    def ts(self, out, in0, s1, op0, s2=None, op1=None, accum_out=None, eng='dve'):
        rd = [in0, s1, s2]
        wr = [out, accum_out]

        def fn(e):
            kw = {}
            if accum_out is not None:
                kw['accum_out'] = accum_out
            if op1 is not None:
                return e.tensor_scalar(out, in0, s1, s2, op0, op1, **kw)
            return e.tensor_scalar(out, in0, s1, None, op0, **kw)
        return self.add(eng, fn, rd, wr)

    def stt(self, out, in0, scalar, in1, op0, op1, eng='dve'):
        return self.add(eng, lambda e: e.scalar_tensor_tensor(out, in0, scalar, in1, op0, op1),
                        [in0, scalar, in1], [out])

    def copy(self, out, in_, eng='dve'):
        if eng == 'act':
            return self.add('act', lambda e: e.activation(out, in_, AF.Copy), [in_], [out])
        return self.add(eng, lambda e: e.tensor_copy(out, in_), [in_], [out])

    def memset(self, ap, val, eng='dve'):
        return self.add(eng, lambda e: e.memset(ap, val), [], [ap])

    def reduce(self, out, in_, op, axis=AX.X, eng='dve'):
        return self.add(eng, lambda e: e.tensor_reduce(out, in_, axis, op), [in_], [out])

    def recip(self, out, in_):
        return self.add('dve', lambda e: e.reciprocal(out, in_), [in_], [out])


D = 1024
IN_W = 3888
SEQ = 2048
NT = SEQ // 128
PAST = 4096
EPS = 1e-6
C_ZA = 1536
C_TM = 2048
T_BA, T_AA, T_QB, T_KB, T_VB, T_ZB, T_KI, T_WI, T_QM, T_ZM = 0, 4, 8, 264, 520, 776, 1288, 1320, 1328, 1584
C_QI = 3080
TMW = IN_W - C_TM
IDX_SCALE = (8 ** -0.5) * (32 ** -0.5)
NBIS = 17
NEGBIG = -1.0e30
G_KB, G_KI, G_KM, G_QB, G_QM, G_DTB, G_ALOG = 0, 64, 96, 160, 224, 288, 292
GSM = 296


class _Stop(Exception):
    pass


def build_program(kstop=0):
    import contextlib

    def ck(n):
        if kstop == n:
            raise _Stop()
    nc = bass.Bass("TRN2", target_bir_lowering=False)

    def din(name, shape):
        return nc.dram_tensor(name, shape, F32, kind="ExternalInput").ap()

    def dout(name, shape):
        return nc.dram_tensor(name, shape, F32, kind="ExternalOutput").ap()

    x_p = din("x_p", [SEQ, D])
    x_s = din("x_s", [16, D])
    mem = din("mem", [256, D])
    w_in = din("w_in", [D, IN_W])
    w_mem = din("w_mem", [D, 512])
    w_out = din("w_out", [D, D])
    g_all = din("g_all", [16, 128])
    gsm_d = din("gsm", [1, GSM])
    g_o_d = din("g_o", [1, 128])
    convw_d = din("conv_w", [4, 1536])
    st_conv = din("st_conv", [3, 1536])
    st_ssm = din("st_ssm", [4, 128, 128])
    c_k = din("c_k", [PAST, 256])
    c_v = din("c_v", [PAST, 256])
    c_ki = din("c_ki", [PAST, 32])
    c_mk = din("c_mk", [256, 256])
    c_mv = din("c_mv", [256, 256])

    y_p = dout("y_p", [SEQ, D])
    y_s = dout("y_s", [16, D])
    p_conv = dout("p_conv", [3, 1536])
    p_ssm = dout("p_ssm", [4, 128, 128])
    p_k = dout("p_k", [SEQ, 256])
    p_v = dout("p_v", [SEQ, 256])
    p_ki = dout("p_ki", [SEQ, 32])
    p_mk = dout("p_mk", [256, 256])
    p_mv = dout("p_mv", [256, 256])
    s_conv = dout("s_conv", [3, 1536])
    s_ssm = dout("s_ssm", [4, 128, 128])
    s_k = dout("s_k", [16, 256])
    s_v = dout("s_v", [16, 256])
    s_ki = dout("s_ki", [16, 32])

    dbg_outs = [dout("dbg%d" % i, [128, 1024]) for i in range(6)] if kstop else []
    dbg_n = [0]

    with contextlib.ExitStack() as st:
        def sb(name, shape, dt=F32):
            return st.enter_context(nc.sbuf_tensor(name, shape, dt))

        def ps(name, shape, dt=F32):
            return st.enter_context(nc.psum_tensor(name, shape, dt))

        P = Prog(nc, n_dma_sems=16)

        W = sb("W", [128, 8, IN_W], BF16)
        Wo = sb("Wo", [128, 8, D], BF16)
        KbT = sb("KbT", [128, 2, 2048], BF16)
        Wm = KbT[:].rearrange("p a (b c) -> p (a b) c", c=512)
        Vaug = sb("Vaug", [128, 16, 260], BF16)
        kiT = sb("kiT", [32, 2048], BF16)
        scores = sb("scores", [128, 2048])
        g16 = scores[0:16, 1536:1664]
        go1 = scores[0:1, 1664:1792]
        cw4 = scores[0:4, 0:1536]
        stc = scores[0:3, 0:1536]
        scoresS = W[:].rearrange("p a b -> p (a b)").bitcast(F32)[:, 0:4224]
        id32 = sb("id32", [128, 128])
        idb = sb("idb", [128, 128], BF16)
        ones32 = sb("ones32", [128, 128])
        onesb = sb("onesb", [128, 128], BF16)
        negI = sb("negI", [128, 128], BF16)
        zerob = sb("zerob", [128, 128], BF16)
        blkP = sb("blkP", [128, 128])
        blkS = sb("blkS", [128, 128])
        triP = sb("triP", [128, 128])
        triS = sb("triS", [128, 128])
        lsP = sb("lsP", [128, 128])
        lsS = sb("lsS", [128, 128])
        admP = sb("admP", [128, 128])
        nadmP = sb("nadmP", [128, 128])
        nadmS = sb("nadmS", [128, 128])
        gcol = sb("gcol", [128, 16])
        gsm = sb("gsm_t", [128, GSM])
        negA = sb("negA", [128, 4])
        gocol = sb("gocol", [128, 1])
        wc = sb("wc", [128, 12, 4])
        xtb = [sb("xt%d" % i, [128, D]) for i in range(2)]
        jk8 = sb("jk8", [128, 2048], mybir.dt.uint8)
        xs = sb("xs", [128, D], BF16)
        hT = sb("hT", [128, 8, 128], BF16)
        rx = sb("rx", [128, 1])
        tm = sb("tm", [128, TMW])
        raw = sb("raw", [128, 12, 131])
        acc = sb("acc", [128, 12, 128])
        vsb = sb("vsb", [128, 4, 128], BF16)
        zas = sb("zas", [128, 4, 128], BF16)
        zs = sb("zs", [128, 512], BF16)
        sqb = sb("sqb", [128, 8, 128], BF16)
        rbc = sb("rbc", [128, 8, 128])
        qn = sb("qn", [128, 4, 128], BF16)
        knT = sb("knT", [128, 4, 128], BF16)
        qiT = sb("qiT", [32, 8, 128], BF16)
        gt = sb("gt", [128, 28])
        egr = sb("egr", [128, 4, 128])
        DT = sb("DT", [128, 4, 128])
        Xm = sb("Xm", [128, 4, 128])
        Ym = sb("Ym", [128, 4, 128])
        Dg = Xm
        Dst = Ym
        Pm = sb("Pm", [128, 4, 128])
        Tt = sb("Tt", [128, 4, 128], BF16)
        bv = sb("bv", [128, 4, 128], BF16)
        kbg = sb("kbg", [128, 4, 128], BF16)
        kd = sb("kd", [128, 4, 128], BF16)
        nwk = sb("nwk", [128, 4, 128], BF16)
        qd = sb("qd", [128, 4, 128], BF16)
        qkT = sb("qkT", [128, 4, 128], BF16)
        usb = sb("usb", [128, 4, 128], BF16)
        S = sb("S", [128, 4, 128])
        Sb = sb("Sb", [128, 4, 128], BF16)
        mixT = sb("mixT", [128, 8, 128], BF16)
        mixBC = sb("mixBC", [128, 512], BF16)
        s5 = sb("s5", [128, 16])
        rbcf = rbc[:].rearrange("p a b -> p (a b)")
        sq = rbcf
        accf = acc[:].rearrange("p a b -> p (a b)")
        sqB = sb("sqB", [128, 512])
        nrm = sb("nrm", [128, 768], BF16)
        knf = sb("knf", [128, 256])
        kxn = sb("kxn", [128, 32])
        kxb = sb("kxb", [128, 32], BF16)
        QbT = sb("QbT", [128, 2, 128], BF16)
        QmT = sb("QmT", [128, 2, 128], BF16)
        MkT = sb("MkT", [128, 2, 256], BF16)
        MvA = sb("MvA", [128, 2, 260], BF16)
        wdiag = sb("wdiag", [128, 8, 128], BF16)
        wabs = sb("wabs", [128, 8])
        wsgn = sb("wsgn", [128, 8])
        rl = [sb("rl%d" % i, [128, 512]) for i in range(2)]
        kvc = rl
        up32 = rl[0][:, 0:128]
        lo32 = rl[0][:, 128:256]
        PT = [sb("PT%d" % i, [128, 4, 128], BF16) for i in range(2)]
        nmw = sb("nmw", [128, 1024], BF16)
        bis = sb("bis", [128, 8])
        wtab = sb("wtab", [128, NBIS + 1])
        p2 = sb("p2", [128, NBIS + 1])
        rcp = sb("rcp", [128, 8])
        ob = rbcf[:, 768:1024]
        cv = accf
        kvb = sb("kvb", [128, 256], BF16)
        kic = sb("kic", [128, 32])
        yo = accf[:, 0:D]

        pb = [ps("pb%d" % i, [128, 512]) for i in range(8)]


        def aff(out, cmp, mult, pat):
            P.add('pool', lambda e: e.affine_select(out, out, [[pat, 128]], cmp, 0.0, base=0,
                                                    channel_multiplier=mult), [out], [out])
        P.memset(id32[:], 0.0)
        P.add('pool', lambda e: e.affine_select(id32[:], id32[:], [[-1, 128]], ALU.not_equal, 1.0,
                                                base=0, channel_multiplier=1), [id32[:]], [id32[:]])
        P.copy(idb[:], id32[:])
        P.ts(negI[:], id32[:], -30000.0, ALU.mult)
        P.memset(zerob[:], 0.0)
        for k_ in range(NBIS + 1):
            P.memset(p2[:, k_:k_ + 1], 2.0 ** -(k_ + 1), eng='pool')
        P.memset(ones32[:], 1.0)
        P.memset(onesb[:], 1.0)
        P.memset(up32, 1.0)
        aff(up32, ALU.is_ge, -1, 1)
        P.memset(lo32, 1.0)
        aff(lo32, ALU.is_gt, 1, -1)
        P.memset(blkP[:], 0.0)
        P.memset(blkP[0:64, 0:64], 1.0)
        P.memset(blkP[64:128, 64:128], 1.0)
        P.memset(blkS[:], 0.0)
        P.memset(blkS[0:16, 0:16], 1.0)
        P.tt(triP[:], up32, blkP[:], ALU.mult)
        P.tt(triS[:], up32, blkS[:], ALU.mult)
        P.tt(lsP[:], lo32, blkP[:], ALU.mult)
        P.tt(lsS[:], lo32, blkS[:], ALU.mult)
        P.memset(admP[:], 1.0)
        P.memset(admP[0:64, 64:128], 0.0)
        P.ts(nadmP[:], admP[:], -1.0, ALU.add, 1.0e30, ALU.mult)
        P.memset(nadmS[:], NEGBIG)
        P.memset(nadmS[:, 0:16], 0.0)

        P.dma(g16, g_all)
        P.dma(gsm[:], gsm_d.partition_broadcast(128))
        P.dma(go1, g_o_d)
        P.dma(cw4, convw_d)
        P.matmul(pb[0][:, 0:16], g16, id32[0:16, 0:16])
        P.copy(gcol[:], pb[0][:, 0:16])
        P.matmul(pb[0][:, 16:17], go1, id32[0:1, 0:1])
        P.copy(gocol[:], pb[0][:, 16:17])
        for c in range(12):
            P.matmul(pb[1][:, c * 4:(c + 1) * 4], cw4[0:4, c * 128:(c + 1) * 128], id32[0:4, 0:4])
        P.copy(wc[:], pb[1][:, 0:48].rearrange("p (c j) -> p c j", j=4))
        P.act(negA[:], gsm[:, G_ALOG:G_ALOG + 4], AF.Exp)
        P.ts(negA[:], negA[:], -1.0, ALU.mult)

        def cast_w(dst, src, gc, ncols):
            a = ncols * 9 // 25
            b = ncols * 17 // 25
            if gc is None:
                P.copy(dst[:, 0:a], src[:, 0:a])
                P.copy(dst[:, a:b], src[:, a:b], eng='act')
                P.copy(dst[:, b:ncols], src[:, b:ncols], eng='pool')
            else:
                P.ts(dst[:, 0:a], src[:, 0:a], gc, ALU.mult)
                P.act(dst[:, a:b], src[:, a:b], AF.Copy, scale=gc)
                P.ts(dst[:, b:ncols], src[:, b:ncols], gc, ALU.mult, 1.0, ALU.mult, eng='pool')

        CH = IN_W // 3
        stg3 = sb("stg3", [128, IN_W // 3])
        slots = [scores[:, 0:CH], accf[:, 0:CH], stg3[:, 0:CH]]
        k = 0
        for kt in range(8):
            for hf in range(3):
                s_ = slots[k % 3]
                k += 1
                P.dma(s_, w_in[kt * 128:(kt + 1) * 128, hf * CH:(hf + 1) * CH])
                cast_w(W[:, kt, hf * CH:(hf + 1) * CH], s_, gcol[:, kt:kt + 1], CH)
        for kt in range(8):
            s_ = slots[k % 3][:, 0:512]
            k += 1
            P.dma(s_, w_mem[kt * 128:(kt + 1) * 128, :])
            cast_w(Wm[:, kt, :], s_, gcol[:, 8 + kt:9 + kt], 512)
        for kt in range(8):
            s_ = slots[k % 3][:, 0:1024]
            k += 1
            P.dma(s_, w_out[kt * 128:(kt + 1) * 128, :])
            cast_w(Wo[:, kt, :], s_, None, 1024)

        def load_h(x_src, nrows, xt, banks):
            if nrows < 128:
                P.memset(xt[:], 0.0)
            P.dma(xt[0:nrows, :], x_src)
            P.add('dve', lambda e: e.scalar_tensor_tensor(yo, xt[:], 1.0, xt[:], ALU.mult, ALU.mult,
                                                          accum_out=rx[:]), [xt[:]], [yo, rx[:]])
            P.ts(rx[:], rx[:], 1.0 / D, ALU.mult, EPS, ALU.add)
            P.act(rx[:], rx[:], AF.Ln)
            P.act(rx[:], rx[:], AF.Exp, scale=-0.5)
            P.ts(xs[:], xt[:], rx[:], ALU.mult)
            for half in range(2):
                for j in range(4):
                    kt = half * 4 + j
                    P.matmul(banks[half][:, j * 128:(j + 1) * 128], xs[:, kt * 128:(kt + 1) * 128], idb[:])
            P.copy(hT[:, 0:4, :], banks[0][:].rearrange("p (a b) -> p a b", a=4), eng='act')
            P.copy(hT[:, 4:8, :], banks[1][:].rearrange("p (a b) -> p a b", a=4), eng='act')

        def rstd_inplace(ap, scale, eps=EPS, post=None):
            P.ts(ap, ap, scale, ALU.mult, eps, ALU.add)
            P.act(ap, ap, AF.Ln)
            if post is None:
                P.act(ap, ap, AF.Exp, scale=-0.5)
            else:
                P.act(ap, ap, AF.Exp, scale=-0.5, bias=post)

        def transpose_bf(dst, src_list, pbank):
            for j, s_ in enumerate(src_list):
                P.matmul(pbank[:, j * 128:(j + 1) * 128], s_, idb[:])
            n = len(src_list)
            P.copy(dst, pbank[:, 0:n * 128].rearrange("p (a b) -> p a b", a=n), eng='act')

        def mem_tile(mt):
            r = slice(mt * 128, (mt + 1) * 128)
            load_h(mem[r, :], 128, xtb[0], (pb[0], pb[1]))
            for kt in range(8):
                P.matmul(pb[2][:], hT[:, kt, :], Wm[:, kt, :], start=(kt == 0), stop=(kt == 7))
            KV = kvc[mt]
            P.copy(KV[:], pb[2][:], eng='act')
            P.tt(sq[:, 0:256], KV[:, 0:256], KV[:, 0:256], ALU.mult)
            P.reduce(s5[:, 0:4], sq[:, 0:256].rearrange("p (h d) -> p h d", h=4), ALU.add)
            rstd_inplace(s5[:, 0:4], 1.0 / 64)
            k3 = KV[:, 0:256].rearrange("p (h d) -> p h d", h=4)
            n3 = knf[:].rearrange("p (h d) -> p h d", h=4)
            P.tt(n3, k3, s5[:, 0:4].unsqueeze(2).to_broadcast([128, 4, 64]), ALU.mult)
            P.tt(n3, n3, gsm[:, G_KM:G_KM + 64].unsqueeze(1).to_broadcast([128, 4, 64]), ALU.mult)
            P.dma(p_mk[r, :], knf[:], eng='pool')
            P.dma(p_mv[r, :], KV[:, 256:512], eng='pool')
            P.copy(kvb[:], knf[:])
            transpose_bf(MkT[:, :, r], [kvb[:, 0:128], kvb[:, 128:256]], pb[3])
            P.memset(MvA[:, mt, :].rearrange("p (h e) -> p h e", e=65)[:, :, 64:65], 1.0)
            P.copy(MvA[:, mt, :].rearrange("p (h e) -> p h e", e=65)[:, :, 0:64],
                   KV[:, 256:512].rearrange("p (h d) -> p h d", h=4), eng='pool')

        def layer_common(x_src, nrows, xt, preloaded=False):
            if not preloaded:
                load_h(x_src, nrows, xt, (pb[0], pb[1]))
            for c in range(16):
                bank = pb[2 + c // 4]
                for kt in range(8):
                    P.matmul(bank[:, (c % 4) * 128:(c % 4 + 1) * 128], W[:, kt, c * 128:(c + 1) * 128], hT[:, kt, :],
                             start=(kt == 0), stop=(kt == 7))
            for c in range(3):
                P.copy(raw[:, c * 4:(c + 1) * 4, 3:131], pb[2 + c][:].rearrange("p (a b) -> p a b", a=4), eng='act')
            for h in range(8):
                bank = pb[6 + h // 4]
                for kt in range(8):
                    P.matmul(bank[0:32, (h % 4) * 128:(h % 4 + 1) * 128], W[:, kt, C_QI + h * 32:C_QI + (h + 1) * 32],
                             hT[:, kt, :], start=(kt == 0), stop=(kt == 7))
            P.copy(qiT[:, 0:4, :], pb[6][0:32, :].rearrange("p (a b) -> p a b", a=4), eng='act')
            P.copy(qiT[:, 4:8, :], pb[7][0:32, :].rearrange("p (a b) -> p a b", a=4), eng='act')
            ck(10)
            P.copy(zas[:], pb[5][:].rearrange("p (a b) -> p a b", a=4), eng='act')
            for ci, c0 in enumerate(range(0, TMW, 512)):
                wd = min(512, TMW - c0)
                bank = pb[ci % 2]
                for kt in range(8):
                    P.matmul(bank[:, 0:wd], hT[:, kt, :], W[:, kt, C_TM + c0:C_TM + c0 + wd], start=(kt == 0), stop=(kt == 7))
                P.copy(tm[:, c0:c0 + wd], bank[:, 0:wd], eng='act')
            ck(12)

        def chainA(sample, tail=None):
            tri, ls = (triS, lsS) if sample else (triP, lsP)
            beta, gg, gc, nbeta, egc, ekd, c1 = (gt[:, 0:4], gt[:, 4:8], gt[:, 8:12], gt[:, 12:16],
                                                 gt[:, 16:20], gt[:, 20:24], gt[:, 24:28])

            def a1():
                for c in range(12):
                    P.ts(acc[:, c, :], raw[:, c, 0:128], wc[:, c, 0:1], ALU.mult)
                    for j in range(1, 4):
                        P.stt(acc[:, c, :], raw[:, c, j:j + 128], wc[:, c, j:j + 1], acc[:, c, :], ALU.mult, ALU.add)
                    if c % 3 == 2:
                        yield
                P.copy(raw[:, :, 0:3], raw[:, :, 128:131], eng='pool')
                P.act(acc[:, 0:8, :], acc[:, 0:8, :], AF.Silu)
                P.act(vsb[:], acc[:, 8:12, :], AF.Silu)
                P.act(zas[:], zas[:], AF.Silu)
                P.act(zs[:, 0:256], tm[:, T_ZB:T_ZB + 256], AF.Silu)
                P.act(zs[:, 256:512], tm[:, T_ZM:T_ZM + 256], AF.Silu)
                yield
                P.tt(sqb[:], acc[:, 0:8, :], acc[:, 0:8, :], ALU.mult, eng='pool')
                for g_ in range(2):
                    P.matmul(pb[2 + g_][:], onesb[:], sqb[:, g_ * 4:(g_ + 1) * 4, :].rearrange("p a b -> p (a b)"))
                P.act(rbc[:, 0:4, :], pb[2][:].rearrange("p (a b) -> p a b", a=4), AF.Ln, bias=EPS)
                P.act(rbc[:, 4:8, :], pb[3][:].rearrange("p (a b) -> p a b", a=4), AF.Ln, bias=EPS)
                P.act(rbc[:, 0:4, :], rbc[:, 0:4, :], AF.Exp, scale=-0.5, bias=float(np.log(128.0 ** -0.5)))
                P.act(rbc[:, 4:8, :], rbc[:, 4:8, :], AF.Exp, scale=-0.5)
                P.tt(qn[:], acc[:, 0:4, :], rbc[:, 0:4, :], ALU.mult)
                P.tt(knT[:], acc[:, 4:8, :], rbc[:, 4:8, :], ALU.mult)
                if kstop == 13:
                    dbg(raw[:, 4:8, 3:131], 512, stage=None) if False else None
                    dbg(acc[:, 4:8, :].rearrange("p a b -> p (a b)"), 512)
                    dbg(rbc[:, 4:8, :].rearrange("p a b -> p (a b)"), 512)
                    dbg(acc[:, 0:4, :].rearrange("p a b -> p (a b)"), 512)
                    dbg(rbc[:, 0:4, :].rearrange("p a b -> p (a b)"), 512)
                    dbg(knT[:].rearrange("p a b -> p (a b)"), 512, stage=tm)
                    dbg(qn[:].rearrange("p a b -> p (a b)"), 512, stage=tm)
                ck(13)
                yield

                for h in range(4):
                    P.matmul(pb[2][:, h * 128:(h + 1) * 128], knT[:, h, :], idb[:])
                    P.matmul(pb[3][:, h * 128:(h + 1) * 128], vsb[:, h, :], idb[:])

            def a2():
                P.act(beta, tm[:, T_BA:T_BA + 4], AF.Exp, scale=-1.0)
                P.ts(beta, beta, 1.0, ALU.add)
                P.recip(beta, beta)
                P.ts(nbeta, beta, -1.0, ALU.mult)
                P.tt(gg, tm[:, T_AA:T_AA + 4], gsm[:, G_DTB:G_DTB + 4], ALU.add)
                P.act(gg, gg, AF.Exp)
                P.act(gg, gg, AF.Ln, bias=1.0)
                P.tt(gg, gg, negA[:], ALU.mult)
                ck(131)
                yield
                blk = blkS if sample else blkP
                P.matmul(pb[4][:, 0:4], tri[:], gg)
                P.matmul(pb[4][:, 4:8], blk[:], gg)
                P.copy(gc, pb[4][:, 0:4])
                P.tt(ekd, pb[4][:, 4:8], gc, ALU.subtract)
                P.act(ekd, ekd, AF.Exp)
                P.act(egc, gc, AF.Exp)
                P.tt(c1, beta, egc, ALU.mult)
                ck(132)
                yield
                for h in range(4):
                    P.ts(Dg[:, h, :], tri[:], gg[:, h:h + 1], ALU.mult)
                    P.matmul(pb[5][:, h * 128:(h + 1) * 128], ones32[:], Dg[:, h, :])
                gcr = pb[5][:].rearrange("p (a b) -> p a b", a=4)
                ck(133)
                yield
                P.act(egr[:], gcr, AF.Exp)
                ck(134)
                yield
                P.copy(DT[:], gcr, eng='act')
                for h in range(4):
                    P.ts(Dst[:, h, :], DT[:, h, :], gc[:, h:h + 1], ALU.subtract, 0.0, ALU.max)
                    P.ts(DT[:, h, :], DT[:, h, :], gc[:, h:h + 1], ALU.subtract, 0.0, ALU.min)
                ck(135)
                yield
                P.act(Dst[:], Dst[:], AF.Exp, scale=-1.0)
                P.act(DT[:], DT[:], AF.Exp)
                ck(136)
                yield
                P.tt(Dst[:], Dst[:], ls[:].unsqueeze(1).to_broadcast([128, 4, 128]), ALU.mult)
                P.tt(DT[:], DT[:], tri[:].unsqueeze(1).to_broadcast([128, 4, 128]), ALU.mult)
                ck(14)
                yield


            def mix(g1, g2):
                d1 = d2 = False
                while not (d1 and d2):
                    if not d1:
                        try:
                            next(g1)
                            yield
                        except StopIteration:
                            d1 = True
                    if not d2:
                        try:
                            next(g2)
                            yield
                        except StopIteration:
                            d2 = True
            yield from mix(a1(), a2())
            ktm = pb[2][:].rearrange("p (a b) -> p a b", a=4)
            vtm = pb[3][:].rearrange("p (a b) -> p a b", a=4)
            P.tt(bv[:], vtm, beta.unsqueeze(2).to_broadcast([128, 4, 128]), ALU.mult)
            P.tt(kbg[:], ktm, c1.unsqueeze(2).to_broadcast([128, 4, 128]), ALU.mult)
            P.tt(kd[:], ktm, ekd.unsqueeze(2).to_broadcast([128, 4, 128]), ALU.mult)
            ck(15)
            yield

            for h in range(4):
                P.matmul(pb[2][:, h * 128:(h + 1) * 128], knT[:, h, :], knT[:, h, :])
            for h in range(4):
                P.ts(Dst[:, h, :], Dst[:, h, :], nbeta[:, h:h + 1], ALU.mult)
            P.tt(Xm[:], pb[2][:].rearrange("p (a b) -> p a b", a=4), Dst[:], ALU.mult)
            for h in range(4):
                P.matmul(pb[3][:, h * 128:(h + 1) * 128], Xm[:, h, :], id32[:])
            P.copy(Ym[:], pb[3][:].rearrange("p (a b) -> p a b", a=4), eng='act')
            P.tt(Pm[:], Ym[:], id32[:].unsqueeze(1).to_broadcast([128, 4, 128]), ALU.add)
            if kstop == 16:
                dbg(Xm[:].rearrange("p a b -> p (a b)"), 512)
                dbg(Ym[:].rearrange("p a b -> p (a b)"), 512)
                dbg(Pm[:].rearrange("p a b -> p (a b)"), 512)
                dbg(DT[:].rearrange("p a b -> p (a b)"), 512)
                dbg(gt[:, 0:12], 12)
                dbg(knT[:].rearrange("p a b -> p (a b)"), 512, stage=tm)
            ck(16)
            yield
            nlev = 3 if sample else 5
            for lv in range(1, nlev + 1):
                for h in range(4):
                    P.matmul(pb[2][:, h * 128:(h + 1) * 128], Ym[:, h, :], Xm[:, h, :])
                if lv < nlev:
                    for h in range(4):
                        P.matmul(pb[3][:, h * 128:(h + 1) * 128], Xm[:, h, :], Ym[:, h, :])
                P.copy(Xm[:], pb[2][:].rearrange("p (a b) -> p a b", a=4), eng='act')
                if lv < nlev:
                    P.copy(Ym[:], pb[3][:].rearrange("p (a b) -> p a b", a=4), eng='act')
                for h in range(4):
                    P.matmul(pb[4][:, h * 128:(h + 1) * 128], Xm[:, h, :], Pm[:, h, :])
                P.tt(Pm[:], Pm[:], pb[4][:].rearrange("p (a b) -> p a b", a=4), ALU.add)
                yield
            P.copy(Tt[:], Pm[:], eng='act')
            ck(17)
            yield

            for h in range(4):
                P.matmul(pb[2][:, h * 128:(h + 1) * 128], kbg[:, h, :], Tt[:, h, :])
            P.act(nwk[:], pb[2][:].rearrange("p (a b) -> p a b", a=4), AF.Copy, scale=-1.0)
            P.tt(qd[:], qn[:], egr[:], ALU.mult)
            for h in range(4):
                P.matmul(pb[3][:, h * 128:(h + 1) * 128], knT[:, h, :], qn[:, h, :])
            P.tt(qkT[:], pb[3][:].rearrange("p (a b) -> p a b", a=4), DT[:], ALU.mult)
            ck(18)
            yield

            chunks = [(0, 16)] if sample else [(0, 64), (64, 128)]
            obank = pb[4]
            for (r0, r1) in chunks:
                for h in range(4):
                    P.matmul(pb[5][:, h * 128:(h + 1) * 128], Tt[:, h, :], bv[:, h, :], start=True, stop=False)
                    P.matmul(pb[5][:, h * 128:(h + 1) * 128], nwk[:, h, :], Sb[:, h, :], start=False, stop=True)
                P.copy(usb[r0:r1, :, :], pb[5][r0:r1, :].rearrange("p (a b) -> p a b", a=4), eng='act')
                for h in range(4):
                    oc = obank[:, h * 128 + r0:h * 128 + r1]
                    P.matmul(oc, Sb[:, h, :], qd[:, h, r0:r1], start=True, stop=False)
                    P.matmul(oc, usb[r0:r1, h, :], qkT[r0:r1, h, r0:r1], start=False, stop=True)
                for h in range(4):
                    P.matmul(pb[2][:, h * 128:(h + 1) * 128], kd[r0:r1, h, :], usb[r0:r1, h, :])
                for h in range(4):
                    P.ts(S[:, h, :], S[:, h, :], egr[:, h, r1 - 1:r1], ALU.mult)
                P.tt(S[:], S[:], pb[2][:].rearrange("p (a b) -> p a b", a=4), ALU.add)
                P.copy(Sb[:], S[:], eng='act')
                yield
            o3 = obank[:].rearrange("p (a b) -> p a b", a=4)
            ck(19)
            yield
            P.act(sqb[:, 0:4, :], o3, AF.Square)
            P.matmul(pb[3][:], onesb[:], sqb[:, 0:4, :].rearrange("p a b -> p (a b)"))
            P.act(rbc[:, 0:4, :], pb[3][:].rearrange("p (a b) -> p a b", a=4), AF.Ln, scale=1.0 / 128, bias=EPS)
            P.act(rbc[:, 0:4, :], rbc[:, 0:4, :], AF.Exp, scale=-0.5)
            P.tt(rbc[:, 0:4, :], rbc[:, 0:4, :], o3, ALU.mult)
            P.stt(mixT[:, 0:4, :], rbc[:, 0:4, :], gocol[:, 0:1], zas[:], ALU.mult, ALU.mult)
            ck(20)
            yield
            if tail is not None:
                tail()
                yield


        def chainB_heads(nrows, outs):
            P.tt(sqB[:, 0:512], tm[:, T_QB:T_QB + 512], tm[:, T_QB:T_QB + 512], ALU.mult)
            P.reduce(s5[:, 0:8], sqB[:, 0:512].rearrange("p (h d) -> p h d", d=64), ALU.add)
            P.tt(sqB[:, 0:256], tm[:, T_QM:T_QM + 256], tm[:, T_QM:T_QM + 256], ALU.mult)
            P.reduce(s5[:, 8:12], sqB[:, 0:256].rearrange("p (h d) -> p h d", d=64), ALU.add)
            P.tt(sqB[:, 256:288], tm[:, T_KI:T_KI + 32], tm[:, T_KI:T_KI + 32], ALU.mult)
            P.reduce(s5[:, 12:13], sqB[:, 256:288], ALU.add)
            yield
            P.ts(s5[:, 12:13], s5[:, 12:13], 2.0, ALU.mult)
            rstd_inplace(s5[:, 0:13], 1.0 / 64)
            k3 = tm[:, T_KB:T_KB + 256].rearrange("p (h d) -> p h d", h=4)
            n3 = knf[:].rearrange("p (h d) -> p h d", h=4)
            P.tt(n3, k3, s5[:, 4:8].unsqueeze(2).to_broadcast([128, 4, 64]), ALU.mult)
            P.tt(n3, n3, gsm[:, G_KB:G_KB + 64].unsqueeze(1).to_broadcast([128, 4, 64]), ALU.mult)
            P.dma(outs["k"], knf[0:nrows, :], eng='pool')
            P.dma(outs["v"], tm[0:nrows, T_VB:T_VB + 256], eng='pool')
            P.copy(nrm[:, 256:512], knf[:], eng='pool')
            q3 = tm[:, T_QB:T_QB + 256].rearrange("p (h d) -> p h d", h=4)
            yield
            m3 = sqB[:, 0:256].rearrange("p (h d) -> p h d", h=4)
            P.tt(m3, q3, s5[:, 0:4].unsqueeze(2).to_broadcast([128, 4, 64]), ALU.mult)
            P.tt(nrm[:, 0:256].rearrange("p (h d) -> p h d", h=4), m3,
                 gsm[:, G_QB:G_QB + 64].unsqueeze(1).to_broadcast([128, 4, 64]), ALU.mult)
            q3 = tm[:, T_QM:T_QM + 256].rearrange("p (h d) -> p h d", h=4)
            m3 = sqB[:, 256:512].rearrange("p (h d) -> p h d", h=4)
            P.tt(m3, q3, s5[:, 8:12].unsqueeze(2).to_broadcast([128, 4, 64]), ALU.mult)
            P.tt(nrm[:, 512:768].rearrange("p (h d) -> p h d", h=4), m3,
                 gsm[:, G_QM:G_QM + 64].unsqueeze(1).to_broadcast([128, 4, 64]), ALU.mult)
            P.ts(kxn[:], tm[:, T_KI:T_KI + 32], s5[:, 12:13], ALU.mult)
            P.tt(kxn[:], kxn[:], gsm[:, G_KI:G_KI + 32], ALU.mult)
            P.dma(outs["ki"], kxn[0:nrows, :], eng='pool')
            P.copy(kxb[:], kxn[:], eng='pool')
            kslot = outs["kslot"]
            kc = slice(kslot * 128, (kslot + 1) * 128)
            yield
            transpose_bf(QbT[:], [nrm[:, 0:128], nrm[:, 128:256]], pb[0])
            transpose_bf(KbT[:, :, kc], [nrm[:, 256:384], nrm[:, 384:512]], pb[1])
            yield
            transpose_bf(QmT[:], [nrm[:, 512:640], nrm[:, 640:768]], pb[6])
            P.matmul(pb[7][0:32, 0:128], kxb[:], idb[:])
            P.copy(kiT[:, kc], pb[7][0:32, 0:128])
            yield
            va = Vaug[:, kslot, :].rearrange("p (h e) -> p h e", e=65)
            P.memset(va[:, :, 64:65], 1.0)
            P.copy(va[:, :, 0:64], tm[:, T_VB:T_VB + 256].rearrange("p (h d) -> p h d", h=4), eng='pool')
            P.ts(wabs[:], tm[:, T_WI:T_WI + 8], IDX_SCALE, ALU.mult)
            for h in range(8):
                P.ts(wdiag[:, h, :], id32[:], wabs[:, h:h + 1], ALU.mult, eng=('dve' if h % 2 == 0 else 'pool'))
            yield

        def conv_out(nrows, dst):
            for c in range(3):
                pc = pb[c % 2]
                for kt in range(8):
                    P.matmul(pc[:], hT[:, kt, :], W[:, kt, c * 512:(c + 1) * 512], start=(kt == 0), stop=(kt == 7))
                P.copy(cv[:, c * 512:(c + 1) * 512], pc[:], eng=('act' if c % 2 == 0 else 'dve'))
            P.dma(dst, cv[nrows - 3:nrows, :], eng='pool')

        def index_scores(sc, col0, ktiles, kbase, alt=False):
            for g0 in range(0, ktiles, 4):
                nk = min(4, ktiles - g0) * 128
                kcs = slice((kbase + g0) * 128, (kbase + g0) * 128 + nk)
                dst = sc[:, col0 + g0 * 128:col0 + g0 * 128 + nk]

                def logits(h):
                    P.matmul(pb[h % 2][:, 0:nk], qiT[:, h, :], kiT[:, kcs])

                def relu_sum(h):
                    r_ = rl[h % 2][:].bitcast(BF16)
                    if alt and h % 2 == 1:
                        P.ts(r_[:, 0:nk], pb[h % 2][:, 0:nk], 0.0, ALU.max)
                    else:
                        P.act(r_[:, 0:nk], pb[h % 2][:, 0:nk], AF.Relu)
                    return r_
                logits(0)
                for h in range(8):
                    r_ = relu_sum(h)
                    if h + 1 < 8:
                        logits(h + 1)
                    P.matmul(pb[6][:, 0:nk], wdiag[:, h, :], r_[:, 0:nk], start=(h == 0), stop=(h == 7))
                    if h % 2 == 1:
                        yield
                P.copy(dst, pb[6][:, 0:nk], eng='act')
                yield

        def bisect(sc, ncols, lo_cols):
            thr, w0, t_, cnt, hh = bis[:, 0:1], bis[:, 1:2], bis[:, 2:3], bis[:, 3:4], bis[:, 4:5]
            P.reduce(w0, sc[:, 0:ncols], ALU.max)
            P.reduce(thr, sc[:, 0:lo_cols], ALU.min)
            P.ts(thr, thr, -1.0, ALU.add)
            P.tt(w0, w0, thr, ALU.subtract)
            P.ts(wtab[:], p2[:], w0, ALU.mult)
            P.tt(t_, thr, wtab[:, 0:1], ALU.add)
            for k in range(NBIS):
                jk = jk8
                for ci, c0 in enumerate(range(0, ncols, 2048)):
                    wd = min(2048, ncols - c0)
                    P.ts(jk[:, 0:wd], sc[:, c0:c0 + wd], t_, ALU.is_gt, (None if ci == 0 else cnt), ALU.add,
                         accum_out=cnt)
                P.ts(hh, cnt, 255.5, ALU.is_ge, 0.5, ALU.subtract)
                P.stt(t_, hh, wtab[:, k:k + 1], t_, ALU.mult, ALU.add)
                yield
            P.tt(thr, t_, wtab[:, NBIS:NBIS + 1], ALU.subtract)

        def attend(qT, keys, obank_, first, last, mask_cols=None, sc=None):
            n = len(keys)

            def scores_t(i):
                kT, va = keys[i]
                bank = pb[i % 2]
                nm = None
                if mask_cols is not None:
                    if i % 8 == 0:
                        nb_ = min(8, n - i)
                        assert all(mask_cols[i + q_] == mask_cols[i] + q_ * 128 for q_ in range(nb_))
                        P.ts(nmw[:, 0:nb_ * 128], sc[:, mask_cols[i]:mask_cols[i] + nb_ * 128], bis[:, 0:1], ALU.is_le)
                    nm = nmw[:, (i % 8) * 128:(i % 8 + 1) * 128]
                for h in range(4):
                    pr = slice((h % 2) * 64, (h % 2) * 64 + 64)
                    oc = bank[:, h * 128:(h + 1) * 128]
                    P.matmul(oc, kT[pr, h // 2, :], qT[pr, h // 2, :], start=True, stop=False)
                    P.matmul(oc, (nm if nm is not None else zerob[:]), negI[:], start=False, stop=True)

            scores_t(0)
            for i in range(n):
                kT, va = keys[i]
                pt = PT[i % 2]
                P.act(pt[:], pb[i % 2][:].rearrange("p (a b) -> p a b", a=4), AF.Exp, scale=0.125)
                if i + 1 < n:
                    scores_t(i + 1)
                for h in range(4):
                    P.matmul(obank_[:, h * 65:(h + 1) * 65], pt[:, h, :], va[:, h, :],
                             start=(first and i == 0 and h == 0), stop=(last and i == n - 1 and h == 3))
                yield

        def finish_heads(obank_, zcols, dst_cols):
            P.copy(sqB[:, 0:260], obank_[:, 0:260], eng='act')
            o3 = sqB[:, 0:260].rearrange("p (h e) -> p h e", e=65)
            P.copy(rcp[:, 0:4], o3[:, :, 64])
            P.recip(rcp[:, 0:4], rcp[:, 0:4])
            P.tt(o3[:, :, 0:64], o3[:, :, 0:64], rcp[:, 0:4].unsqueeze(2).to_broadcast([128, 4, 64]), ALU.mult)
            P.tt(mixBC[:, dst_cols:dst_cols + 256].rearrange("p (h d) -> p h d", h=4), o3[:, :, 0:64],
                 zs[:, zcols:zcols + 256].rearrange("p (h d) -> p h d", h=4), ALU.mult)
            yield

        def mix_bc_T(bank):
            transpose_bf(mixT[:, 4:8, :], [mixBC[:, j * 128:(j + 1) * 128] for j in range(4)], bank)

        def out_proj(nrows, y_dst, xt):
            for c in range(2):
                bank = pb[c]
                for kt in range(8):
                    P.matmul(bank[:], mixT[:, kt, :], Wo[:, kt, c * 512:(c + 1) * 512], start=(kt == 0), stop=(kt == 7))
                P.tt(yo[:, c * 512:(c + 1) * 512], bank[:], xt[:, c * 512:(c + 1) * 512], ALU.add)
            P.dma(y_dst, yo[0:nrows, :], eng='pool')

        def dbg(ap2d, ncols, stage=None):
            if not kstop:
                return
            i = dbg_n[0]
            dbg_n[0] += 1
            if stage is not None:
                P.copy(stage[:, 0:ncols], ap2d)
                P.dma(dbg_outs[i][:, 0:ncols], stage[:, 0:ncols], eng='pool')
            else:
                P.dma(dbg_outs[i][:, 0:ncols], ap2d, eng='pool')

        def drain(g):
            for _ in g:
                pass

        def count_steps(mk):
            P.dry = True
            n = 0
            for _ in mk():
                n += 1
            P.dry = False
            return n + 1

        def interleave(mka, mkb):
            na, nb = count_steps(mka), count_steps(mkb)
            ga, gb = mka(), mkb()
            ia = ib = 0
            da = db = False
            while not (da and db):
                pick_a = (not da) and (db or ia * nb <= ib * na)
                if pick_a:
                    try:
                        next(ga)
                        ia += 1
                    except StopIteration:
                        da = True
                else:
                    try:
                        next(gb)
                        ib += 1
                    except StopIteration:
                        db = True

        def mixg(g1, g2):
            d1 = d2 = False
            while not (d1 and d2):
                if not d1:
                    try:
                        next(g1)
                        yield
                    except StopIteration:
                        d1 = True
                if not d2:
                    try:
                        next(g2)
                        yield
                    except StopIteration:
                        d2 = True

        def prompt_chainB(tt, outs):
            yield from chainB_heads(128, outs)
            nk = tt + 1
            yield from index_scores(scores, 0, nk, 0)
            dcol = tt * 128
            P.stt(scores[:, dcol:dcol + 128], scores[:, dcol:dcol + 128], 1.0, admP[:], ALU.mult, ALU.mult)
            P.tt(scores[:, dcol:dcol + 128], scores[:, dcol:dcol + 128], nadmP[:], ALU.add)
            mkeys = [(MkT[:, :, j * 128:(j + 1) * 128], MvA[:, j, :].rearrange("p (h e) -> p h e", e=65)) for j in range(2)]

            def cpart():
                yield from attend(QmT, mkeys, pb[7], True, True)
                yield from finish_heads(pb[7], 256, 256)
            if tt >= 2:
                yield from mixg(bisect(scores, nk * 128, (nk - 1) * 128), cpart())
            else:
                P.memset(bis[:, 0:1], -1.0e29)
                yield from cpart()
            keys = [(KbT[:, :, j * 128:(j + 1) * 128], Vaug[:, j, :].rearrange("p (h e) -> p h e", e=65)) for j in range(nk)]
            yield from attend(QbT, keys, pb[6], True, True, mask_cols=[j * 128 for j in range(nk)], sc=scores)
            yield from finish_heads(pb[6], 0, 0)
            mix_bc_T(pb[0])
            yield

        try:
            for mt in range(2):
                mem_tile(mt)
            ck(2)
            P.memset(raw[:, :, 0:3], 0.0)
            P.memset(S[:], 0.0)
            P.memset(Sb[:], 0.0)
            for tt in range(NT):
                r = slice(tt * 128, (tt + 1) * 128)
                outs = {"k": p_k[r, :], "v": p_v[r, :], "ki": p_ki[r, :], "kslot": tt}
                xt = xtb[tt % 2]
                layer_common(x_p[r, :], 128, xt, preloaded=(tt > 0 and not kstop))
                tail = None
                if tt + 1 < NT and not kstop:
                    r2 = slice((tt + 1) * 128, (tt + 2) * 128)
                    tail = (lambda r2=r2, tt=tt: load_h(x_p[r2, :], 128, xtb[(tt + 1) % 2], (pb[2], pb[3])))
                if kstop:
                    drain(chainA(False))
                    drain(prompt_chainB(tt, outs))
                else:
                    interleave(lambda tail=tail: chainA(False, tail), lambda tt=tt, outs=outs: prompt_chainB(tt, outs))
                if tt == NT - 1:
                    conv_out(128, p_conv)
                out_proj(128, y_p[r, :], xt)
                ck(25)
            P.dma(p_ssm.rearrange("h k v -> k h v"), S[:], eng='pool')
            ck(30)

            P.dma(S[:], st_ssm.rearrange("h k v -> k h v"))
            P.copy(Sb[:], S[:])
            P.dma(stc, st_conv)
            for c in range(12):
                P.matmul(pb[2][:, c * 3:(c + 1) * 3], stc[0:3, c * 128:(c + 1) * 128], id32[0:3, 0:3])
            P.copy(raw[:, :, 0:3], pb[2][:, 0:36].rearrange("p (c j) -> p c j", j=3))
            for mt in range(2):
                r = slice(mt * 128, (mt + 1) * 128)
                KV = kvc[mt]
                P.dma(KV[:, 0:256], c_mk[r, :])
                P.dma(KV[:, 256:512], c_mv[r, :])
                P.copy(kvb[:], KV[:, 0:256])
                transpose_bf(MkT[:, :, r], [kvb[:, 0:128], kvb[:, 128:256]], pb[3])
                P.copy(MvA[:, mt, :].rearrange("p (h e) -> p h e", e=65)[:, :, 0:64],
                       KV[:, 256:512].rearrange("p (h d) -> p h d", h=4), eng='pool')
            outs = {"k": s_k, "v": s_v, "ki": s_ki, "kslot": 0}
            layer_common(x_s, 16, xtb[0])
            drain(chainA(True))
            drain(chainB_heads(16, outs))
            conv_out(16, s_conv)
            P.dma(s_ssm.rearrange("h k v -> k h v"), S[:], eng='pool')
            ck(31)
            Wf = W[:].rearrange("p a b -> p (a b)").bitcast(F32)
            Wb = W[:].rearrange("p a b -> p (a b)")
            kiS = Wf[:, 4224:5248].rearrange("p (t c) -> p t c", c=32)
            KS = Wf[:, 5248:9344].rearrange("p (t c) -> p t c", c=256)
            VS = Wf[:, 9344:13440].rearrange("p (t c) -> p t c", c=256)
            Kb16 = Wb[:, 26880:30976].rearrange("p (t c) -> p t c", c=256)
            kib = Wb[:, 18688:19712].rearrange("p (t c) -> p t c", c=32)
            def dma_tiles(dst, src_rows, ntile, step):
                v = src_rows.rearrange("(t p) c -> p t c", p=128)
                for t0 in range(0, ntile, step):
                    P.dma(dst[:, t0:t0 + step, :], v[:, t0:t0 + step, :])
            dma_tiles(kiS, c_ki, 32, 8)
            dma_tiles(KS, c_k[0:2048, :], 16, 8)
            drain(index_scores(scoresS, PAST, 1, 0, alt=True))
            P.tt(scoresS[:, PAST:PAST + 128], scoresS[:, PAST:PAST + 128], nadmS[:], ALU.add)
            P.copy(kib, kiS)
            for g_ in range(2):
                for q4 in range(4):
                    bank = pb[2 + q4]
                    for j in range(4):
                        t_ = g_ * 16 + q4 * 4 + j
                        P.matmul(bank[0:32, j * 128:(j + 1) * 128], kib[:, t_, :], idb[:])
                    P.copy(kiT[:, q4 * 512:(q4 + 1) * 512], bank[0:32, :], eng='act')
                drain(index_scores(scoresS, g_ * 2048, 16, 0, alt=True))
            dma_tiles(VS, c_v[0:2048, :], 16, 8)
            mkeys = [(MkT[:, :, j * 128:(j + 1) * 128], MvA[:, j, :].rearrange("p (h e) -> p h e", e=65)) for j in range(2)]

            def cpart_s():
                yield from attend(QmT, mkeys, pb[7], True, True)
                yield from finish_heads(pb[7], 256, 256)
            drain(mixg(bisect(scoresS, PAST + 128, PAST), cpart_s()))
            ck(32)
            keys = [(KbT[:, :, 0:128], Vaug[:, 0, :].rearrange("p (h e) -> p h e", e=65))]
            drain(attend(QbT, keys, pb[6], True, False, mask_cols=[PAST], sc=scoresS))
            for g_ in range(2):
                if g_ == 1:
                    dma_tiles(KS, c_k[2048:4096, :], 16, 8)
                    dma_tiles(VS, c_v[2048:4096, :], 16, 8)
                P.copy(Kb16[:, 0:6, :], KS[:, 0:6, :])
                P.copy(Kb16[:, 6:11, :], KS[:, 6:11, :], eng='act')
                P.copy(Kb16[:, 11:16, :], KS[:, 11:16, :], eng='pool')
                for q2 in range(8):
                    bank = pb[2 + q2 % 4]
                    for pr_ in range(2):
                        for j in range(2):
                            t_ = q2 * 2 + j
                            P.matmul(bank[:, (pr_ * 2 + j) * 128:(pr_ * 2 + j + 1) * 128],
                                     Kb16[:, t_, pr_ * 128:(pr_ + 1) * 128], idb[:])
                    P.copy(KbT[:, :, q2 * 256:(q2 + 1) * 256], bank[:].rearrange("p (a b) -> p a b", a=2), eng='act')
                v4 = Vaug[:, 0:16, :].rearrange("p t (h e) -> p t h e", e=65)
                for t4 in range(4):
                    P.copy(v4[:, t4 * 4:(t4 + 1) * 4, :, 0:64].rearrange("p t h d -> p (t h) d"),
                           VS[:, t4 * 4:(t4 + 1) * 4, :].rearrange("p t (h d) -> p (t h) d", d=64),
                           eng=('pool' if t4 % 2 == 0 else 'dve'))
                keys = [(KbT[:, :, j * 128:(j + 1) * 128], Vaug[:, j, :].rearrange("p (h e) -> p h e", e=65)) for j in range(16)]
                drain(attend(QbT, keys, pb[6], False, g_ == 1, mask_cols=[g_ * 2048 + j * 128 for j in range(16)], sc=scoresS))
            drain(finish_heads(pb[6], 0, 0))
            mix_bc_T(pb[0])
            out_proj(16, y_s, xtb[0])
        except _Stop:
            pass
        P.emit(final_wait_engine='pool')
    return nc


_CACHE = {}


def _make_in_maps(inp):
    g_all = np.ascontiguousarray(np.concatenate([inp['g_in'][0].reshape(8, 128), inp['g_mem'][0].reshape(8, 128)], 0))
    gsm = np.ascontiguousarray(np.concatenate([
        inp['g_k_B'][0], inp['g_kidx_B'][0], inp['g_k_M'][0], inp['g_q_B'][0], inp['g_q_M'][0],
        inp['dt_bias_A'][0], inp['a_log_A'][0]]).reshape(1, GSM).astype(np.float32))
    maps = []
    for b in range(8):
        maps.append({
            "x_p": np.ascontiguousarray(inp['x_prompt'][b]),
            "x_s": np.ascontiguousarray(inp['x_sample'][b]),
            "mem": np.ascontiguousarray(inp['mem_prompt'][b]),
            "w_in": np.ascontiguousarray(inp['w_in'][0]),
            "w_mem": np.ascontiguousarray(inp['w_mem_kv'][0]),
            "w_out": np.ascontiguousarray(inp['w_out'][0]),
            "g_all": g_all,
            "gsm": gsm,
            "g_o": np.ascontiguousarray(inp['g_o_A'][0].reshape(1, 128)),
            "conv_w": np.ascontiguousarray(inp['conv_w_A'][0]),
            "st_conv": np.ascontiguousarray(inp['state_conv_A'][0, b]),
            "st_ssm": np.ascontiguousarray(inp['state_ssm_A'][0, b]),
            "c_k": np.ascontiguousarray(inp['cache_k_B'][0, b].reshape(PAST, 256)),
            "c_v": np.ascontiguousarray(inp['cache_v_B'][0, b].reshape(PAST, 256)),
            "c_ki": np.ascontiguousarray(inp['cache_kidx_B'][0, b]),
            "c_mk": np.ascontiguousarray(inp['cache_mem_k'][0, b].reshape(256, 256)),
            "c_mv": np.ascontiguousarray(inp['cache_mem_v'][0, b].reshape(256, 256)),
        })
    return maps


def _assemble(res):
    def stack(name, shape):
        return np.stack([np.asarray(r[name], dtype=np.float32).reshape(shape) for r in res], 0)
    outs = (
        stack("y_p", (SEQ, D)), stack("y_s", (16, D)),
        stack("p_conv", (3, 1536))[None],
        stack("p_ssm", (4, 128, 128))[None],
        stack("p_k", (SEQ, 4, 64))[None],
        stack("p_v", (SEQ, 4, 64))[None],
        stack("p_ki", (SEQ, 32))[None],
        stack("p_mk", (256, 4, 64))[None],
        stack("p_mv", (256, 4, 64))[None],
        stack("s_conv", (3, 1536))[None],
        stack("s_ssm", (4, 128, 128))[None],
        stack("s_k", (16, 4, 64))[None],
        stack("s_v", (16, 4, 64))[None],
        stack("s_ki", (16, 32))[None],
    )
    return outs


def kernel(**inputs):
    inp = {k: np.asarray(v) for k, v in inputs.items()}
    nc = build_program()
    in_maps = _make_in_maps(inp)
    res = run_bass_kernel_spmd(nc, in_maps, core_ids=list(range(8)))
    return _assemble(res.results)
```

```python
import numpy as np
import concourse.bass as bass
import concourse.mybir as mybir
from concourse.bass_utils import run_bass_kernel_spmd

F32 = mybir.dt.float32
BF16 = mybir.dt.bfloat16
ALU = mybir.AluOpType
AF = mybir.ActivationFunctionType
AX = mybir.AxisListType


def _region(ap):
    t = ap.tensor
    name = t.name
    esz = mybir.dt.size(ap.dtype)
    pat = tuple((st_ * esz, c_) for st_, c_ in ap.ap)
    off = ap.offset * esz
    space = str(ap.space)
    if 'DRAM' in space.upper() or 'Dram' in space or 'dram' in space:
        lo = off
        hi = off + sum((c - 1) * abs(s) for s, c in pat) + esz
        return (name, 0, 1, lo, hi)
    ps, pc = pat[0]
    if ps == 0:
        ps = 1 << 30
    p0 = off // ps
    fo = off % ps
    hi = fo + sum((c - 1) * abs(s) for s, c in pat[1:]) + esz
    if 'PSUM' in space.upper():
        return (name, (p0 // 32) * 32, ((p0 + pc + 31) // 32) * 32, 0, 1 << 30)
    return (name, p0, p0 + pc, fo, hi)


def _overlap(a, b):
    return a[1] < b[2] and b[1] < a[2] and a[3] < b[4] and b[3] < a[4]


def _contains(a, b):
    return a[1] <= b[1] and a[2] >= b[2] and a[3] <= b[3] and a[4] >= b[4]


class Op:
    __slots__ = ('eng', 'fn', 'reads', 'writes', 'dma', 'deps', 'signals', 'sem', 'val', 'idx', 'pe_acc')

    def __init__(self, eng, fn, reads, writes, dma=False):
        self.eng = eng
        self.fn = fn
        self.reads = reads
        self.writes = writes
        self.dma = dma
        self.deps = set()
        self.signals = False
        self.sem = None
        self.val = 0


class Prog:
    ENGS = ('pe', 'act', 'dve', 'pool', 'sp')

    def __init__(self, nc, n_dma_sems=12, same_engine_sync=True):
        self.nc = nc
        self.ops = []
        self.n_dma_sems = n_dma_sems
        self.same_engine_sync = same_engine_sync
        self.alias = {}

    def set_alias(self, name, group):
        self.alias[name] = group

    def add(self, eng, fn, reads, writes, dma=False):
        if getattr(self, 'dry', False):
            return None
        rr = [_region(a) for a in reads if a is not None and not isinstance(a, (int, float))]
        ww = [_region(a) for a in writes if a is not None]
        op = Op(eng, fn, rr, ww, dma)
        op.idx = len(self.ops)
        self.ops.append(op)
        return op

    def resolve(self):
        writers = {}
        readers = {}
        for op in self.ops:
            for r in op.reads:
                key = self.alias.get(r[0], r[0])
                full = key != r[0]
                for (wr, wi) in writers.get(key, ()):
                    if full or _overlap(r, wr):
                        op.deps.add(wi)
            for w in op.writes:
                key = self.alias.get(w[0], w[0])
                full = key != w[0]
                for (wr, wi) in writers.get(key, ()):
                    if full or _overlap(w, wr):
                        op.deps.add(wi)
                for (rr, ri) in readers.get(key, ()):
                    if full or _overlap(w, rr):
                        op.deps.add(ri)
            for w in op.writes:
                key = self.alias.get(w[0], w[0])
                full = key != w[0]
                if full:
                    writers[key] = [(w, op.idx)]
                    readers[key] = []
                else:
                    writers[key] = [(wr, wi) for (wr, wi) in writers.get(key, ()) if not _contains(w, wr)] + [(w, op.idx)]
                    readers[key] = [(rr, ri) for (rr, ri) in readers.get(key, ()) if not _contains(w, rr)]
            for r in op.reads:
                key = self.alias.get(r[0], r[0])
                lst = readers.setdefault(key, [])
                if not op.dma:
                    lst[:] = [(rr, ri) for (rr, ri) in lst
                              if not (rr == r and self.ops[ri].eng == op.eng and not self.ops[ri].dma)]
                lst.append((r, op.idx))
            op.deps.discard(op.idx)
        for op in self.ops:
            keep = set()
            for d in op.deps:
                o2 = self.ops[d]
                if not o2.dma and not op.dma and o2.eng == op.eng:
                    if op.eng == 'pe':
                        continue
                    if not self.same_engine_sync:
                        continue
                keep.add(d)
            op.deps = keep
            for d in keep:
                self.ops[d].signals = True
        for op in self.ops:
            if op.dma:
                op.signals = True

    def emit(self, final_wait_engine='sp'):
        nc = self.nc
        self.resolve()
        import contextlib
        with contextlib.ExitStack() as st:
            esem = {e: st.enter_context(nc.semaphore('sem_' + e)) for e in ('pe', 'act', 'dve', 'pool')}
            dsem = [st.enter_context(nc.semaphore('dsem%d' % i)) for i in range(self.n_dma_sems)]
            ecount = {e: 0 for e in esem}
            dcount = [0] * self.n_dma_sems
            half = self.n_dma_sems // 2
            kq = {'sp': 0, 'pool': 0, 'act': 0}
            for op in self.ops:
                if not op.signals:
                    continue
                if op.dma:
                    if op.eng == 'pool':
                        i = half + kq['pool'] % (self.n_dma_sems - half)
                        kq['pool'] += 1
                    else:
                        i = kq['sp'] % half
                        kq['sp'] += 1
                    dcount[i] += 16
                    op.sem = ('d', i)
                    op.val = dcount[i]
                else:
                    ecount[op.eng] += 1
                    op.sem = ('e', op.eng)
                    op.val = ecount[op.eng]
            final = {}
            for op in self.ops:
                if op.dma:
                    final[op.sem] = max(final.get(op.sem, 0), op.val)

            def semh(s):
                return esem[s[1]] if s[0] == 'e' else dsem[s[1]]

            per_eng = {e: [o for o in self.ops if o.eng == e] for e in self.ENGS}
            block = st.enter_context(nc.Block())

            def run(engname, eng):
                known = {}
                for op in per_eng[engname]:
                    need = {}
                    for d in op.deps:
                        o2 = self.ops[d]
                        need[o2.sem] = max(need.get(o2.sem, 0), o2.val)
                    if op.dma and op.val > 16:
                        need[op.sem] = max(need.get(op.sem, 0), op.val - 16)
                    for s, v in need.items():
                        if known.get(s, 0) >= v:
                            continue
                        eng.wait_ge(semh(s), v)
                        known[s] = v
                    ins = op.fn(eng)
                    if op.signals:
                        ins.then_inc(semh(op.sem), 16 if op.dma else 1)
                if engname == final_wait_engine:
                    for s, v in final.items():
                        if known.get(s, 0) < v:
                            eng.wait_ge(semh(s), v)

            @block.tensor
            def _(e):
                run('pe', e)

            @block.scalar
            def _(e):
                run('act', e)

            @block.vector
            def _(e):
                run('dve', e)

            @block.gpsimd
            def _(e):
                run('pool', e)

            @block.sync
            def _(e):
                run('sp', e)

    def dma(self, out, in_, eng='sp'):
        return self.add(eng, lambda e: e.dma_start(out=out, in_=in_), [in_], [out], dma=True)

    def matmul(self, out, lhsT, rhs, start=True, stop=True):
        rd = [lhsT, rhs] + ([] if start else [out])
        return self.add('pe', lambda e: e.matmul(out, lhsT, rhs, start=start, stop=stop), rd, [out])

    def transpose(self, out, in_, ident):
        return self.add('pe', lambda e: e.transpose(out, in_, ident), [in_, ident], [out])

    def act(self, out, in_, func, bias=None, scale=None, accum_out=None, eng='act'):
        kw = {}
        rd = [in_]
        if bias is not None:
            kw['bias'] = bias
            rd.append(bias)
        if scale is not None:
            kw['scale'] = scale
            rd.append(scale)
        wr = [out]
        if accum_out is not None:
            kw['accum_out'] = accum_out
            wr.append(accum_out)
        return self.add('act', lambda e: e.activation(out, in_, func, **kw), rd, wr)

    def tt(self, out, in0, in1, op, eng='dve'):
        return self.add(eng, lambda e: e.tensor_tensor(out, in0, in1, op), [in0, in1], [out])

    def ts(self, out, in0, s1, op0, s2=None, op1=None, accum_out=None, eng='dve'):
        rd = [in0, s1, s2]
        wr = [out, accum_out]

        def fn(e):
            kw = {}
            if accum_out is not None:
                kw['accum_out'] = accum_out
            if op1 is not None:
                return e.tensor_scalar(out, in0, s1, s2, op0, op1, **kw)
            return e.tensor_scalar(out, in0, s1, None, op0, **kw)
        return self.add(eng, fn, rd, wr)

    def stt(self, out, in0, scalar, in1, op0, op1, eng='dve'):
        return self.add(eng, lambda e: e.scalar_tensor_tensor(out, in0, scalar, in1, op0, op1),
                        [in0, scalar, in1], [out])

    def copy(self, out, in_, eng='dve'):
        if eng == 'act':
            return self.add('act', lambda e: e.activation(out, in_, AF.Copy), [in_], [out])
        return self.add(eng, lambda e: e.tensor_copy(out, in_), [in_], [out])

    def memset(self, ap, val, eng='dve'):
        return self.add(eng, lambda e: e.memset(ap, val), [], [ap])

    def reduce(self, out, in_, op, axis=AX.X, eng='dve'):
        return self.add(eng, lambda e: e.tensor_reduce(out, in_, axis, op), [in_], [out])

    def recip(self, out, in_):
        return self.add('dve', lambda e: e.reciprocal(out, in_), [in_], [out])


D = 1024
IN_W = 3888
SEQ = 2048
NT = SEQ // 128
PAST = 4096
EPS = 1e-6
C_ZA = 1536
C_TM = 2048
T_BA, T_AA, T_QB, T_KB, T_VB, T_ZB, T_KI, T_WI, T_QM, T_ZM = 0, 4, 8, 264, 520, 776, 1288, 1320, 1328, 1584
C_QI = 3080
TMW = IN_W - C_TM
IDX_SCALE = (8 ** -0.5) * (32 ** -0.5)
NBIS = 16
NEGBIG = -1.0e30
G_KB, G_KI, G_KM, G_QB, G_QM, G_DTB, G_ALOG = 0, 64, 96, 160, 224, 288, 292
GSM = 296


class _Stop(Exception):
    pass


def build_program(kstop=0):
    import contextlib

    def ck(n):
        if kstop == n:
            raise _Stop()
    nc = bass.Bass("TRN2", target_bir_lowering=False)

    def din(name, shape):
        return nc.dram_tensor(name, shape, F32, kind="ExternalInput").ap()

    def dout(name, shape):
        return nc.dram_tensor(name, shape, F32, kind="ExternalOutput").ap()

    x_p = din("x_p", [SEQ, D])
    x_s = din("x_s", [16, D])
    mem = din("mem", [256, D])
    w_in = din("w_in", [D, IN_W])
    w_mem = din("w_mem", [D, 512])
    w_out = din("w_out", [D, D])
    g_all = din("g_all", [16, 128])
    gsm_d = din("gsm", [1, GSM])
    g_o_d = din("g_o", [1, 128])
    convw_d = din("conv_w", [4, 1536])
    st_conv = din("st_conv", [3, 1536])
    st_ssm = din("st_ssm", [4, 128, 128])
    c_k = din("c_k", [PAST, 256])
    c_v = din("c_v", [PAST, 256])
    c_ki = din("c_ki", [PAST, 32])
    c_mk = din("c_mk", [256, 256])
    c_mv = din("c_mv", [256, 256])

    y_p = dout("y_p", [SEQ, D])
    y_s = dout("y_s", [16, D])
    p_conv = dout("p_conv", [3, 1536])
    p_ssm = dout("p_ssm", [4, 128, 128])
    p_k = dout("p_k", [SEQ, 256])
    p_v = dout("p_v", [SEQ, 256])
    p_ki = dout("p_ki", [SEQ, 32])
    p_mk = dout("p_mk", [256, 256])
    p_mv = dout("p_mv", [256, 256])
    s_conv = dout("s_conv", [3, 1536])
    s_ssm = dout("s_ssm", [4, 128, 128])
    s_k = dout("s_k", [16, 256])
    s_v = dout("s_v", [16, 256])
    s_ki = dout("s_ki", [16, 32])

    dbg_outs = [dout("dbg%d" % i, [128, 1024]) for i in range(6)] if kstop else []
    dbg_n = [0]

    with contextlib.ExitStack() as st:
        def sb(name, shape, dt=F32):
            return st.enter_context(nc.sbuf_tensor(name, shape, dt))

        def ps(name, shape, dt=F32):
            return st.enter_context(nc.psum_tensor(name, shape, dt))

        P = Prog(nc, n_dma_sems=16)

        W = sb("W", [128, 8, IN_W], BF16)
        Wo = sb("Wo", [128, 8, D], BF16)
        KbT = sb("KbT", [128, 2, 2048], BF16)
        Wm = KbT[:].rearrange("p a (b c) -> p (a b) c", c=512)
        Vaug = sb("Vaug", [128, 16, 260], BF16)
        kiT = sb("kiT", [32, 2048], BF16)
        scores = sb("scores", [128, 2048])
        g16 = scores[0:16, 1536:1664]
        go1 = scores[0:1, 1664:1792]
        cw4 = scores[0:4, 0:1536]
        stc = scores[0:3, 0:1536]
        scoresS = W[:].rearrange("p a b -> p (a b)").bitcast(F32)[:, 0:4224]
        id32 = sb("id32", [128, 128])
        idb = sb("idb", [128, 128], BF16)
        ones32 = sb("ones32", [128, 128])
        onesb = sb("onesb", [128, 128], BF16)
        negI = sb("negI", [128, 128], BF16)
        zerob = sb("zerob", [128, 128], BF16)
        blkP = sb("blkP", [128, 128])
        blkS = sb("blkS", [128, 128])
        triP = sb("triP", [128, 128])
        triS = sb("triS", [128, 128])
        lsP = sb("lsP", [128, 128])
        lsS = sb("lsS", [128, 128])
        admP = sb("admP", [128, 128])
        nadmP = sb("nadmP", [128, 128])
        nadmS = sb("nadmS", [128, 128])
        gcol = sb("gcol", [128, 16])
        gsm = sb("gsm_t", [128, GSM])
        negA = sb("negA", [128, 4])
        gocol = sb("gocol", [128, 1])
        wc = sb("wc", [128, 12, 4])
        xtb = [sb("xt%d" % i, [128, D]) for i in range(2)]
        jk8 = sb("jk8", [128, 2048], mybir.dt.uint8)
        xs = sb("xs", [128, D], BF16)
        hT = sb("hT", [128, 8, 128], BF16)
        rx = sb("rx", [128, 1])
        tm = sb("tm", [128, TMW])
        raw = sb("raw", [128, 12, 131])
        acc = sb("acc", [128, 12, 128])
        vsb = sb("vsb", [128, 4, 128], BF16)
        zas = sb("zas", [128, 4, 128], BF16)
        zs = sb("zs", [128, 512], BF16)
        sqb = sb("sqb", [128, 8, 128], BF16)
        rbc = sb("rbc", [128, 8, 128])
        qn = sb("qn", [128, 4, 128], BF16)
        knT = sb("knT", [128, 4, 128], BF16)
        qiT = sb("qiT", [32, 8, 128], BF16)
        gt = sb("gt", [128, 28])
        egr = sb("egr", [128, 4, 128])
        DT = sb("DT", [128, 4, 128])
        Xm = sb("Xm", [128, 4, 128])
        Ym = sb("Ym", [128, 4, 128])
        Dg = Xm
        Dst = Ym
        Pm = sb("Pm", [128, 4, 128])
        Tt = sb("Tt", [128, 4, 128], BF16)
        bv = sb("bv", [128, 4, 128], BF16)
        kbg = sb("kbg", [128, 4, 128], BF16)
        kd = sb("kd", [128, 4, 128], BF16)
        nwk = sb("nwk", [128, 4, 128], BF16)
        qd = sb("qd", [128, 4, 128], BF16)
        qkT = sb("qkT", [128, 4, 128], BF16)
        usb = sb("usb", [128, 4, 128], BF16)
        S = sb("S", [128, 4, 128])
        Sb = sb("Sb", [128, 4, 128], BF16)
        mixT = sb("mixT", [128, 8, 128], BF16)
        mixBC = sb("mixBC", [128, 512], BF16)
        s5 = sb("s5", [128, 16])
        rbcf = rbc[:].rearrange("p a b -> p (a b)")
        sq = rbcf
        accf = acc[:].rearrange("p a b -> p (a b)")
        sqB = sb("sqB", [128, 512])
        nrm = sb("nrm", [128, 768], BF16)
        knf = sb("knf", [128, 256])
        kxn = sb("kxn", [128, 32])
        kxb = sb("kxb", [128, 32], BF16)
        QbT = sb("QbT", [128, 2, 128], BF16)
        QmT = sb("QmT", [128, 2, 128], BF16)
        MkT = sb("MkT", [128, 2, 256], BF16)
        MvA = sb("MvA", [128, 2, 260], BF16)
        wdiag = sb("wdiag", [128, 8, 128], BF16)
        wabs = sb("wabs", [128, 8])
        wsgn = sb("wsgn", [128, 8])
        rl = [sb("rl%d" % i, [128, 512]) for i in range(2)]
        kvc = rl
        up32 = rl[0][:, 0:128]
        lo32 = rl[0][:, 128:256]
        PT = [sb("PT%d" % i, [128, 4, 128], BF16) for i in range(2)]
        nmw = sb("nmw", [128, 1024], BF16)
        bis = sb("bis", [128, 8])
        wtab = sb("wtab", [128, NBIS + 1])
        p2 = sb("p2", [128, NBIS + 1])
        rcp = sb("rcp", [128, 8])
        ob = rbcf[:, 768:1024]
        cv = accf
        kvb = sb("kvb", [128, 256], BF16)
        kic = sb("kic", [128, 32])
        yo = accf[:, 0:D]

        pb = [ps("pb%d" % i, [128, 512]) for i in range(8)]


        def aff(out, cmp, mult, pat):
            P.add('pool', lambda e: e.affine_select(out, out, [[pat, 128]], cmp, 0.0, base=0,
                                                    channel_multiplier=mult), [out], [out])
        P.memset(id32[:], 0.0)
        P.add('pool', lambda e: e.affine_select(id32[:], id32[:], [[-1, 128]], ALU.not_equal, 1.0,
                                                base=0, channel_multiplier=1), [id32[:]], [id32[:]])
        P.copy(idb[:], id32[:])
        P.ts(negI[:], id32[:], -30000.0, ALU.mult)
        P.memset(zerob[:], 0.0)
        for k_ in range(NBIS + 1):
            P.memset(p2[:, k_:k_ + 1], 2.0 ** -(k_ + 1), eng='pool')
        P.memset(ones32[:], 1.0)
        P.memset(onesb[:], 1.0)
        P.memset(up32, 1.0)
        aff(up32, ALU.is_ge, -1, 1)
        P.memset(lo32, 1.0)
        aff(lo32, ALU.is_gt, 1, -1)
        P.memset(blkP[:], 0.0)
        P.memset(blkP[0:64, 0:64], 1.0)
        P.memset(blkP[64:128, 64:128], 1.0)
        P.memset(blkS[:], 0.0)
        P.memset(blkS[0:16, 0:16], 1.0)
        P.tt(triP[:], up32, blkP[:], ALU.mult)
        P.tt(triS[:], up32, blkS[:], ALU.mult)
        P.tt(lsP[:], lo32, blkP[:], ALU.mult)
        P.tt(lsS[:], lo32, blkS[:], ALU.mult)
        P.memset(admP[:], 1.0)
        P.memset(admP[0:64, 64:128], 0.0)
        P.ts(nadmP[:], admP[:], -1.0, ALU.add, 1.0e30, ALU.mult)
        P.memset(nadmS[:], NEGBIG)
        P.memset(nadmS[:, 0:16], 0.0)

        P.dma(g16, g_all)
        P.dma(gsm[:], gsm_d.partition_broadcast(128))
        P.dma(go1, g_o_d)
        P.dma(cw4, convw_d)
        P.matmul(pb[0][:, 0:16], g16, id32[0:16, 0:16])
        P.copy(gcol[:], pb[0][:, 0:16])
        P.matmul(pb[0][:, 16:17], go1, id32[0:1, 0:1])
        P.copy(gocol[:], pb[0][:, 16:17])
        for c in range(12):
            P.matmul(pb[1][:, c * 4:(c + 1) * 4], cw4[0:4, c * 128:(c + 1) * 128], id32[0:4, 0:4])
        P.copy(wc[:], pb[1][:, 0:48].rearrange("p (c j) -> p c j", j=4))
        P.act(negA[:], gsm[:, G_ALOG:G_ALOG + 4], AF.Exp)
        P.ts(negA[:], negA[:], -1.0, ALU.mult)

        def cast_w(dst, src, gc, ncols):
            a = ncols * 9 // 25
            b = ncols * 17 // 25
            if gc is None:
                P.copy(dst[:, 0:a], src[:, 0:a])
                P.copy(dst[:, a:b], src[:, a:b], eng='act')
                P.copy(dst[:, b:ncols], src[:, b:ncols], eng='pool')
            else:
                P.ts(dst[:, 0:a], src[:, 0:a], gc, ALU.mult)
                P.act(dst[:, a:b], src[:, a:b], AF.Copy, scale=gc)
                P.ts(dst[:, b:ncols], src[:, b:ncols], gc, ALU.mult, 1.0, ALU.mult, eng='pool')

        CH = IN_W // 3
        stg3 = sb("stg3", [128, IN_W // 3])
        slots = [scores[:, 0:CH], accf[:, 0:CH], stg3[:, 0:CH]]
        k = 0
        for kt in range(8):
            for hf in range(3):
                s_ = slots[k % 3]
                k += 1
                P.dma(s_, w_in[kt * 128:(kt + 1) * 128, hf * CH:(hf + 1) * CH])
                cast_w(W[:, kt, hf * CH:(hf + 1) * CH], s_, gcol[:, kt:kt + 1], CH)
        for kt in range(8):
            s_ = slots[k % 3][:, 0:512]
            k += 1
            P.dma(s_, w_mem[kt * 128:(kt + 1) * 128, :])
            cast_w(Wm[:, kt, :], s_, gcol[:, 8 + kt:9 + kt], 512)
        for kt in range(8):
            s_ = slots[k % 3][:, 0:1024]
            k += 1
            P.dma(s_, w_out[kt * 128:(kt + 1) * 128, :])
            cast_w(Wo[:, kt, :], s_, None, 1024)

        def load_h(x_src, nrows, xt, banks):
            if nrows < 128:
                P.memset(xt[:], 0.0)
            P.dma(xt[0:nrows, :], x_src)
            P.add('dve', lambda e: e.scalar_tensor_tensor(yo, xt[:], 1.0, xt[:], ALU.mult, ALU.mult,
                                                          accum_out=rx[:]), [xt[:]], [yo, rx[:]])
            P.ts(rx[:], rx[:], 1.0 / D, ALU.mult, EPS, ALU.add)
            P.act(rx[:], rx[:], AF.Ln)
            P.act(rx[:], rx[:], AF.Exp, scale=-0.5)
            P.ts(xs[:], xt[:], rx[:], ALU.mult)
            for half in range(2):
                for j in range(4):
                    kt = half * 4 + j
                    P.matmul(banks[half][:, j * 128:(j + 1) * 128], xs[:, kt * 128:(kt + 1) * 128], idb[:])
            P.copy(hT[:, 0:4, :], banks[0][:].rearrange("p (a b) -> p a b", a=4), eng='act')
            P.copy(hT[:, 4:8, :], banks[1][:].rearrange("p (a b) -> p a b", a=4), eng='act')

        def rstd_inplace(ap, scale, eps=EPS, post=None):
            P.ts(ap, ap, scale, ALU.mult, eps, ALU.add)
            P.act(ap, ap, AF.Ln)
            if post is None:
                P.act(ap, ap, AF.Exp, scale=-0.5)
            else:
                P.act(ap, ap, AF.Exp, scale=-0.5, bias=post)

        def transpose_bf(dst, src_list, pbank):
            for j, s_ in enumerate(src_list):
                P.matmul(pbank[:, j * 128:(j + 1) * 128], s_, idb[:])
            n = len(src_list)
            P.copy(dst, pbank[:, 0:n * 128].rearrange("p (a b) -> p a b", a=n), eng='act')

        def mem_tile(mt):
            r = slice(mt * 128, (mt + 1) * 128)
            load_h(mem[r, :], 128, xtb[0], (pb[0], pb[1]))
            for kt in range(8):
                P.matmul(pb[2][:], hT[:, kt, :], Wm[:, kt, :], start=(kt == 0), stop=(kt == 7))
            KV = kvc[mt]
            P.copy(KV[:], pb[2][:], eng='act')
            P.tt(sq[:, 0:256], KV[:, 0:256], KV[:, 0:256], ALU.mult)
            P.reduce(s5[:, 0:4], sq[:, 0:256].rearrange("p (h d) -> p h d", h=4), ALU.add)
            rstd_inplace(s5[:, 0:4], 1.0 / 64)
            k3 = KV[:, 0:256].rearrange("p (h d) -> p h d", h=4)
            n3 = knf[:].rearrange("p (h d) -> p h d", h=4)
            P.tt(n3, k3, s5[:, 0:4].unsqueeze(2).to_broadcast([128, 4, 64]), ALU.mult)
            P.tt(n3, n3, gsm[:, G_KM:G_KM + 64].unsqueeze(1).to_broadcast([128, 4, 64]), ALU.mult)
            P.dma(p_mk[r, :], knf[:], eng='pool')
            P.dma(p_mv[r, :], KV[:, 256:512], eng='pool')
            P.copy(kvb[:], knf[:])
            transpose_bf(MkT[:, :, r], [kvb[:, 0:128], kvb[:, 128:256]], pb[3])
            P.memset(MvA[:, mt, :].rearrange("p (h e) -> p h e", e=65)[:, :, 64:65], 1.0)
            P.copy(MvA[:, mt, :].rearrange("p (h e) -> p h e", e=65)[:, :, 0:64],
                   KV[:, 256:512].rearrange("p (h d) -> p h d", h=4), eng='pool')

        def layer_common(x_src, nrows, xt, preloaded=False):
            if not preloaded:
                load_h(x_src, nrows, xt, (pb[0], pb[1]))
            for c in range(16):
                bank = pb[2 + c // 4]
                for kt in range(8):
                    P.matmul(bank[:, (c % 4) * 128:(c % 4 + 1) * 128], W[:, kt, c * 128:(c + 1) * 128], hT[:, kt, :],
                             start=(kt == 0), stop=(kt == 7))
            for c in range(3):
                P.copy(raw[:, c * 4:(c + 1) * 4, 3:131], pb[2 + c][:].rearrange("p (a b) -> p a b", a=4), eng='act')
            for h in range(8):
                bank = pb[6 + h // 4]
                for kt in range(8):
                    P.matmul(bank[0:32, (h % 4) * 128:(h % 4 + 1) * 128], W[:, kt, C_QI + h * 32:C_QI + (h + 1) * 32],
                             hT[:, kt, :], start=(kt == 0), stop=(kt == 7))
            P.copy(qiT[:, 0:4, :], pb[6][0:32, :].rearrange("p (a b) -> p a b", a=4), eng='act')
            P.copy(qiT[:, 4:8, :], pb[7][0:32, :].rearrange("p (a b) -> p a b", a=4), eng='act')
            ck(10)
            P.copy(zas[:], pb[5][:].rearrange("p (a b) -> p a b", a=4), eng='act')
            for ci, c0 in enumerate(range(0, TMW, 512)):
                wd = min(512, TMW - c0)
                bank = pb[ci % 2]
                for kt in range(8):
                    P.matmul(bank[:, 0:wd], hT[:, kt, :], W[:, kt, C_TM + c0:C_TM + c0 + wd], start=(kt == 0), stop=(kt == 7))
                P.copy(tm[:, c0:c0 + wd], bank[:, 0:wd], eng='act')
            ck(12)

        def chainA(sample, tail=None):
            tri, ls = (triS, lsS) if sample else (triP, lsP)
            beta, gg, gc, nbeta, egc, ekd, c1 = (gt[:, 0:4], gt[:, 4:8], gt[:, 8:12], gt[:, 12:16],
                                                 gt[:, 16:20], gt[:, 20:24], gt[:, 24:28])

            def a1():
                for c in range(12):
                    P.ts(acc[:, c, :], raw[:, c, 0:128], wc[:, c, 0:1], ALU.mult)
                    for j in range(1, 4):
                        P.stt(acc[:, c, :], raw[:, c, j:j + 128], wc[:, c, j:j + 1], acc[:, c, :], ALU.mult, ALU.add)
                    if c % 3 == 2:
                        yield
                P.copy(raw[:, :, 0:3], raw[:, :, 128:131], eng='pool')
                P.act(acc[:, 0:8, :], acc[:, 0:8, :], AF.Silu)
                P.act(vsb[:], acc[:, 8:12, :], AF.Silu)
                P.act(zas[:], zas[:], AF.Silu)
                P.act(zs[:, 0:256], tm[:, T_ZB:T_ZB + 256], AF.Silu)
                P.act(zs[:, 256:512], tm[:, T_ZM:T_ZM + 256], AF.Silu)
                yield
                P.tt(sqb[:], acc[:, 0:8, :], acc[:, 0:8, :], ALU.mult, eng='pool')
                for g_ in range(2):
                    P.matmul(pb[2 + g_][:], onesb[:], sqb[:, g_ * 4:(g_ + 1) * 4, :].rearrange("p a b -> p (a b)"))
                P.act(rbc[:, 0:4, :], pb[2][:].rearrange("p (a b) -> p a b", a=4), AF.Ln, bias=EPS)
                P.act(rbc[:, 4:8, :], pb[3][:].rearrange("p (a b) -> p a b", a=4), AF.Ln, bias=EPS)
                P.act(rbc[:, 0:4, :], rbc[:, 0:4, :], AF.Exp, scale=-0.5, bias=float(np.log(128.0 ** -0.5)))
                P.act(rbc[:, 4:8, :], rbc[:, 4:8, :], AF.Exp, scale=-0.5)
                P.tt(qn[:], acc[:, 0:4, :], rbc[:, 0:4, :], ALU.mult)
                P.tt(knT[:], acc[:, 4:8, :], rbc[:, 4:8, :], ALU.mult)
                if kstop == 13:
                    dbg(raw[:, 4:8, 3:131], 512, stage=None) if False else None
                    dbg(acc[:, 4:8, :].rearrange("p a b -> p (a b)"), 512)
                    dbg(rbc[:, 4:8, :].rearrange("p a b -> p (a b)"), 512)
                    dbg(acc[:, 0:4, :].rearrange("p a b -> p (a b)"), 512)
                    dbg(rbc[:, 0:4, :].rearrange("p a b -> p (a b)"), 512)
                    dbg(knT[:].rearrange("p a b -> p (a b)"), 512, stage=tm)
                    dbg(qn[:].rearrange("p a b -> p (a b)"), 512, stage=tm)
                ck(13)
                yield

                for h in range(4):
                    P.matmul(pb[2][:, h * 128:(h + 1) * 128], knT[:, h, :], idb[:])
                    P.matmul(pb[3][:, h * 128:(h + 1) * 128], vsb[:, h, :], idb[:])

            def a2():
                P.act(beta, tm[:, T_BA:T_BA + 4], AF.Exp, scale=-1.0)
                P.ts(beta, beta, 1.0, ALU.add)
                P.recip(beta, beta)
                P.ts(nbeta, beta, -1.0, ALU.mult)
                P.tt(gg, tm[:, T_AA:T_AA + 4], gsm[:, G_DTB:G_DTB + 4], ALU.add)
                P.act(gg, gg, AF.Exp)
                P.act(gg, gg, AF.Ln, bias=1.0)
                P.tt(gg, gg, negA[:], ALU.mult)
                ck(131)
                yield
                blk = blkS if sample else blkP
                P.matmul(pb[4][:, 0:4], tri[:], gg)
                P.matmul(pb[4][:, 4:8], blk[:], gg)
                P.copy(gc, pb[4][:, 0:4])
                P.tt(ekd, pb[4][:, 4:8], gc, ALU.subtract)
                P.act(ekd, ekd, AF.Exp)
                P.act(egc, gc, AF.Exp)
                P.tt(c1, beta, egc, ALU.mult)
                ck(132)
                yield
                for h in range(4):
                    P.ts(Dg[:, h, :], tri[:], gg[:, h:h + 1], ALU.mult)
                    P.matmul(pb[5][:, h * 128:(h + 1) * 128], ones32[:], Dg[:, h, :])
                gcr = pb[5][:].rearrange("p (a b) -> p a b", a=4)
                ck(133)
                yield
                P.act(egr[:], gcr, AF.Exp)
                ck(134)
                yield
                P.copy(DT[:], gcr, eng='act')
                for h in range(4):
                    P.ts(Dst[:, h, :], DT[:, h, :], gc[:, h:h + 1], ALU.subtract, 0.0, ALU.max)
                    P.ts(DT[:, h, :], DT[:, h, :], gc[:, h:h + 1], ALU.subtract, 0.0, ALU.min)
                ck(135)
                yield
                P.act(Dst[:], Dst[:], AF.Exp, scale=-1.0)
                P.act(DT[:], DT[:], AF.Exp)
                ck(136)
                yield
                P.tt(Dst[:], Dst[:], ls[:].unsqueeze(1).to_broadcast([128, 4, 128]), ALU.mult)
                P.tt(DT[:], DT[:], tri[:].unsqueeze(1).to_broadcast([128, 4, 128]), ALU.mult)
                ck(14)
                yield


            def mix(g1, g2):
                d1 = d2 = False
                while not (d1 and d2):
                    if not d1:
                        try:
                            next(g1)
                            yield
                        except StopIteration:
                            d1 = True
                    if not d2:
                        try:
                            next(g2)
                            yield
                        except StopIteration:
                            d2 = True
            yield from mix(a1(), a2())
            ktm = pb[2][:].rearrange("p (a b) -> p a b", a=4)
            vtm = pb[3][:].rearrange("p (a b) -> p a b", a=4)
            P.tt(bv[:], vtm, beta.unsqueeze(2).to_broadcast([128, 4, 128]), ALU.mult)
            P.tt(kbg[:], ktm, c1.unsqueeze(2).to_broadcast([128, 4, 128]), ALU.mult)
            P.tt(kd[:], ktm, ekd.unsqueeze(2).to_broadcast([128, 4, 128]), ALU.mult)
            ck(15)
            yield

            for h in range(4):
                P.matmul(pb[2][:, h * 128:(h + 1) * 128], knT[:, h, :], knT[:, h, :])
            for h in range(4):
                P.ts(Dst[:, h, :], Dst[:, h, :], nbeta[:, h:h + 1], ALU.mult)
            P.tt(Xm[:], pb[2][:].rearrange("p (a b) -> p a b", a=4), Dst[:], ALU.mult)
            for h in range(4):
                P.matmul(pb[3][:, h * 128:(h + 1) * 128], Xm[:, h, :], id32[:])
            P.copy(Ym[:], pb[3][:].rearrange("p (a b) -> p a b", a=4), eng='act')
            P.tt(Pm[:], Ym[:], id32[:].unsqueeze(1).to_broadcast([128, 4, 128]), ALU.add)
            if kstop == 16:
                dbg(Xm[:].rearrange("p a b -> p (a b)"), 512)
                dbg(Ym[:].rearrange("p a b -> p (a b)"), 512)
                dbg(Pm[:].rearrange("p a b -> p (a b)"), 512)
                dbg(DT[:].rearrange("p a b -> p (a b)"), 512)
                dbg(gt[:, 0:12], 12)
                dbg(knT[:].rearrange("p a b -> p (a b)"), 512, stage=tm)
            ck(16)
            yield
            nlev = 3 if sample else 5
            for lv in range(1, nlev + 1):
                for h in range(4):
                    P.matmul(pb[2][:, h * 128:(h + 1) * 128], Ym[:, h, :], Xm[:, h, :])
                if lv < nlev:
                    for h in range(4):
                        P.matmul(pb[3][:, h * 128:(h + 1) * 128], Xm[:, h, :], Ym[:, h, :])
                P.copy(Xm[:], pb[2][:].rearrange("p (a b) -> p a b", a=4), eng='act')
                if lv < nlev:
                    P.copy(Ym[:], pb[3][:].rearrange("p (a b) -> p a b", a=4), eng='act')
                for h in range(4):
                    P.matmul(pb[4][:, h * 128:(h + 1) * 128], Xm[:, h, :], Pm[:, h, :])
                P.tt(Pm[:], Pm[:], pb[4][:].rearrange("p (a b) -> p a b", a=4), ALU.add)
                yield
            P.copy(Tt[:], Pm[:], eng='act')
            ck(17)
            yield

            for h in range(4):
                P.matmul(pb[2][:, h * 128:(h + 1) * 128], kbg[:, h, :], Tt[:, h, :])
            P.act(nwk[:], pb[2][:].rearrange("p (a b) -> p a b", a=4), AF.Copy, scale=-1.0)
            P.tt(qd[:], qn[:], egr[:], ALU.mult)
            for h in range(4):
                P.matmul(pb[3][:, h * 128:(h + 1) * 128], knT[:, h, :], qn[:, h, :])
            P.tt(qkT[:], pb[3][:].rearrange("p (a b) -> p a b", a=4), DT[:], ALU.mult)
            ck(18)
            yield

            chunks = [(0, 16)] if sample else [(0, 64), (64, 128)]
            obank = pb[4]
            for (r0, r1) in chunks:
                for h in range(4):
                    P.matmul(pb[5][:, h * 128:(h + 1) * 128], Tt[:, h, :], bv[:, h, :], start=True, stop=False)
                    P.matmul(pb[5][:, h * 128:(h + 1) * 128], nwk[:, h, :], Sb[:, h, :], start=False, stop=True)
                P.copy(usb[r0:r1, :, :], pb[5][r0:r1, :].rearrange("p (a b) -> p a b", a=4), eng='act')
                for h in range(4):
                    oc = obank[:, h * 128 + r0:h * 128 + r1]
                    P.matmul(oc, Sb[:, h, :], qd[:, h, r0:r1], start=True, stop=False)
                    P.matmul(oc, usb[r0:r1, h, :], qkT[r0:r1, h, r0:r1], start=False, stop=True)
                for h in range(4):
                    P.matmul(pb[2][:, h * 128:(h + 1) * 128], kd[r0:r1, h, :], usb[r0:r1, h, :])
                for h in range(4):
                    P.ts(S[:, h, :], S[:, h, :], egr[:, h, r1 - 1:r1], ALU.mult)
                P.tt(S[:], S[:], pb[2][:].rearrange("p (a b) -> p a b", a=4), ALU.add)
                P.copy(Sb[:], S[:], eng='act')
                yield
            o3 = obank[:].rearrange("p (a b) -> p a b", a=4)
            ck(19)
            yield
            P.act(sqb[:, 0:4, :], o3, AF.Square)
            P.matmul(pb[3][:], onesb[:], sqb[:, 0:4, :].rearrange("p a b -> p (a b)"))
            P.act(rbc[:, 0:4, :], pb[3][:].rearrange("p (a b) -> p a b", a=4), AF.Ln, scale=1.0 / 128, bias=EPS)
            P.act(rbc[:, 0:4, :], rbc[:, 0:4, :], AF.Exp, scale=-0.5)
            P.tt(rbc[:, 0:4, :], rbc[:, 0:4, :], o3, ALU.mult)
            P.stt(mixT[:, 0:4, :], rbc[:, 0:4, :], gocol[:, 0:1], zas[:], ALU.mult, ALU.mult)
            ck(20)
            yield
            if tail is not None:
                tail()
                yield


        def chainB_heads(nrows, outs):
            P.tt(sqB[:, 0:512], tm[:, T_QB:T_QB + 512], tm[:, T_QB:T_QB + 512], ALU.mult)
            P.reduce(s5[:, 0:8], sqB[:, 0:512].rearrange("p (h d) -> p h d", d=64), ALU.add)
            P.tt(sqB[:, 0:256], tm[:, T_QM:T_QM + 256], tm[:, T_QM:T_QM + 256], ALU.mult)
            P.reduce(s5[:, 8:12], sqB[:, 0:256].rearrange("p (h d) -> p h d", d=64), ALU.add)
            P.tt(sqB[:, 256:288], tm[:, T_KI:T_KI + 32], tm[:, T_KI:T_KI + 32], ALU.mult)
            P.reduce(s5[:, 12:13], sqB[:, 256:288], ALU.add)
            yield
            P.ts(s5[:, 12:13], s5[:, 12:13], 2.0, ALU.mult)
            rstd_inplace(s5[:, 0:13], 1.0 / 64)
            k3 = tm[:, T_KB:T_KB + 256].rearrange("p (h d) -> p h d", h=4)
            n3 = knf[:].rearrange("p (h d) -> p h d", h=4)
            P.tt(n3, k3, s5[:, 4:8].unsqueeze(2).to_broadcast([128, 4, 64]), ALU.mult)
            P.tt(n3, n3, gsm[:, G_KB:G_KB + 64].unsqueeze(1).to_broadcast([128, 4, 64]), ALU.mult)
            P.dma(outs["k"], knf[0:nrows, :], eng='pool')
            P.dma(outs["v"], tm[0:nrows, T_VB:T_VB + 256], eng='pool')
            P.copy(nrm[:, 256:512], knf[:], eng='pool')
            q3 = tm[:, T_QB:T_QB + 256].rearrange("p (h d) -> p h d", h=4)
            yield
            m3 = sqB[:, 0:256].rearrange("p (h d) -> p h d", h=4)
            P.tt(m3, q3, s5[:, 0:4].unsqueeze(2).to_broadcast([128, 4, 64]), ALU.mult)
            P.tt(nrm[:, 0:256].rearrange("p (h d) -> p h d", h=4), m3,
                 gsm[:, G_QB:G_QB + 64].unsqueeze(1).to_broadcast([128, 4, 64]), ALU.mult)
            q3 = tm[:, T_QM:T_QM + 256].rearrange("p (h d) -> p h d", h=4)
            m3 = sqB[:, 256:512].rearrange("p (h d) -> p h d", h=4)
            P.tt(m3, q3, s5[:, 8:12].unsqueeze(2).to_broadcast([128, 4, 64]), ALU.mult)
            P.tt(nrm[:, 512:768].rearrange("p (h d) -> p h d", h=4), m3,
                 gsm[:, G_QM:G_QM + 64].unsqueeze(1).to_broadcast([128, 4, 64]), ALU.mult)
            P.ts(kxn[:], tm[:, T_KI:T_KI + 32], s5[:, 12:13], ALU.mult)
            P.tt(kxn[:], kxn[:], gsm[:, G_KI:G_KI + 32], ALU.mult)
            P.dma(outs["ki"], kxn[0:nrows, :], eng='pool')
            P.copy(kxb[:], kxn[:], eng='pool')
            kslot = outs["kslot"]
            kc = slice(kslot * 128, (kslot + 1) * 128)
            yield
            transpose_bf(QbT[:], [nrm[:, 0:128], nrm[:, 128:256]], pb[0])
            transpose_bf(KbT[:, :, kc], [nrm[:, 256:384], nrm[:, 384:512]], pb[1])
            yield
            transpose_bf(QmT[:], [nrm[:, 512:640], nrm[:, 640:768]], pb[6])
            P.matmul(pb[7][0:32, 0:128], kxb[:], idb[:])
            P.copy(kiT[:, kc], pb[7][0:32, 0:128])
            yield
            va = Vaug[:, kslot, :].rearrange("p (h e) -> p h e", e=65)
            P.memset(va[:, :, 64:65], 1.0)
            P.copy(va[:, :, 0:64], tm[:, T_VB:T_VB + 256].rearrange("p (h d) -> p h d", h=4), eng='pool')
            P.ts(wabs[:], tm[:, T_WI:T_WI + 8], IDX_SCALE, ALU.mult)
            for h in range(8):
                P.ts(wdiag[:, h, :], id32[:], wabs[:, h:h + 1], ALU.mult, eng=('dve' if h % 2 == 0 else 'pool'))
            yield

        def conv_out(nrows, dst):
            for c in range(3):
                pc = pb[c % 2]
                for kt in range(8):
                    P.matmul(pc[:], hT[:, kt, :], W[:, kt, c * 512:(c + 1) * 512], start=(kt == 0), stop=(kt == 7))
                P.copy(cv[:, c * 512:(c + 1) * 512], pc[:], eng=('act' if c % 2 == 0 else 'dve'))
            P.dma(dst, cv[nrows - 3:nrows, :], eng='pool')

        def index_scores(sc, col0, ktiles, kbase, alt=False):
            for g0 in range(0, ktiles, 4):
                nk = min(4, ktiles - g0) * 128
                kcs = slice((kbase + g0) * 128, (kbase + g0) * 128 + nk)
                dst = sc[:, col0 + g0 * 128:col0 + g0 * 128 + nk]

                def logits(h):
                    P.matmul(pb[h % 2][:, 0:nk], qiT[:, h, :], kiT[:, kcs])

                def relu_sum(h):
                    r_ = rl[h % 2][:].bitcast(BF16)
                    if alt and h % 2 == 1:
                        P.ts(r_[:, 0:nk], pb[h % 2][:, 0:nk], 0.0, ALU.max)
                    else:
                        P.act(r_[:, 0:nk], pb[h % 2][:, 0:nk], AF.Relu)
                    return r_
                logits(0)
                for h in range(8):
                    r_ = relu_sum(h)
                    if h + 1 < 8:
                        logits(h + 1)
                    P.matmul(pb[6][:, 0:nk], wdiag[:, h, :], r_[:, 0:nk], start=(h == 0), stop=(h == 7))
                    if h % 2 == 1:
                        yield
                P.copy(dst, pb[6][:, 0:nk], eng='act')
                yield

        def bisect(sc, ncols, lo_cols):
            thr, w0, t_, cnt, hh = bis[:, 0:1], bis[:, 1:2], bis[:, 2:3], bis[:, 3:4], bis[:, 4:5]
            P.reduce(w0, sc[:, 0:ncols], ALU.max)
            P.reduce(thr, sc[:, 0:lo_cols], ALU.min)
            P.ts(thr, thr, -1.0, ALU.add)
            P.tt(w0, w0, thr, ALU.subtract)
            P.ts(wtab[:], p2[:], w0, ALU.mult)
            P.tt(t_, thr, wtab[:, 0:1], ALU.add)
            for k in range(NBIS):
                jk = jk8
                for ci, c0 in enumerate(range(0, ncols, 2048)):
                    wd = min(2048, ncols - c0)
                    P.ts(jk[:, 0:wd], sc[:, c0:c0 + wd], t_, ALU.is_gt, (None if ci == 0 else cnt), ALU.add,
                         accum_out=cnt)
                P.ts(hh, cnt, 255.5, ALU.is_ge, 0.5, ALU.subtract)
                P.stt(t_, hh, wtab[:, k:k + 1], t_, ALU.mult, ALU.add)
                yield
            P.tt(thr, t_, wtab[:, NBIS:NBIS + 1], ALU.subtract)

        def attend(qT, keys, obank_, first, last, mask_cols=None, sc=None):
            n = len(keys)

            def scores_t(i):
                kT, va = keys[i]
                bank = pb[i % 2]
                nm = None
                if mask_cols is not None:
                    if i % 8 == 0:
                        nb_ = min(8, n - i)
                        assert all(mask_cols[i + q_] == mask_cols[i] + q_ * 128 for q_ in range(nb_))
                        P.ts(nmw[:, 0:nb_ * 128], sc[:, mask_cols[i]:mask_cols[i] + nb_ * 128], bis[:, 0:1], ALU.is_le)
                    nm = nmw[:, (i % 8) * 128:(i % 8 + 1) * 128]
                for h in range(4):
                    pr = slice((h % 2) * 64, (h % 2) * 64 + 64)
                    oc = bank[:, h * 128:(h + 1) * 128]
                    P.matmul(oc, kT[pr, h // 2, :], qT[pr, h // 2, :], start=True, stop=False)
                    P.matmul(oc, (nm if nm is not None else zerob[:]), negI[:], start=False, stop=True)

            scores_t(0)
            for i in range(n):
                kT, va = keys[i]
                pt = PT[i % 2]
                P.act(pt[:], pb[i % 2][:].rearrange("p (a b) -> p a b", a=4), AF.Exp, scale=0.125)
                if i + 1 < n:
                    scores_t(i + 1)
                for h in range(4):
                    P.matmul(obank_[:, h * 65:(h + 1) * 65], pt[:, h, :], va[:, h, :],
                             start=(first and i == 0 and h == 0), stop=(last and i == n - 1 and h == 3))
                yield

        def finish_heads(obank_, zcols, dst_cols):
            P.copy(sqB[:, 0:260], obank_[:, 0:260], eng='act')
            o3 = sqB[:, 0:260].rearrange("p (h e) -> p h e", e=65)
            P.copy(rcp[:, 0:4], o3[:, :, 64])
            P.recip(rcp[:, 0:4], rcp[:, 0:4])
            P.tt(o3[:, :, 0:64], o3[:, :, 0:64], rcp[:, 0:4].unsqueeze(2).to_broadcast([128, 4, 64]), ALU.mult)
            P.tt(mixBC[:, dst_cols:dst_cols + 256].rearrange("p (h d) -> p h d", h=4), o3[:, :, 0:64],
                 zs[:, zcols:zcols + 256].rearrange("p (h d) -> p h d", h=4), ALU.mult)
            yield

        def mix_bc_T(bank):
            transpose_bf(mixT[:, 4:8, :], [mixBC[:, j * 128:(j + 1) * 128] for j in range(4)], bank)

        def out_proj(nrows, y_dst, xt):
            for c in range(2):
                bank = pb[c]
                for kt in range(8):
                    P.matmul(bank[:], mixT[:, kt, :], Wo[:, kt, c * 512:(c + 1) * 512], start=(kt == 0), stop=(kt == 7))
                P.tt(yo[:, c * 512:(c + 1) * 512], bank[:], xt[:, c * 512:(c + 1) * 512], ALU.add)
            P.dma(y_dst, yo[0:nrows, :], eng='pool')

        def dbg(ap2d, ncols, stage=None):
            if not kstop:
                return
            i = dbg_n[0]
            dbg_n[0] += 1
            if stage is not None:
                P.copy(stage[:, 0:ncols], ap2d)
                P.dma(dbg_outs[i][:, 0:ncols], stage[:, 0:ncols], eng='pool')
            else:
                P.dma(dbg_outs[i][:, 0:ncols], ap2d, eng='pool')

        def drain(g):
            for _ in g:
                pass

        def count_steps(mk):
            P.dry = True
            n = 0
            for _ in mk():
                n += 1
            P.dry = False
            return n + 1

        def interleave(mka, mkb):
            na, nb = count_steps(mka), count_steps(mkb)
            ga, gb = mka(), mkb()
            ia = ib = 0
            da = db = False
            while not (da and db):
                pick_a = (not da) and (db or ia * nb <= ib * na)
                if pick_a:
                    try:
                        next(ga)
                        ia += 1
                    except StopIteration:
                        da = True
                else:
                    try:
                        next(gb)
                        ib += 1
                    except StopIteration:
                        db = True

        def mixg(g1, g2):
            d1 = d2 = False
            while not (d1 and d2):
                if not d1:
                    try:
                        next(g1)
                        yield
                    except StopIteration:
                        d1 = True
                if not d2:
                    try:
                        next(g2)
                        yield
                    except StopIteration:
                        d2 = True

        def prompt_chainB(tt, outs):
            yield from chainB_heads(128, outs)
            nk = tt + 1
            yield from index_scores(scores, 0, nk, 0)
            dcol = tt * 128
            P.stt(scores[:, dcol:dcol + 128], scores[:, dcol:dcol + 128], 1.0, admP[:], ALU.mult, ALU.mult)
            P.tt(scores[:, dcol:dcol + 128], scores[:, dcol:dcol + 128], nadmP[:], ALU.add)
            mkeys = [(MkT[:, :, j * 128:(j + 1) * 128], MvA[:, j, :].rearrange("p (h e) -> p h e", e=65)) for j in range(2)]

            def cpart():
                yield from attend(QmT, mkeys, pb[7], True, True)
                yield from finish_heads(pb[7], 256, 256)
            if tt >= 2:
                yield from mixg(bisect(scores, nk * 128, (nk - 1) * 128), cpart())
            else:
                P.memset(bis[:, 0:1], -1.0e29)
                yield from cpart()
            keys = [(KbT[:, :, j * 128:(j + 1) * 128], Vaug[:, j, :].rearrange("p (h e) -> p h e", e=65)) for j in range(nk)]
            yield from attend(QbT, keys, pb[6], True, True, mask_cols=[j * 128 for j in range(nk)], sc=scores)
            yield from finish_heads(pb[6], 0, 0)
            mix_bc_T(pb[0])
            yield

        try:
            for mt in range(2):
                mem_tile(mt)
            ck(2)
            P.memset(raw[:, :, 0:3], 0.0)
            P.memset(S[:], 0.0)
            P.memset(Sb[:], 0.0)
            for tt in range(NT):
                r = slice(tt * 128, (tt + 1) * 128)
                outs = {"k": p_k[r, :], "v": p_v[r, :], "ki": p_ki[r, :], "kslot": tt}
                xt = xtb[tt % 2]
                layer_common(x_p[r, :], 128, xt, preloaded=(tt > 0 and not kstop))
                tail = None
                if tt + 1 < NT and not kstop:
                    r2 = slice((tt + 1) * 128, (tt + 2) * 128)
                    tail = (lambda r2=r2, tt=tt: load_h(x_p[r2, :], 128, xtb[(tt + 1) % 2], (pb[2], pb[3])))
                if kstop:
                    drain(chainA(False))
                    drain(prompt_chainB(tt, outs))
                else:
                    interleave(lambda tail=tail: chainA(False, tail), lambda tt=tt, outs=outs: prompt_chainB(tt, outs))
                if tt == NT - 1:
                    conv_out(128, p_conv)
                out_proj(128, y_p[r, :], xt)
                ck(25)
            P.dma(p_ssm.rearrange("h k v -> k h v"), S[:], eng='pool')
            ck(30)

            P.dma(S[:], st_ssm.rearrange("h k v -> k h v"))
            P.copy(Sb[:], S[:])
            P.dma(stc, st_conv)
            for c in range(12):
                P.matmul(pb[2][:, c * 3:(c + 1) * 3], stc[0:3, c * 128:(c + 1) * 128], id32[0:3, 0:3])
            P.copy(raw[:, :, 0:3], pb[2][:, 0:36].rearrange("p (c j) -> p c j", j=3))
            for mt in range(2):
                r = slice(mt * 128, (mt + 1) * 128)
                KV = kvc[mt]
                P.dma(KV[:, 0:256], c_mk[r, :])
                P.dma(KV[:, 256:512], c_mv[r, :])
                P.copy(kvb[:], KV[:, 0:256])
                transpose_bf(MkT[:, :, r], [kvb[:, 0:128], kvb[:, 128:256]], pb[3])
                P.copy(MvA[:, mt, :].rearrange("p (h e) -> p h e", e=65)[:, :, 0:64],
                       KV[:, 256:512].rearrange("p (h d) -> p h d", h=4), eng='pool')
            outs = {"k": s_k, "v": s_v, "ki": s_ki, "kslot": 0}
            layer_common(x_s, 16, xtb[0])
            drain(chainA(True))
            drain(chainB_heads(16, outs))
            conv_out(16, s_conv)
            P.dma(s_ssm.rearrange("h k v -> k h v"), S[:], eng='pool')
            ck(31)
            Wf = W[:].rearrange("p a b -> p (a b)").bitcast(F32)
            Wb = W[:].rearrange("p a b -> p (a b)")
            kiS = Wf[:, 4224:5248].rearrange("p (t c) -> p t c", c=32)
            KS = Wf[:, 5248:9344].rearrange("p (t c) -> p t c", c=256)
            VS = Wf[:, 9344:13440].rearrange("p (t c) -> p t c", c=256)
            Kb16 = Wb[:, 26880:30976].rearrange("p (t c) -> p t c", c=256)
            kib = Wb[:, 18688:19712].rearrange("p (t c) -> p t c", c=32)
            def dma_tiles(dst, src_rows, ntile, step):
                v = src_rows.rearrange("(t p) c -> p t c", p=128)
                for t0 in range(0, ntile, step):
                    P.dma(dst[:, t0:t0 + step, :], v[:, t0:t0 + step, :])
            dma_tiles(kiS, c_ki, 32, 8)
            dma_tiles(KS, c_k[0:2048, :], 16, 8)
            drain(index_scores(scoresS, PAST, 1, 0, alt=True))
            P.tt(scoresS[:, PAST:PAST + 128], scoresS[:, PAST:PAST + 128], nadmS[:], ALU.add)
            P.copy(kib, kiS)
            for g_ in range(2):
                for q4 in range(4):
                    bank = pb[2 + q4]
                    for j in range(4):
                        t_ = g_ * 16 + q4 * 4 + j
                        P.matmul(bank[0:32, j * 128:(j + 1) * 128], kib[:, t_, :], idb[:])
                    P.copy(kiT[:, q4 * 512:(q4 + 1) * 512], bank[0:32, :], eng='act')
                drain(index_scores(scoresS, g_ * 2048, 16, 0, alt=True))
            dma_tiles(VS, c_v[0:2048, :], 16, 8)
            mkeys = [(MkT[:, :, j * 128:(j + 1) * 128], MvA[:, j, :].rearrange("p (h e) -> p h e", e=65)) for j in range(2)]

            def cpart_s():
                yield from attend(QmT, mkeys, pb[7], True, True)
                yield from finish_heads(pb[7], 256, 256)
            drain(mixg(bisect(scoresS, PAST + 128, PAST), cpart_s()))
            ck(32)
            keys = [(KbT[:, :, 0:128], Vaug[:, 0, :].rearrange("p (h e) -> p h e", e=65))]
            drain(attend(QbT, keys, pb[6], True, False, mask_cols=[PAST], sc=scoresS))
            for g_ in range(2):
                if g_ == 1:
                    dma_tiles(KS, c_k[2048:4096, :], 16, 8)
                    dma_tiles(VS, c_v[2048:4096, :], 16, 8)
                P.copy(Kb16[:, 0:6, :], KS[:, 0:6, :])
                P.copy(Kb16[:, 6:11, :], KS[:, 6:11, :], eng='act')
                P.copy(Kb16[:, 11:16, :], KS[:, 11:16, :], eng='pool')
                for q2 in range(8):
                    bank = pb[2 + q2 % 4]
                    for pr_ in range(2):
                        for j in range(2):
                            t_ = q2 * 2 + j
                            P.matmul(bank[:, (pr_ * 2 + j) * 128:(pr_ * 2 + j + 1) * 128],
                                     Kb16[:, t_, pr_ * 128:(pr_ + 1) * 128], idb[:])
                    P.copy(KbT[:, :, q2 * 256:(q2 + 1) * 256], bank[:].rearrange("p (a b) -> p a b", a=2), eng='act')
                v4 = Vaug[:, 0:16, :].rearrange("p t (h e) -> p t h e", e=65)
                for t4 in range(4):
                    P.copy(v4[:, t4 * 4:(t4 + 1) * 4, :, 0:64].rearrange("p t h d -> p (t h) d"),
                           VS[:, t4 * 4:(t4 + 1) * 4, :].rearrange("p t (h d) -> p (t h) d", d=64),
                           eng=('pool' if t4 % 2 == 0 else 'dve'))
                keys = [(KbT[:, :, j * 128:(j + 1) * 128], Vaug[:, j, :].rearrange("p (h e) -> p h e", e=65)) for j in range(16)]
                drain(attend(QbT, keys, pb[6], False, g_ == 1, mask_cols=[g_ * 2048 + j * 128 for j in range(16)], sc=scoresS))
            drain(finish_heads(pb[6], 0, 0))
            mix_bc_T(pb[0])
            out_proj(16, y_s, xtb[0])
        except _Stop:
            pass
        P.emit(final_wait_engine='pool')
    return nc


_CACHE = {}


def _make_in_maps(inp):
    g_all = np.ascontiguousarray(np.concatenate([inp['g_in'][0].reshape(8, 128), inp['g_mem'][0].reshape(8, 128)], 0))
    gsm = np.ascontiguousarray(np.concatenate([
        inp['g_k_B'][0], inp['g_kidx_B'][0], inp['g_k_M'][0], inp['g_q_B'][0], inp['g_q_M'][0],
        inp['dt_bias_A'][0], inp['a_log_A'][0]]).reshape(1, GSM).astype(np.float32))
    maps = []
    for b in range(8):
        maps.append({
            "x_p": np.ascontiguousarray(inp['x_prompt'][b]),
            "x_s": np.ascontiguousarray(inp['x_sample'][b]),
            "mem": np.ascontiguousarray(inp['mem_prompt'][b]),
            "w_in": np.ascontiguousarray(inp['w_in'][0]),
            "w_mem": np.ascontiguousarray(inp['w_mem_kv'][0]),
            "w_out": np.ascontiguousarray(inp['w_out'][0]),
            "g_all": g_all,
            "gsm": gsm,
            "g_o": np.ascontiguousarray(inp['g_o_A'][0].reshape(1, 128)),
            "conv_w": np.ascontiguousarray(inp['conv_w_A'][0]),
            "st_conv": np.ascontiguousarray(inp['state_conv_A'][0, b]),
            "st_ssm": np.ascontiguousarray(inp['state_ssm_A'][0, b]),
            "c_k": np.ascontiguousarray(inp['cache_k_B'][0, b].reshape(PAST, 256)),
            "c_v": np.ascontiguousarray(inp['cache_v_B'][0, b].reshape(PAST, 256)),
            "c_ki": np.ascontiguousarray(inp['cache_kidx_B'][0, b]),
            "c_mk": np.ascontiguousarray(inp['cache_mem_k'][0, b].reshape(256, 256)),
            "c_mv": np.ascontiguousarray(inp['cache_mem_v'][0, b].reshape(256, 256)),
        })
    return maps


def _assemble(res):
    def stack(name, shape):
        return np.stack([np.asarray(r[name], dtype=np.float32).reshape(shape) for r in res], 0)
    outs = (
        stack("y_p", (SEQ, D)), stack("y_s", (16, D)),
        stack("p_conv", (3, 1536))[None],
        stack("p_ssm", (4, 128, 128))[None],
        stack("p_k", (SEQ, 4, 64))[None],
        stack("p_v", (SEQ, 4, 64))[None],
        stack("p_ki", (SEQ, 32))[None],
        stack("p_mk", (256, 4, 64))[None],
        stack("p_mv", (256, 4, 64))[None],
        stack("s_conv", (3, 1536))[None],
        stack("s_ssm", (4, 128, 128))[None],
        stack("s_k", (16, 4, 64))[None],
        stack("s_v", (16, 4, 64))[None],
        stack("s_ki", (16, 32))[None],
    )
    return outs


def kernel(**inputs):
    inp = {k: np.asarray(v) for k, v in inputs.items()}
    nc = build_program()
    in_maps = _make_in_maps(inp)
    res = run_bass_kernel_spmd(nc, in_maps, core_ids=list(range(8)))
    return _assemble(res.results)
```

```python
import numpy as np
import concourse.bass as bass
import concourse.mybir as mybir
from concourse.bass_utils import run_bass_kernel_spmd

F32 = mybir.dt.float32
BF16 = mybir.dt.bfloat16
ALU = mybir.AluOpType
AF = mybir.ActivationFunctionType
AX = mybir.AxisListType


def _region(ap):
    t = ap.tensor
    name = t.name
    esz = mybir.dt.size(ap.dtype)
    pat = tuple((st_ * esz, c_) for st_, c_ in ap.ap)
    off = ap.offset * esz
    space = str(ap.space)
    if 'DRAM' in space.upper() or 'Dram' in space or 'dram' in space:
        lo = off
        hi = off + sum((c - 1) * abs(s) for s, c in pat) + esz
        return (name, 0, 1, lo, hi)
    ps, pc = pat[0]
    if ps == 0:
        ps = 1 << 30
    p0 = off // ps
    fo = off % ps
    hi = fo + sum((c - 1) * abs(s) for s, c in pat[1:]) + esz
    if 'PSUM' in space.upper():
        return (name, (p0 // 32) * 32, ((p0 + pc + 31) // 32) * 32, 0, 1 << 30)
    return (name, p0, p0 + pc, fo, hi)


def _overlap(a, b):
    return a[1] < b[2] and b[1] < a[2] and a[3] < b[4] and b[3] < a[4]


def _contains(a, b):
    return a[1] <= b[1] and a[2] >= b[2] and a[3] <= b[3] and a[4] >= b[4]


class Op:
    __slots__ = ('eng', 'fn', 'reads', 'writes', 'dma', 'deps', 'signals', 'sem', 'val', 'idx', 'pe_acc')

    def __init__(self, eng, fn, reads, writes, dma=False):
        self.eng = eng
        self.fn = fn
        self.reads = reads
        self.writes = writes
        self.dma = dma
        self.deps = set()
        self.signals = False
        self.sem = None
        self.val = 0


class Prog:
    ENGS = ('pe', 'act', 'dve', 'pool', 'sp')

    def __init__(self, nc, n_dma_sems=12, same_engine_sync=True):
        self.nc = nc
        self.ops = []
        self.n_dma_sems = n_dma_sems
        self.same_engine_sync = same_engine_sync
        self.alias = {}

    def set_alias(self, name, group):
        self.alias[name] = group

    def add(self, eng, fn, reads, writes, dma=False):
        if getattr(self, 'dry', False):
            return None
        rr = [_region(a) for a in reads if a is not None and not isinstance(a, (int, float))]
        ww = [_region(a) for a in writes if a is not None]
        op = Op(eng, fn, rr, ww, dma)
        op.idx = len(self.ops)
        self.ops.append(op)
        return op

    def resolve(self):
        writers = {}
        readers = {}
        for op in self.ops:
            for r in op.reads:
                key = self.alias.get(r[0], r[0])
                full = key != r[0]
                for (wr, wi) in writers.get(key, ()):
                    if full or _overlap(r, wr):
                        op.deps.add(wi)
            for w in op.writes:
                key = self.alias.get(w[0], w[0])
                full = key != w[0]
                for (wr, wi) in writers.get(key, ()):
                    if full or _overlap(w, wr):
                        op.deps.add(wi)
                for (rr, ri) in readers.get(key, ()):
                    if full or _overlap(w, rr):
                        op.deps.add(ri)
            for w in op.writes:
                key = self.alias.get(w[0], w[0])
                full = key != w[0]
                if full:
                    writers[key] = [(w, op.idx)]
                    readers[key] = []
                else:
                    writers[key] = [(wr, wi) for (wr, wi) in writers.get(key, ()) if not _contains(w, wr)] + [(w, op.idx)]
                    readers[key] = [(rr, ri) for (rr, ri) in readers.get(key, ()) if not _contains(w, rr)]
            for r in op.reads:
                key = self.alias.get(r[0], r[0])
                lst = readers.setdefault(key, [])
                if not op.dma:
                    lst[:] = [(rr, ri) for (rr, ri) in lst
                              if not (rr == r and self.ops[ri].eng == op.eng and not self.ops[ri].dma)]
                lst.append((r, op.idx))
            op.deps.discard(op.idx)
        for op in self.ops:
            keep = set()
            for d in op.deps:
                o2 = self.ops[d]
                if not o2.dma and not op.dma and o2.eng == op.eng:
                    if op.eng == 'pe':
                        continue
                    if not self.same_engine_sync:
                        continue
                keep.add(d)
            op.deps = keep
            for d in keep:
                self.ops[d].signals = True
        for op in self.ops:
            if op.dma:
                op.signals = True

    def emit(self, final_wait_engine='sp'):
        nc = self.nc
        self.resolve()
        import contextlib
        with contextlib.ExitStack() as st:
            esem = {e: st.enter_context(nc.semaphore('sem_' + e)) for e in ('pe', 'act', 'dve', 'pool')}
            dsem = [st.enter_context(nc.semaphore('dsem%d' % i)) for i in range(self.n_dma_sems)]
            ecount = {e: 0 for e in esem}
            dcount = [0] * self.n_dma_sems
            half = self.n_dma_sems // 2
            kq = {'sp': 0, 'pool': 0, 'act': 0}
            for op in self.ops:
                if not op.signals:
                    continue
                if op.dma:
                    if op.eng == 'pool':
                        i = half + kq['pool'] % (self.n_dma_sems - half)
                        kq['pool'] += 1
                    else:
                        i = kq['sp'] % half
                        kq['sp'] += 1
                    dcount[i] += 16
                    op.sem = ('d', i)
                    op.val = dcount[i]
                else:
                    ecount[op.eng] += 1
                    op.sem = ('e', op.eng)
                    op.val = ecount[op.eng]
            final = {}
            for op in self.ops:
                if op.dma:
                    final[op.sem] = max(final.get(op.sem, 0), op.val)

            def semh(s):
                return esem[s[1]] if s[0] == 'e' else dsem[s[1]]

            per_eng = {e: [o for o in self.ops if o.eng == e] for e in self.ENGS}
            block = st.enter_context(nc.Block())

            def run(engname, eng):
                known = {}
                for op in per_eng[engname]:
                    need = {}
                    for d in op.deps:
                        o2 = self.ops[d]
                        need[o2.sem] = max(need.get(o2.sem, 0), o2.val)
                    if op.dma and op.val > 16:
                        need[op.sem] = max(need.get(op.sem, 0), op.val - 16)
                    for s, v in need.items():
                        if known.get(s, 0) >= v:
                            continue
                        eng.wait_ge(semh(s), v)
                        known[s] = v
                    ins = op.fn(eng)
                    if op.signals:
                        ins.then_inc(semh(op.sem), 16 if op.dma else 1)
                if engname == final_wait_engine:
                    for s, v in final.items():
                        if known.get(s, 0) < v:
                            eng.wait_ge(semh(s), v)

            @block.tensor
            def _(e):
                run('pe', e)

            @block.scalar
            def _(e):
                run('act', e)

            @block.vector
            def _(e):
                run('dve', e)

            @block.gpsimd
            def _(e):
                run('pool', e)

            @block.sync
            def _(e):
                run('sp', e)

    def dma(self, out, in_, eng='sp'):
        return self.add(eng, lambda e: e.dma_start(out=out, in_=in_), [in_], [out], dma=True)

    def matmul(self, out, lhsT, rhs, start=True, stop=True):
        rd = [lhsT, rhs] + ([] if start else [out])
        return self.add('pe', lambda e: e.matmul(out, lhsT, rhs, start=start, stop=stop), rd, [out])

    def transpose(self, out, in_, ident):
        return self.add('pe', lambda e: e.transpose(out, in_, ident), [in_, ident], [out])

    def act(self, out, in_, func, bias=None, scale=None, accum_out=None, eng='act'):
        kw = {}
        rd = [in_]
        if bias is not None:
            kw['bias'] = bias
            rd.append(bias)
        if scale is not None:
            kw['scale'] = scale
            rd.append(scale)
        wr = [out]
        if accum_out is not None:
            kw['accum_out'] = accum_out
            wr.append(accum_out)
        return self.add('act', lambda e: e.activation(out, in_, func, **kw), rd, wr)

    def tt(self, out, in0, in1, op, eng='dve'):
        return self.add(eng, lambda e: e.tensor_tensor(out, in0, in1, op), [in0, in1], [out])

    def ts(self, out, in0, s1, op0, s2=None, op1=None, accum_out=None, eng='dve'):
        rd = [in0, s1, s2]
        wr = [out, accum_out]

        def fn(e):
            kw = {}
            if accum_out is not None:
                kw['accum_out'] = accum_out
            if op1 is not None:
                return e.tensor_scalar(out, in0, s1, s2, op0, op1, **kw)
            return e.tensor_scalar(out, in0, s1, None, op0, **kw)
        return self.add(eng, fn, rd, wr)

    def stt(self, out, in0, scalar, in1, op0, op1, eng='dve'):
        return self.add(eng, lambda e: e.scalar_tensor_tensor(out, in0, scalar, in1, op0, op1),
                        [in0, scalar, in1], [out])

    def copy(self, out, in_, eng='dve'):
        if eng == 'act':
            return self.add('act', lambda e: e.activation(out, in_, AF.Copy), [in_], [out])
        return self.add(eng, lambda e: e.tensor_copy(out, in_), [in_], [out])

    def memset(self, ap, val, eng='dve'):
        return self.add(eng, lambda e: e.memset(ap, val), [], [ap])

    def reduce(self, out, in_, op, axis=AX.X, eng='dve'):
        return self.add(eng, lambda e: e.tensor_reduce(out, in_, axis, op), [in_], [out])

    def recip(self, out, in_):
        return self.add('dve', lambda e: e.reciprocal(out, in_), [in_], [out])


D = 1024
IN_W = 3888
SEQ = 2048
NT = SEQ // 128
PAST = 4096
EPS = 1e-6
C_ZA = 1536
C_TM = 2048
T_BA, T_AA, T_QB, T_KB, T_VB, T_ZB, T_KI, T_WI, T_QM, T_ZM = 0, 4, 8, 264, 520, 776, 1288, 1320, 1328, 1584
C_QI = 3080
TMW = IN_W - C_TM
IDX_SCALE = (8 ** -0.5) * (32 ** -0.5)
NBIS = 16
NEGBIG = -1.0e30
G_KB, G_KI, G_KM, G_QB, G_QM, G_DTB, G_ALOG = 0, 64, 96, 160, 224, 288, 292
GSM = 296


class _Stop(Exception):
    pass


def build_program(kstop=0):
    import contextlib

    def ck(n):
        if kstop == n:
            raise _Stop()
    nc = bass.Bass("TRN2", target_bir_lowering=False)

    def din(name, shape):
        return nc.dram_tensor(name, shape, F32, kind="ExternalInput").ap()

    def dout(name, shape):
        return nc.dram_tensor(name, shape, F32, kind="ExternalOutput").ap()

    x_p = din("x_p", [SEQ, D])
    x_s = din("x_s", [16, D])
    mem = din("mem", [256, D])
    w_in = din("w_in", [D, IN_W])
    w_mem = din("w_mem", [D, 512])
    w_out = din("w_out", [D, D])
    g_all = din("g_all", [16, 128])
    gsm_d = din("gsm", [1, GSM])
    g_o_d = din("g_o", [1, 128])
    convw_d = din("conv_w", [4, 1536])
    st_conv = din("st_conv", [3, 1536])
    st_ssm = din("st_ssm", [4, 128, 128])
    c_k = din("c_k", [PAST, 256])
    c_v = din("c_v", [PAST, 256])
    c_ki = din("c_ki", [PAST, 32])
    c_mk = din("c_mk", [256, 256])
    c_mv = din("c_mv", [256, 256])

    y_p = dout("y_p", [SEQ, D])
    y_s = dout("y_s", [16, D])
    p_conv = dout("p_conv", [3, 1536])
    p_ssm = dout("p_ssm", [4, 128, 128])
    p_k = dout("p_k", [SEQ, 256])
    p_v = dout("p_v", [SEQ, 256])
    p_ki = dout("p_ki", [SEQ, 32])
    p_mk = dout("p_mk", [256, 256])
    p_mv = dout("p_mv", [256, 256])
    s_conv = dout("s_conv", [3, 1536])
    s_ssm = dout("s_ssm", [4, 128, 128])
    s_k = dout("s_k", [16, 256])
    s_v = dout("s_v", [16, 256])
    s_ki = dout("s_ki", [16, 32])

    dbg_outs = [dout("dbg%d" % i, [128, 1024]) for i in range(6)] if kstop else []
    dbg_n = [0]

    with contextlib.ExitStack() as st:
        def sb(name, shape, dt=F32):
            return st.enter_context(nc.sbuf_tensor(name, shape, dt))

        def ps(name, shape, dt=F32):
            return st.enter_context(nc.psum_tensor(name, shape, dt))

        P = Prog(nc, n_dma_sems=16)

        W = sb("W", [128, 8, IN_W], BF16)
        Wo = sb("Wo", [128, 8, D], BF16)
        KbT = sb("KbT", [128, 2, 2048], BF16)
        Wm = KbT[:].rearrange("p a (b c) -> p (a b) c", c=512)
        Vaug = sb("Vaug", [128, 16, 260], BF16)
        kiT = sb("kiT", [32, 2048], BF16)
        scores = sb("scores", [128, 2048])
        g16 = scores[0:16, 1536:1664]
        go1 = scores[0:1, 1664:1792]
        cw4 = scores[0:4, 0:1536]
        stc = scores[0:3, 0:1536]
        scoresS = W[:].rearrange("p a b -> p (a b)").bitcast(F32)[:, 0:4224]
        id32 = sb("id32", [128, 128])
        idb = sb("idb", [128, 128], BF16)
        ones32 = sb("ones32", [128, 128])
        onesb = sb("onesb", [128, 128], BF16)
        negI = sb("negI", [128, 128], BF16)
        zerob = sb("zerob", [128, 128], BF16)
        blkP = sb("blkP", [128, 128])
        blkS = sb("blkS", [128, 128])
        triP = sb("triP", [128, 128])
        triS = sb("triS", [128, 128])
        lsP = sb("lsP", [128, 128])
        lsS = sb("lsS", [128, 128])
        admP = sb("admP", [128, 128])
        nadmP = sb("nadmP", [128, 128])
        nadmS = sb("nadmS", [128, 128])
        gcol = sb("gcol", [128, 16])
        gsm = sb("gsm_t", [128, GSM])
        negA = sb("negA", [128, 4])
        gocol = sb("gocol", [128, 1])
        wc = sb("wc", [128, 12, 4])
        xtb = [sb("xt%d" % i, [128, D]) for i in range(2)]
        jk8 = sb("jk8", [128, 2048], mybir.dt.uint8)
        xs = sb("xs", [128, D], BF16)
        hT = sb("hT", [128, 8, 128], BF16)
        rx = sb("rx", [128, 1])
        tm = sb("tm", [128, TMW])
        raw = sb("raw", [128, 12, 131])
        acc = sb("acc", [128, 12, 128])
        vsb = sb("vsb", [128, 4, 128], BF16)
        zas = sb("zas", [128, 4, 128], BF16)
        zs = sb("zs", [128, 512], BF16)
        sqb = sb("sqb", [128, 8, 128], BF16)
        rbc = sb("rbc", [128, 8, 128])
        qn = sb("qn", [128, 4, 128], BF16)
        knT = sb("knT", [128, 4, 128], BF16)
        qiT = sb("qiT", [32, 8, 128], BF16)
        gt = sb("gt", [128, 28])
        egr = sb("egr", [128, 4, 128])
        DT = sb("DT", [128, 4, 128])
        Xm = sb("Xm", [128, 4, 128])
        Ym = sb("Ym", [128, 4, 128])
        Dg = Xm
        Dst = Ym
        Pm = sb("Pm", [128, 4, 128])
        Tt = sb("Tt", [128, 4, 128], BF16)
        bv = sb("bv", [128, 4, 128], BF16)
        kbg = sb("kbg", [128, 4, 128], BF16)
        kd = sb("kd", [128, 4, 128], BF16)
        nwk = sb("nwk", [128, 4, 128], BF16)
        qd = sb("qd", [128, 4, 128], BF16)
        qkT = sb("qkT", [128, 4, 128], BF16)
        usb = sb("usb", [128, 4, 128], BF16)
        S = sb("S", [128, 4, 128])
        Sb = sb("Sb", [128, 4, 128], BF16)
        mixT = sb("mixT", [128, 8, 128], BF16)
        mixBC = sb("mixBC", [128, 512], BF16)
        s5 = sb("s5", [128, 16])
        rbcf = rbc[:].rearrange("p a b -> p (a b)")
        sq = rbcf
        accf = acc[:].rearrange("p a b -> p (a b)")
        sqB = sb("sqB", [128, 512])
        nrm = sb("nrm", [128, 768], BF16)
        knf = sb("knf", [128, 256])
        kxn = sb("kxn", [128, 32])
        kxb = sb("kxb", [128, 32], BF16)
        QbT = sb("QbT", [128, 2, 128], BF16)
        QmT = sb("QmT", [128, 2, 128], BF16)
        MkT = sb("MkT", [128, 2, 256], BF16)
        MvA = sb("MvA", [128, 2, 260], BF16)
        wdiag = sb("wdiag", [128, 8, 128], BF16)
        wabs = sb("wabs", [128, 8])
        wsgn = sb("wsgn", [128, 8])
        rl = [sb("rl%d" % i, [128, 512]) for i in range(2)]
        kvc = rl
        up32 = rl[0][:, 0:128]
        lo32 = rl[0][:, 128:256]
        PT = [sb("PT%d" % i, [128, 4, 128], BF16) for i in range(2)]
        nmw = sb("nmw", [128, 1024], BF16)
        bis = sb("bis", [128, 8])
        wtab = sb("wtab", [128, NBIS + 1])
        p2 = sb("p2", [128, NBIS + 1])
        rcp = sb("rcp", [128, 8])
        ob = rbcf[:, 768:1024]
        cv = accf
        kvb = sb("kvb", [128, 256], BF16)
        kic = sb("kic", [128, 32])
        yo = accf[:, 0:D]

        pb = [ps("pb%d" % i, [128, 512]) for i in range(8)]


        def aff(out, cmp, mult, pat):
            P.add('pool', lambda e: e.affine_select(out, out, [[pat, 128]], cmp, 0.0, base=0,
                                                    channel_multiplier=mult), [out], [out])
        P.memset(id32[:], 0.0)
        P.add('pool', lambda e: e.affine_select(id32[:], id32[:], [[-1, 128]], ALU.not_equal, 1.0,
                                                base=0, channel_multiplier=1), [id32[:]], [id32[:]])
        P.copy(idb[:], id32[:])
        P.ts(negI[:], id32[:], -30000.0, ALU.mult)
        P.memset(zerob[:], 0.0)
        for k_ in range(NBIS + 1):
            P.memset(p2[:, k_:k_ + 1], 2.0 ** -(k_ + 1), eng='pool')
        P.memset(ones32[:], 1.0)
        P.memset(onesb[:], 1.0)
        P.memset(up32, 1.0)
        aff(up32, ALU.is_ge, -1, 1)
        P.memset(lo32, 1.0)
        aff(lo32, ALU.is_gt, 1, -1)
        P.memset(blkP[:], 0.0)
        P.memset(blkP[0:64, 0:64], 1.0)
        P.memset(blkP[64:128, 64:128], 1.0)
        P.memset(blkS[:], 0.0)
        P.memset(blkS[0:16, 0:16], 1.0)
        P.tt(triP[:], up32, blkP[:], ALU.mult)
        P.tt(triS[:], up32, blkS[:], ALU.mult)
        P.tt(lsP[:], lo32, blkP[:], ALU.mult)
        P.tt(lsS[:], lo32, blkS[:], ALU.mult)
        P.memset(admP[:], 1.0)
        P.memset(admP[0:64, 64:128], 0.0)
        P.ts(nadmP[:], admP[:], -1.0, ALU.add, 1.0e30, ALU.mult)
        P.memset(nadmS[:], NEGBIG)
        P.memset(nadmS[:, 0:16], 0.0)

        P.dma(g16, g_all)
        P.dma(gsm[:], gsm_d.partition_broadcast(128))
        P.dma(go1, g_o_d)
        P.dma(cw4, convw_d)
        P.matmul(pb[0][:, 0:16], g16, id32[0:16, 0:16])
        P.copy(gcol[:], pb[0][:, 0:16])
        P.matmul(pb[0][:, 16:17], go1, id32[0:1, 0:1])
        P.copy(gocol[:], pb[0][:, 16:17])
        for c in range(12):
            P.matmul(pb[1][:, c * 4:(c + 1) * 4], cw4[0:4, c * 128:(c + 1) * 128], id32[0:4, 0:4])
        P.copy(wc[:], pb[1][:, 0:48].rearrange("p (c j) -> p c j", j=4))
        P.act(negA[:], gsm[:, G_ALOG:G_ALOG + 4], AF.Exp)
        P.ts(negA[:], negA[:], -1.0, ALU.mult)

        def cast_w(dst, src, gc, ncols):
            a = ncols * 9 // 25
            b = ncols * 17 // 25
            if gc is None:
                P.copy(dst[:, 0:a], src[:, 0:a])
                P.copy(dst[:, a:b], src[:, a:b], eng='act')
                P.copy(dst[:, b:ncols], src[:, b:ncols], eng='pool')
            else:
                P.ts(dst[:, 0:a], src[:, 0:a], gc, ALU.mult)
                P.act(dst[:, a:b], src[:, a:b], AF.Copy, scale=gc)
                P.ts(dst[:, b:ncols], src[:, b:ncols], gc, ALU.mult, 1.0, ALU.mult, eng='pool')

        CH = IN_W // 3
        stg3 = sb("stg3", [128, IN_W // 3])
        slots = [scores[:, 0:CH], accf[:, 0:CH], stg3[:, 0:CH]]
        k = 0
        for kt in range(8):
            for hf in range(3):
                s_ = slots[k % 3]
                k += 1
                P.dma(s_, w_in[kt * 128:(kt + 1) * 128, hf * CH:(hf + 1) * CH])
                cast_w(W[:, kt, hf * CH:(hf + 1) * CH], s_, gcol[:, kt:kt + 1], CH)
        for kt in range(8):
            s_ = slots[k % 3][:, 0:512]
            k += 1
            P.dma(s_, w_mem[kt * 128:(kt + 1) * 128, :])
            cast_w(Wm[:, kt, :], s_, gcol[:, 8 + kt:9 + kt], 512)
        for kt in range(8):
            s_ = slots[k % 3][:, 0:1024]
            k += 1
            P.dma(s_, w_out[kt * 128:(kt + 1) * 128, :])
            cast_w(Wo[:, kt, :], s_, None, 1024)

        def load_h(x_src, nrows, xt, banks):
            if nrows < 128:
                P.memset(xt[:], 0.0)
            P.dma(xt[0:nrows, :], x_src)
            P.add('dve', lambda e: e.scalar_tensor_tensor(yo, xt[:], 1.0, xt[:], ALU.mult, ALU.mult,
                                                          accum_out=rx[:]), [xt[:]], [yo, rx[:]])
            P.ts(rx[:], rx[:], 1.0 / D, ALU.mult, EPS, ALU.add)
            P.act(rx[:], rx[:], AF.Ln)
            P.act(rx[:], rx[:], AF.Exp, scale=-0.5)
            P.ts(xs[:], xt[:], rx[:], ALU.mult)
            for half in range(2):
                for j in range(4):
                    kt = half * 4 + j
                    P.matmul(banks[half][:, j * 128:(j + 1) * 128], xs[:, kt * 128:(kt + 1) * 128], idb[:])
            P.copy(hT[:, 0:4, :], banks[0][:].rearrange("p (a b) -> p a b", a=4), eng='act')
            P.copy(hT[:, 4:8, :], banks[1][:].rearrange("p (a b) -> p a b", a=4), eng='act')

        def rstd_inplace(ap, scale, eps=EPS, post=None):
            P.ts(ap, ap, scale, ALU.mult, eps, ALU.add)
            P.act(ap, ap, AF.Ln)
            if post is None:
                P.act(ap, ap, AF.Exp, scale=-0.5)
            else:
                P.act(ap, ap, AF.Exp, scale=-0.5, bias=post)

        def transpose_bf(dst, src_list, pbank):
            for j, s_ in enumerate(src_list):
                P.matmul(pbank[:, j * 128:(j + 1) * 128], s_, idb[:])
            n = len(src_list)
            P.copy(dst, pbank[:, 0:n * 128].rearrange("p (a b) -> p a b", a=n), eng='act')

        def mem_tile(mt):
            r = slice(mt * 128, (mt + 1) * 128)
            load_h(mem[r, :], 128, xtb[0], (pb[0], pb[1]))
            for kt in range(8):
                P.matmul(pb[2][:], hT[:, kt, :], Wm[:, kt, :], start=(kt == 0), stop=(kt == 7))
            KV = kvc[mt]
            P.copy(KV[:], pb[2][:], eng='act')
            P.tt(sq[:, 0:256], KV[:, 0:256], KV[:, 0:256], ALU.mult)
            P.reduce(s5[:, 0:4], sq[:, 0:256].rearrange("p (h d) -> p h d", h=4), ALU.add)
            rstd_inplace(s5[:, 0:4], 1.0 / 64)
            k3 = KV[:, 0:256].rearrange("p (h d) -> p h d", h=4)
            n3 = knf[:].rearrange("p (h d) -> p h d", h=4)
            P.tt(n3, k3, s5[:, 0:4].unsqueeze(2).to_broadcast([128, 4, 64]), ALU.mult)
            P.tt(n3, n3, gsm[:, G_KM:G_KM + 64].unsqueeze(1).to_broadcast([128, 4, 64]), ALU.mult)
            P.dma(p_mk[r, :], knf[:], eng='pool')
            P.dma(p_mv[r, :], KV[:, 256:512], eng='pool')
            P.copy(kvb[:], knf[:])
            transpose_bf(MkT[:, :, r], [kvb[:, 0:128], kvb[:, 128:256]], pb[3])
            P.memset(MvA[:, mt, :].rearrange("p (h e) -> p h e", e=65)[:, :, 64:65], 1.0)
            P.copy(MvA[:, mt, :].rearrange("p (h e) -> p h e", e=65)[:, :, 0:64],
                   KV[:, 256:512].rearrange("p (h d) -> p h d", h=4), eng='pool')

        def layer_common(x_src, nrows, xt, preloaded=False):
            if not preloaded:
                load_h(x_src, nrows, xt, (pb[0], pb[1]))
            for c in range(16):
                bank = pb[2 + c // 4]
                for kt in range(8):
                    P.matmul(bank[:, (c % 4) * 128:(c % 4 + 1) * 128], W[:, kt, c * 128:(c + 1) * 128], hT[:, kt, :],
                             start=(kt == 0), stop=(kt == 7))
            for c in range(3):
                P.copy(raw[:, c * 4:(c + 1) * 4, 3:131], pb[2 + c][:].rearrange("p (a b) -> p a b", a=4), eng='act')
            for h in range(8):
                bank = pb[6 + h // 4]
                for kt in range(8):
                    P.matmul(bank[0:32, (h % 4) * 128:(h % 4 + 1) * 128], W[:, kt, C_QI + h * 32:C_QI + (h + 1) * 32],
                             hT[:, kt, :], start=(kt == 0), stop=(kt == 7))
            P.copy(qiT[:, 0:4, :], pb[6][0:32, :].rearrange("p (a b) -> p a b", a=4), eng='act')
            P.copy(qiT[:, 4:8, :], pb[7][0:32, :].rearrange("p (a b) -> p a b", a=4), eng='act')
            ck(10)
            P.copy(zas[:], pb[5][:].rearrange("p (a b) -> p a b", a=4), eng='act')
            for ci, c0 in enumerate(range(0, TMW, 512)):
                wd = min(512, TMW - c0)
                bank = pb[ci % 2]
                for kt in range(8):
                    P.matmul(bank[:, 0:wd], hT[:, kt, :], W[:, kt, C_TM + c0:C_TM + c0 + wd], start=(kt == 0), stop=(kt == 7))
                P.copy(tm[:, c0:c0 + wd], bank[:, 0:wd], eng='act')
            ck(12)

        def chainA(sample, tail=None):
            tri, ls = (triS, lsS) if sample else (triP, lsP)
            beta, gg, gc, nbeta, egc, ekd, c1 = (gt[:, 0:4], gt[:, 4:8], gt[:, 8:12], gt[:, 12:16],
                                                 gt[:, 16:20], gt[:, 20:24], gt[:, 24:28])

            def a1():
                for c in range(12):
                    P.ts(acc[:, c, :], raw[:, c, 0:128], wc[:, c, 0:1], ALU.mult)
                    for j in range(1, 4):
                        P.stt(acc[:, c, :], raw[:, c, j:j + 128], wc[:, c, j:j + 1], acc[:, c, :], ALU.mult, ALU.add)
                    if c % 3 == 2:
                        yield
                P.copy(raw[:, :, 0:3], raw[:, :, 128:131], eng='pool')
                P.act(acc[:, 0:8, :], acc[:, 0:8, :], AF.Silu)
                P.act(vsb[:], acc[:, 8:12, :], AF.Silu)
                P.act(zas[:], zas[:], AF.Silu)
                P.act(zs[:, 0:256], tm[:, T_ZB:T_ZB + 256], AF.Silu)
                P.act(zs[:, 256:512], tm[:, T_ZM:T_ZM + 256], AF.Silu)
                yield
                P.tt(sqb[:], acc[:, 0:8, :], acc[:, 0:8, :], ALU.mult, eng='pool')
                for g_ in range(2):
                    P.matmul(pb[2 + g_][:], onesb[:], sqb[:, g_ * 4:(g_ + 1) * 4, :].rearrange("p a b -> p (a b)"))
                P.act(rbc[:, 0:4, :], pb[2][:].rearrange("p (a b) -> p a b", a=4), AF.Ln, bias=EPS)
                P.act(rbc[:, 4:8, :], pb[3][:].rearrange("p (a b) -> p a b", a=4), AF.Ln, bias=EPS)
                P.act(rbc[:, 0:4, :], rbc[:, 0:4, :], AF.Exp, scale=-0.5, bias=float(np.log(128.0 ** -0.5)))
                P.act(rbc[:, 4:8, :], rbc[:, 4:8, :], AF.Exp, scale=-0.5)
                P.tt(qn[:], acc[:, 0:4, :], rbc[:, 0:4, :], ALU.mult)
                P.tt(knT[:], acc[:, 4:8, :], rbc[:, 4:8, :], ALU.mult)
                if kstop == 13:
                    dbg(raw[:, 4:8, 3:131], 512, stage=None) if False else None
                    dbg(acc[:, 4:8, :].rearrange("p a b -> p (a b)"), 512)
                    dbg(rbc[:, 4:8, :].rearrange("p a b -> p (a b)"), 512)
                    dbg(acc[:, 0:4, :].rearrange("p a b -> p (a b)"), 512)
                    dbg(rbc[:, 0:4, :].rearrange("p a b -> p (a b)"), 512)
                    dbg(knT[:].rearrange("p a b -> p (a b)"), 512, stage=tm)
                    dbg(qn[:].rearrange("p a b -> p (a b)"), 512, stage=tm)
                ck(13)
                yield

                for h in range(4):
                    P.matmul(pb[2][:, h * 128:(h + 1) * 128], knT[:, h, :], idb[:])
                    P.matmul(pb[3][:, h * 128:(h + 1) * 128], vsb[:, h, :], idb[:])

            def a2():
                P.act(beta, tm[:, T_BA:T_BA + 4], AF.Exp, scale=-1.0)
                P.ts(beta, beta, 1.0, ALU.add)
                P.recip(beta, beta)
                P.ts(nbeta, beta, -1.0, ALU.mult)
                P.tt(gg, tm[:, T_AA:T_AA + 4], gsm[:, G_DTB:G_DTB + 4], ALU.add)
                P.act(gg, gg, AF.Exp)
                P.act(gg, gg, AF.Ln, bias=1.0)
                P.tt(gg, gg, negA[:], ALU.mult)
                ck(131)
                yield
                blk = blkS if sample else blkP
                P.matmul(pb[4][:, 0:4], tri[:], gg)
                P.matmul(pb[4][:, 4:8], blk[:], gg)
                P.copy(gc, pb[4][:, 0:4])
                P.tt(ekd, pb[4][:, 4:8], gc, ALU.subtract)
                P.act(ekd, ekd, AF.Exp)
                P.act(egc, gc, AF.Exp)
                P.tt(c1, beta, egc, ALU.mult)
                ck(132)
                yield
                for h in range(4):
                    P.ts(Dg[:, h, :], tri[:], gg[:, h:h + 1], ALU.mult)
                    P.matmul(pb[5][:, h * 128:(h + 1) * 128], ones32[:], Dg[:, h, :])
                gcr = pb[5][:].rearrange("p (a b) -> p a b", a=4)
                ck(133)
                yield
                P.act(egr[:], gcr, AF.Exp)
                ck(134)
                yield
                P.copy(DT[:], gcr, eng='act')
                for h in range(4):
                    P.ts(Dst[:, h, :], DT[:, h, :], gc[:, h:h + 1], ALU.subtract, 0.0, ALU.max)
                    P.ts(DT[:, h, :], DT[:, h, :], gc[:, h:h + 1], ALU.subtract, 0.0, ALU.min)
                ck(135)
                yield
                P.act(Dst[:], Dst[:], AF.Exp, scale=-1.0)
                P.act(DT[:], DT[:], AF.Exp)
                ck(136)
                yield
                P.tt(Dst[:], Dst[:], ls[:].unsqueeze(1).to_broadcast([128, 4, 128]), ALU.mult)
                P.tt(DT[:], DT[:], tri[:].unsqueeze(1).to_broadcast([128, 4, 128]), ALU.mult)
                ck(14)
                yield


            def mix(g1, g2):
                d1 = d2 = False
                while not (d1 and d2):
                    if not d1:
                        try:
                            next(g1)
                            yield
                        except StopIteration:
                            d1 = True
                    if not d2:
                        try:
                            next(g2)
                            yield
                        except StopIteration:
                            d2 = True
            yield from mix(a1(), a2())
            ktm = pb[2][:].rearrange("p (a b) -> p a b", a=4)
            vtm = pb[3][:].rearrange("p (a b) -> p a b", a=4)
            P.tt(bv[:], vtm, beta.unsqueeze(2).to_broadcast([128, 4, 128]), ALU.mult)
            P.tt(kbg[:], ktm, c1.unsqueeze(2).to_broadcast([128, 4, 128]), ALU.mult)
            P.tt(kd[:], ktm, ekd.unsqueeze(2).to_broadcast([128, 4, 128]), ALU.mult)
            ck(15)
            yield

            for h in range(4):
                P.matmul(pb[2][:, h * 128:(h + 1) * 128], knT[:, h, :], knT[:, h, :])
            for h in range(4):
                P.ts(Dst[:, h, :], Dst[:, h, :], nbeta[:, h:h + 1], ALU.mult)
            P.tt(Xm[:], pb[2][:].rearrange("p (a b) -> p a b", a=4), Dst[:], ALU.mult)
            for h in range(4):
                P.matmul(pb[3][:, h * 128:(h + 1) * 128], Xm[:, h, :], id32[:])
            P.copy(Ym[:], pb[3][:].rearrange("p (a b) -> p a b", a=4), eng='act')
            P.tt(Pm[:], Ym[:], id32[:].unsqueeze(1).to_broadcast([128, 4, 128]), ALU.add)
            if kstop == 16:
                dbg(Xm[:].rearrange("p a b -> p (a b)"), 512)
                dbg(Ym[:].rearrange("p a b -> p (a b)"), 512)
                dbg(Pm[:].rearrange("p a b -> p (a b)"), 512)
                dbg(DT[:].rearrange("p a b -> p (a b)"), 512)
                dbg(gt[:, 0:12], 12)
                dbg(knT[:].rearrange("p a b -> p (a b)"), 512, stage=tm)
            ck(16)
            yield
            nlev = 3 if sample else 5
            for lv in range(1, nlev + 1):
                for h in range(4):
                    P.matmul(pb[2][:, h * 128:(h + 1) * 128], Ym[:, h, :], Xm[:, h, :])
                if lv < nlev:
                    for h in range(4):
                        P.matmul(pb[3][:, h * 128:(h + 1) * 128], Xm[:, h, :], Ym[:, h, :])
                P.copy(Xm[:], pb[2][:].rearrange("p (a b) -> p a b", a=4), eng='act')
                if lv < nlev:
                    P.copy(Ym[:], pb[3][:].rearrange("p (a b) -> p a b", a=4), eng='act')
                for h in range(4):
                    P.matmul(pb[4][:, h * 128:(h + 1) * 128], Xm[:, h, :], Pm[:, h, :])
                P.tt(Pm[:], Pm[:], pb[4][:].rearrange("p (a b) -> p a b", a=4), ALU.add)
                yield
            P.copy(Tt[:], Pm[:], eng='act')
            ck(17)
            yield

            for h in range(4):
                P.matmul(pb[2][:, h * 128:(h + 1) * 128], kbg[:, h, :], Tt[:, h, :])
            P.act(nwk[:], pb[2][:].rearrange("p (a b) -> p a b", a=4), AF.Copy, scale=-1.0)
            P.tt(qd[:], qn[:], egr[:], ALU.mult)
            for h in range(4):
                P.matmul(pb[3][:, h * 128:(h + 1) * 128], knT[:, h, :], qn[:, h, :])
            P.tt(qkT[:], pb[3][:].rearrange("p (a b) -> p a b", a=4), DT[:], ALU.mult)
            ck(18)
            yield

            chunks = [(0, 16)] if sample else [(0, 64), (64, 128)]
            obank = pb[4]
            for (r0, r1) in chunks:
                for h in range(4):
                    P.matmul(pb[5][:, h * 128:(h + 1) * 128], Tt[:, h, :], bv[:, h, :], start=True, stop=False)
                    P.matmul(pb[5][:, h * 128:(h + 1) * 128], nwk[:, h, :], Sb[:, h, :], start=False, stop=True)
                P.copy(usb[r0:r1, :, :], pb[5][r0:r1, :].rearrange("p (a b) -> p a b", a=4), eng='act')
                for h in range(4):
                    oc = obank[:, h * 128 + r0:h * 128 + r1]
                    P.matmul(oc, Sb[:, h, :], qd[:, h, r0:r1], start=True, stop=False)
                    P.matmul(oc, usb[r0:r1, h, :], qkT[r0:r1, h, r0:r1], start=False, stop=True)
                for h in range(4):
                    P.matmul(pb[2][:, h * 128:(h + 1) * 128], kd[r0:r1, h, :], usb[r0:r1, h, :])
                for h in range(4):
                    P.ts(S[:, h, :], S[:, h, :], egr[:, h, r1 - 1:r1], ALU.mult)
                P.tt(S[:], S[:], pb[2][:].rearrange("p (a b) -> p a b", a=4), ALU.add)
                P.copy(Sb[:], S[:], eng='act')
                yield
            o3 = obank[:].rearrange("p (a b) -> p a b", a=4)
            ck(19)
            yield
            P.act(sqb[:, 0:4, :], o3, AF.Square)
            P.matmul(pb[3][:], onesb[:], sqb[:, 0:4, :].rearrange("p a b -> p (a b)"))
            P.act(rbc[:, 0:4, :], pb[3][:].rearrange("p (a b) -> p a b", a=4), AF.Ln, scale=1.0 / 128, bias=EPS)
            P.act(rbc[:, 0:4, :], rbc[:, 0:4, :], AF.Exp, scale=-0.5)
            P.tt(rbc[:, 0:4, :], rbc[:, 0:4, :], o3, ALU.mult)
            P.stt(mixT[:, 0:4, :], rbc[:, 0:4, :], gocol[:, 0:1], zas[:], ALU.mult, ALU.mult)
            ck(20)
            yield
            if tail is not None:
                tail()
                yield


        def chainB_heads(nrows, outs):
            P.tt(sqB[:, 0:512], tm[:, T_QB:T_QB + 512], tm[:, T_QB:T_QB + 512], ALU.mult)
            P.reduce(s5[:, 0:8], sqB[:, 0:512].rearrange("p (h d) -> p h d", d=64), ALU.add)
            P.tt(sqB[:, 0:256], tm[:, T_QM:T_QM + 256], tm[:, T_QM:T_QM + 256], ALU.mult)
            P.reduce(s5[:, 8:12], sqB[:, 0:256].rearrange("p (h d) -> p h d", d=64), ALU.add)
            P.tt(sqB[:, 256:288], tm[:, T_KI:T_KI + 32], tm[:, T_KI:T_KI + 32], ALU.mult)
            P.reduce(s5[:, 12:13], sqB[:, 256:288], ALU.add)
            yield
            P.ts(s5[:, 12:13], s5[:, 12:13], 2.0, ALU.mult)
            rstd_inplace(s5[:, 0:13], 1.0 / 64)
            k3 = tm[:, T_KB:T_KB + 256].rearrange("p (h d) -> p h d", h=4)
            n3 = knf[:].rearrange("p (h d) -> p h d", h=4)
            P.tt(n3, k3, s5[:, 4:8].unsqueeze(2).to_broadcast([128, 4, 64]), ALU.mult)
            P.tt(n3, n3, gsm[:, G_KB:G_KB + 64].unsqueeze(1).to_broadcast([128, 4, 64]), ALU.mult)
            P.dma(outs["k"], knf[0:nrows, :], eng='pool')
            P.dma(outs["v"], tm[0:nrows, T_VB:T_VB + 256], eng='pool')
            P.copy(nrm[:, 256:512], knf[:], eng='pool')
            q3 = tm[:, T_QB:T_QB + 256].rearrange("p (h d) -> p h d", h=4)
            yield
            m3 = sqB[:, 0:256].rearrange("p (h d) -> p h d", h=4)
            P.tt(m3, q3, s5[:, 0:4].unsqueeze(2).to_broadcast([128, 4, 64]), ALU.mult)
            P.tt(nrm[:, 0:256].rearrange("p (h d) -> p h d", h=4), m3,
                 gsm[:, G_QB:G_QB + 64].unsqueeze(1).to_broadcast([128, 4, 64]), ALU.mult)
            q3 = tm[:, T_QM:T_QM + 256].rearrange("p (h d) -> p h d", h=4)
            m3 = sqB[:, 256:512].rearrange("p (h d) -> p h d", h=4)
            P.tt(m3, q3, s5[:, 8:12].unsqueeze(2).to_broadcast([128, 4, 64]), ALU.mult)
            P.tt(nrm[:, 512:768].rearrange("p (h d) -> p h d", h=4), m3,
                 gsm[:, G_QM:G_QM + 64].unsqueeze(1).to_broadcast([128, 4, 64]), ALU.mult)
            P.ts(kxn[:], tm[:, T_KI:T_KI + 32], s5[:, 12:13], ALU.mult)
            P.tt(kxn[:], kxn[:], gsm[:, G_KI:G_KI + 32], ALU.mult)
            P.dma(outs["ki"], kxn[0:nrows, :], eng='pool')
            P.copy(kxb[:], kxn[:], eng='pool')
            kslot = outs["kslot"]
            kc = slice(kslot * 128, (kslot + 1) * 128)
            yield
            transpose_bf(QbT[:], [nrm[:, 0:128], nrm[:, 128:256]], pb[0])
            transpose_bf(KbT[:, :, kc], [nrm[:, 256:384], nrm[:, 384:512]], pb[1])
            yield
            transpose_bf(QmT[:], [nrm[:, 512:640], nrm[:, 640:768]], pb[6])
            P.matmul(pb[7][0:32, 0:128], kxb[:], idb[:])
            P.copy(kiT[:, kc], pb[7][0:32, 0:128])
            yield
            va = Vaug[:, kslot, :].rearrange("p (h e) -> p h e", e=65)
            P.memset(va[:, :, 64:65], 1.0)
            P.copy(va[:, :, 0:64], tm[:, T_VB:T_VB + 256].rearrange("p (h d) -> p h d", h=4), eng='pool')
            P.ts(wabs[:], tm[:, T_WI:T_WI + 8], IDX_SCALE, ALU.mult)
            for h in range(8):
                P.ts(wdiag[:, h, :], id32[:], wabs[:, h:h + 1], ALU.mult, eng=('dve' if h % 2 == 0 else 'pool'))
            yield

        def conv_out(nrows, dst):
            for c in range(3):
                pc = pb[c % 2]
                for kt in range(8):
                    P.matmul(pc[:], hT[:, kt, :], W[:, kt, c * 512:(c + 1) * 512], start=(kt == 0), stop=(kt == 7))
                P.copy(cv[:, c * 512:(c + 1) * 512], pc[:], eng=('act' if c % 2 == 0 else 'dve'))
            P.dma(dst, cv[nrows - 3:nrows, :], eng='pool')

        def index_scores(sc, col0, ktiles, kbase, alt=False):
            for g0 in range(0, ktiles, 4):
                nk = min(4, ktiles - g0) * 128
                kcs = slice((kbase + g0) * 128, (kbase + g0) * 128 + nk)
                dst = sc[:, col0 + g0 * 128:col0 + g0 * 128 + nk]

                def logits(h):
                    P.matmul(pb[h % 2][:, 0:nk], qiT[:, h, :], kiT[:, kcs])

                def relu_sum(h):
                    r_ = rl[h % 2][:].bitcast(BF16)
                    if alt and h % 2 == 1:
                        P.ts(r_[:, 0:nk], pb[h % 2][:, 0:nk], 0.0, ALU.max)
                    else:
                        P.act(r_[:, 0:nk], pb[h % 2][:, 0:nk], AF.Relu)
                    return r_
                logits(0)
                for h in range(8):
                    r_ = relu_sum(h)
                    if h + 1 < 8:
                        logits(h + 1)
                    P.matmul(pb[6][:, 0:nk], wdiag[:, h, :], r_[:, 0:nk], start=(h == 0), stop=(h == 7))
                    if h % 2 == 1:
                        yield
                P.copy(dst, pb[6][:, 0:nk], eng='act')
                yield

        def bisect(sc, ncols, lo_cols):
            thr, w0, t_, cnt, hh = bis[:, 0:1], bis[:, 1:2], bis[:, 2:3], bis[:, 3:4], bis[:, 4:5]
            P.reduce(w0, sc[:, 0:ncols], ALU.max)
            P.reduce(thr, sc[:, 0:lo_cols], ALU.min)
            P.ts(thr, thr, -1.0, ALU.add)
            P.tt(w0, w0, thr, ALU.subtract)
            P.ts(wtab[:], p2[:], w0, ALU.mult)
            P.tt(t_, thr, wtab[:, 0:1], ALU.add)
            for k in range(NBIS):
                jk = jk8
                for ci, c0 in enumerate(range(0, ncols, 2048)):
                    wd = min(2048, ncols - c0)
                    P.ts(jk[:, 0:wd], sc[:, c0:c0 + wd], t_, ALU.is_gt, (None if ci == 0 else cnt), ALU.add,
                         accum_out=cnt)
                P.ts(hh, cnt, 255.5, ALU.is_ge, 0.5, ALU.subtract)
                P.stt(t_, hh, wtab[:, k:k + 1], t_, ALU.mult, ALU.add)
                yield
            P.tt(thr, t_, wtab[:, NBIS:NBIS + 1], ALU.subtract)

        def attend(qT, keys, obank_, first, last, mask_cols=None, sc=None):
            n = len(keys)

            def scores_t(i):
                kT, va = keys[i]
                bank = pb[i % 2]
                nm = None
                if mask_cols is not None:
                    if i % 8 == 0:
                        nb_ = min(8, n - i)
                        assert all(mask_cols[i + q_] == mask_cols[i] + q_ * 128 for q_ in range(nb_))
                        P.ts(nmw[:, 0:nb_ * 128], sc[:, mask_cols[i]:mask_cols[i] + nb_ * 128], bis[:, 0:1], ALU.is_le)
                    nm = nmw[:, (i % 8) * 128:(i % 8 + 1) * 128]
                for h in range(4):
                    pr = slice((h % 2) * 64, (h % 2) * 64 + 64)
                    oc = bank[:, h * 128:(h + 1) * 128]
                    P.matmul(oc, kT[pr, h // 2, :], qT[pr, h // 2, :], start=True, stop=False)
                    P.matmul(oc, (nm if nm is not None else zerob[:]), negI[:], start=False, stop=True)

            scores_t(0)
            for i in range(n):
                kT, va = keys[i]
                pt = PT[i % 2]
                P.act(pt[:], pb[i % 2][:].rearrange("p (a b) -> p a b", a=4), AF.Exp, scale=0.125)
                if i + 1 < n:
                    scores_t(i + 1)
                for h in range(4):
                    P.matmul(obank_[:, h * 65:(h + 1) * 65], pt[:, h, :], va[:, h, :],
                             start=(first and i == 0 and h == 0), stop=(last and i == n - 1 and h == 3))
                yield

        def finish_heads(obank_, zcols, dst_cols):
            P.copy(sqB[:, 0:260], obank_[:, 0:260], eng='act')
            o3 = sqB[:, 0:260].rearrange("p (h e) -> p h e", e=65)
            P.copy(rcp[:, 0:4], o3[:, :, 64])
            P.recip(rcp[:, 0:4], rcp[:, 0:4])
            P.tt(o3[:, :, 0:64], o3[:, :, 0:64], rcp[:, 0:4].unsqueeze(2).to_broadcast([128, 4, 64]), ALU.mult)
            P.tt(mixBC[:, dst_cols:dst_cols + 256].rearrange("p (h d) -> p h d", h=4), o3[:, :, 0:64],
                 zs[:, zcols:zcols + 256].rearrange("p (h d) -> p h d", h=4), ALU.mult)
            yield

        def mix_bc_T(bank):
            transpose_bf(mixT[:, 4:8, :], [mixBC[:, j * 128:(j + 1) * 128] for j in range(4)], bank)

        def out_proj(nrows, y_dst, xt):
            for c in range(2):
                bank = pb[c]
                for kt in range(8):
                    P.matmul(bank[:], mixT[:, kt, :], Wo[:, kt, c * 512:(c + 1) * 512], start=(kt == 0), stop=(kt == 7))
                P.tt(yo[:, c * 512:(c + 1) * 512], bank[:], xt[:, c * 512:(c + 1) * 512], ALU.add)
            P.dma(y_dst, yo[0:nrows, :])

        def dbg(ap2d, ncols, stage=None):
            if not kstop:
                return
            i = dbg_n[0]
            dbg_n[0] += 1
            if stage is not None:
                P.copy(stage[:, 0:ncols], ap2d)
                P.dma(dbg_outs[i][:, 0:ncols], stage[:, 0:ncols], eng='pool')
            else:
                P.dma(dbg_outs[i][:, 0:ncols], ap2d, eng='pool')

        def drain(g):
            for _ in g:
                pass

        def count_steps(mk):
            P.dry = True
            n = 0
            for _ in mk():
                n += 1
            P.dry = False
            return n + 1

        def interleave(mka, mkb):
            na, nb = count_steps(mka), count_steps(mkb)
            ga, gb = mka(), mkb()
            ia = ib = 0
            da = db = False
            while not (da and db):
                pick_a = (not da) and (db or ia * nb <= ib * na)
                if pick_a:
                    try:
                        next(ga)
                        ia += 1
                    except StopIteration:
                        da = True
                else:
                    try:
                        next(gb)
                        ib += 1
                    except StopIteration:
                        db = True

        def mixg(g1, g2):
            d1 = d2 = False
            while not (d1 and d2):
                if not d1:
                    try:
                        next(g1)
                        yield
                    except StopIteration:
                        d1 = True
                if not d2:
                    try:
                        next(g2)
                        yield
                    except StopIteration:
                        d2 = True

        def prompt_chainB(tt, outs):
            yield from chainB_heads(128, outs)
            nk = tt + 1
            yield from index_scores(scores, 0, nk, 0)
            dcol = tt * 128
            P.stt(scores[:, dcol:dcol + 128], scores[:, dcol:dcol + 128], 1.0, admP[:], ALU.mult, ALU.mult)
            P.tt(scores[:, dcol:dcol + 128], scores[:, dcol:dcol + 128], nadmP[:], ALU.add)
            mkeys = [(MkT[:, :, j * 128:(j + 1) * 128], MvA[:, j, :].rearrange("p (h e) -> p h e", e=65)) for j in range(2)]

            def cpart():
                yield from attend(QmT, mkeys, pb[7], True, True)
                yield from finish_heads(pb[7], 256, 256)
            if tt >= 2:
                yield from mixg(bisect(scores, nk * 128, (nk - 1) * 128), cpart())
            else:
                P.memset(bis[:, 0:1], -1.0e29)
                yield from cpart()
            keys = [(KbT[:, :, j * 128:(j + 1) * 128], Vaug[:, j, :].rearrange("p (h e) -> p h e", e=65)) for j in range(nk)]
            yield from attend(QbT, keys, pb[6], True, True, mask_cols=[j * 128 for j in range(nk)], sc=scores)
            yield from finish_heads(pb[6], 0, 0)
            mix_bc_T(pb[0])
            yield

        try:
            for mt in range(2):
                mem_tile(mt)
            ck(2)
            P.memset(raw[:, :, 0:3], 0.0)
            P.memset(S[:], 0.0)
            P.memset(Sb[:], 0.0)
            for tt in range(NT):
                r = slice(tt * 128, (tt + 1) * 128)
                outs = {"k": p_k[r, :], "v": p_v[r, :], "ki": p_ki[r, :], "kslot": tt}
                xt = xtb[tt % 2]
                layer_common(x_p[r, :], 128, xt, preloaded=(tt > 0 and not kstop))
                tail = None
                if tt + 1 < NT and not kstop:
                    r2 = slice((tt + 1) * 128, (tt + 2) * 128)
                    tail = (lambda r2=r2, tt=tt: load_h(x_p[r2, :], 128, xtb[(tt + 1) % 2], (pb[2], pb[3])))
                if kstop:
                    drain(chainA(False))
                    drain(prompt_chainB(tt, outs))
                else:
                    interleave(lambda tail=tail: chainA(False, tail), lambda tt=tt, outs=outs: prompt_chainB(tt, outs))
                if tt == NT - 1:
                    conv_out(128, p_conv)
                out_proj(128, y_p[r, :], xt)
                ck(25)
            P.dma(p_ssm.rearrange("h k v -> k h v"), S[:], eng='pool')
            ck(30)

            P.dma(S[:], st_ssm.rearrange("h k v -> k h v"))
            P.copy(Sb[:], S[:])
            P.dma(stc, st_conv)
            for c in range(12):
                P.matmul(pb[2][:, c * 3:(c + 1) * 3], stc[0:3, c * 128:(c + 1) * 128], id32[0:3, 0:3])
            P.copy(raw[:, :, 0:3], pb[2][:, 0:36].rearrange("p (c j) -> p c j", j=3))
            for mt in range(2):
                r = slice(mt * 128, (mt + 1) * 128)
                KV = kvc[mt]
                P.dma(KV[:, 0:256], c_mk[r, :])
                P.dma(KV[:, 256:512], c_mv[r, :])
                P.copy(kvb[:], KV[:, 0:256])
                transpose_bf(MkT[:, :, r], [kvb[:, 0:128], kvb[:, 128:256]], pb[3])
                P.copy(MvA[:, mt, :].rearrange("p (h e) -> p h e", e=65)[:, :, 0:64],
                       KV[:, 256:512].rearrange("p (h d) -> p h d", h=4), eng='pool')
            outs = {"k": s_k, "v": s_v, "ki": s_ki, "kslot": 0}
            layer_common(x_s, 16, xtb[0])
            drain(chainA(True))
            drain(chainB_heads(16, outs))
            conv_out(16, s_conv)
            P.dma(s_ssm.rearrange("h k v -> k h v"), S[:], eng='pool')
            ck(31)
            Wf = W[:].rearrange("p a b -> p (a b)").bitcast(F32)
            Wb = W[:].rearrange("p a b -> p (a b)")
            kiS = Wf[:, 4224:5248].rearrange("p (t c) -> p t c", c=32)
            KS = Wf[:, 5248:9344].rearrange("p (t c) -> p t c", c=256)
            VS = Wf[:, 9344:13440].rearrange("p (t c) -> p t c", c=256)
            Kb16 = Wb[:, 26880:30976].rearrange("p (t c) -> p t c", c=256)
            kib = Wb[:, 18688:19712].rearrange("p (t c) -> p t c", c=32)
            def dma_tiles(dst, src_rows, ntile, step):
                v = src_rows.rearrange("(t p) c -> p t c", p=128)
                for t0 in range(0, ntile, step):
                    P.dma(dst[:, t0:t0 + step, :], v[:, t0:t0 + step, :])
            dma_tiles(kiS, c_ki, 32, 8)
            dma_tiles(KS, c_k[0:2048, :], 16, 8)
            drain(index_scores(scoresS, PAST, 1, 0, alt=True))
            P.tt(scoresS[:, PAST:PAST + 128], scoresS[:, PAST:PAST + 128], nadmS[:], ALU.add)
            P.copy(kib, kiS)
            for g_ in range(2):
                for q4 in range(4):
                    bank = pb[2 + q4]
                    for j in range(4):
                        t_ = g_ * 16 + q4 * 4 + j
                        P.matmul(bank[0:32, j * 128:(j + 1) * 128], kib[:, t_, :], idb[:])
                    P.copy(kiT[:, q4 * 512:(q4 + 1) * 512], bank[0:32, :], eng='act')
                drain(index_scores(scoresS, g_ * 2048, 16, 0, alt=True))
            dma_tiles(VS, c_v[0:2048, :], 16, 8)
            mkeys = [(MkT[:, :, j * 128:(j + 1) * 128], MvA[:, j, :].rearrange("p (h e) -> p h e", e=65)) for j in range(2)]

            def cpart_s():
                yield from attend(QmT, mkeys, pb[7], True, True)
                yield from finish_heads(pb[7], 256, 256)
            drain(mixg(bisect(scoresS, PAST + 128, PAST), cpart_s()))
            ck(32)
            keys = [(KbT[:, :, 0:128], Vaug[:, 0, :].rearrange("p (h e) -> p h e", e=65))]
            drain(attend(QbT, keys, pb[6], True, False, mask_cols=[PAST], sc=scoresS))
            for g_ in range(2):
                if g_ == 1:
                    dma_tiles(KS, c_k[2048:4096, :], 16, 8)
                    dma_tiles(VS, c_v[2048:4096, :], 16, 8)
                P.copy(Kb16[:, 0:6, :], KS[:, 0:6, :])
                P.copy(Kb16[:, 6:11, :], KS[:, 6:11, :], eng='act')
                P.copy(Kb16[:, 11:16, :], KS[:, 11:16, :], eng='pool')
                for q2 in range(8):
                    bank = pb[2 + q2 % 4]
                    for pr_ in range(2):
                        for j in range(2):
                            t_ = q2 * 2 + j
                            P.matmul(bank[:, (pr_ * 2 + j) * 128:(pr_ * 2 + j + 1) * 128],
                                     Kb16[:, t_, pr_ * 128:(pr_ + 1) * 128], idb[:])
                    P.copy(KbT[:, :, q2 * 256:(q2 + 1) * 256], bank[:].rearrange("p (a b) -> p a b", a=2), eng='act')
                v4 = Vaug[:, 0:16, :].rearrange("p t (h e) -> p t h e", e=65)
                for t4 in range(4):
                    P.copy(v4[:, t4 * 4:(t4 + 1) * 4, :, 0:64].rearrange("p t h d -> p (t h) d"),
                           VS[:, t4 * 4:(t4 + 1) * 4, :].rearrange("p t (h d) -> p (t h) d", d=64),
                           eng=('pool' if t4 % 2 == 0 else 'dve'))
                keys = [(KbT[:, :, j * 128:(j + 1) * 128], Vaug[:, j, :].rearrange("p (h e) -> p h e", e=65)) for j in range(16)]
                drain(attend(QbT, keys, pb[6], False, g_ == 1, mask_cols=[g_ * 2048 + j * 128 for j in range(16)], sc=scoresS))
            drain(finish_heads(pb[6], 0, 0))
            mix_bc_T(pb[0])
            out_proj(16, y_s, xtb[0])
        except _Stop:
            pass
        P.emit(final_wait_engine='pool')
    return nc


_CACHE = {}


def _make_in_maps(inp):
    g_all = np.ascontiguousarray(np.concatenate([inp['g_in'][0].reshape(8, 128), inp['g_mem'][0].reshape(8, 128)], 0))
    gsm = np.ascontiguousarray(np.concatenate([
        inp['g_k_B'][0], inp['g_kidx_B'][0], inp['g_k_M'][0], inp['g_q_B'][0], inp['g_q_M'][0],
        inp['dt_bias_A'][0], inp['a_log_A'][0]]).reshape(1, GSM).astype(np.float32))
    maps = []
    for b in range(8):
        maps.append({
            "x_p": np.ascontiguousarray(inp['x_prompt'][b]),
            "x_s": np.ascontiguousarray(inp['x_sample'][b]),
            "mem": np.ascontiguousarray(inp['mem_prompt'][b]),
            "w_in": np.ascontiguousarray(inp['w_in'][0]),
            "w_mem": np.ascontiguousarray(inp['w_mem_kv'][0]),
            "w_out": np.ascontiguousarray(inp['w_out'][0]),
            "g_all": g_all,
            "gsm": gsm,
            "g_o": np.ascontiguousarray(inp['g_o_A'][0].reshape(1, 128)),
            "conv_w": np.ascontiguousarray(inp['conv_w_A'][0]),
            "st_conv": np.ascontiguousarray(inp['state_conv_A'][0, b]),
            "st_ssm": np.ascontiguousarray(inp['state_ssm_A'][0, b]),
            "c_k": np.ascontiguousarray(inp['cache_k_B'][0, b].reshape(PAST, 256)),
            "c_v": np.ascontiguousarray(inp['cache_v_B'][0, b].reshape(PAST, 256)),
            "c_ki": np.ascontiguousarray(inp['cache_kidx_B'][0, b]),
            "c_mk": np.ascontiguousarray(inp['cache_mem_k'][0, b].reshape(256, 256)),
            "c_mv": np.ascontiguousarray(inp['cache_mem_v'][0, b].reshape(256, 256)),
        })
    return maps


def _assemble(res):
    def stack(name, shape):
        return np.stack([np.asarray(r[name], dtype=np.float32).reshape(shape) for r in res], 0)
    outs = (
        stack("y_p", (SEQ, D)), stack("y_s", (16, D)),
        stack("p_conv", (3, 1536))[None],
        stack("p_ssm", (4, 128, 128))[None],
        stack("p_k", (SEQ, 4, 64))[None],
        stack("p_v", (SEQ, 4, 64))[None],
        stack("p_ki", (SEQ, 32))[None],
        stack("p_mk", (256, 4, 64))[None],
        stack("p_mv", (256, 4, 64))[None],
        stack("s_conv", (3, 1536))[None],
        stack("s_ssm", (4, 128, 128))[None],
        stack("s_k", (16, 4, 64))[None],
        stack("s_v", (16, 4, 64))[None],
        stack("s_ki", (16, 32))[None],
    )
    return outs


def kernel(**inputs):
    inp = {k: np.asarray(v) for k, v in inputs.items()}
    nc = build_program()
    in_maps = _make_in_maps(inp)
    res = run_bass_kernel_spmd(nc, in_maps, core_ids=list(range(8)))
    return _assemble(res.results)
```
